# Optimizing a Trainium2 kernel written in Bass

```python
import math
import jax, jax.numpy as jnp
from jax import lax
import numpy as np

D_MODEL = 1024
BATCH = 8
SEQ = 2048
DEPTH = 4

SSM_WIDTH = 512
SSM_GROUP = 16
SSM_GROUPS = SSM_WIDTH // SSM_GROUP
SSM_STATE = 64
DT_MIN = 1e-3
DT_MAX = 1e-1
HEAD_DIM = 128
HEADS_PER_GROUP = 4
DILATION_PATTERNS = ((128, 1), (512, 4), (2048, 16))
N_ATTN_GROUPS = len(DILATION_PATTERNS)
ATTN_HEADS = N_ATTN_GROUPS * HEADS_PER_GROUP
ATTN_QKV_WIDTH = ATTN_HEADS * HEAD_DIM
ATTN_WIDTH = HEADS_PER_GROUP * HEAD_DIM
IN_SPLITS = (SSM_WIDTH, SSM_WIDTH, ATTN_QKV_WIDTH, ATTN_QKV_WIDTH, ATTN_QKV_WIDTH,
             ATTN_WIDTH, D_MODEL, D_MODEL)
IN_COLS = sum(IN_SPLITS)
SPLIT_POINTS = tuple(int(s) for s in np.cumsum(IN_SPLITS)[:-1])
RMS_EPS = 1e-6

kernel_name = "hybrid_s5_dilated_attn_gated_block"


def rmsnorm(x, g):
    xf = x.astype(jnp.float32)
    inv = lax.rsqrt(jnp.mean(xf * xf, axis=-1, keepdims=True) + RMS_EPS)
    return (xf * inv * g.astype(jnp.float32)).astype(x.dtype)


def _ssm_combine(left, right):
    a_l, b_l = left
    a_r, b_r = right
    return a_r * a_l, a_r * b_l + b_r


def s5_branch(u, lam_re, lam_im, log_dt, b_re, b_im, c_re, c_im, d_skip, w_glu, b_glu):
    bsz, L, _ = u.shape
    uf = u.astype(jnp.float32)
    ug = uf.reshape(bsz, L, SSM_GROUPS, SSM_GROUP)
    lam = lax.complex(jnp.minimum(lam_re.astype(jnp.float32), -1e-4), lam_im.astype(jnp.float32))
    dt = jnp.exp(log_dt.astype(jnp.float32))[:, None]
    lam_bar = jnp.exp(lam * dt)
    b = lax.complex(b_re.astype(jnp.float32), b_im.astype(jnp.float32))
    b_bar = ((lam_bar - 1.0) / lam)[..., None] * b
    c = lax.complex(c_re.astype(jnp.float32), c_im.astype(jnp.float32))
    drive = jnp.einsum('gpc,blgc->blgp', b_bar, ug)
    decay = jnp.broadcast_to(lam_bar, drive.shape)
    _, states = lax.associative_scan(_ssm_combine, (decay, drive), axis=1)
    y = jnp.einsum('gcp,blgp->blgc', c, states).real.reshape(bsz, L, SSM_WIDTH)
    y = y + d_skip.astype(jnp.float32) * uf
    y = jax.nn.gelu(y)
    y = y * jax.nn.sigmoid(y @ w_glu.astype(jnp.float32) + b_glu.astype(jnp.float32))
    return y.astype(u.dtype)


def dilated_group_attention(q, k, v, window, dilation):
    bsz, L, hg, hd = q.shape
    span = window // dilation
    n = L // dilation
    nb = -(-n // span)
    pad = nb * span - n

    def to_sub(t):
        return t.reshape(bsz, n, dilation, hg, hd).transpose(0, 2, 3, 1, 4)

    qb = jnp.pad(to_sub(q), ((0, 0),) * 3 + ((0, pad), (0, 0))).reshape(bsz, dilation, hg, nb, span, hd)

    def kv_blocks(t):
        tp = jnp.pad(to_sub(t), ((0, 0),) * 3 + ((span, pad), (0, 0))).reshape(bsz, dilation, hg, nb + 1, span, hd)
        return jnp.concatenate([tp[:, :, :, :-1], tp[:, :, :, 1:]], axis=4)

    kb, vb = kv_blocks(k), kv_blocks(v)
    scores = jnp.einsum('brhnqd,brhnkd->brhnqk', qb, kb).astype(jnp.float32) * (hd ** -0.5)
    qi = jnp.arange(span)[:, None]
    ki = jnp.arange(2 * span)[None, :]
    blk = jnp.arange(nb)[:, None, None]
    dist = span + qi - ki
    valid = (dist >= 0) & (dist <= span) & (blk * span + ki - span >= 0)
    scores = jnp.where(valid, scores, -jnp.inf)
    lse = jax.nn.logsumexp(scores, axis=-1)
    p = jnp.exp(scores - lse[..., None])
    out = jnp.einsum('brhnqk,brhnkd->brhnqd', p.astype(v.dtype), vb)

    def from_sub(t):
        rest = t.shape[5:]
        t = t.reshape(bsz, dilation, hg, nb * span, *rest)[:, :, :, :n]
        t = jnp.moveaxis(t, 3, 1)
        return t.reshape(bsz, L, hg, *rest)

    return from_sub(out), from_sub(lse)


def dilated_attention_branch(q, k, v):
    bsz, L, _ = q.shape
    shp = (bsz, L, N_ATTN_GROUPS, HEADS_PER_GROUP, HEAD_DIM)
    q, k, v = q.reshape(shp), k.reshape(shp), v.reshape(shp)
    outs, lses = [], []
    for gi, (window, dilation) in enumerate(DILATION_PATTERNS):
        o, s = dilated_group_attention(q[:, :, gi], k[:, :, gi], v[:, :, gi], window, dilation)
        outs.append(o)
        lses.append(s)
    outs = jnp.stack(outs, axis=0)
    alpha = jax.nn.softmax(jnp.stack(lses, axis=0), axis=0)
    y = jnp.sum(alpha[..., None] * outs.astype(jnp.float32), axis=0)
    return y.reshape(bsz, L, ATTN_WIDTH).astype(q.dtype)


def setup_inputs(seed: int = 0) -> dict:
    key = jax.random.key(seed)
    ks = jax.random.split(key, 20)
    f32 = jnp.float32
    nrm = lambda k, shape, s: jax.random.normal(k, shape, f32) * s
    x = jax.random.normal(ks[0], (BATCH, SEQ, D_MODEL), f32)
    pre_norm_g = 1.0 + nrm(ks[1], (DEPTH, D_MODEL), 0.02)
    w_in = nrm(ks[2], (DEPTH, D_MODEL, IN_COLS), D_MODEL ** -0.5)
    lambda_re = -0.5 + nrm(ks[3], (DEPTH, SSM_GROUPS, SSM_STATE), 0.01)
    lambda_im = (math.pi * jnp.arange(SSM_STATE, dtype=f32))[None, None, :] + nrm(ks[4], (DEPTH, SSM_GROUPS, SSM_STATE), 0.01)
    log_dt = jax.random.uniform(ks[5], (DEPTH, SSM_GROUPS), f32, math.log(DT_MIN), math.log(DT_MAX))
    b_scale = (2.0 * SSM_GROUP) ** -0.5
    b_re = nrm(ks[6], (DEPTH, SSM_GROUPS, SSM_STATE, SSM_GROUP), b_scale)
    b_im = nrm(ks[7], (DEPTH, SSM_GROUPS, SSM_STATE, SSM_GROUP), b_scale)
    c_scale = (2.0 * SSM_STATE) ** -0.5
    c_re = nrm(ks[8], (DEPTH, SSM_GROUPS, SSM_GROUP, SSM_STATE), c_scale)
    c_im = nrm(ks[9], (DEPTH, SSM_GROUPS, SSM_GROUP, SSM_STATE), c_scale)
    d_skip = nrm(ks[10], (DEPTH, SSM_WIDTH), 1.0)
    w_glu = nrm(ks[11], (DEPTH, SSM_WIDTH, SSM_WIDTH), SSM_WIDTH ** -0.5)
    b_glu = nrm(ks[12], (DEPTH, SSM_WIDTH), 0.01)
    w_branch_s = nrm(ks[13], (DEPTH, SSM_WIDTH, D_MODEL), SSM_WIDTH ** -0.5)
    w_branch_a = nrm(ks[14], (DEPTH, ATTN_WIDTH, D_MODEL), ATTN_WIDTH ** -0.5)
    w_out = nrm(ks[15], (DEPTH, D_MODEL, D_MODEL), D_MODEL ** -0.5)
    post_norm_g = 1.0 + nrm(ks[16], (DEPTH, D_MODEL), 0.02)
    return {"x": x, "pre_norm_g": pre_norm_g, "w_in": w_in, "lambda_re": lambda_re,
            "lambda_im": lambda_im, "log_dt": log_dt, "b_re": b_re, "b_im": b_im,
            "c_re": c_re, "c_im": c_im, "d_skip": d_skip, "w_glu": w_glu, "b_glu": b_glu,
            "w_branch_s": w_branch_s, "w_branch_a": w_branch_a, "w_out": w_out,
            "post_norm_g": post_norm_g}


def reference(x, pre_norm_g, w_in, lambda_re, lambda_im, log_dt, b_re, b_im, c_re, c_im,
              d_skip, w_glu, b_glu, w_branch_s, w_branch_a, w_out, post_norm_g):
    for l in range(DEPTH):
        h = rmsnorm(x, pre_norm_g[l])
        proj = h @ w_in[l]
        u_s, z_s, q, k, v, z_a, g_s, g_a = jnp.split(proj, SPLIT_POINTS, axis=-1)
        y_s = s5_branch(u_s, lambda_re[l], lambda_im[l], log_dt[l], b_re[l], b_im[l],
                        c_re[l], c_im[l], d_skip[l], w_glu[l], b_glu[l]) * jax.nn.silu(z_s)
        y_a = dilated_attention_branch(q, k, v) * jax.nn.silu(z_a)
        merged = (jax.nn.sigmoid(g_s) * (y_s @ w_branch_s[l])
                  + jax.nn.sigmoid(g_a) * (y_a @ w_branch_a[l]))
        out = merged @ w_out[l]
        x = x + rmsnorm(out, post_norm_g[l]).astype(x.dtype)
    return x
```

```python
import math
from contextlib import ExitStack

import numpy as np
import ml_dtypes

import concourse.bass as bass
import concourse.mybir as mybir
from concourse.bass_utils import run_bass_kernel_spmd

F32 = mybir.dt.float32
BF16 = mybir.dt.bfloat16
ALU = mybir.AluOpType
AF = mybir.ActivationFunctionType

L = 2048
D = 1024
NK = 8
DEPTH = 4
TS = 16
NCH = L // TS
ENGS = ("pe", "act", "dve", "pool", "sp")
STRICT = True


class Res:
    __slots__ = ("w", "r", "name", "dsem", "dcnt")

    def __init__(self, name=""):
        self.w = []
        self.r = []
        self.name = name
        self.dsem = None
        self.dcnt = 0


class Sched:
    def __init__(self, nc, es):
        self.nc = nc
        self.es = es
        self.prog = {e: [] for e in ENGS}
        self.cnt = {e: 0 for e in ENGS}
        self.sem = {e: es.enter_context(nc.semaphore("sem_" + e)) for e in ENGS if e != "sp"}
        self.known = {e: {} for e in ENGS}
        self.nsem = 0
        self.final = []

    def _waits(self, eng, reads, writes):
        toks = []
        for t in reads:
            toks += t.w
        for t in writes:
            toks += [x for x in t.w if STRICT or x[2] != eng]
            toks += [x for x in t.r if STRICT or x[2] != eng]
        need = {}
        for (sem, val, src) in toks:
            if eng == "pe" and src == "pe":
                continue
            if self.known[eng].get(id(sem), 0) >= val:
                continue
            if need.get(id(sem), (None, 0))[1] < val:
                need[id(sem)] = (sem, val)
        out = []
        for k, (sem, val) in need.items():
            self.known[eng][k] = val
            out.append((sem, val))
        return out

    def _commit(self, tok, reads, writes):
        for t in writes:
            t.w = [tok]
            t.r = []
        for t in reads:
            t.r.append(tok)
            if len(t.r) > 24:
                best = {}
                for (s, v, src) in t.r:
                    if id(s) not in best or best[id(s)][1] < v:
                        best[id(s)] = (s, v, src)
                t.r = list(best.values())

    def op(self, eng, fn, reads=(), writes=()):
        waits = self._waits(eng, reads, writes)
        self.cnt[eng] += 1
        sem = self.sem[eng]
        val = self.cnt[eng]

        def emit(e, waits=waits, fn=fn, sem=sem):
            for (s, v) in waits:
                e.wait_ge(s, v)
            fn(e).then_inc(sem, 1)

        self.prog[eng].append(emit)
        self._commit((sem, val, eng), reads, writes)

    def dma(self, q, out, in_, reads=(), writes=(), owner=None):
        waits = self._waits(q, reads, writes)
        own = owner if owner is not None else (writes[0] if writes else reads[0])
        if own.dsem is None:
            own.dsem = self.es.enter_context(self.nc.semaphore("dsem%d" % self.nsem))
            self.nsem += 1
        own.dcnt += 16
        sem, val = own.dsem, own.dcnt

        def emit(e, waits=waits, sem=sem, out=out, in_=in_):
            for (s, v) in waits:
                e.wait_ge(s, v)
            e.dma_start(out=out, in_=in_).then_inc(sem, 16)

        self.prog[q].append(emit)
        tok = (sem, val, "dma")
        for t in writes:
            t.w = [tok]
            t.r = []
        for t in reads:
            t.r.append(tok)
        return tok

    def wait_all(self, q, toks):
        def emit(e, toks=toks):
            for (s, v, _) in toks:
                e.wait_ge(s, v)
        self.prog[q].append(emit)

    def barrier(self, extra=()):
        for e in ENGS:
            waits = [(s, v) for (s, v, _) in extra]
            for o in ENGS:
                if o == "sp" or o == e or self.cnt[o] == 0:
                    continue
                if self.known[e].get(id(self.sem[o]), 0) < self.cnt[o]:
                    self.known[e][id(self.sem[o])] = self.cnt[o]
                    waits.append((self.sem[o], self.cnt[o]))

            def emit(en, waits=waits):
                for (s, v) in waits:
                    en.wait_ge(s, v)
            self.prog[e].append(emit)

    def run(self):
        nc = self.nc
        with nc.Block() as block:
            @block.tensor
            def _(e):
                for f in self.prog["pe"]:
                    f(e)

            @block.scalar
            def _(e):
                for f in self.prog["act"]:
                    f(e)

            @block.vector
            def _(e):
                for f in self.prog["dve"]:
                    f(e)

            @block.gpsimd
            def _(e):
                for f in self.prog["pool"]:
                    f(e)

            @block.sync
            def _(e):
                for f in self.prog["sp"]:
                    f(e)


def mkap(t, offset, dims):
    return bass.AP(tensor=t, offset=offset, ap=[list(d) for d in dims])


def _in_col_perm():
    cols = list(range(0, 1024))
    for j in range(4):
        for gi in range(3):
            for base in (1024, 2560, 4096):
                c0 = base + (gi * 4 + j) * 128
                cols += list(range(c0, c0 + 128))
        cols += list(range(5632 + j * 128, 5632 + (j + 1) * 128))
    cols += list(range(6144, 8192))
    assert len(cols) == 8192 and len(set(cols)) == 8192
    return np.asarray(cols)


def prep_host(inp):
    f32 = np.float32
    out = {}
    out["w_in"] = np.ascontiguousarray(np.asarray(inp["w_in"], f32)[:, :, _in_col_perm()])
    out["w_glu"] = np.ascontiguousarray(np.asarray(inp["w_glu"], f32))
    out["w_bs"] = np.ascontiguousarray(np.asarray(inp["w_branch_s"], f32))
    out["w_ba"] = np.ascontiguousarray(np.asarray(inp["w_branch_a"], f32))
    out["w_out"] = np.ascontiguousarray(np.asarray(inp["w_out"], f32))
    vecs = np.zeros((128, DEPTH * 24), f32)
    for l in range(DEPTH):
        vecs[:, l * 24 + 0:l * 24 + 8] = np.asarray(inp["pre_norm_g"], f32)[l].reshape(8, 128).T
        vecs[:, l * 24 + 8:l * 24 + 16] = np.asarray(inp["post_norm_g"], f32)[l].reshape(8, 128).T
        vecs[:, l * 24 + 16:l * 24 + 20] = np.asarray(inp["d_skip"], f32)[l].reshape(4, 128).T
        vecs[:, l * 24 + 20:l * 24 + 24] = np.asarray(inp["b_glu"], f32)[l].reshape(4, 128).T
    out["vecs"] = vecs
    lre = np.asarray(inp["lambda_re"], f32)
    lim = np.asarray(inp["lambda_im"], f32)
    ldt = np.asarray(inp["log_dt"], f32)
    bre = np.asarray(inp["b_re"], f32)
    bim = np.asarray(inp["b_im"], f32)
    cre = np.asarray(inp["c_re"], f32)
    cim = np.asarray(inp["c_im"], f32)
    rb = np.zeros((DEPTH, 128, 5, 4, 128), f32)
    rc = np.zeros((DEPTH, 128, 48 + 512), f32)
    for l in range(DEPTH):
        lam_g = lre[l].reshape(4, 4, 2, 64)
        lim_g = lim[l].reshape(4, 4, 2, 64)
        ldt_g = np.broadcast_to(ldt[l].reshape(4, 4, 2, 1), (4, 4, 2, 64))
        for ptl in range(4):
            rows = slice(32 * ptl, 32 * ptl + 32)
            rb[l, rows, 0] = lam_g[:, ptl].reshape(4, 128)[None]
            rb[l, rows, 1] = lim_g[:, ptl].reshape(4, 128)[None]
            rb[l, rows, 2] = ldt_g[:, ptl].reshape(4, 128)[None]
            for h in range(2):
                for j in range(4):
                    g = 8 * j + 2 * ptl + h
                    r0 = 32 * ptl + 16 * h
                    rb[l, r0:r0 + 16, 3, j, 64 * h:64 * h + 64] = bre[l, g].T
                    rb[l, r0:r0 + 16, 4, j, 64 * h:64 * h + 64] = bim[l, g].T
        rc[l, :, 0:16] = lre[l].reshape(16, 128).T
        rc[l, :, 16:32] = lim[l].reshape(16, 128).T
        rc[l, :, 32:48] = np.broadcast_to(ldt[l].reshape(16, 2, 1), (16, 2, 64)).reshape(16, 128).T
        cc = cre[l].reshape(16, 2, 16, 64).transpose(1, 3, 0, 2).reshape(128, 16 * 16)
        ci = cim[l].reshape(16, 2, 16, 64).transpose(1, 3, 0, 2).reshape(128, 16 * 16)
        rc[l, :, 48:48 + 256] = cc
        rc[l, :, 48 + 256:48 + 512] = ci
    out["ssm_rb"] = rb.reshape(DEPTH, 128, 5 * 512)
    out["ssm_rc"] = rc
    am = np.zeros((128, 256), f32)
    kk = np.arange(128)[:, None]
    qq = np.arange(128)[None, :]
    am[:, 0:128] = (qq >= kk)
    am[:, 128:256] = (qq <= kk)
    out["amask"] = am.astype(ml_dtypes.bfloat16)
    return out


class TL:
    def __init__(self, t, name):
        self.t = t
        self.r = Res(name)

    def __getitem__(self, idx):
        return self.t[idx]


class Builder:
    def __init__(self, n_layers=DEPTH, debug=None):
        self.n_layers = n_layers
        self.debug = debug or {}
        self.nc = bass.Bass("TRN2", target_bir_lowering=False)
        self.es = ExitStack()
        self.S = Sched(self.nc, self.es)
        self.uid = 0

    def sb(self, es, shape, dtype, name=None):
        self.uid += 1
        nm = (name or "t") + "_%d" % self.uid
        return TL(es.enter_context(self.nc.sbuf_tensor(nm, list(shape), dtype)), nm)

    def ps(self, es, shape, dtype=F32, name=None):
        self.uid += 1
        nm = (name or "p") + "_%d" % self.uid
        return TL(es.enter_context(self.nc.psum_tensor(nm, list(shape), dtype)), nm)

    @staticmethod
    def _res(lst):
        return [x.r if isinstance(x, TL) else x for x in lst]

    def tt(self, eng, out, a, b, op, R, W):
        self.S.op(eng, lambda e: e.tensor_tensor(out=out, in0=a, in1=b, op=op), self._res(R), self._res(W))

    def ts(self, eng, out, a, s1, op0, R, W, s2=None, op1=None):
        if op1 is None:
            self.S.op(eng, lambda e: e.tensor_scalar(out=out, in0=a, scalar1=s1, scalar2=None, op0=op0),
                      self._res(R), self._res(W))
        else:
            self.S.op(eng, lambda e: e.tensor_scalar(out=out, in0=a, scalar1=s1, scalar2=s2, op0=op0, op1=op1),
                      self._res(R), self._res(W))

    def stt(self, out, a, scalar, b, op0, op1, R, W):
        self.S.op("dve", lambda e: e.scalar_tensor_tensor(out=out, in0=a, scalar=scalar, in1=b, op0=op0, op1=op1),
                  self._res(R), self._res(W))

    def act(self, out, a, func, R, W, scale=1.0, bias=0.0):
        self.S.op("act", lambda e: e.activation(out=out, in_=a, func=func, bias=bias, scale=scale),
                  self._res(R), self._res(W))

    def copy(self, eng, out, a, R, W):
        if eng == "act":
            self.S.op("act", lambda e: e.activation(out=out, in_=a, func=AF.Copy), self._res(R), self._res(W))
        else:
            self.S.op(eng, lambda e: e.tensor_copy(out=out, in_=a), self._res(R), self._res(W))

    def memset(self, eng, ap, val, W):
        self.S.op(eng, lambda e: e.memset(ap, val), [], self._res(W))

    def mm(self, out, lhsT, rhs, start, stop, R, W, tp=None):
        if tp is None:
            self.S.op("pe", lambda e: e.matmul(out, lhsT=lhsT, rhs=rhs, start=start, stop=stop),
                      self._res(R), self._res(W))
        else:
            self.S.op("pe", lambda e: e.matmul(out, lhsT=lhsT, rhs=rhs, start=start, stop=stop, tile_position=tp),
                      self._res(R), self._res(W))

    def dma(self, q, out, in_, R, W, owner=None):
        return self.S.dma(q, out, in_, self._res(R), self._res(W),
                          owner.r if isinstance(owner, TL) else owner)

    def cmul(self, eng, o, a, b, t1, t2, R, W, Tm):
        (orr, oi), (ar, ai), (br, bi) = o, a, b
        self.tt(eng, t1, ar, br, ALU.mult, R, Tm)
        self.tt(eng, t2, ai, bi, ALU.mult, R, Tm)
        self.tt(eng, orr, t1, t2, ALU.subtract, Tm, W)
        self.tt(eng, t1, ar, bi, ALU.mult, R, Tm)
        self.tt(eng, t2, ai, br, ALU.mult, R, Tm)
        self.tt(eng, oi, t1, t2, ALU.add, Tm, W)

    def declare_io(self):
        nc = self.nc
        nl = DEPTH
        d = {}
        d["xT"] = nc.dram_tensor("xT", [D, L], F32, kind="ExternalInput").ap()
        d["w_in"] = nc.dram_tensor("w_in", [nl, D, 8192], F32, kind="ExternalInput").ap()
        d["w_glu"] = nc.dram_tensor("w_glu", [nl, 512, 512], F32, kind="ExternalInput").ap()
        d["w_bs"] = nc.dram_tensor("w_bs", [nl, 512, D], F32, kind="ExternalInput").ap()
        d["w_ba"] = nc.dram_tensor("w_ba", [nl, 512, D], F32, kind="ExternalInput").ap()
        d["w_out"] = nc.dram_tensor("w_out", [nl, D, D], F32, kind="ExternalInput").ap()
        d["vecs"] = nc.dram_tensor("vecs", [128, nl * 24], F32, kind="ExternalInput").ap()
        d["ssm_rb"] = nc.dram_tensor("ssm_rb", [nl, 128, 2560], F32, kind="ExternalInput").ap()
        d["ssm_rc"] = nc.dram_tensor("ssm_rc", [nl, 128, 560], F32, kind="ExternalInput").ap()
        d["amask"] = nc.dram_tensor("amask", [128, 256], BF16, kind="ExternalInput").ap()
        d["yT"] = nc.dram_tensor("yT", [D, L], F32, kind="ExternalOutput").ap()
        tk = "ExternalOutput" if self.debug.get("tables") else "Internal"
        d["tabB"] = nc.dram_tensor("tabB", [nl, 128, 16384], BF16, kind=tk).ap()
        d["tabC"] = nc.dram_tensor("tabC", [nl, 128, 16384], BF16, kind=tk).ap()
        for name, (shape, dt) in self.debug.get("outs", {}).items():
            d[name] = nc.dram_tensor(name, list(shape), dt, kind="ExternalOutput").ap()
        self.io = d

    def lam_alloc(self, es, Fd, tag):
        names = ("dt", "lr", "a", "th", "mag", "em2a", "c", "s", "t1", "t2", "lbr", "lbi")
        return {n: self.sb(es, [128, Fd], F32, tag + n) for n in names}

    def lam_math(self, m, src):
        B = self
        dt, lr, a, th, mag, em2a, c, s, t1, t2, lbr, lbi = [m[n] for n in
                                                          ("dt", "lr", "a", "th", "mag", "em2a", "c", "s", "t1", "t2", "lbr", "lbi")]
        R = src["R"]
        B.act(dt[:], src["ldt"], AF.Exp, R, [dt])
        B.ts("dve", lr[:], src["lre"], -1e-4, ALU.min, R, [lr])
        B.tt("dve", a[:], lr[:], dt[:], ALU.mult, [lr, dt], [a])
        B.tt("dve", th[:], src["lim"], dt[:], ALU.mult, R + [dt], [th])
        B.act(mag[:], a[:], AF.Exp, [a], [mag])
        B.act(em2a[:], a[:], AF.Exp, [a], [em2a], scale=-2.0)
        B.act(s[:], th[:], AF.Sin, [th], [s], scale=1.0 / 16.0)
        B.act(c[:], th[:], AF.Sin, [th, self.halfpi], [c], scale=1.0 / 16.0, bias=self.halfpi[:, 0:1])
        for _ in range(4):
            B.tt("dve", t1[:], c[:], c[:], ALU.mult, [c], [t1])
            B.tt("dve", t2[:], s[:], s[:], ALU.mult, [s], [t2])
            B.stt(s[:], c[:], 2.0, s[:], ALU.mult, ALU.mult, [c, s], [s])
            B.tt("dve", c[:], t1[:], t2[:], ALU.subtract, [t1, t2], [c])
        B.tt("dve", lbr[:], mag[:], c[:], ALU.mult, [mag, c], [lbr])
        B.tt("dve", lbi[:], mag[:], s[:], ALU.mult, [mag, s], [lbi])

    def prologue(self):
        B = self
        io = self.io
        toks = []
        with ExitStack() as es:
            mb = B.lam_alloc(es, 512, "rb")
            inpb = B.sb(es, [128, 5, 512], F32, "rbin")
            mk = lambda n: B.sb(es, [128, 512], F32, "rb" + n)
            den, nr, cr, ci, invr, invi, bbr, bbi = [mk(n) for n in ("den", "nr", "cr", "ci", "invr", "invi", "bbr", "bbi")]
            xs = [(mk("xr0"), mk("xi0")), (mk("xr1"), mk("xi1"))]
            bt = B.sb(es, [128, 4, TS, 2, 128], BF16, "bt")
            mc = B.lam_alloc(es, 16, "rc")
            inpc = B.sb(es, [128, 560], F32, "rcin")
            pw = B.sb(es, [128, 2, 16, TS], F32, "pw")
            ct = B.sb(es, [128, 16, TS, 2, 32], BF16, "ct")
            big = lambda n: B.sb(es, [128, 16, TS, 16], F32, "rc" + n)
            u1, u2, u3 = big("u1"), big("u2"), big("u3")
            B.memset("pool", ct[:], 0.0, [ct])
            for l in range(self.n_layers):
                inp = inpb
                B.dma("sp", inp[:], io["ssm_rb"][l].rearrange("p (a f) -> p a f", a=5), [], [inp])
                B.lam_math(mb, dict(lre=inp[:, 0, :], lim=inp[:, 1, :], ldt=inp[:, 2, :], R=[inp]))
                lr, lbr, lbi, em2a, t1, t2 = [mb[n] for n in ("lr", "lbr", "lbi", "em2a", "t1", "t2")]
                li = inp[:, 1, :]
                B.tt("dve", t1[:], lr[:], lr[:], ALU.mult, [lr], [t1])
                B.tt("dve", t2[:], li, li, ALU.mult, [inp], [t2])
                B.tt("dve", den[:], t1[:], t2[:], ALU.add, [t1, t2], [den])
                B.S.op("dve", lambda e, den=den: e.reciprocal(out=den[:], in_=den[:]), [den.r], [den.r])
                B.ts("dve", nr[:], lbr[:], -1.0, ALU.add, [lbr], [nr])
                B.tt("dve", t1[:], nr[:], lr[:], ALU.mult, [nr, lr], [t1])
                B.tt("dve", t2[:], lbi[:], li, ALU.mult, [lbi, inp], [t2])
                B.tt("dve", t1[:], t1[:], t2[:], ALU.add, [t1, t2], [t1])
                B.tt("dve", cr[:], t1[:], den[:], ALU.mult, [t1, den], [cr])
                B.tt("dve", t1[:], lbi[:], lr[:], ALU.mult, [lbi, lr], [t1])
                B.tt("dve", t2[:], nr[:], li, ALU.mult, [nr, inp], [t2])
                B.tt("dve", t1[:], t1[:], t2[:], ALU.subtract, [t1, t2], [t1])
                B.tt("dve", ci[:], t1[:], den[:], ALU.mult, [t1, den], [ci])
                B.tt("dve", invr[:], lbr[:], em2a[:], ALU.mult, [lbr, em2a], [invr])
                B.stt(invi[:], lbi[:], -1.0, em2a[:], ALU.mult, ALU.mult, [lbi, em2a], [invi])
                B.cmul("dve", (bbr[:], bbi[:]), (cr[:], ci[:]), (inp[:, 3, :], inp[:, 4, :]), t1[:], t2[:],
                       [cr, ci, inp], [bbr, bbi], [t1, t2])
                prev = (bbr, bbi)
                for i in range(TS):
                    cur = xs[i % 2]
                    B.cmul("dve", (cur[0][:], cur[1][:]), (prev[0][:], prev[1][:]), (invr[:], invi[:]), t1[:], t2[:],
                           [prev[0], prev[1], invr, invi], [cur[0], cur[1]], [t1, t2])
                    B.copy("act", bt[:, :, i, 0, :], cur[0][:].rearrange("p (j q) -> p j q", j=4), [cur[0]], [bt])
                    B.copy("act", bt[:, :, i, 1, :], cur[1][:].rearrange("p (j q) -> p j q", j=4), [cur[1]], [bt])
                    prev = cur
                toks.append(B.dma("sp", io["tabB"][l], bt[:].rearrange("p j i r q -> p (j i r q)"),
                                  [bt], [self.tabB_r[l]], owner=bt))
                inp = inpc
                B.dma("sp", inp[:], io["ssm_rc"][l], [], [inp])
                B.lam_math(mc, dict(lre=inp[:, 0:16], lim=inp[:, 16:32], ldt=inp[:, 32:48], R=[inp]))
                lbr, lbi, t1, t2 = [mc[n] for n in ("lbr", "lbi", "t1", "t2")]
                B.copy("dve", pw[:, 0, :, 0], lbr[:], [lbr], [pw])
                B.copy("dve", pw[:, 1, :, 0], lbi[:], [lbi], [pw])
                for i in range(1, TS):
                    B.cmul("dve", (pw[:, 0, :, i], pw[:, 1, :, i]), (pw[:, 0, :, i - 1], pw[:, 1, :, i - 1]),
                           (lbr[:], lbi[:]), t1[:], t2[:], [pw, lbr, lbi], [pw], [t1, t2])
                PL = self.PL
                B.copy("dve", PL[:, l, :, 0], pw[:, 0, :, TS - 1], [pw], [PL])
                B.copy("dve", PL[:, l, :, 1], pw[:, 1, :, TS - 1], [pw], [PL])
                for k in range(1, 7):
                    pr, pi = PL[:, l, :, 3 * k - 3], PL[:, l, :, 3 * k - 2]
                    B.tt("dve", t1[:], pr, pr, ALU.mult, [PL], [t1])
                    B.tt("dve", t2[:], pi, pi, ALU.mult, [PL], [t2])
                    B.tt("dve", PL[:, l, :, 3 * k], t1[:], t2[:], ALU.subtract, [t1, t2], [PL])
                    B.stt(PL[:, l, :, 3 * k + 1], pr, 2.0, pi, ALU.mult, ALU.mult, [PL], [PL])
                for k in range(7):
                    B.ts("dve", PL[:, l, :, 3 * k + 2], PL[:, l, :, 3 * k + 1], -1.0, ALU.mult, [PL], [PL])
                cre = inp[:, 48:304].rearrange("p (t c) -> p t c", c=16)
                cim = inp[:, 304:560].rearrange("p (t c) -> p t c", c=16)

                def bc_c(ap3):
                    return mkap(ap3.tensor, ap3.offset, [ap3.ap[0], ap3.ap[1], [0, TS], ap3.ap[2]])

                def bc_p(k):
                    a = pw[:, k, :, :]
                    return mkap(a.tensor, a.offset, [a.ap[0], a.ap[1], a.ap[2], [0, 16]])
                B.tt("dve", u1[:], bc_c(cre), bc_p(0), ALU.mult, [inp, pw], [u1])
                B.tt("dve", u2[:], bc_c(cim), bc_p(1), ALU.mult, [inp, pw], [u2])
                B.tt("dve", u3[:], u1[:], u2[:], ALU.subtract, [u1, u2], [u3])
                B.copy("act", ct[0:64, :, :, 0, 0:16], u3[0:64], [u3], [ct])
                B.copy("act", ct[64:128, :, :, 0, 16:32], u3[64:128], [u3], [ct])
                B.tt("dve", u1[:], bc_c(cre), bc_p(1), ALU.mult, [inp, pw], [u1])
                B.tt("dve", u2[:], bc_c(cim), bc_p(0), ALU.mult, [inp, pw], [u2])
                B.stt(u3[:], u1[:], -1.0, u2[:], ALU.mult, ALU.subtract, [u1, u2], [u3])
                B.copy("act", ct[0:64, :, :, 1, 0:16], u3[0:64], [u3], [ct])
                B.copy("act", ct[64:128, :, :, 1, 16:32], u3[64:128], [u3], [ct])
                toks.append(B.dma("sp", io["tabC"][l], ct[:].rearrange("p t i r c -> p (t i r c)"),
                                  [ct], [self.tabC_r[l]], owner=ct))
        self.S.barrier(toks)

    def build(self):
        B = self
        nc = self.nc
        es = self.es
        self.declare_io()
        io = self.io
        self.tabB_r = [Res("tabB%d" % l) for l in range(DEPTH)]
        self.tabC_r = [Res("tabC%d" % l) for l in range(DEPTH)]
        self.halfpi = B.sb(es, [128, 1], F32, "halfpi")
        B.memset("dve", self.halfpi[:], math.pi / 2.0, [self.halfpi])
        self.PL = B.sb(es, [128, DEPTH, 16, 24], F32, "PL")
        self.prologue()
        if self.debug.get("tables"):
            self.finish([])
            return
        self.main()

    def finish(self, out_toks):
        self.S.wait_all("sp", out_toks)
        self.S.barrier(out_toks)
        self.S.run()


class V:
    def __init__(self, ap, name=""):
        self.ap = ap
        self.r = Res(name)

    def __getitem__(self, idx):
        return self.ap[idx]


def _res_of(lst):
    return [x.r if hasattr(x, "r") and not isinstance(x, Res) else x for x in lst]


Builder._res = staticmethod(_res_of)


def dil_ap(base, r, col0, ncols):
    t, off, pdim, st = base.tensor, base.offset, list(base.ap[0]), base.ap[-1][0]
    assert len(base.ap) == 2
    nsub = L // r
    c0, i0 = divmod(col0, nsub)
    if r == 1:
        dims, o = [[st, ncols]], col0
    elif i0 + ncols <= nsub:
        dims, o = [[r * st, ncols]], c0 + r * i0
    else:
        assert i0 == 0 and ncols % nsub == 0
        dims, o = [[st, ncols // nsub], [r * st, nsub]], c0
    return mkap(t, off + o * st, [pdim] + dims)


def like(bank_ap, ap):
    if len(ap.ap) == 3:
        return bank_ap.rearrange("p (a b) -> p a b", a=ap.ap[1][1])
    return bank_ap


AR_BYTES = 88064


def _main(self):
    B = self
    es = self.es
    io = self.io
    S = self.S
    nc = self.nc
    dbg = self.debug
    xT = B.sb(es, [128, NK, L], F32, "xT")
    xT_r = [[Res("xT%d_%d" % (k, n)) for n in range(4)] for k in range(NK)]
    hT = B.sb(es, [128, NK, L], BF16, "hT")
    hT_r = [Res("hT%d" % k) for k in range(NK)]
    vecs = B.sb(es, [128, DEPTH * 24], F32, "vecs")
    amask = B.sb(es, [128, 256], BF16, "amask")
    ones = B.sb(es, [128, 128], BF16, "ones")
    epsT = B.sb(es, [128, 1], F32, "eps")
    wbuf = [B.sb(es, [128, NK, 512], BF16, "wbuf%d" % i) for i in range(2)]
    arena = B.sb(es, [128, AR_BYTES // 2], BF16, "arena")
    banks = [B.ps(es, [128, 512], F32, "bank%d" % i) for i in range(8)]
    self.wi = 0

    def carve(off, shape, dtype, name):
        nel = int(np.prod(shape[1:]))
        esz = 2 if dtype == BF16 else 4
        assert off % 4 == 0 and off + nel * esz <= AR_BYTES, (name, off, nel * esz)
        a = arena.t[:, off // 2: off // 2 + nel * esz // 2]
        if dtype == F32:
            a = a.bitcast(F32)
        if len(shape) == 3:
            a = a.rearrange("p (a b) -> p a b", a=shape[1])
        elif len(shape) == 4:
            a = a.rearrange("p (a b c) -> p a b c", a=shape[1], b=shape[2])
        elif len(shape) == 5:
            a = a.rearrange("p (a b c d) -> p a b c d", a=shape[1], b=shape[2], c=shape[3])
        return V(a, name)

    B.memset("dve", ones[:], 1.0, [ones])
    B.memset("dve", epsT[:], 1e-6, [epsT])
    B.dma("sp", vecs[:], io["vecs"], [], [vecs])
    B.dma("sp", amask[:], io["amask"], [], [amask])
    xin = io["xT"].rearrange("(k p) t -> p k t", p=128)
    for k in range(NK):
        B.dma("sp", xT[:, k, :], xin[:, k, :], [], xT_r[k], owner=xT_r[k][0])

    out_toks = []

    def dump(name, ap, R):
        out_toks.append(B.dma("sp", io[name], ap, R, [], owner=Res("dump_" + name)))

    def load_wblock(l, col0, ncols):
        wb = wbuf[self.wi % 2]
        self.wi += 1
        src = io["w_in"][l][:, col0:col0 + ncols].rearrange("(k p) c -> p k c", p=128)
        B.dma("pool", wb[:, :, 0:ncols], src, [], [wb])
        return wb

    def load_w(dst, src2d):
        B.dma("pool", dst[:], src2d.rearrange("(k p) c -> p k c", p=128), [], [dst])

    bank_ctr = [0]

    def nb(lst):
        b = banks[lst[bank_ctr[0] % len(lst)]]
        bank_ctr[0] += 1
        return b

    def hk(k):
        return hT.t[:, k, :]

    for l in range(self.n_layers):
        vb = l * 24
        sq = [carve(i * 8192, [128, NK, 512], BF16, "sq%d" % i) for i in range(2)]
        rt = [carve(16384 + i * 2048, [128, 512], F32, "rt%d" % i) for i in range(2)]
        rs = [carve(20480 + i * 2048, [128, 512], F32, "rs%d" % i) for i in range(2)]
        for n in range(4):
            ns = slice(n * 512, (n + 1) * 512)
            s_, rt_, rs_ = sq[n % 2], rt[n % 2], rs[n % 2]
            for k in range(NK):
                B.act(s_[:, k, :], xT[:, k, ns], AF.Square, [xT_r[k][n]], [s_])
            bk = nb([6, 7])
            for k in range(NK):
                B.mm(bk[:], ones[:], s_[:, k, :], k == 0, k == NK - 1, [ones, s_], [bk])
            B.act(rt_[:], bk[:], AF.Sqrt, [bk, epsT], [rt_], scale=1.0 / D, bias=epsT[:, 0:1])
            S.op("dve", lambda e, a=rs_, b=rt_: e.reciprocal(out=a[:], in_=b[:]), [rt_.r], [rs_.r])
            for k in range(NK):
                B.stt(hT[:, k, ns], xT[:, k, ns], vecs[:, vb + k:vb + k + 1], rs_[:], ALU.mult, ALU.mult,
                      [xT_r[k][n], vecs, rs_], [hT_r[k]])
        if dbg.get("stop") == "P0":
            dump("dbg_h", hT[:], hT_r)
            break
        S.barrier()
        yT = carve(32768, [128, 4, L], BF16, "yT")
        yT_r = [[Res("yT%d_%d" % (k, n)) for n in range(4)] for k in range(4)]
        r32 = carve(0, [128, 2, TS, 128], F32, "r32")
        r32_r = [Res("r32re"), Res("r32im")]
        rbf = carve(16384, [128, 2, TS, 128], BF16, "rbf")
        Bt = carve(24576, [128, TS, 2, 128], BF16, "Bt")
        Ct = carve(49152, [128, 5120], BF16, "Ct")
        CtA = Ct[:, 0:3072].rearrange("p (t i r c) -> p t i r c", t=3, i=TS, r=2)
        CtB = Ct[:, 3072:5120].rearrange("p (i r c) -> p i r c", i=TS, r=2)
        uT = carve(59392, [128, L], BF16, "uT")
        wglu = carve(63488, [128, 4, 512], BF16, "wglu")
        Sab = [carve(67584 + i * 1024, [128, 2, 128], F32, "S%d" % i) for i in range(2)]
        zt = carve(69632, [128, 2, 128], F32, "zt")
        tq = carve(70656, [128, 2, 128], F32, "tq")
        Sbf = carve(71680, [128, 2, 128], BF16, "Sbf")
        gv = [carve(72192 + i * 2048, [128, 512], F32, "gv%d" % i) for i in range(2)]
        gw = [carve(76288 + i * 2048, [128, 512], F32, "gw%d" % i) for i in range(2)]
        gs = [carve(80384 + i * 2048, [128, 512], F32, "gs%d" % i) for i in range(2)]
        gz = [carve(84480 + i * 1024, [128, 512], BF16, "gz%d" % i) for i in range(2)]
        Ctz = V(CtB[:, :, :, 0:32], "Ctz")
        B.memset("pool", Ctz[:], 0.0, [Ctz])
        B.memset("dve", Sbf[:], 0.0, [Sbf])
        wbU = load_wblock(l, 0, 512)
        wbZ = load_wblock(l, 512, 512)
        load_w(wglu, io["w_glu"][l])
        PLl = self.PL
        for j in dbg.get("js", range(4)):
            for n in range(4):
                bk = nb([6, 7])
                for k in range(NK):
                    B.mm(bk[:], wbU[:, k, 128 * j:128 * j + 128], like(dil_ap(hk(k), 16, 512 * n, 512), None) if False else dil_ap(hk(k), 16, 512 * n, 512),
                         k == 0, k == NK - 1, [wbU, hT_r[k]], [bk])
                B.copy("act", uT[:, 512 * n:512 * n + 512], bk[:], [bk], [uT])
            B.dma("sp", Bt[:].rearrange("p i r q -> p (i r q)"), io["tabB"][l][:, 4096 * j:4096 * (j + 1)],
                  [self.tabB_r[l]], [Bt])
            B.dma("sp", Ct[:, 0:3072], io["tabC"][l][:, 4096 * j:4096 * j + 3072], [self.tabC_r[l]], [Ct])
            B.dma("sp", CtB[:, :, :, 32:64],
                  io["tabC"][l][:, 4096 * j + 3072:4096 * (j + 1)].rearrange("p (i r c) -> p i r c", i=TS, r=2),
                  [self.tabC_r[l]], [Ct])
            for ptl in dbg.get("ptls", (0, 1, 3, 2)):
                pt = 4 * j + ptl
                rows = slice(32 * ptl, 32 * ptl + 32)
                for iq in (range(4) if not dbg.get("skip_x") else []):
                    xb = [nb([4, 5]), nb([4, 5])]
                    for ri in range(2):
                        for il in range(4):
                            i = 4 * iq + il
                            B.mm(xb[ri][:, il * 128:(il + 1) * 128], Bt[rows, i, ri, :], uT[rows, i * 128:(i + 1) * 128],
                                 True, True, [Bt, uT], [xb[ri]], tp=(32 * ptl, 0))
                    for il in range(4):
                        i = 4 * iq + il
                        for ri in range(2):
                            src = xb[ri][:, il * 128:(il + 1) * 128]
                            if i == 0:
                                B.copy("dve", r32[:, ri, 0, :], src, [xb[ri]], [r32_r[ri]])
                            else:
                                B.tt("dve", r32[:, ri, i, :], r32[:, ri, i - 1, :], src, ALU.add,
                                     [xb[ri], r32_r[ri]], [r32_r[ri]])
                B.copy("act", rbf[:], r32[:], r32_r, [rbf])
                pl = lambda c: PLl[:, l, pt, c:c + 1]
                rr, rim = r32[:, 0, TS - 1, :], r32[:, 1, TS - 1, :]
                B.ts("dve", tq[:, 0, :], rim, pl(1), ALU.mult, r32_r + [PLl], [tq])
                B.stt(zt[:, 0, :], rr, pl(0), tq[:, 0, :], ALU.mult, ALU.subtract, r32_r + [PLl, tq], [zt])
                B.ts("dve", tq[:, 1, :], rr, pl(1), ALU.mult, r32_r + [PLl], [tq])
                B.stt(zt[:, 1, :], rim, pl(0), tq[:, 1, :], ALU.mult, ALU.add, r32_r + [PLl, tq], [zt])
                old = zt
                for kk in range(7):
                    d = 1 << kk
                    new = Sab[kk % 2]
                    a_, b_, nb_ = pl(3 * kk), pl(3 * kk + 1), pl(3 * kk + 2)
                    B.stt(new[:, 0, d:], old[:, 0, :128 - d], a_, old[:, 0, d:], ALU.mult, ALU.add, [old, PLl], [new])
                    B.stt(new[:, 0, d:], old[:, 1, :128 - d], nb_, new[:, 0, d:], ALU.mult, ALU.add, [old, new, PLl], [new])
                    B.stt(new[:, 1, d:], old[:, 1, :128 - d], a_, old[:, 1, d:], ALU.mult, ALU.add, [old, PLl], [new])
                    B.stt(new[:, 1, d:], old[:, 0, :128 - d], b_, new[:, 1, d:], ALU.mult, ALU.add, [old, new, PLl], [new])
                    B.copy("dve", new[:, :, :d], old[:, :, :d], [old], [new])
                    old = new
                B.copy("dve", Sbf[:, :, 1:128], old[:, :, 0:127], [old], [Sbf])
                for i in (range(TS) if not dbg.get("skip_ro") else []):
                    yb = banks[i // 4]
                    cs_ = slice((i % 4) * 128, (i % 4) * 128 + 128)
                    if ptl == 3:
                        o_ = yb[64:128, cs_]
                        tp = (0, 64)
                        lh = [CtB[:, i, rr_, :] for rr_ in range(2)]
                        RC_ = [Ct, Ctz]
                    else:
                        o_ = yb[rows, cs_]
                        tp = (0, 32 * ptl)
                        lh = [CtA[:, ptl, i, rr_, :] for rr_ in range(2)]
                        RC_ = [Ct]
                    B.mm(o_, lh[0], rbf[:, 0, i, :], True, False, RC_ + [rbf], [yb], tp=tp)
                    B.mm(o_, lh[1], rbf[:, 1, i, :], False, False, RC_ + [rbf], [yb], tp=tp)
                    B.mm(o_, lh[0], Sbf[:, 0, :], False, False, RC_ + [Sbf], [yb], tp=tp)
                    B.mm(o_, lh[1], Sbf[:, 1, :], False, True, RC_ + [Sbf], [yb], tp=tp)
            for ib in range(4):
                cs = slice(512 * ib, 512 * ib + 512)
                v_, w_, g_ = gv[ib % 2], gw[ib % 2], gs[ib % 2]
                B.stt(v_[:], uT[:, cs], vecs[:, vb + 16 + j:vb + 17 + j], banks[ib][:], ALU.mult, ALU.add,
                      [uT, vecs, banks[ib]], [v_])
                B.act(w_[:], v_[:], AF.Square, [v_], [w_])
                B.ts("pool", w_[:], w_[:], 0.044715, ALU.mult, [w_], [w_], s2=1.0, op1=ALU.add)
                B.tt("pool", w_[:], w_[:], v_[:], ALU.mult, [w_, v_], [w_])
                B.act(g_[:], w_[:], AF.Sigmoid, [w_], [g_], scale=1.5957691216057308)
                B.tt("dve", yT[:, j, cs], v_[:], g_[:], ALU.mult, [v_, g_], [yT_r[j][ib]])
        if dbg.get("stop") == "PA1":
            dump("dbg_y", yT[:], [x for row in yT_r for x in row])
            break
        for n in range(4):
            cs = slice(512 * n, 512 * n + 512)
            for mo in range(4):
                gb = banks[mo]
                for k in range(4):
                    B.mm(gb[:], wglu[:, k, 128 * mo:128 * mo + 128], yT[:, k, cs], k == 0, k == 3,
                         [wglu, yT_r[k][n]], [gb])
            for mo in range(4):
                zb = nb([4, 5])
                for k in range(NK):
                    B.mm(zb[:], wbZ[:, k, 128 * mo:128 * mo + 128], dil_ap(hk(k), 16, 512 * n, 512),
                         k == 0, k == NK - 1, [wbZ, hT_r[k]], [zb])
                z_, g_ = gz[mo % 2], gs[mo % 2]
                B.act(z_[:], zb[:], AF.Silu, [zb], [z_])
                B.act(g_[:], banks[mo][:], AF.Sigmoid, [banks[mo], vecs], [g_], bias=vecs[:, vb + 20 + mo:vb + 21 + mo])
                B.tt("dve", g_[:], g_[:], z_[:], ALU.mult, [g_, z_], [g_])
                B.tt("dve", yT[:, mo, cs], yT[:, mo, cs], g_[:], ALU.mult, [yT_r[mo][n], g_], [yT_r[mo][n]])
        if dbg.get("stop") == "PA":
            dump("dbg_y", yT[:], [x for row in yT_r for x in row])
            break
        S.barrier()
        merged = carve(0, [128, NK, L], BF16, "merged")
        merged_r = [Res("mg%d" % m) for m in range(NK)]
        wbs = carve(49152, [128, 4, D], BF16, "wbs")
        sgg = [carve(57344 + i * 4096, [128, L], BF16, "sgg%d" % i) for i in range(2)]
        load_w(wbs, io["w_bs"][l])
        for m in range(NK):
            if m % 4 == 0:
                wb = load_wblock(l, 6144 + 512 * (m // 4), 512)
            sg_ = sgg[m % 2]
            for n in range(4):
                bk = nb([6, 7])
                for k in range(NK):
                    B.mm(bk[:], wb[:, k, 128 * (m % 4):128 * (m % 4) + 128], hT[:, k, 512 * n:512 * n + 512],
                         k == 0, k == NK - 1, [wb, hT_r[k]], [bk])
                B.act(sg_[:, 512 * n:512 * n + 512], bk[:], AF.Sigmoid, [bk], [sg_])
            for n2 in range(4):
                bk = nb([0, 1, 2, 3])
                for k in range(4):
                    B.mm(bk[:], wbs[:, k, 128 * m:128 * m + 128], yT[:, k, 512 * n2:512 * n2 + 512], k == 0, k == 3,
                         [wbs, yT_r[k][n2]], [bk])
                dst = dil_ap(merged[:, m, :], 16, 512 * n2, 512)
                B.tt("dve", dst, like(bk[:], dst), dil_ap(sg_[:], 16, 512 * n2, 512), ALU.mult,
                     [bk, sg_], [merged_r[m]])
        if dbg.get("stop") == "PA2":
            dump("dbg_m", merged[:], merged_r)
            break
        S.barrier()
        yaT = carve(32768, [128, 4, L], BF16, "yaT")
        yaT_r = [Res("ya%d" % j) for j in range(4)]
        qT = carve(49152, [128, L], BF16, "qT")
        kT = carve(53248, [128, L], BF16, "kT")
        vd = carve(57344, [128, 16, 128], BF16, "vd")
        etm = [carve(61440 + i * 512, [128, 256], BF16, "etm%d" % i) for i in range(4)]
        ett = [carve(63488 + i * 512, [128, 256], BF16, "ett%d" % i) for i in range(2)]
        accn = carve(64512, [128, L], F32, "accn")
        accd = carve(72704, [128, L], F32, "accd")
        sza = carve(80896, [128, L], BF16, "sza")
        for j in range(4):
            for gi, r in enumerate((1, 4, 16)):
                ncols = 512 if gi == 2 else 384
                wb = load_wblock(l, 1024 + 1280 * j + 384 * gi, ncols)
                nsub = L // r
                nblk = nsub // 128
                for (c_in, dst) in ((0, qT), (128, kT)):
                    for n in range(4):
                        bk = nb([6, 7])
                        for k in range(NK):
                            B.mm(bk[:], wb[:, k, c_in:c_in + 128], dil_ap(hk(k), r, 512 * n, 512), k == 0, k == NK - 1,
                                 [wb, hT_r[k]], [bk])
                        B.copy("act", dst[:, 512 * n:512 * n + 512], bk[:], [bk], [dst])
                for Bq in range(4):
                    bk = nb([6, 7])
                    for bl in range(4):
                        Bk = 4 * Bq + bl
                        for k in range(NK):
                            B.mm(bk[:, bl * 128:bl * 128 + 128], dil_ap(hk(k), r, 128 * Bk, 128), wb[:, k, 256:384],
                                 k == 0, k == NK - 1, [wb, hT_r[k]], [bk])
                    B.copy("act", vd[:, 4 * Bq:4 * Bq + 4, :], bk[:].rearrange("p (a b) -> p a b", a=4), [bk], [vd])
                for Bk in range(16):
                    c_, blk = divmod(Bk, nblk)
                    nq = 256 if blk < nblk - 1 else 128
                    sb_ = banks[4 + (Bk // 2) % 2]
                    reg = sb_[:, 256 * (Bk % 2):256 * (Bk % 2) + nq]
                    B.mm(reg, kT[:, 128 * Bk:128 * Bk + 128], qT[:, 128 * Bk:128 * Bk + nq], True, True, [kT, qT], [sb_])
                    et_, em_ = ett[Bk % 2], etm[Bk % 4]
                    B.act(et_[:, :nq], reg, AF.Exp, [sb_], [et_], scale=1.0 / math.sqrt(128.0))
                    B.tt("dve", em_[:, :nq], et_[:, :nq], amask[:, :nq], ALU.mult, [et_, amask], [em_])
                    pvb, dnb = banks[(Bk // 4) % 2], banks[2 + (Bk // 4) % 2]
                    cs = slice((Bk % 4) * 128, (Bk % 4) * 128 + 128)
                    for (bank_, lh, lr_) in ((pvb, None, None), (dnb, ones, [ones])):
                        l0 = vd[:, Bk, :] if lh is None else ones[:]
                        R0 = [vd] if lh is None else [ones]
                        B.mm(bank_[:, cs], l0, em_[:, 0:128], True, blk == 0, R0 + [em_], [bank_])
                        if blk > 0:
                            emp = etm[(Bk - 1) % 4]
                            l1 = vd[:, Bk - 1, :] if lh is None else ones[:]
                            B.mm(bank_[:, cs], l1, emp[:, 128:256], False, True, R0 + [emp], [bank_])
                    if Bk % 4 == 3:
                        for (bank_, acc) in ((pvb, accn), (dnb, accd)):
                            dst = dil_ap(acc[:], r, 512 * (Bk // 4), 512)
                            if gi == 0:
                                B.copy("dve", dst, like(bank_[:], dst), [bank_], [acc])
                            else:
                                B.tt("dve", dst, dst, like(bank_[:], dst), ALU.add, [bank_, acc], [acc])
            for n in range(4):
                bk = nb([6, 7])
                for k in range(NK):
                    B.mm(bk[:], wb[:, k, 384:512], hT[:, k, 512 * n:512 * n + 512], k == 0, k == NK - 1,
                         [wb, hT_r[k]], [bk])
                B.act(sza[:, 512 * n:512 * n + 512], bk[:], AF.Silu, [bk], [sza])
            S.op("dve", lambda e, a=accd: e.reciprocal(out=a[:], in_=a[:]), [accd.r], [accd.r])
            B.tt("dve", accn[:], accn[:], accd[:], ALU.mult, [accn, accd], [accn])
            B.tt("dve", yaT[:, j, :], accn[:], sza[:], ALU.mult, [accn, sza], [yaT_r[j]])
        if dbg.get("stop") == "PB":
            dump("dbg_ya", yaT[:], yaT_r)
            break
        S.barrier()
        wba = carve(49152, [128, 4, D], BF16, "wba")
        sgg = [carve(57344 + i * 4096, [128, L], BF16, "sgga%d" % i) for i in range(2)]
        tmm = [carve(65536 + i * 2048, [128, 512], F32, "tmm%d" % i) for i in range(2)]
        load_w(wba, io["w_ba"][l])
        for m in range(NK):
            if m % 4 == 0:
                wb = load_wblock(l, 7168 + 512 * (m // 4), 512)
            sg_ = sgg[m % 2]
            for n in range(4):
                bk = nb([6, 7])
                for k in range(NK):
                    B.mm(bk[:], wb[:, k, 128 * (m % 4):128 * (m % 4) + 128], hT[:, k, 512 * n:512 * n + 512],
                         k == 0, k == NK - 1, [wb, hT_r[k]], [bk])
                B.act(sg_[:, 512 * n:512 * n + 512], bk[:], AF.Sigmoid, [bk], [sg_])
            for n in range(4):
                ns = slice(512 * n, 512 * n + 512)
                bk = nb([0, 1, 2, 3])
                for k in range(4):
                    B.mm(bk[:], wba[:, k, 128 * m:128 * m + 128], yaT[:, k, ns], k == 0, k == 3,
                         [wba, yaT_r[k]], [bk])
                t_ = tmm[n % 2]
                B.tt("dve", t_[:], bk[:], sg_[:, ns], ALU.mult, [bk, sg_], [t_])
                B.tt("dve", merged[:, m, ns], merged[:, m, ns], t_[:], ALU.add, [t_, merged_r[m]], [merged_r[m]])
        if dbg.get("stop") == "PC2":
            dump("dbg_m", merged[:], merged_r)
            break
        S.barrier()
        wout = carve(32768, [128, NK, D], BF16, "wout")
        ot = carve(49152, [128, NK, 512], F32, "ot")
        sqo = carve(65536, [128, NK, 512], BF16, "sqo")
        rt2 = carve(73728, [128, 512], F32, "rt2")
        rs2 = carve(75776, [128, 512], F32, "rs2")
        tm2 = [carve(77824 + i * 2048, [128, 512], F32, "tm2%d" % i) for i in range(2)]
        load_w(wout, io["w_out"][l])
        for n in range(4):
            ns = slice(512 * n, 512 * n + 512)
            for mo in range(NK):
                bk = nb([0, 1, 2, 3, 4, 5])
                for k in range(NK):
                    B.mm(bk[:], wout[:, k, 128 * mo:128 * mo + 128], merged[:, k, ns], k == 0, k == NK - 1,
                         [wout, merged_r[k]], [bk])
                B.copy("act", ot[:, mo, :], bk[:], [bk], [ot])
                B.act(sqo[:, mo, :], bk[:], AF.Square, [bk], [sqo])
            sb_ = nb([6, 7])
            for mo in range(NK):
                B.mm(sb_[:], ones[:], sqo[:, mo, :], mo == 0, mo == NK - 1, [ones, sqo], [sb_])
            B.act(rt2[:], sb_[:], AF.Sqrt, [sb_, epsT], [rt2], scale=1.0 / D, bias=epsT[:, 0:1])
            S.op("dve", lambda e, a=rs2, b=rt2: e.reciprocal(out=a[:], in_=b[:]), [rt2.r], [rs2.r])
            for mo in range(NK):
                t_ = tm2[mo % 2]
                B.stt(t_[:], ot[:, mo, :], vecs[:, vb + 8 + mo:vb + 9 + mo], rs2[:], ALU.mult, ALU.mult,
                      [ot, vecs, rs2], [t_])
                B.tt("pool", xT[:, mo, ns], xT[:, mo, ns], t_[:], ALU.add, [t_, xT_r[mo][n]], [xT_r[mo][n]])
        S.barrier()
    else:
        yout = io["yT"].rearrange("(k p) t -> p k t", p=128)
        for k in range(NK):
            out_toks.append(B.dma("sp", yout[:, k, :], xT[:, k, :], xT_r[k], [], owner=Res("out%d" % k)))
    self.finish(out_toks)


Builder.main = _main


_CACHE = {}


def kernel(**inputs):
    h = prep_host(inputs)
    x = np.asarray(inputs["x"], np.float32)
    if "nc" not in _CACHE:
        b = Builder()
        b.build()
        _CACHE["nc"] = b.nc
    nc = _CACHE["nc"]
    shared = {k: h[k] for k in ("w_in", "w_glu", "w_bs", "w_ba", "w_out", "vecs", "ssm_rb", "ssm_rc", "amask")}
    in_maps = []
    for b_ in range(8):
        m = dict(shared)
        m["xT"] = np.ascontiguousarray(x[b_].T)
        in_maps.append(m)
    res = run_bass_kernel_spmd(nc, in_maps, core_ids=list(range(8)))
    out = np.stack([np.ascontiguousarray(res.results[b_]["yT"].T) for b_ in range(8)], axis=0)
    return out.astype(np.float32)
```

```python
import math
from contextlib import ExitStack

import numpy as np
import ml_dtypes

import concourse.bass as bass
import concourse.mybir as mybir
from concourse.bass_utils import run_bass_kernel_spmd

F32 = mybir.dt.float32
BF16 = mybir.dt.bfloat16
ALU = mybir.AluOpType
AF = mybir.ActivationFunctionType

L = 2048
D = 1024
NK = 8
DEPTH = 4
TS = 16
NCH = L // TS
ENGS = ("pe", "act", "dve", "pool", "sp")
STRICT = False


class Res:
    __slots__ = ("w", "r", "name", "dsem", "dcnt")

    def __init__(self, name=""):
        self.w = []
        self.r = []
        self.name = name
        self.dsem = None
        self.dcnt = 0


class Sched:
    def __init__(self, nc, es):
        self.nc = nc
        self.es = es
        self.prog = {e: [] for e in ENGS}
        self.cnt = {e: 0 for e in ENGS}
        self.sem = {e: es.enter_context(nc.semaphore("sem_" + e)) for e in ENGS if e != "sp"}
        self.known = {e: {} for e in ENGS}
        self.nsem = 0
        self.final = []

    def _waits(self, eng, reads, writes):
        toks = []
        for t in reads:
            toks += t.w
        for t in writes:
            toks += [x for x in t.w if STRICT or x[2] != eng]
            toks += [x for x in t.r if STRICT or x[2] != eng]
        need = {}
        for (sem, val, src) in toks:
            if eng == "pe" and src == "pe":
                continue
            if self.known[eng].get(id(sem), 0) >= val:
                continue
            if need.get(id(sem), (None, 0))[1] < val:
                need[id(sem)] = (sem, val)
        out = []
        for k, (sem, val) in need.items():
            self.known[eng][k] = val
            out.append((sem, val))
        return out

    def _commit(self, tok, reads, writes):
        for t in writes:
            t.w = [tok]
            t.r = []
        for t in reads:
            t.r.append(tok)
            if len(t.r) > 24:
                best = {}
                for (s, v, src) in t.r:
                    if id(s) not in best or best[id(s)][1] < v:
                        best[id(s)] = (s, v, src)
                t.r = list(best.values())

    def op(self, eng, fn, reads=(), writes=()):
        waits = self._waits(eng, reads, writes)
        self.cnt[eng] += 1
        sem = self.sem[eng]
        val = self.cnt[eng]

        def emit(e, waits=waits, fn=fn, sem=sem):
            for (s, v) in waits:
                e.wait_ge(s, v)
            fn(e).then_inc(sem, 1)

        self.prog[eng].append(emit)
        self._commit((sem, val, eng), reads, writes)

    def dma(self, q, out, in_, reads=(), writes=(), owner=None):
        waits = self._waits(q, reads, writes)
        own = owner if owner is not None else (writes[0] if writes else reads[0])
        if own.dsem is None:
            own.dsem = self.es.enter_context(self.nc.semaphore("dsem%d" % self.nsem))
            self.nsem += 1
        own.dcnt += 16
        sem, val = own.dsem, own.dcnt

        def emit(e, waits=waits, sem=sem, out=out, in_=in_):
            for (s, v) in waits:
                e.wait_ge(s, v)
            e.dma_start(out=out, in_=in_).then_inc(sem, 16)

        self.prog[q].append(emit)
        tok = (sem, val, "dma")
        for t in writes:
            t.w = [tok]
            t.r = []
        for t in reads:
            t.r.append(tok)
        return tok

    def wait_all(self, q, toks):
        def emit(e, toks=toks):
            for (s, v, _) in toks:
                e.wait_ge(s, v)
        self.prog[q].append(emit)

    def barrier(self, extra=()):
        for e in ENGS:
            waits = [(s, v) for (s, v, _) in extra]
            for o in ENGS:
                if o == "sp" or o == e or self.cnt[o] == 0:
                    continue
                if self.known[e].get(id(self.sem[o]), 0) < self.cnt[o]:
                    self.known[e][id(self.sem[o])] = self.cnt[o]
                    waits.append((self.sem[o], self.cnt[o]))

            def emit(en, waits=waits):
                for (s, v) in waits:
                    en.wait_ge(s, v)
            self.prog[e].append(emit)

    def run(self):
        nc = self.nc
        with nc.Block() as block:
            @block.tensor
            def _(e):
                for f in self.prog["pe"]:
                    f(e)

            @block.scalar
            def _(e):
                for f in self.prog["act"]:
                    f(e)

            @block.vector
            def _(e):
                for f in self.prog["dve"]:
                    f(e)

            @block.gpsimd
            def _(e):
                for f in self.prog["pool"]:
                    f(e)

            @block.sync
            def _(e):
                for f in self.prog["sp"]:
                    f(e)


def mkap(t, offset, dims):
    return bass.AP(tensor=t, offset=offset, ap=[list(d) for d in dims])


def _in_col_perm():
    cols = list(range(0, 1024))
    for j in range(4):
        for gi in range(3):
            for base in (1024, 2560, 4096):
                c0 = base + (gi * 4 + j) * 128
                cols += list(range(c0, c0 + 128))
        cols += list(range(5632 + j * 128, 5632 + (j + 1) * 128))
    cols += list(range(6144, 8192))
    assert len(cols) == 8192 and len(set(cols)) == 8192
    return np.asarray(cols)


def prep_host(inp):
    f32 = np.float32
    out = {}
    out["w_in"] = np.ascontiguousarray(np.asarray(inp["w_in"], f32)[:, :, _in_col_perm()])
    out["w_glu"] = np.ascontiguousarray(np.asarray(inp["w_glu"], f32))
    out["w_bs"] = np.ascontiguousarray(np.asarray(inp["w_branch_s"], f32))
    out["w_ba"] = np.ascontiguousarray(np.asarray(inp["w_branch_a"], f32))
    out["w_out"] = np.ascontiguousarray(np.asarray(inp["w_out"], f32))
    vecs = np.zeros((128, DEPTH * 24), f32)
    for l in range(DEPTH):
        vecs[:, l * 24 + 0:l * 24 + 8] = np.asarray(inp["pre_norm_g"], f32)[l].reshape(8, 128).T
        vecs[:, l * 24 + 8:l * 24 + 16] = np.asarray(inp["post_norm_g"], f32)[l].reshape(8, 128).T
        vecs[:, l * 24 + 16:l * 24 + 20] = np.asarray(inp["d_skip"], f32)[l].reshape(4, 128).T
        vecs[:, l * 24 + 20:l * 24 + 24] = np.asarray(inp["b_glu"], f32)[l].reshape(4, 128).T
    out["vecs"] = vecs
    lre = np.asarray(inp["lambda_re"], f32)
    lim = np.asarray(inp["lambda_im"], f32)
    ldt = np.asarray(inp["log_dt"], f32)
    bre = np.asarray(inp["b_re"], f32)
    bim = np.asarray(inp["b_im"], f32)
    cre = np.asarray(inp["c_re"], f32)
    cim = np.asarray(inp["c_im"], f32)
    rb = np.zeros((DEPTH, 128, 5, 4, 128), f32)
    rc = np.zeros((DEPTH, 128, 48 + 512), f32)
    for l in range(DEPTH):
        lam_g = lre[l].reshape(4, 4, 2, 64)
        lim_g = lim[l].reshape(4, 4, 2, 64)
        ldt_g = np.broadcast_to(ldt[l].reshape(4, 4, 2, 1), (4, 4, 2, 64))
        for ptl in range(4):
            rows = slice(32 * ptl, 32 * ptl + 32)
            rb[l, rows, 0] = lam_g[:, ptl].reshape(4, 128)[None]
            rb[l, rows, 1] = lim_g[:, ptl].reshape(4, 128)[None]
            rb[l, rows, 2] = ldt_g[:, ptl].reshape(4, 128)[None]
            for h in range(2):
                for j in range(4):
                    g = 8 * j + 2 * ptl + h
                    r0 = 32 * ptl + 16 * h
                    rb[l, r0:r0 + 16, 3, j, 64 * h:64 * h + 64] = bre[l, g].T
                    rb[l, r0:r0 + 16, 4, j, 64 * h:64 * h + 64] = bim[l, g].T
        rc[l, :, 0:16] = lre[l].reshape(16, 128).T
        rc[l, :, 16:32] = lim[l].reshape(16, 128).T
        rc[l, :, 32:48] = np.broadcast_to(ldt[l].reshape(16, 2, 1), (16, 2, 64)).reshape(16, 128).T
        cc = cre[l].reshape(16, 2, 16, 64).transpose(1, 3, 0, 2).reshape(128, 16 * 16)
        ci = cim[l].reshape(16, 2, 16, 64).transpose(1, 3, 0, 2).reshape(128, 16 * 16)
        rc[l, :, 48:48 + 256] = cc
        rc[l, :, 48 + 256:48 + 512] = ci
    out["ssm_rb"] = rb.reshape(DEPTH, 128, 5 * 512)
    out["ssm_rc"] = rc
    am = np.zeros((128, 256), f32)
    kk = np.arange(128)[:, None]
    qq = np.arange(128)[None, :]
    am[:, 0:128] = (qq >= kk)
    am[:, 128:256] = (qq <= kk)
    out["amask"] = am.astype(ml_dtypes.bfloat16)
    return out


class TL:
    def __init__(self, t, name):
        self.t = t
        self.r = Res(name)

    def __getitem__(self, idx):
        return self.t[idx]


class Builder:
    def __init__(self, n_layers=DEPTH, debug=None):
        self.n_layers = n_layers
        self.debug = debug or {}
        self.nc = bass.Bass("TRN2", target_bir_lowering=False)
        self.es = ExitStack()
        self.S = Sched(self.nc, self.es)
        self.uid = 0

    def sb(self, es, shape, dtype, name=None):
        self.uid += 1
        nm = (name or "t") + "_%d" % self.uid
        return TL(es.enter_context(self.nc.sbuf_tensor(nm, list(shape), dtype)), nm)

    def ps(self, es, shape, dtype=F32, name=None):
        self.uid += 1
        nm = (name or "p") + "_%d" % self.uid
        return TL(es.enter_context(self.nc.psum_tensor(nm, list(shape), dtype)), nm)

    @staticmethod
    def _res(lst):
        return [x.r if isinstance(x, TL) else x for x in lst]

    def tt(self, eng, out, a, b, op, R, W):
        self.S.op(eng, lambda e: e.tensor_tensor(out=out, in0=a, in1=b, op=op), self._res(R), self._res(W))

    def ts(self, eng, out, a, s1, op0, R, W, s2=None, op1=None):
        if op1 is None:
            self.S.op(eng, lambda e: e.tensor_scalar(out=out, in0=a, scalar1=s1, scalar2=None, op0=op0),
                      self._res(R), self._res(W))
        else:
            self.S.op(eng, lambda e: e.tensor_scalar(out=out, in0=a, scalar1=s1, scalar2=s2, op0=op0, op1=op1),
                      self._res(R), self._res(W))

    def stt(self, out, a, scalar, b, op0, op1, R, W):
        self.S.op("dve", lambda e: e.scalar_tensor_tensor(out=out, in0=a, scalar=scalar, in1=b, op0=op0, op1=op1),
                  self._res(R), self._res(W))

    def act(self, out, a, func, R, W, scale=1.0, bias=0.0):
        self.S.op("act", lambda e: e.activation(out=out, in_=a, func=func, bias=bias, scale=scale),
                  self._res(R), self._res(W))

    def copy(self, eng, out, a, R, W):
        if eng == "act":
            self.S.op("act", lambda e: e.activation(out=out, in_=a, func=AF.Copy), self._res(R), self._res(W))
        else:
            self.S.op(eng, lambda e: e.tensor_copy(out=out, in_=a), self._res(R), self._res(W))

    def memset(self, eng, ap, val, W):
        self.S.op(eng, lambda e: e.memset(ap, val), [], self._res(W))

    def mm(self, out, lhsT, rhs, start, stop, R, W, tp=None):
        if tp is None:
            self.S.op("pe", lambda e: e.matmul(out, lhsT=lhsT, rhs=rhs, start=start, stop=stop),
                      self._res(R), self._res(W))
        else:
            self.S.op("pe", lambda e: e.matmul(out, lhsT=lhsT, rhs=rhs, start=start, stop=stop, tile_position=tp),
                      self._res(R), self._res(W))

    def dma(self, q, out, in_, R, W, owner=None):
        return self.S.dma(q, out, in_, self._res(R), self._res(W),
                          owner.r if isinstance(owner, TL) else owner)

    def cmul(self, eng, o, a, b, t1, t2, R, W, Tm):
        (orr, oi), (ar, ai), (br, bi) = o, a, b
        self.tt(eng, t1, ar, br, ALU.mult, R, Tm)
        self.tt(eng, t2, ai, bi, ALU.mult, R, Tm)
        self.tt(eng, orr, t1, t2, ALU.subtract, Tm, W)
        self.tt(eng, t1, ar, bi, ALU.mult, R, Tm)
        self.tt(eng, t2, ai, br, ALU.mult, R, Tm)
        self.tt(eng, oi, t1, t2, ALU.add, Tm, W)

    def declare_io(self):
        nc = self.nc
        nl = DEPTH
        d = {}
        d["xT"] = nc.dram_tensor("xT", [D, L], F32, kind="ExternalInput").ap()
        d["w_in"] = nc.dram_tensor("w_in", [nl, D, 8192], F32, kind="ExternalInput").ap()
        d["w_glu"] = nc.dram_tensor("w_glu", [nl, 512, 512], F32, kind="ExternalInput").ap()
        d["w_bs"] = nc.dram_tensor("w_bs", [nl, 512, D], F32, kind="ExternalInput").ap()
        d["w_ba"] = nc.dram_tensor("w_ba", [nl, 512, D], F32, kind="ExternalInput").ap()
        d["w_out"] = nc.dram_tensor("w_out", [nl, D, D], F32, kind="ExternalInput").ap()
        d["vecs"] = nc.dram_tensor("vecs", [128, nl * 24], F32, kind="ExternalInput").ap()
        d["ssm_rb"] = nc.dram_tensor("ssm_rb", [nl, 128, 2560], F32, kind="ExternalInput").ap()
        d["ssm_rc"] = nc.dram_tensor("ssm_rc", [nl, 128, 560], F32, kind="ExternalInput").ap()
        d["amask"] = nc.dram_tensor("amask", [128, 256], BF16, kind="ExternalInput").ap()
        d["yT"] = nc.dram_tensor("yT", [D, L], F32, kind="ExternalOutput").ap()
        tk = "ExternalOutput" if self.debug.get("tables") else "Internal"
        d["tabB"] = nc.dram_tensor("tabB", [nl, 128, 16384], BF16, kind=tk).ap()
        d["tabC"] = nc.dram_tensor("tabC", [nl, 128, 16384], BF16, kind=tk).ap()
        for name, (shape, dt) in self.debug.get("outs", {}).items():
            d[name] = nc.dram_tensor(name, list(shape), dt, kind="ExternalOutput").ap()
        self.io = d

    def lam_alloc(self, es, Fd, tag):
        names = ("dt", "lr", "a", "th", "mag", "em2a", "c", "s", "t1", "t2", "lbr", "lbi")
        return {n: self.sb(es, [128, Fd], F32, tag + n) for n in names}

    def lam_math(self, m, src):
        B = self
        dt, lr, a, th, mag, em2a, c, s, t1, t2, lbr, lbi = [m[n] for n in
                                                          ("dt", "lr", "a", "th", "mag", "em2a", "c", "s", "t1", "t2", "lbr", "lbi")]
        R = src["R"]
        B.act(dt[:], src["ldt"], AF.Exp, R, [dt])
        B.ts("dve", lr[:], src["lre"], -1e-4, ALU.min, R, [lr])
        B.tt("dve", a[:], lr[:], dt[:], ALU.mult, [lr, dt], [a])
        B.tt("dve", th[:], src["lim"], dt[:], ALU.mult, R + [dt], [th])
        B.act(mag[:], a[:], AF.Exp, [a], [mag])
        B.act(em2a[:], a[:], AF.Exp, [a], [em2a], scale=-2.0)
        B.act(s[:], th[:], AF.Sin, [th], [s], scale=1.0 / 16.0)
        B.act(c[:], th[:], AF.Sin, [th, self.halfpi], [c], scale=1.0 / 16.0, bias=self.halfpi[:, 0:1])
        for _ in range(4):
            B.tt("dve", t1[:], c[:], c[:], ALU.mult, [c], [t1])
            B.tt("dve", t2[:], s[:], s[:], ALU.mult, [s], [t2])
            B.stt(s[:], c[:], 2.0, s[:], ALU.mult, ALU.mult, [c, s], [s])
            B.tt("dve", c[:], t1[:], t2[:], ALU.subtract, [t1, t2], [c])
        B.tt("dve", lbr[:], mag[:], c[:], ALU.mult, [mag, c], [lbr])
        B.tt("dve", lbi[:], mag[:], s[:], ALU.mult, [mag, s], [lbi])

    def prologue(self):
        B = self
        io = self.io
        toks = []
        with ExitStack() as es:
            mb = B.lam_alloc(es, 512, "rb")
            inpb = B.sb(es, [128, 5, 512], F32, "rbin")
            mk = lambda n: B.sb(es, [128, 512], F32, "rb" + n)
            den, nr, cr, ci, invr, invi, bbr, bbi = [mk(n) for n in ("den", "nr", "cr", "ci", "invr", "invi", "bbr", "bbi")]
            xs = [(mk("xr0"), mk("xi0")), (mk("xr1"), mk("xi1"))]
            bt = B.sb(es, [128, 4, TS, 2, 128], BF16, "bt")
            mc = B.lam_alloc(es, 16, "rc")
            inpc = B.sb(es, [128, 560], F32, "rcin")
            pw = B.sb(es, [128, 2, 16, TS], F32, "pw")
            ct = B.sb(es, [128, 16, TS, 2, 32], BF16, "ct")
            big = lambda n: B.sb(es, [128, 16, TS, 16], F32, "rc" + n)
            u1, u2, u3 = big("u1"), big("u2"), big("u3")
            B.memset("pool", ct[:], 0.0, [ct])
            for l in range(self.n_layers):
                inp = inpb
                B.dma("sp", inp[:], io["ssm_rb"][l].rearrange("p (a f) -> p a f", a=5), [], [inp])
                B.lam_math(mb, dict(lre=inp[:, 0, :], lim=inp[:, 1, :], ldt=inp[:, 2, :], R=[inp]))
                lr, lbr, lbi, em2a, t1, t2 = [mb[n] for n in ("lr", "lbr", "lbi", "em2a", "t1", "t2")]
                li = inp[:, 1, :]
                B.tt("dve", t1[:], lr[:], lr[:], ALU.mult, [lr], [t1])
                B.tt("dve", t2[:], li, li, ALU.mult, [inp], [t2])
                B.tt("dve", den[:], t1[:], t2[:], ALU.add, [t1, t2], [den])
                B.S.op("dve", lambda e, den=den: e.reciprocal(out=den[:], in_=den[:]), [den.r], [den.r])
                B.ts("dve", nr[:], lbr[:], -1.0, ALU.add, [lbr], [nr])
                B.tt("dve", t1[:], nr[:], lr[:], ALU.mult, [nr, lr], [t1])
                B.tt("dve", t2[:], lbi[:], li, ALU.mult, [lbi, inp], [t2])
                B.tt("dve", t1[:], t1[:], t2[:], ALU.add, [t1, t2], [t1])
                B.tt("dve", cr[:], t1[:], den[:], ALU.mult, [t1, den], [cr])
                B.tt("dve", t1[:], lbi[:], lr[:], ALU.mult, [lbi, lr], [t1])
                B.tt("dve", t2[:], nr[:], li, ALU.mult, [nr, inp], [t2])
                B.tt("dve", t1[:], t1[:], t2[:], ALU.subtract, [t1, t2], [t1])
                B.tt("dve", ci[:], t1[:], den[:], ALU.mult, [t1, den], [ci])
                B.tt("dve", invr[:], lbr[:], em2a[:], ALU.mult, [lbr, em2a], [invr])
                B.stt(invi[:], lbi[:], -1.0, em2a[:], ALU.mult, ALU.mult, [lbi, em2a], [invi])
                B.cmul("dve", (bbr[:], bbi[:]), (cr[:], ci[:]), (inp[:, 3, :], inp[:, 4, :]), t1[:], t2[:],
                       [cr, ci, inp], [bbr, bbi], [t1, t2])
                prev = (bbr, bbi)
                for i in range(TS):
                    cur = xs[i % 2]
                    B.cmul("dve", (cur[0][:], cur[1][:]), (prev[0][:], prev[1][:]), (invr[:], invi[:]), t1[:], t2[:],
                           [prev[0], prev[1], invr, invi], [cur[0], cur[1]], [t1, t2])
                    B.copy("act", bt[:, :, i, 0, :], cur[0][:].rearrange("p (j q) -> p j q", j=4), [cur[0]], [bt])
                    B.copy("act", bt[:, :, i, 1, :], cur[1][:].rearrange("p (j q) -> p j q", j=4), [cur[1]], [bt])
                    prev = cur
                toks.append(B.dma("sp", io["tabB"][l], bt[:].rearrange("p j i r q -> p (j i r q)"),
                                  [bt], [self.tabB_r[l]], owner=bt))
                inp = inpc
                B.dma("sp", inp[:], io["ssm_rc"][l], [], [inp])
                B.lam_math(mc, dict(lre=inp[:, 0:16], lim=inp[:, 16:32], ldt=inp[:, 32:48], R=[inp]))
                lbr, lbi, t1, t2 = [mc[n] for n in ("lbr", "lbi", "t1", "t2")]
                B.copy("dve", pw[:, 0, :, 0], lbr[:], [lbr], [pw])
                B.copy("dve", pw[:, 1, :, 0], lbi[:], [lbi], [pw])
                for i in range(1, TS):
                    B.cmul("dve", (pw[:, 0, :, i], pw[:, 1, :, i]), (pw[:, 0, :, i - 1], pw[:, 1, :, i - 1]),
                           (lbr[:], lbi[:]), t1[:], t2[:], [pw, lbr, lbi], [pw], [t1, t2])
                PL = self.PL
                B.copy("dve", PL[:, l, :, 0], pw[:, 0, :, TS - 1], [pw], [PL])
                B.copy("dve", PL[:, l, :, 1], pw[:, 1, :, TS - 1], [pw], [PL])
                for k in range(1, 7):
                    pr, pi = PL[:, l, :, 3 * k - 3], PL[:, l, :, 3 * k - 2]
                    B.tt("dve", t1[:], pr, pr, ALU.mult, [PL], [t1])
                    B.tt("dve", t2[:], pi, pi, ALU.mult, [PL], [t2])
                    B.tt("dve", PL[:, l, :, 3 * k], t1[:], t2[:], ALU.subtract, [t1, t2], [PL])
                    B.stt(PL[:, l, :, 3 * k + 1], pr, 2.0, pi, ALU.mult, ALU.mult, [PL], [PL])
                for k in range(7):
                    B.ts("dve", PL[:, l, :, 3 * k + 2], PL[:, l, :, 3 * k + 1], -1.0, ALU.mult, [PL], [PL])
                cre = inp[:, 48:304].rearrange("p (t c) -> p t c", c=16)
                cim = inp[:, 304:560].rearrange("p (t c) -> p t c", c=16)

                def bc_c(ap3):
                    return mkap(ap3.tensor, ap3.offset, [ap3.ap[0], ap3.ap[1], [0, TS], ap3.ap[2]])

                def bc_p(k):
                    a = pw[:, k, :, :]
                    return mkap(a.tensor, a.offset, [a.ap[0], a.ap[1], a.ap[2], [0, 16]])
                B.tt("dve", u1[:], bc_c(cre), bc_p(0), ALU.mult, [inp, pw], [u1])
                B.tt("dve", u2[:], bc_c(cim), bc_p(1), ALU.mult, [inp, pw], [u2])
                B.tt("dve", u3[:], u1[:], u2[:], ALU.subtract, [u1, u2], [u3])
                B.copy("act", ct[0:64, :, :, 0, 0:16], u3[0:64], [u3], [ct])
                B.copy("act", ct[64:128, :, :, 0, 16:32], u3[64:128], [u3], [ct])
                B.tt("dve", u1[:], bc_c(cre), bc_p(1), ALU.mult, [inp, pw], [u1])
                B.tt("dve", u2[:], bc_c(cim), bc_p(0), ALU.mult, [inp, pw], [u2])
                B.stt(u3[:], u1[:], -1.0, u2[:], ALU.mult, ALU.subtract, [u1, u2], [u3])
                B.copy("act", ct[0:64, :, :, 1, 0:16], u3[0:64], [u3], [ct])
                B.copy("act", ct[64:128, :, :, 1, 16:32], u3[64:128], [u3], [ct])
                toks.append(B.dma("sp", io["tabC"][l], ct[:].rearrange("p t i r c -> p (t i r c)"),
                                  [ct], [self.tabC_r[l]], owner=ct))
        self.S.barrier(toks)

    def build(self):
        B = self
        nc = self.nc
        es = self.es
        self.declare_io()
        io = self.io
        self.tabB_r = [Res("tabB%d" % l) for l in range(DEPTH)]
        self.tabC_r = [Res("tabC%d" % l) for l in range(DEPTH)]
        self.halfpi = B.sb(es, [128, 1], F32, "halfpi")
        B.memset("dve", self.halfpi[:], math.pi / 2.0, [self.halfpi])
        self.PL = B.sb(es, [128, DEPTH, 16, 24], F32, "PL")
        self.prologue()
        if self.debug.get("tables"):
            self.finish([])
            return
        self.main()

    def finish(self, out_toks):
        self.S.wait_all("sp", out_toks)
        self.S.barrier(out_toks)
        self.S.run()


class V:
    def __init__(self, ap, name=""):
        self.ap = ap
        self.r = Res(name)

    def __getitem__(self, idx):
        return self.ap[idx]


def _res_of(lst):
    return [x.r if hasattr(x, "r") and not isinstance(x, Res) else x for x in lst]


Builder._res = staticmethod(_res_of)


def dil_ap(base, r, col0, ncols):
    t, off, pdim, st = base.tensor, base.offset, list(base.ap[0]), base.ap[-1][0]
    assert len(base.ap) == 2
    nsub = L // r
    c0, i0 = divmod(col0, nsub)
    if r == 1:
        dims, o = [[st, ncols]], col0
    elif i0 + ncols <= nsub:
        dims, o = [[r * st, ncols]], c0 + r * i0
    else:
        assert i0 == 0 and ncols % nsub == 0
        dims, o = [[st, ncols // nsub], [r * st, nsub]], c0
    return mkap(t, off + o * st, [pdim] + dims)


def like(bank_ap, ap):
    if len(ap.ap) == 3:
        return bank_ap.rearrange("p (a b) -> p a b", a=ap.ap[1][1])
    return bank_ap


AR_BYTES = 88064


def _main(self):
    B = self
    es = self.es
    io = self.io
    S = self.S
    nc = self.nc
    dbg = self.debug
    xT = B.sb(es, [128, NK, L], F32, "xT")
    xT_r = [[Res("xT%d_%d" % (k, n)) for n in range(4)] for k in range(NK)]
    hT = B.sb(es, [128, NK, L], BF16, "hT")
    hT_r = [Res("hT%d" % k) for k in range(NK)]
    vecs = B.sb(es, [128, DEPTH * 24], F32, "vecs")
    amask = B.sb(es, [128, 256], BF16, "amask")
    ones = B.sb(es, [128, 128], BF16, "ones")
    epsT = B.sb(es, [128, 1], F32, "eps")
    wbuf = [B.sb(es, [128, NK, 512], BF16, "wbuf%d" % i) for i in range(2)]
    arena = B.sb(es, [128, AR_BYTES // 2], BF16, "arena")
    banks = [B.ps(es, [128, 512], F32, "bank%d" % i) for i in range(8)]
    self.wi = 0

    def carve(off, shape, dtype, name):
        nel = int(np.prod(shape[1:]))
        esz = 2 if dtype == BF16 else 4
        assert off % 4 == 0 and off + nel * esz <= AR_BYTES, (name, off, nel * esz)
        a = arena.t[:, off // 2: off // 2 + nel * esz // 2]
        if dtype == F32:
            a = a.bitcast(F32)
        if len(shape) == 3:
            a = a.rearrange("p (a b) -> p a b", a=shape[1])
        elif len(shape) == 4:
            a = a.rearrange("p (a b c) -> p a b c", a=shape[1], b=shape[2])
        elif len(shape) == 5:
            a = a.rearrange("p (a b c d) -> p a b c d", a=shape[1], b=shape[2], c=shape[3])
        return V(a, name)

    B.memset("dve", ones[:], 1.0, [ones])
    B.memset("dve", epsT[:], 1e-6, [epsT])
    B.dma("sp", vecs[:], io["vecs"], [], [vecs])
    B.dma("sp", amask[:], io["amask"], [], [amask])
    xin = io["xT"].rearrange("(k p) t -> p k t", p=128)
    for k in range(NK):
        B.dma("sp", xT[:, k, :], xin[:, k, :], [], xT_r[k], owner=xT_r[k][0])

    out_toks = []

    def dump(name, ap, R):
        out_toks.append(B.dma("sp", io[name], ap, R, [], owner=Res("dump_" + name)))

    def load_wblock(l, col0, ncols):
        wb = wbuf[self.wi % 2]
        self.wi += 1
        src = io["w_in"][l][:, col0:col0 + ncols].rearrange("(k p) c -> p k c", p=128)
        B.dma("pool", wb[:, :, 0:ncols], src, [], [wb])
        return wb

    def load_w(dst, src2d):
        B.dma("pool", dst[:], src2d.rearrange("(k p) c -> p k c", p=128), [], [dst])

    bank_ctr = [0]

    def nb(lst):
        b = banks[lst[bank_ctr[0] % len(lst)]]
        bank_ctr[0] += 1
        return b

    def hk(k):
        return hT.t[:, k, :]

    for l in range(self.n_layers):
        vb = l * 24
        sq = [carve(i * 8192, [128, NK, 512], BF16, "sq%d" % i) for i in range(2)]
        rt = [carve(16384 + i * 2048, [128, 512], F32, "rt%d" % i) for i in range(2)]
        rs = [carve(20480 + i * 2048, [128, 512], F32, "rs%d" % i) for i in range(2)]
        for n in range(4):
            ns = slice(n * 512, (n + 1) * 512)
            s_, rt_, rs_ = sq[n % 2], rt[n % 2], rs[n % 2]
            for k in range(NK):
                B.act(s_[:, k, :], xT[:, k, ns], AF.Square, [xT_r[k][n]], [s_])
            bk = nb([6, 7])
            for k in range(NK):
                B.mm(bk[:], ones[:], s_[:, k, :], k == 0, k == NK - 1, [ones, s_], [bk])
            B.act(rt_[:], bk[:], AF.Sqrt, [bk, epsT], [rt_], scale=1.0 / D, bias=epsT[:, 0:1])
            S.op("dve", lambda e, a=rs_, b=rt_: e.reciprocal(out=a[:], in_=b[:]), [rt_.r], [rs_.r])
            for k in range(NK):
                B.stt(hT[:, k, ns], xT[:, k, ns], vecs[:, vb + k:vb + k + 1], rs_[:], ALU.mult, ALU.mult,
                      [xT_r[k][n], vecs, rs_], [hT_r[k]])
        if dbg.get("stop") == "P0":
            dump("dbg_h", hT[:], hT_r)
            break
        S.barrier()
        yT = carve(32768, [128, 4, L], BF16, "yT")
        yT_r = [[Res("yT%d_%d" % (k, n)) for n in range(4)] for k in range(4)]
        r32 = carve(0, [128, 2, TS, 128], F32, "r32")
        rbf = carve(16384, [128, 2, TS, 128], BF16, "rbf")
        Bt = carve(24576, [128, TS, 2, 128], BF16, "Bt")
        Ct = carve(49152, [128, 5120], BF16, "Ct")
        CtA = Ct[:, 0:3072].rearrange("p (t i r c) -> p t i r c", t=3, i=TS, r=2)
        CtB = Ct[:, 3072:5120].rearrange("p (i r c) -> p i r c", i=TS, r=2)
        uT = carve(59392, [128, L], BF16, "uT")
        wglu = carve(63488, [128, 4, 512], BF16, "wglu")
        Sab = [carve(67584 + i * 1024, [128, 2, 128], F32, "S%d" % i) for i in range(2)]
        zt = carve(69632, [128, 2, 128], F32, "zt")
        tq = carve(70656, [128, 2, 128], F32, "tq")
        Sbf = carve(71680, [128, 2, 128], BF16, "Sbf")
        gv = [carve(72192 + i * 2048, [128, 512], F32, "gv%d" % i) for i in range(2)]
        gw = [carve(76288 + i * 2048, [128, 512], F32, "gw%d" % i) for i in range(2)]
        gs = [carve(80384 + i * 2048, [128, 512], F32, "gs%d" % i) for i in range(2)]
        gz = [carve(84480 + i * 1024, [128, 512], BF16, "gz%d" % i) for i in range(2)]
        Ctz = V(CtB[:, :, :, 0:32], "Ctz")
        B.memset("pool", Ctz[:], 0.0, [Ctz])
        B.memset("dve", Sbf[:], 0.0, [Sbf])
        wbU = load_wblock(l, 0, 512)
        wbZ = load_wblock(l, 512, 512)
        load_w(wglu, io["w_glu"][l])
        PLl = self.PL
        for j in dbg.get("js", range(4)):
            for n in range(4):
                bk = nb([6, 7])
                for k in range(NK):
                    B.mm(bk[:], wbU[:, k, 128 * j:128 * j + 128], like(dil_ap(hk(k), 16, 512 * n, 512), None) if False else dil_ap(hk(k), 16, 512 * n, 512),
                         k == 0, k == NK - 1, [wbU, hT_r[k]], [bk])
                B.copy("act", uT[:, 512 * n:512 * n + 512], bk[:], [bk], [uT])
            B.dma("sp", Bt[:].rearrange("p i r q -> p (i r q)"), io["tabB"][l][:, 4096 * j:4096 * (j + 1)],
                  [self.tabB_r[l]], [Bt])
            B.dma("sp", Ct[:, 0:3072], io["tabC"][l][:, 4096 * j:4096 * j + 3072], [self.tabC_r[l]], [Ct])
            B.dma("sp", CtB[:, :, :, 32:64],
                  io["tabC"][l][:, 4096 * j + 3072:4096 * (j + 1)].rearrange("p (i r c) -> p i r c", i=TS, r=2),
                  [self.tabC_r[l]], [Ct])
            def xt_chain(ptl):
                rows = slice(32 * ptl, 32 * ptl + 32)
                for ih in range(8):
                    xb = nb([4, 5])
                    xv = xb[:].rearrange("p (r i c) -> p r i c", r=2, i=2)
                    for ri in range(2):
                        for il in range(2):
                            i = 2 * ih + il
                            B.mm(xv[:, ri, il, :], Bt[rows, i, ri, :], uT[rows, i * 128:(i + 1) * 128],
                                 True, True, [Bt, uT], [xb], tp=(32 * ptl, 0))
                    for il in range(2):
                        i = 2 * ih + il
                        if i == 0:
                            B.copy("dve", r32[:, :, 0, :], xv[:, :, 0, :], [xb], [r32])
                        else:
                            B.tt("dve", r32[:, :, i, :], r32[:, :, i - 1, :], xv[:, :, il, :], ALU.add,
                                 [xb, r32], [r32])

            def l2(ptl):
                pt = 4 * j + ptl
                B.copy("act", rbf[:], r32[:], [r32], [rbf])
                pl = lambda c: PLl[:, l, pt, c:c + 1]
                rr, rim = r32[:, 0, TS - 1, :], r32[:, 1, TS - 1, :]
                B.ts("dve", tq[:, 0, :], rim, pl(1), ALU.mult, [r32, PLl], [tq])
                B.stt(zt[:, 0, :], rr, pl(0), tq[:, 0, :], ALU.mult, ALU.subtract, [r32, PLl, tq], [zt])
                B.ts("dve", tq[:, 1, :], rr, pl(1), ALU.mult, [r32, PLl], [tq])
                B.stt(zt[:, 1, :], rim, pl(0), tq[:, 1, :], ALU.mult, ALU.add, [r32, PLl, tq], [zt])
                old = zt
                for kk in range(7):
                    d = 1 << kk
                    new = Sab[kk % 2]
                    a_, b_, nb_ = pl(3 * kk), pl(3 * kk + 1), pl(3 * kk + 2)
                    B.stt(new[:, :, d:], old[:, :, :128 - d], a_, old[:, :, d:], ALU.mult, ALU.add, [old, PLl], [new])
                    B.stt(new[:, 0, d:], old[:, 1, :128 - d], nb_, new[:, 0, d:], ALU.mult, ALU.add, [old, new, PLl], [new])
                    B.stt(new[:, 1, d:], old[:, 0, :128 - d], b_, new[:, 1, d:], ALU.mult, ALU.add, [old, new, PLl], [new])
                    B.copy("dve", new[:, :, :d], old[:, :, :d], [old], [new])
                    old = new
                B.copy("dve", Sbf[:, :, 1:128], old[:, :, 0:127], [old], [Sbf])

            def readout(ptl):
                rows = slice(32 * ptl, 32 * ptl + 32)
                for i in range(TS):
                    yb = banks[i // 4]
                    cs_ = slice((i % 4) * 128, (i % 4) * 128 + 128)
                    if ptl == 3:
                        o_ = yb[64:128, cs_]
                        tp = (0, 64)
                        lh = [CtB[:, i, rr_, :] for rr_ in range(2)]
                        RC_ = [Ct, Ctz]
                    else:
                        o_ = yb[rows, cs_]
                        tp = (0, 32 * ptl)
                        lh = [CtA[:, ptl, i, rr_, :] for rr_ in range(2)]
                        RC_ = [Ct]
                    B.mm(o_, lh[0], rbf[:, 0, i, :], True, False, RC_ + [rbf], [yb], tp=tp)
                    B.mm(o_, lh[1], rbf[:, 1, i, :], False, False, RC_ + [rbf], [yb], tp=tp)
                    B.mm(o_, lh[0], Sbf[:, 0, :], False, False, RC_ + [Sbf], [yb], tp=tp)
                    B.mm(o_, lh[1], Sbf[:, 1, :], False, True, RC_ + [Sbf], [yb], tp=tp)

            order = (0, 1, 3, 2)
            xt_chain(order[0])
            for idx, ptl in enumerate(order):
                l2(ptl)
                if idx + 1 < 4:
                    xt_chain(order[idx + 1])
                readout(ptl)
            for ib in range(4):
                cs = slice(512 * ib, 512 * ib + 512)
                v_, w_, g_ = gv[ib % 2], gw[ib % 2], gs[ib % 2]
                B.stt(v_[:], uT[:, cs], vecs[:, vb + 16 + j:vb + 17 + j], banks[ib][:], ALU.mult, ALU.add,
                      [uT, vecs, banks[ib]], [v_])
                B.act(w_[:], v_[:], AF.Square, [v_], [w_])
                B.ts("pool", w_[:], w_[:], 0.044715, ALU.mult, [w_], [w_], s2=1.0, op1=ALU.add)
                B.tt("pool", w_[:], w_[:], v_[:], ALU.mult, [w_, v_], [w_])
                B.act(g_[:], w_[:], AF.Sigmoid, [w_], [g_], scale=1.5957691216057308)
                B.tt("dve", yT[:, j, cs], v_[:], g_[:], ALU.mult, [v_, g_], [yT_r[j][ib]])
        if dbg.get("stop") == "PA1":
            dump("dbg_y", yT[:], [x for row in yT_r for x in row])
            break
        for n in range(4):
            cs = slice(512 * n, 512 * n + 512)
            for mo in range(4):
                gb = banks[mo]
                for k in range(4):
                    B.mm(gb[:], wglu[:, k, 128 * mo:128 * mo + 128], yT[:, k, cs], k == 0, k == 3,
                         [wglu, yT_r[k][n]], [gb])
            for mo in range(4):
                zb = nb([4, 5])
                for k in range(NK):
                    B.mm(zb[:], wbZ[:, k, 128 * mo:128 * mo + 128], dil_ap(hk(k), 16, 512 * n, 512),
                         k == 0, k == NK - 1, [wbZ, hT_r[k]], [zb])
                z_, g_ = gz[mo % 2], gs[mo % 2]
                B.act(z_[:], zb[:], AF.Silu, [zb], [z_])
                B.act(g_[:], banks[mo][:], AF.Sigmoid, [banks[mo], vecs], [g_], bias=vecs[:, vb + 20 + mo:vb + 21 + mo])
                B.tt("dve", g_[:], g_[:], z_[:], ALU.mult, [g_, z_], [g_])
                B.tt("dve", yT[:, mo, cs], yT[:, mo, cs], g_[:], ALU.mult, [yT_r[mo][n], g_], [yT_r[mo][n]])
        if dbg.get("stop") == "PA":
            dump("dbg_y", yT[:], [x for row in yT_r for x in row])
            break
        S.barrier()
        merged = carve(0, [128, NK, L], BF16, "merged")
        merged_r = [Res("mg%d" % m) for m in range(NK)]
        wbs = carve(49152, [128, 4, D], BF16, "wbs")
        sgg = [carve(57344 + i * 4096, [128, L], BF16, "sgg%d" % i) for i in range(2)]
        load_w(wbs, io["w_bs"][l])
        for m in range(NK):
            if m % 4 == 0:
                wb = load_wblock(l, 6144 + 512 * (m // 4), 512)
            sg_ = sgg[m % 2]
            for n in range(4):
                bk = nb([6, 7])
                for k in range(NK):
                    B.mm(bk[:], wb[:, k, 128 * (m % 4):128 * (m % 4) + 128], hT[:, k, 512 * n:512 * n + 512],
                         k == 0, k == NK - 1, [wb, hT_r[k]], [bk])
                B.act(sg_[:, 512 * n:512 * n + 512], bk[:], AF.Sigmoid, [bk], [sg_])
            for n2 in range(4):
                bk = nb([0, 1, 2, 3])
                for k in range(4):
                    B.mm(bk[:], wbs[:, k, 128 * m:128 * m + 128], yT[:, k, 512 * n2:512 * n2 + 512], k == 0, k == 3,
                         [wbs, yT_r[k][n2]], [bk])
                dst = dil_ap(merged[:, m, :], 16, 512 * n2, 512)
                B.tt("dve", dst, like(bk[:], dst), dil_ap(sg_[:], 16, 512 * n2, 512), ALU.mult,
                     [bk, sg_], [merged_r[m]])
        if dbg.get("stop") == "PA2":
            dump("dbg_m", merged[:], merged_r)
            break
        S.barrier()
        yaT = carve(32768, [128, 4, L], BF16, "yaT")
        yaT_r = [Res("ya%d" % j) for j in range(4)]
        qT = carve(49152, [128, L], BF16, "qT")
        kT = carve(53248, [128, L], BF16, "kT")
        vd = carve(57344, [128, 16, 128], BF16, "vd")
        etm = [carve(61440 + i * 512, [128, 256], BF16, "etm%d" % i) for i in range(6)]
        ett = [carve(64512 + i * 512, [128, 256], BF16, "ett%d" % i) for i in range(4)]
        accn = carve(66560, [128, L], F32, "accn")
        accd = carve(74752, [128, L], F32, "accd")
        sza = carve(82944, [128, L], BF16, "sza")
        for j in range(4):
            for gi, r in enumerate((1, 4, 16)):
                ncols = 512 if gi == 2 else 384
                wb = load_wblock(l, 1024 + 1280 * j + 384 * gi, ncols)
                nsub = L // r
                nblk = nsub // 128
                for (c_in, dst) in ((0, qT), (128, kT)):
                    for n in range(4):
                        bk = nb([6, 7])
                        for k in range(NK):
                            B.mm(bk[:], wb[:, k, c_in:c_in + 128], dil_ap(hk(k), r, 512 * n, 512), k == 0, k == NK - 1,
                                 [wb, hT_r[k]], [bk])
                        B.copy("act", dst[:, 512 * n:512 * n + 512], bk[:], [bk], [dst])
                for Bq in range(4):
                    bk = nb([6, 7])
                    for bl in range(4):
                        Bk = 4 * Bq + bl
                        for k in range(NK):
                            B.mm(bk[:, bl * 128:bl * 128 + 128], dil_ap(hk(k), r, 128 * Bk, 128), wb[:, k, 256:384],
                                 k == 0, k == NK - 1, [wb, hT_r[k]], [bk])
                    B.copy("act", vd[:, 4 * Bq:4 * Bq + 4, :], bk[:].rearrange("p (a b) -> p a b", a=4), [bk], [vd])
                def score(Bk):
                    c_, blk = divmod(Bk, nblk)
                    nq = 256 if blk < nblk - 1 else 128
                    sb_ = banks[4 + (Bk // 2) % 2]
                    reg = sb_[:, 256 * (Bk % 2):256 * (Bk % 2) + nq]
                    B.mm(reg, kT[:, 128 * Bk:128 * Bk + 128], qT[:, 128 * Bk:128 * Bk + nq], True, True, [kT, qT], [sb_])
                    et_, em_ = ett[Bk % 4], etm[Bk % 6]
                    B.act(et_[:, :nq], reg, AF.Exp, [sb_], [et_], scale=1.0 / math.sqrt(128.0))
                    B.tt("dve", em_[:, :nq], et_[:, :nq], amask[:, :nq], ALU.mult, [et_, amask], [em_])

                def pv(Bk):
                    c_, blk = divmod(Bk, nblk)
                    em_ = etm[Bk % 6]
                    pvb, dnb = banks[(Bk // 4) % 2], banks[2 + (Bk // 4) % 2]
                    cs = slice((Bk % 4) * 128, (Bk % 4) * 128 + 128)
                    for (bank_, isden) in ((pvb, False), (dnb, True)):
                        l0 = ones[:] if isden else vd[:, Bk, :]
                        R0 = [ones] if isden else [vd]
                        B.mm(bank_[:, cs], l0, em_[:, 0:128], True, blk == 0, R0 + [em_], [bank_])
                        if blk > 0:
                            emp = etm[(Bk - 1) % 6]
                            l1 = ones[:] if isden else vd[:, Bk - 1, :]
                            B.mm(bank_[:, cs], l1, emp[:, 128:256], False, True, R0 + [emp], [bank_])
                    if Bk % 4 == 3:
                        for (bank_, acc) in ((pvb, accn), (dnb, accd)):
                            dst = dil_ap(acc[:], r, 512 * (Bk // 4), 512)
                            if gi == 0:
                                B.copy("act" if acc is accd else "dve", dst, like(bank_[:], dst), [bank_], [acc])
                            else:
                                B.tt("dve", dst, dst, like(bank_[:], dst), ALU.add, [bank_, acc], [acc])

                LOOK = 3
                for Bk in range(16 + LOOK):
                    if Bk < 16:
                        score(Bk)
                    if Bk >= LOOK:
                        pv(Bk - LOOK)
            for n in range(4):
                bk = nb([6, 7])
                for k in range(NK):
                    B.mm(bk[:], wb[:, k, 384:512], hT[:, k, 512 * n:512 * n + 512], k == 0, k == NK - 1,
                         [wb, hT_r[k]], [bk])
                B.act(sza[:, 512 * n:512 * n + 512], bk[:], AF.Silu, [bk], [sza])
            S.op("dve", lambda e, a=accd: e.reciprocal(out=a[:], in_=a[:]), [accd.r], [accd.r])
            B.tt("dve", accn[:], accn[:], accd[:], ALU.mult, [accn, accd], [accn])
            B.tt("dve", yaT[:, j, :], accn[:], sza[:], ALU.mult, [accn, sza], [yaT_r[j]])
        if dbg.get("stop") == "PB":
            dump("dbg_ya", yaT[:], yaT_r)
            break
        S.barrier()
        wba = carve(49152, [128, 4, D], BF16, "wba")
        sgg = [carve(57344 + i * 4096, [128, L], BF16, "sgga%d" % i) for i in range(2)]
        tmm = [carve(65536 + i * 2048, [128, 512], F32, "tmm%d" % i) for i in range(2)]
        load_w(wba, io["w_ba"][l])
        for m in range(NK):
            if m % 4 == 0:
                wb = load_wblock(l, 7168 + 512 * (m // 4), 512)
            sg_ = sgg[m % 2]
            for n in range(4):
                bk = nb([6, 7])
                for k in range(NK):
                    B.mm(bk[:], wb[:, k, 128 * (m % 4):128 * (m % 4) + 128], hT[:, k, 512 * n:512 * n + 512],
                         k == 0, k == NK - 1, [wb, hT_r[k]], [bk])
                B.act(sg_[:, 512 * n:512 * n + 512], bk[:], AF.Sigmoid, [bk], [sg_])
            for n in range(4):
                ns = slice(512 * n, 512 * n + 512)
                bk = nb([0, 1, 2, 3])
                for k in range(4):
                    B.mm(bk[:], wba[:, k, 128 * m:128 * m + 128], yaT[:, k, ns], k == 0, k == 3,
                         [wba, yaT_r[k]], [bk])
                t_ = tmm[n % 2]
                B.tt("dve", t_[:], bk[:], sg_[:, ns], ALU.mult, [bk, sg_], [t_])
                B.tt("dve", merged[:, m, ns], merged[:, m, ns], t_[:], ALU.add, [t_, merged_r[m]], [merged_r[m]])
        if dbg.get("stop") == "PC2":
            dump("dbg_m", merged[:], merged_r)
            break
        S.barrier()
        wout = carve(32768, [128, NK, D], BF16, "wout")
        ot = carve(49152, [128, NK, 512], F32, "ot")
        sqo = carve(65536, [128, NK, 512], BF16, "sqo")
        rt2 = carve(73728, [128, 512], F32, "rt2")
        rs2 = carve(75776, [128, 512], F32, "rs2")
        tm2 = [carve(77824 + i * 2048, [128, 512], F32, "tm2%d" % i) for i in range(2)]
        load_w(wout, io["w_out"][l])
        for n in range(4):
            ns = slice(512 * n, 512 * n + 512)
            for mo in range(NK):
                bk = nb([0, 1, 2, 3, 4, 5])
                for k in range(NK):
                    B.mm(bk[:], wout[:, k, 128 * mo:128 * mo + 128], merged[:, k, ns], k == 0, k == NK - 1,
                         [wout, merged_r[k]], [bk])
                B.copy("act", ot[:, mo, :], bk[:], [bk], [ot])
                B.act(sqo[:, mo, :], bk[:], AF.Square, [bk], [sqo])
            sb_ = nb([6, 7])
            for mo in range(NK):
                B.mm(sb_[:], ones[:], sqo[:, mo, :], mo == 0, mo == NK - 1, [ones, sqo], [sb_])
            B.act(rt2[:], sb_[:], AF.Sqrt, [sb_, epsT], [rt2], scale=1.0 / D, bias=epsT[:, 0:1])
            S.op("dve", lambda e, a=rs2, b=rt2: e.reciprocal(out=a[:], in_=b[:]), [rt2.r], [rs2.r])
            for mo in range(NK):
                t_ = tm2[mo % 2]
                B.stt(t_[:], ot[:, mo, :], vecs[:, vb + 8 + mo:vb + 9 + mo], rs2[:], ALU.mult, ALU.mult,
                      [ot, vecs, rs2], [t_])
                B.tt("pool", xT[:, mo, ns], xT[:, mo, ns], t_[:], ALU.add, [t_, xT_r[mo][n]], [xT_r[mo][n]])
        S.barrier()
    else:
        yout = io["yT"].rearrange("(k p) t -> p k t", p=128)
        for k in range(NK):
            out_toks.append(B.dma("sp", yout[:, k, :], xT[:, k, :], xT_r[k], [], owner=Res("out%d" % k)))
    self.finish(out_toks)


Builder.main = _main


_CACHE = {}


def kernel(**inputs):
    h = prep_host(inputs)
    x = np.asarray(inputs["x"], np.float32)
    if "nc" not in _CACHE:
        b = Builder()
        b.build()
        _CACHE["nc"] = b.nc
    nc = _CACHE["nc"]
    shared = {k: h[k] for k in ("w_in", "w_glu", "w_bs", "w_ba", "w_out", "vecs", "ssm_rb", "ssm_rc", "amask")}
    in_maps = []
    for b_ in range(8):
        m = dict(shared)
        m["xT"] = np.ascontiguousarray(x[b_].T)
        in_maps.append(m)
    res = run_bass_kernel_spmd(nc, in_maps, core_ids=list(range(8)))
    out = np.stack([np.ascontiguousarray(res.results[b_]["yT"].T) for b_ in range(8)], axis=0)
    return out.astype(np.float32)
```

```python
import math
from contextlib import ExitStack

import numpy as np
import ml_dtypes

import concourse.bass as bass
import concourse.mybir as mybir
from concourse.bass_utils import run_bass_kernel_spmd

F32 = mybir.dt.float32
BF16 = mybir.dt.bfloat16
ALU = mybir.AluOpType
AF = mybir.ActivationFunctionType

L = 2048
D = 1024
NK = 8
DEPTH = 4
TS = 16
NCH = L // TS
ENGS = ("pe", "act", "dve", "pool", "sp")
STRICT = False


class Res:
    __slots__ = ("w", "r", "name", "dsem", "dcnt")

    def __init__(self, name=""):
        self.w = []
        self.r = []
        self.name = name
        self.dsem = None
        self.dcnt = 0


class Sched:
    def __init__(self, nc, es):
        self.nc = nc
        self.es = es
        self.prog = {e: [] for e in ENGS}
        self.cnt = {e: 0 for e in ENGS}
        self.sem = {e: es.enter_context(nc.semaphore("sem_" + e)) for e in ENGS if e != "sp"}
        self.known = {e: {} for e in ENGS}
        self.nsem = 0
        self.final = []

    def _waits(self, eng, reads, writes):
        toks = []
        for t in reads:
            toks += t.w
        for t in writes:
            toks += [x for x in t.w if STRICT or x[2] != eng]
            toks += [x for x in t.r if STRICT or x[2] != eng]
        need = {}
        for (sem, val, src) in toks:
            if eng == "pe" and src == "pe":
                continue
            if self.known[eng].get(id(sem), 0) >= val:
                continue
            if need.get(id(sem), (None, 0))[1] < val:
                need[id(sem)] = (sem, val)
        out = []
        for k, (sem, val) in need.items():
            self.known[eng][k] = val
            out.append((sem, val))
        return out

    def _commit(self, tok, reads, writes):
        for t in writes:
            t.w = [tok]
            t.r = []
        for t in reads:
            t.r.append(tok)
            if len(t.r) > 24:
                best = {}
                for (s, v, src) in t.r:
                    if id(s) not in best or best[id(s)][1] < v:
                        best[id(s)] = (s, v, src)
                t.r = list(best.values())

    def op(self, eng, fn, reads=(), writes=()):
        waits = self._waits(eng, reads, writes)
        self.cnt[eng] += 1
        sem = self.sem[eng]
        val = self.cnt[eng]

        def emit(e, waits=waits, fn=fn, sem=sem):
            for (s, v) in waits:
                e.wait_ge(s, v)
            fn(e).then_inc(sem, 1)

        self.prog[eng].append(emit)
        self._commit((sem, val, eng), reads, writes)

    def dma(self, q, out, in_, reads=(), writes=(), owner=None):
        waits = self._waits(q, reads, writes)
        own = owner if owner is not None else (writes[0] if writes else reads[0])
        if own.dsem is None:
            own.dsem = self.es.enter_context(self.nc.semaphore("dsem%d" % self.nsem))
            self.nsem += 1
        own.dcnt += 16
        sem, val = own.dsem, own.dcnt

        def emit(e, waits=waits, sem=sem, out=out, in_=in_):
            for (s, v) in waits:
                e.wait_ge(s, v)
            e.dma_start(out=out, in_=in_).then_inc(sem, 16)

        self.prog[q].append(emit)
        tok = (sem, val, "dma")
        for t in writes:
            t.w = [tok]
            t.r = []
        for t in reads:
            t.r.append(tok)
        return tok

    def wait_all(self, q, toks):
        def emit(e, toks=toks):
            for (s, v, _) in toks:
                e.wait_ge(s, v)
        self.prog[q].append(emit)

    def barrier(self, extra=()):
        for e in ENGS:
            waits = [(s, v) for (s, v, _) in extra]
            for o in ENGS:
                if o == "sp" or o == e or self.cnt[o] == 0:
                    continue
                if self.known[e].get(id(self.sem[o]), 0) < self.cnt[o]:
                    self.known[e][id(self.sem[o])] = self.cnt[o]
                    waits.append((self.sem[o], self.cnt[o]))

            def emit(en, waits=waits):
                for (s, v) in waits:
                    en.wait_ge(s, v)
            self.prog[e].append(emit)

    def run(self):
        nc = self.nc
        with nc.Block() as block:
            @block.tensor
            def _(e):
                for f in self.prog["pe"]:
                    f(e)

            @block.scalar
            def _(e):
                for f in self.prog["act"]:
                    f(e)

            @block.vector
            def _(e):
                for f in self.prog["dve"]:
                    f(e)

            @block.gpsimd
            def _(e):
                for f in self.prog["pool"]:
                    f(e)

            @block.sync
            def _(e):
                for f in self.prog["sp"]:
                    f(e)


def mkap(t, offset, dims):
    return bass.AP(tensor=t, offset=offset, ap=[list(d) for d in dims])


def _in_col_perm():
    cols = list(range(0, 1024))
    for j in range(4):
        for gi in range(3):
            for base in (1024, 2560, 4096):
                c0 = base + (gi * 4 + j) * 128
                cols += list(range(c0, c0 + 128))
        cols += list(range(5632 + j * 128, 5632 + (j + 1) * 128))
    cols += list(range(6144, 8192))
    assert len(cols) == 8192 and len(set(cols)) == 8192
    return np.asarray(cols)


def prep_host(inp):
    f32 = np.float32
    out = {}
    out["w_in"] = np.ascontiguousarray(np.asarray(inp["w_in"], f32)[:, :, _in_col_perm()])
    out["w_glu"] = np.ascontiguousarray(np.asarray(inp["w_glu"], f32))
    out["w_bs"] = np.ascontiguousarray(np.asarray(inp["w_branch_s"], f32))
    out["w_ba"] = np.ascontiguousarray(np.asarray(inp["w_branch_a"], f32))
    out["w_out"] = np.ascontiguousarray(np.asarray(inp["w_out"], f32))
    vecs = np.zeros((128, DEPTH * 24), f32)
    for l in range(DEPTH):
        vecs[:, l * 24 + 0:l * 24 + 8] = np.asarray(inp["pre_norm_g"], f32)[l].reshape(8, 128).T
        vecs[:, l * 24 + 8:l * 24 + 16] = np.asarray(inp["post_norm_g"], f32)[l].reshape(8, 128).T
        vecs[:, l * 24 + 16:l * 24 + 20] = np.asarray(inp["d_skip"], f32)[l].reshape(4, 128).T
        vecs[:, l * 24 + 20:l * 24 + 24] = np.asarray(inp["b_glu"], f32)[l].reshape(4, 128).T
    out["vecs"] = vecs
    lre = np.asarray(inp["lambda_re"], f32)
    lim = np.asarray(inp["lambda_im"], f32)
    ldt = np.asarray(inp["log_dt"], f32)
    bre = np.asarray(inp["b_re"], f32)
    bim = np.asarray(inp["b_im"], f32)
    cre = np.asarray(inp["c_re"], f32)
    cim = np.asarray(inp["c_im"], f32)
    rb = np.zeros((DEPTH, 128, 5, 4, 128), f32)
    rc = np.zeros((DEPTH, 128, 48 + 512), f32)
    for l in range(DEPTH):
        lam_g = lre[l].reshape(4, 4, 2, 64)
        lim_g = lim[l].reshape(4, 4, 2, 64)
        ldt_g = np.broadcast_to(ldt[l].reshape(4, 4, 2, 1), (4, 4, 2, 64))
        for ptl in range(4):
            rows = slice(32 * ptl, 32 * ptl + 32)
            rb[l, rows, 0] = lam_g[:, ptl].reshape(4, 128)[None]
            rb[l, rows, 1] = lim_g[:, ptl].reshape(4, 128)[None]
            rb[l, rows, 2] = ldt_g[:, ptl].reshape(4, 128)[None]
            for h in range(2):
                for j in range(4):
                    g = 8 * j + 2 * ptl + h
                    r0 = 32 * ptl + 16 * h
                    rb[l, r0:r0 + 16, 3, j, 64 * h:64 * h + 64] = bre[l, g].T
                    rb[l, r0:r0 + 16, 4, j, 64 * h:64 * h + 64] = bim[l, g].T
        rc[l, :, 0:16] = lre[l].reshape(16, 128).T
        rc[l, :, 16:32] = lim[l].reshape(16, 128).T
        rc[l, :, 32:48] = np.broadcast_to(ldt[l].reshape(16, 2, 1), (16, 2, 64)).reshape(16, 128).T
        cc = cre[l].reshape(16, 2, 16, 64).transpose(1, 3, 0, 2).reshape(128, 16 * 16)
        ci = cim[l].reshape(16, 2, 16, 64).transpose(1, 3, 0, 2).reshape(128, 16 * 16)
        rc[l, :, 48:48 + 256] = cc
        rc[l, :, 48 + 256:48 + 512] = ci
    out["ssm_rb"] = rb.reshape(DEPTH, 128, 5 * 512)
    out["ssm_rc"] = rc
    am = np.zeros((128, 256), f32)
    kk = np.arange(128)[:, None]
    qq = np.arange(128)[None, :]
    am[:, 0:128] = (qq >= kk)
    am[:, 128:256] = (qq <= kk)
    out["amask"] = am.astype(ml_dtypes.bfloat16)
    return out


class TL:
    def __init__(self, t, name):
        self.t = t
        self.r = Res(name)

    def __getitem__(self, idx):
        return self.t[idx]


class Builder:
    def __init__(self, n_layers=DEPTH, debug=None):
        self.n_layers = n_layers
        self.debug = debug or {}
        self.nc = bass.Bass("TRN2", target_bir_lowering=False)
        self.es = ExitStack()
        self.S = Sched(self.nc, self.es)
        self.uid = 0

    def sb(self, es, shape, dtype, name=None):
        self.uid += 1
        nm = (name or "t") + "_%d" % self.uid
        return TL(es.enter_context(self.nc.sbuf_tensor(nm, list(shape), dtype)), nm)

    def ps(self, es, shape, dtype=F32, name=None):
        self.uid += 1
        nm = (name or "p") + "_%d" % self.uid
        return TL(es.enter_context(self.nc.psum_tensor(nm, list(shape), dtype)), nm)

    @staticmethod
    def _res(lst):
        return [x.r if isinstance(x, TL) else x for x in lst]

    def tt(self, eng, out, a, b, op, R, W):
        self.S.op(eng, lambda e: e.tensor_tensor(out=out, in0=a, in1=b, op=op), self._res(R), self._res(W))

    def ts(self, eng, out, a, s1, op0, R, W, s2=None, op1=None):
        if op1 is None:
            self.S.op(eng, lambda e: e.tensor_scalar(out=out, in0=a, scalar1=s1, scalar2=None, op0=op0),
                      self._res(R), self._res(W))
        else:
            self.S.op(eng, lambda e: e.tensor_scalar(out=out, in0=a, scalar1=s1, scalar2=s2, op0=op0, op1=op1),
                      self._res(R), self._res(W))

    def stt(self, out, a, scalar, b, op0, op1, R, W):
        self.S.op("dve", lambda e: e.scalar_tensor_tensor(out=out, in0=a, scalar=scalar, in1=b, op0=op0, op1=op1),
                  self._res(R), self._res(W))

    def act(self, out, a, func, R, W, scale=1.0, bias=0.0):
        self.S.op("act", lambda e: e.activation(out=out, in_=a, func=func, bias=bias, scale=scale),
                  self._res(R), self._res(W))

    def copy(self, eng, out, a, R, W):
        if eng == "act":
            self.S.op("act", lambda e: e.activation(out=out, in_=a, func=AF.Copy), self._res(R), self._res(W))
        else:
            self.S.op(eng, lambda e: e.tensor_copy(out=out, in_=a), self._res(R), self._res(W))

    def memset(self, eng, ap, val, W):
        self.S.op(eng, lambda e: e.memset(ap, val), [], self._res(W))

    def mm(self, out, lhsT, rhs, start, stop, R, W, tp=None):
        if tp is None:
            self.S.op("pe", lambda e: e.matmul(out, lhsT=lhsT, rhs=rhs, start=start, stop=stop),
                      self._res(R), self._res(W))
        else:
            self.S.op("pe", lambda e: e.matmul(out, lhsT=lhsT, rhs=rhs, start=start, stop=stop, tile_position=tp),
                      self._res(R), self._res(W))

    def dma(self, q, out, in_, R, W, owner=None):
        return self.S.dma(q, out, in_, self._res(R), self._res(W),
                          owner.r if isinstance(owner, TL) else owner)

    def cmul(self, eng, o, a, b, t1, t2, R, W, Tm):
        (orr, oi), (ar, ai), (br, bi) = o, a, b
        self.tt(eng, t1, ar, br, ALU.mult, R, Tm)
        self.tt(eng, t2, ai, bi, ALU.mult, R, Tm)
        self.tt(eng, orr, t1, t2, ALU.subtract, Tm, W)
        self.tt(eng, t1, ar, bi, ALU.mult, R, Tm)
        self.tt(eng, t2, ai, br, ALU.mult, R, Tm)
        self.tt(eng, oi, t1, t2, ALU.add, Tm, W)

    def declare_io(self):
        nc = self.nc
        nl = DEPTH
        d = {}
        d["xT"] = nc.dram_tensor("xT", [D, L], F32, kind="ExternalInput").ap()
        d["w_in"] = nc.dram_tensor("w_in", [nl, D, 8192], F32, kind="ExternalInput").ap()
        d["w_glu"] = nc.dram_tensor("w_glu", [nl, 512, 512], F32, kind="ExternalInput").ap()
        d["w_bs"] = nc.dram_tensor("w_bs", [nl, 512, D], F32, kind="ExternalInput").ap()
        d["w_ba"] = nc.dram_tensor("w_ba", [nl, 512, D], F32, kind="ExternalInput").ap()
        d["w_out"] = nc.dram_tensor("w_out", [nl, D, D], F32, kind="ExternalInput").ap()
        d["vecs"] = nc.dram_tensor("vecs", [128, nl * 24], F32, kind="ExternalInput").ap()
        d["ssm_rb"] = nc.dram_tensor("ssm_rb", [nl, 128, 2560], F32, kind="ExternalInput").ap()
        d["ssm_rc"] = nc.dram_tensor("ssm_rc", [nl, 128, 560], F32, kind="ExternalInput").ap()
        d["amask"] = nc.dram_tensor("amask", [128, 256], BF16, kind="ExternalInput").ap()
        d["yT"] = nc.dram_tensor("yT", [D, L], F32, kind="ExternalOutput").ap()
        tk = "ExternalOutput" if self.debug.get("tables") else "Internal"
        d["tabB"] = nc.dram_tensor("tabB", [nl, 128, 16384], BF16, kind=tk).ap()
        d["tabC"] = nc.dram_tensor("tabC", [nl, 128, 16384], BF16, kind=tk).ap()
        for name, (shape, dt) in self.debug.get("outs", {}).items():
            d[name] = nc.dram_tensor(name, list(shape), dt, kind="ExternalOutput").ap()
        self.io = d

    def lam_alloc(self, es, Fd, tag):
        names = ("dt", "lr", "a", "th", "mag", "em2a", "c", "s", "t1", "t2", "lbr", "lbi")
        return {n: self.sb(es, [128, Fd], F32, tag + n) for n in names}

    def lam_math(self, m, src):
        B = self
        dt, lr, a, th, mag, em2a, c, s, t1, t2, lbr, lbi = [m[n] for n in
                                                          ("dt", "lr", "a", "th", "mag", "em2a", "c", "s", "t1", "t2", "lbr", "lbi")]
        R = src["R"]
        B.act(dt[:], src["ldt"], AF.Exp, R, [dt])
        B.ts("dve", lr[:], src["lre"], -1e-4, ALU.min, R, [lr])
        B.tt("dve", a[:], lr[:], dt[:], ALU.mult, [lr, dt], [a])
        B.tt("dve", th[:], src["lim"], dt[:], ALU.mult, R + [dt], [th])
        B.act(mag[:], a[:], AF.Exp, [a], [mag])
        B.act(em2a[:], a[:], AF.Exp, [a], [em2a], scale=-2.0)
        B.act(s[:], th[:], AF.Sin, [th], [s], scale=1.0 / 16.0)
        B.act(c[:], th[:], AF.Sin, [th, self.halfpi], [c], scale=1.0 / 16.0, bias=self.halfpi[:, 0:1])
        for _ in range(4):
            B.tt("dve", t1[:], c[:], c[:], ALU.mult, [c], [t1])
            B.tt("dve", t2[:], s[:], s[:], ALU.mult, [s], [t2])
            B.stt(s[:], c[:], 2.0, s[:], ALU.mult, ALU.mult, [c, s], [s])
            B.tt("dve", c[:], t1[:], t2[:], ALU.subtract, [t1, t2], [c])
        B.tt("dve", lbr[:], mag[:], c[:], ALU.mult, [mag, c], [lbr])
        B.tt("dve", lbi[:], mag[:], s[:], ALU.mult, [mag, s], [lbi])

    def prologue(self):
        B = self
        io = self.io
        toks = []
        with ExitStack() as es:
            mb = B.lam_alloc(es, 512, "rb")
            inpb = B.sb(es, [128, 5, 512], F32, "rbin")
            mk = lambda n: B.sb(es, [128, 512], F32, "rb" + n)
            den, nr, cr, ci, invr, invi, bbr, bbi = [mk(n) for n in ("den", "nr", "cr", "ci", "invr", "invi", "bbr", "bbi")]
            xs = [(mk("xr0"), mk("xi0")), (mk("xr1"), mk("xi1"))]
            bt = B.sb(es, [128, 4, TS, 2, 128], BF16, "bt")
            mc = B.lam_alloc(es, 16, "rc")
            inpc = B.sb(es, [128, 560], F32, "rcin")
            pw = B.sb(es, [128, 2, 16, TS], F32, "pw")
            ct = B.sb(es, [128, 16, TS, 2, 32], BF16, "ct")
            big = lambda n: B.sb(es, [128, 16, TS, 16], F32, "rc" + n)
            u1, u2, u3 = big("u1"), big("u2"), big("u3")
            B.memset("pool", ct[:], 0.0, [ct])
            for l in range(self.n_layers):
                inp = inpb
                B.dma("sp", inp[:], io["ssm_rb"][l].rearrange("p (a f) -> p a f", a=5), [], [inp])
                B.lam_math(mb, dict(lre=inp[:, 0, :], lim=inp[:, 1, :], ldt=inp[:, 2, :], R=[inp]))
                lr, lbr, lbi, em2a, t1, t2 = [mb[n] for n in ("lr", "lbr", "lbi", "em2a", "t1", "t2")]
                li = inp[:, 1, :]
                B.tt("dve", t1[:], lr[:], lr[:], ALU.mult, [lr], [t1])
                B.tt("dve", t2[:], li, li, ALU.mult, [inp], [t2])
                B.tt("dve", den[:], t1[:], t2[:], ALU.add, [t1, t2], [den])
                B.S.op("dve", lambda e, den=den: e.reciprocal(out=den[:], in_=den[:]), [den.r], [den.r])
                B.ts("dve", nr[:], lbr[:], -1.0, ALU.add, [lbr], [nr])
                B.tt("dve", t1[:], nr[:], lr[:], ALU.mult, [nr, lr], [t1])
                B.tt("dve", t2[:], lbi[:], li, ALU.mult, [lbi, inp], [t2])
                B.tt("dve", t1[:], t1[:], t2[:], ALU.add, [t1, t2], [t1])
                B.tt("dve", cr[:], t1[:], den[:], ALU.mult, [t1, den], [cr])
                B.tt("dve", t1[:], lbi[:], lr[:], ALU.mult, [lbi, lr], [t1])
                B.tt("dve", t2[:], nr[:], li, ALU.mult, [nr, inp], [t2])
                B.tt("dve", t1[:], t1[:], t2[:], ALU.subtract, [t1, t2], [t1])
                B.tt("dve", ci[:], t1[:], den[:], ALU.mult, [t1, den], [ci])
                B.tt("dve", invr[:], lbr[:], em2a[:], ALU.mult, [lbr, em2a], [invr])
                B.stt(invi[:], lbi[:], -1.0, em2a[:], ALU.mult, ALU.mult, [lbi, em2a], [invi])
                B.cmul("dve", (bbr[:], bbi[:]), (cr[:], ci[:]), (inp[:, 3, :], inp[:, 4, :]), t1[:], t2[:],
                       [cr, ci, inp], [bbr, bbi], [t1, t2])
                prev = (bbr, bbi)
                for i in range(TS):
                    cur = xs[i % 2]
                    B.cmul("dve", (cur[0][:], cur[1][:]), (prev[0][:], prev[1][:]), (invr[:], invi[:]), t1[:], t2[:],
                           [prev[0], prev[1], invr, invi], [cur[0], cur[1]], [t1, t2])
                    B.copy("act", bt[:, :, i, 0, :], cur[0][:].rearrange("p (j q) -> p j q", j=4), [cur[0]], [bt])
                    B.copy("act", bt[:, :, i, 1, :], cur[1][:].rearrange("p (j q) -> p j q", j=4), [cur[1]], [bt])
                    prev = cur
                toks.append(B.dma("sp", io["tabB"][l], bt[:].rearrange("p j i r q -> p (j i r q)"),
                                  [bt], [self.tabB_r[l]], owner=bt))
                inp = inpc
                B.dma("sp", inp[:], io["ssm_rc"][l], [], [inp])
                B.lam_math(mc, dict(lre=inp[:, 0:16], lim=inp[:, 16:32], ldt=inp[:, 32:48], R=[inp]))
                lbr, lbi, t1, t2 = [mc[n] for n in ("lbr", "lbi", "t1", "t2")]
                B.copy("dve", pw[:, 0, :, 0], lbr[:], [lbr], [pw])
                B.copy("dve", pw[:, 1, :, 0], lbi[:], [lbi], [pw])
                for i in range(1, TS):
                    B.cmul("dve", (pw[:, 0, :, i], pw[:, 1, :, i]), (pw[:, 0, :, i - 1], pw[:, 1, :, i - 1]),
                           (lbr[:], lbi[:]), t1[:], t2[:], [pw, lbr, lbi], [pw], [t1, t2])
                PL = self.PL
                B.copy("dve", PL[:, l, :, 0], pw[:, 0, :, TS - 1], [pw], [PL])
                B.copy("dve", PL[:, l, :, 1], pw[:, 1, :, TS - 1], [pw], [PL])
                for k in range(1, 7):
                    pr, pi = PL[:, l, :, 3 * k - 3], PL[:, l, :, 3 * k - 2]
                    B.tt("dve", t1[:], pr, pr, ALU.mult, [PL], [t1])
                    B.tt("dve", t2[:], pi, pi, ALU.mult, [PL], [t2])
                    B.tt("dve", PL[:, l, :, 3 * k], t1[:], t2[:], ALU.subtract, [t1, t2], [PL])
                    B.stt(PL[:, l, :, 3 * k + 1], pr, 2.0, pi, ALU.mult, ALU.mult, [PL], [PL])
                for k in range(7):
                    B.ts("dve", PL[:, l, :, 3 * k + 2], PL[:, l, :, 3 * k + 1], -1.0, ALU.mult, [PL], [PL])
                cre = inp[:, 48:304].rearrange("p (t c) -> p t c", c=16)
                cim = inp[:, 304:560].rearrange("p (t c) -> p t c", c=16)

                def bc_c(ap3):
                    return mkap(ap3.tensor, ap3.offset, [ap3.ap[0], ap3.ap[1], [0, TS], ap3.ap[2]])

                def bc_p(k):
                    a = pw[:, k, :, :]
                    return mkap(a.tensor, a.offset, [a.ap[0], a.ap[1], a.ap[2], [0, 16]])
                B.tt("dve", u1[:], bc_c(cre), bc_p(0), ALU.mult, [inp, pw], [u1])
                B.tt("dve", u2[:], bc_c(cim), bc_p(1), ALU.mult, [inp, pw], [u2])
                B.tt("dve", u3[:], u1[:], u2[:], ALU.subtract, [u1, u2], [u3])
                B.copy("act", ct[0:64, :, :, 0, 0:16], u3[0:64], [u3], [ct])
                B.copy("act", ct[64:128, :, :, 0, 16:32], u3[64:128], [u3], [ct])
                B.tt("dve", u1[:], bc_c(cre), bc_p(1), ALU.mult, [inp, pw], [u1])
                B.tt("dve", u2[:], bc_c(cim), bc_p(0), ALU.mult, [inp, pw], [u2])
                B.stt(u3[:], u1[:], -1.0, u2[:], ALU.mult, ALU.subtract, [u1, u2], [u3])
                B.copy("act", ct[0:64, :, :, 1, 0:16], u3[0:64], [u3], [ct])
                B.copy("act", ct[64:128, :, :, 1, 16:32], u3[64:128], [u3], [ct])
                toks.append(B.dma("sp", io["tabC"][l], ct[:].rearrange("p t i r c -> p (t i r c)"),
                                  [ct], [self.tabC_r[l]], owner=ct))
        self.S.barrier(toks)

    def build(self):
        B = self
        nc = self.nc
        es = self.es
        self.declare_io()
        io = self.io
        self.tabB_r = [Res("tabB%d" % l) for l in range(DEPTH)]
        self.tabC_r = [Res("tabC%d" % l) for l in range(DEPTH)]
        self.halfpi = B.sb(es, [128, 1], F32, "halfpi")
        B.memset("dve", self.halfpi[:], math.pi / 2.0, [self.halfpi])
        self.PL = B.sb(es, [128, DEPTH, 16, 24], F32, "PL")
        self.prologue()
        if self.debug.get("tables"):
            self.finish([])
            return
        self.main()

    def finish(self, out_toks):
        self.S.wait_all("sp", out_toks)
        self.S.barrier(out_toks)
        self.S.run()


class V:
    def __init__(self, ap, name=""):
        self.ap = ap
        self.r = Res(name)
        self.lo = self.hi = None

    def __getitem__(self, idx):
        return self.ap[idx]


def _res_of(lst):
    return [x.r if hasattr(x, "r") and not isinstance(x, Res) else x for x in lst]


Builder._res = staticmethod(_res_of)


def dil_ap(base, r, col0, ncols):
    t, off, pdim, st = base.tensor, base.offset, list(base.ap[0]), base.ap[-1][0]
    assert len(base.ap) == 2
    nsub = L // r
    c0, i0 = divmod(col0, nsub)
    if r == 1:
        dims, o = [[st, ncols]], col0
    elif i0 + ncols <= nsub:
        dims, o = [[r * st, ncols]], c0 + r * i0
    else:
        assert i0 == 0 and ncols % nsub == 0
        dims, o = [[st, ncols // nsub], [r * st, nsub]], c0
    return mkap(t, off + o * st, [pdim] + dims)


def like(bank_ap, ap):
    if len(ap.ap) == 3:
        return bank_ap.rearrange("p (a b) -> p a b", a=ap.ap[1][1])
    return bank_ap


AR_BYTES = 88064


def _main(self):
    B = self
    es = self.es
    io = self.io
    S = self.S
    nc = self.nc
    dbg = self.debug
    xT = B.sb(es, [128, NK, L], F32, "xT")
    xT_r = [[Res("xT%d_%d" % (k, n)) for n in range(4)] for k in range(NK)]
    hT = B.sb(es, [128, NK, L], BF16, "hT")
    hT_r = [Res("hT%d" % k) for k in range(NK)]
    vecs = B.sb(es, [128, DEPTH * 24], F32, "vecs")
    amask = B.sb(es, [128, 256], BF16, "amask")
    ones = B.sb(es, [128, 128], BF16, "ones")
    epsT = B.sb(es, [128, 1], F32, "eps")
    wbuf = [B.sb(es, [128, NK, 512], BF16, "wbuf%d" % i) for i in range(2)]
    arena = B.sb(es, [128, AR_BYTES // 2], BF16, "arena")
    banks = [B.ps(es, [128, 512], F32, "bank%d" % i) for i in range(8)]
    self.wi = 0

    def carve(off, shape, dtype, name):
        nel = int(np.prod(shape[1:]))
        esz = 2 if dtype == BF16 else 4
        assert off % 4 == 0 and off + nel * esz <= AR_BYTES, (name, off, nel * esz)
        a = arena.t[:, off // 2: off // 2 + nel * esz // 2]
        if dtype == F32:
            a = a.bitcast(F32)
        if len(shape) == 3:
            a = a.rearrange("p (a b) -> p a b", a=shape[1])
        elif len(shape) == 4:
            a = a.rearrange("p (a b c) -> p a b c", a=shape[1], b=shape[2])
        elif len(shape) == 5:
            a = a.rearrange("p (a b c d) -> p a b c d", a=shape[1], b=shape[2], c=shape[3])
        v = V(a, name)
        region(off, off + nel * esz, [v.r])
        v.lo, v.hi = off, off + nel * esz
        return v

    regs = []

    def region(lo, hi, res_list):
        inherit = []
        keep = []
        for (l0, h0, rl) in regs:
            if l0 < hi and lo < h0:
                for r_ in rl:
                    inherit += r_.w + r_.r
                if lo <= l0 and h0 <= hi:
                    continue
            keep.append((l0, h0, rl))
        regs[:] = keep
        best = {}
        for (sm, vl, src) in inherit:
            if id(sm) not in best or best[id(sm)][1] < vl:
                best[id(sm)] = (sm, vl, "alias")
        for r_ in res_list:
            r_.r = list(best.values())
        regs.append((lo, hi, res_list))

    def reslist(v, names):
        rl = [Res(n) for n in names]
        region(v.lo, v.hi, rl)
        return rl

    B.memset("dve", ones[:], 1.0, [ones])
    B.memset("dve", epsT[:], 1e-6, [epsT])
    B.dma("sp", vecs[:], io["vecs"], [], [vecs])
    B.dma("sp", amask[:], io["amask"], [], [amask])
    xin = io["xT"].rearrange("(k p) t -> p k t", p=128)
    for k in range(NK):
        B.dma("sp", xT[:, k, :], xin[:, k, :], [], xT_r[k], owner=xT_r[k][0])

    out_toks = []

    def dump(name, ap, R):
        out_toks.append(B.dma("sp", io[name], ap, R, [], owner=Res("dump_" + name)))

    def load_wblock(l, col0, ncols):
        wb = wbuf[self.wi % 2]
        self.wi += 1
        src = io["w_in"][l][:, col0:col0 + ncols].rearrange("(k p) c -> p k c", p=128)
        B.dma("pool", wb[:, :, 0:ncols], src, [], [wb])
        return wb

    def load_w(dst, src2d):
        B.dma("pool", dst[:], src2d.rearrange("(k p) c -> p k c", p=128), [], [dst])

    bank_ctr = [0]

    def nb(lst):
        b = banks[lst[bank_ctr[0] % len(lst)]]
        bank_ctr[0] += 1
        return b

    def hk(k):
        return hT.t[:, k, :]

    for l in range(self.n_layers):
        vb = l * 24
        sq = [carve(i * 8192, [128, NK, 512], BF16, "sq%d" % i) for i in range(2)]
        rt = [carve(16384 + i * 2048, [128, 512], F32, "rt%d" % i) for i in range(2)]
        rs = [carve(20480 + i * 2048, [128, 512], F32, "rs%d" % i) for i in range(2)]
        for n in range(4):
            ns = slice(n * 512, (n + 1) * 512)
            s_, rt_, rs_ = sq[n % 2], rt[n % 2], rs[n % 2]
            for k in range(NK):
                B.act(s_[:, k, :], xT[:, k, ns], AF.Square, [xT_r[k][n]], [s_])
            bk = nb([6, 7])
            for k in range(NK):
                B.mm(bk[:], ones[:], s_[:, k, :], k == 0, k == NK - 1, [ones, s_], [bk])
            B.act(rt_[:], bk[:], AF.Sqrt, [bk, epsT], [rt_], scale=1.0 / D, bias=epsT[:, 0:1])
            S.op("dve", lambda e, a=rs_, b=rt_: e.reciprocal(out=a[:], in_=b[:]), [rt_.r], [rs_.r])
            for k in range(NK):
                B.stt(hT[:, k, ns], xT[:, k, ns], vecs[:, vb + k:vb + k + 1], rs_[:], ALU.mult, ALU.mult,
                      [xT_r[k][n], vecs, rs_], [hT_r[k]])
        if dbg.get("stop") == "P0":
            dump("dbg_h", hT[:], hT_r)
            break
        pass
        yT = carve(32768, [128, 4, L], BF16, "yT")
        _fl = reslist(yT, ["yT%d_%d" % (k, n) for k in range(4) for n in range(4)])
        yT_r = [[_fl[4 * k + n] for n in range(4)] for k in range(4)]
        r32 = carve(0, [128, 2, TS, 128], F32, "r32")
        rbf = carve(16384, [128, 2, TS, 128], BF16, "rbf")
        Bt = carve(24576, [128, TS, 2, 128], BF16, "Bt")
        Ct = carve(49152, [128, 5120], BF16, "Ct")
        CtA = Ct[:, 0:3072].rearrange("p (t i r c) -> p t i r c", t=3, i=TS, r=2)
        CtB = Ct[:, 3072:5120].rearrange("p (i r c) -> p i r c", i=TS, r=2)
        uT = carve(59392, [128, L], BF16, "uT")
        wglu = carve(63488, [128, 4, 512], BF16, "wglu")
        Sab = [carve(67584 + i * 1536, [128, 2, 192], F32, "S%d" % i) for i in range(2)]
        zt = carve(70656, [128, 2, 192], F32, "zt")
        tq = carve(72192, [128, 2, 128], F32, "tq")
        Sbf = carve(73216, [128, 2, 128], BF16, "Sbf")
        gv = [carve(73728 + i * 2048, [128, 512], F32, "gv%d" % i) for i in range(2)]
        gw = [carve(77824 + i * 2048, [128, 512], F32, "gw%d" % i) for i in range(2)]
        gs = [carve(81920 + i * 2048, [128, 512], F32, "gs%d" % i) for i in range(2)]
        gz = [carve(86016 + i * 1024, [128, 512], BF16, "gz%d" % i) for i in range(2)]
        for t_ in (Sab[0], Sab[1], zt):
            B.memset("dve", t_[:, :, 0:64], 0.0, [t_])
        Ctz = V(CtB[:, :, :, 0:32], "Ctz")
        region(Ct.lo, Ct.hi, [Ctz.r])
        B.memset("pool", Ctz[:], 0.0, [Ctz])
        B.memset("dve", Sbf[:], 0.0, [Sbf])
        wbU = load_wblock(l, 0, 512)
        wbZ = load_wblock(l, 512, 512)
        load_w(wglu, io["w_glu"][l])
        PLl = self.PL
        for j in dbg.get("js", range(4)):
            for n in range(4):
                bk = nb([6, 7])
                for k in range(NK):
                    B.mm(bk[:], wbU[:, k, 128 * j:128 * j + 128], like(dil_ap(hk(k), 16, 512 * n, 512), None) if False else dil_ap(hk(k), 16, 512 * n, 512),
                         k == 0, k == NK - 1, [wbU, hT_r[k]], [bk])
                B.copy("act", uT[:, 512 * n:512 * n + 512], bk[:], [bk], [uT])
            B.dma("sp", Bt[:].rearrange("p i r q -> p (i r q)"), io["tabB"][l][:, 4096 * j:4096 * (j + 1)],
                  [self.tabB_r[l]], [Bt])
            B.dma("sp", Ct[:, 0:3072], io["tabC"][l][:, 4096 * j:4096 * j + 3072], [self.tabC_r[l]], [Ct])
            B.dma("sp", CtB[:, :, :, 32:64],
                  io["tabC"][l][:, 4096 * j + 3072:4096 * (j + 1)].rearrange("p (i r c) -> p i r c", i=TS, r=2),
                  [self.tabC_r[l]], [Ct])
            def xt_chain(ptl):
                rows = slice(32 * ptl, 32 * ptl + 32)
                for ih in range(8):
                    xb = nb([4, 5])
                    xv = xb[:].rearrange("p (r i c) -> p r i c", r=2, i=2)
                    for ri in range(2):
                        for il in range(2):
                            i = 2 * ih + il
                            B.mm(xv[:, ri, il, :], Bt[rows, i, ri, :], uT[rows, i * 128:(i + 1) * 128],
                                 True, True, [Bt, uT], [xb], tp=(32 * ptl, 0))
                    for il in range(2):
                        i = 2 * ih + il
                        if i == 0:
                            B.copy("dve", r32[:, :, 0, :], xv[:, :, 0, :], [xb], [r32])
                        else:
                            B.tt("dve", r32[:, :, i, :], r32[:, :, i - 1, :], xv[:, :, il, :], ALU.add,
                                 [xb, r32], [r32])

            def l2(ptl):
                pt = 4 * j + ptl
                B.copy("act", rbf[:], r32[:], [r32], [rbf])
                pl = lambda c: PLl[:, l, pt, c:c + 1]
                rr, rim = r32[:, 0, TS - 1, :], r32[:, 1, TS - 1, :]
                B.ts("dve", tq[:, 0, :], rim, pl(1), ALU.mult, [r32, PLl], [tq])
                B.stt(zt[:, 0, 64:192], rr, pl(0), tq[:, 0, :], ALU.mult, ALU.subtract, [r32, PLl, tq], [zt])
                B.ts("dve", tq[:, 1, :], rr, pl(1), ALU.mult, [r32, PLl], [tq])
                B.stt(zt[:, 1, 64:192], rim, pl(0), tq[:, 1, :], ALU.mult, ALU.add, [r32, PLl, tq], [zt])
                old = zt
                for kk in range(7):
                    d = 1 << kk
                    new = Sab[kk % 2]
                    a_, b_, nb_ = pl(3 * kk), pl(3 * kk + 1), pl(3 * kk + 2)
                    B.stt(new[:, :, 64:192], old[:, :, 64 - d:192 - d], a_, old[:, :, 64:192], ALU.mult, ALU.add, [old, PLl], [new])
                    B.stt(new[:, 0, 64:192], old[:, 1, 64 - d:192 - d], nb_, new[:, 0, 64:192], ALU.mult, ALU.add, [old, new, PLl], [new])
                    B.stt(new[:, 1, 64:192], old[:, 0, 64 - d:192 - d], b_, new[:, 1, 64:192], ALU.mult, ALU.add, [old, new, PLl], [new])
                    old = new
                B.copy("dve", Sbf[:, :, 1:128], old[:, :, 64:191], [old], [Sbf])

            def readout(ptl):
                rows = slice(32 * ptl, 32 * ptl + 32)
                for i in range(TS):
                    yb = banks[i // 4]
                    cs_ = slice((i % 4) * 128, (i % 4) * 128 + 128)
                    if ptl == 3:
                        o_ = yb[64:128, cs_]
                        tp = (0, 64)
                        lh = [CtB[:, i, rr_, :] for rr_ in range(2)]
                        RC_ = [Ct, Ctz]
                    else:
                        o_ = yb[rows, cs_]
                        tp = (0, 32 * ptl)
                        lh = [CtA[:, ptl, i, rr_, :] for rr_ in range(2)]
                        RC_ = [Ct]
                    B.mm(o_, lh[0], rbf[:, 0, i, :], True, False, RC_ + [rbf], [yb], tp=tp)
                    B.mm(o_, lh[1], rbf[:, 1, i, :], False, False, RC_ + [rbf], [yb], tp=tp)
                    B.mm(o_, lh[0], Sbf[:, 0, :], False, False, RC_ + [Sbf], [yb], tp=tp)
                    B.mm(o_, lh[1], Sbf[:, 1, :], False, True, RC_ + [Sbf], [yb], tp=tp)

            order = (0, 1, 3, 2)
            xt_chain(order[0])
            for idx, ptl in enumerate(order):
                l2(ptl)
                if idx + 1 < 4:
                    xt_chain(order[idx + 1])
                readout(ptl)
            for ib in range(4):
                cs = slice(512 * ib, 512 * ib + 512)
                v_, w_, g_ = gv[ib % 2], gw[ib % 2], gs[ib % 2]
                B.stt(v_[:], uT[:, cs], vecs[:, vb + 16 + j:vb + 17 + j], banks[ib][:], ALU.mult, ALU.add,
                      [uT, vecs, banks[ib]], [v_])
                B.act(w_[:], v_[:], AF.Square, [v_], [w_])
                B.ts("pool", w_[:], w_[:], 0.044715, ALU.mult, [w_], [w_], s2=1.0, op1=ALU.add)
                B.tt("pool", w_[:], w_[:], v_[:], ALU.mult, [w_, v_], [w_])
                B.act(g_[:], w_[:], AF.Sigmoid, [w_], [g_], scale=1.5957691216057308)
                B.tt("dve", yT[:, j, cs], v_[:], g_[:], ALU.mult, [v_, g_], [yT_r[j][ib]])
        if dbg.get("stop") == "PA1":
            dump("dbg_y", yT[:], [x for row in yT_r for x in row])
            break
        for n in range(4):
            cs = slice(512 * n, 512 * n + 512)
            for mo in range(4):
                gb = banks[mo]
                for k in range(4):
                    B.mm(gb[:], wglu[:, k, 128 * mo:128 * mo + 128], yT[:, k, cs], k == 0, k == 3,
                         [wglu, yT_r[k][n]], [gb])
            for mo in range(4):
                zb = nb([4, 5])
                for k in range(NK):
                    B.mm(zb[:], wbZ[:, k, 128 * mo:128 * mo + 128], dil_ap(hk(k), 16, 512 * n, 512),
                         k == 0, k == NK - 1, [wbZ, hT_r[k]], [zb])
                z_, g_ = gz[mo % 2], gs[mo % 2]
                B.act(z_[:], zb[:], AF.Silu, [zb], [z_])
                B.act(g_[:], banks[mo][:], AF.Sigmoid, [banks[mo], vecs], [g_], bias=vecs[:, vb + 20 + mo:vb + 21 + mo])
                B.tt("dve", g_[:], g_[:], z_[:], ALU.mult, [g_, z_], [g_])
                B.tt("dve", yT[:, mo, cs], yT[:, mo, cs], g_[:], ALU.mult, [yT_r[mo][n], g_], [yT_r[mo][n]])
        if dbg.get("stop") == "PA":
            dump("dbg_y", yT[:], [x for row in yT_r for x in row])
            break
        pass
        merged = carve(0, [128, NK, L], BF16, "merged")
        merged_r = reslist(merged, ["mg%d" % m for m in range(NK)])
        wbs = carve(49152, [128, 4, D], BF16, "wbs")
        sgg = [carve(57344 + i * 4096, [128, L], BF16, "sgg%d" % i) for i in range(2)]
        load_w(wbs, io["w_bs"][l])
        for m in range(NK):
            if m % 4 == 0:
                wb = load_wblock(l, 6144 + 512 * (m // 4), 512)
            sg_ = sgg[m % 2]
            for n in range(4):
                bk = nb([6, 7])
                for k in range(NK):
                    B.mm(bk[:], wb[:, k, 128 * (m % 4):128 * (m % 4) + 128], hT[:, k, 512 * n:512 * n + 512],
                         k == 0, k == NK - 1, [wb, hT_r[k]], [bk])
                B.act(sg_[:, 512 * n:512 * n + 512], bk[:], AF.Sigmoid, [bk], [sg_])
            for n2 in range(4):
                bk = nb([0, 1, 2, 3])
                for k in range(4):
                    B.mm(bk[:], wbs[:, k, 128 * m:128 * m + 128], yT[:, k, 512 * n2:512 * n2 + 512], k == 0, k == 3,
                         [wbs, yT_r[k][n2]], [bk])
                dst = dil_ap(merged[:, m, :], 16, 512 * n2, 512)
                B.tt("dve", dst, like(bk[:], dst), dil_ap(sg_[:], 16, 512 * n2, 512), ALU.mult,
                     [bk, sg_], [merged_r[m]])
        if dbg.get("stop") == "PA2":
            dump("dbg_m", merged[:], merged_r)
            break
        pass
        yaT = carve(32768, [128, 4, L], BF16, "yaT")
        yaT_r = reslist(yaT, ["ya%d" % j for j in range(4)])
        qT = carve(49152, [128, L], BF16, "qT")
        kT = carve(53248, [128, L], BF16, "kT")
        vd = carve(57344, [128, 16, 128], BF16, "vd")
        etm = [carve(61440 + i * 512, [128, 256], BF16, "etm%d" % i) for i in range(6)]
        ett = [carve(64512 + i * 512, [128, 256], BF16, "ett%d" % i) for i in range(4)]
        accn = carve(66560, [128, L], F32, "accn")
        accd = carve(74752, [128, L], F32, "accd")
        sza = carve(82944, [128, L], BF16, "sza")
        for j in range(4):
            for gi, r in enumerate((1, 4, 16)):
                ncols = 512 if gi == 2 else 384
                wb = load_wblock(l, 1024 + 1280 * j + 384 * gi, ncols)
                nsub = L // r
                nblk = nsub // 128
                for (c_in, dst) in ((0, qT), (128, kT)):
                    for n in range(4):
                        bk = nb([6, 7])
                        for k in range(NK):
                            B.mm(bk[:], wb[:, k, c_in:c_in + 128], dil_ap(hk(k), r, 512 * n, 512), k == 0, k == NK - 1,
                                 [wb, hT_r[k]], [bk])
                        B.copy("act", dst[:, 512 * n:512 * n + 512], bk[:], [bk], [dst])
                for Bq in range(4):
                    bk = nb([6, 7])
                    for bl in range(4):
                        Bk = 4 * Bq + bl
                        for k in range(NK):
                            B.mm(bk[:, bl * 128:bl * 128 + 128], dil_ap(hk(k), r, 128 * Bk, 128), wb[:, k, 256:384],
                                 k == 0, k == NK - 1, [wb, hT_r[k]], [bk])
                    B.copy("act", vd[:, 4 * Bq:4 * Bq + 4, :], bk[:].rearrange("p (a b) -> p a b", a=4), [bk], [vd])
                def score(Bk):
                    c_, blk = divmod(Bk, nblk)
                    nq = 256 if blk < nblk - 1 else 128
                    sb_ = banks[4 + (Bk // 2) % 2]
                    reg = sb_[:, 256 * (Bk % 2):256 * (Bk % 2) + nq]
                    B.mm(reg, kT[:, 128 * Bk:128 * Bk + 128], qT[:, 128 * Bk:128 * Bk + nq], True, True, [kT, qT], [sb_])
                    et_, em_ = ett[Bk % 4], etm[Bk % 6]
                    B.act(et_[:, :nq], reg, AF.Exp, [sb_], [et_], scale=1.0 / math.sqrt(128.0))
                    B.tt("dve", em_[:, :nq], et_[:, :nq], amask[:, :nq], ALU.mult, [et_, amask], [em_])

                def pv(Bk):
                    c_, blk = divmod(Bk, nblk)
                    em_ = etm[Bk % 6]
                    pvb, dnb = banks[(Bk // 4) % 2], banks[2 + (Bk // 4) % 2]
                    cs = slice((Bk % 4) * 128, (Bk % 4) * 128 + 128)
                    for (bank_, isden) in ((pvb, False), (dnb, True)):
                        l0 = ones[:] if isden else vd[:, Bk, :]
                        R0 = [ones] if isden else [vd]
                        B.mm(bank_[:, cs], l0, em_[:, 0:128], True, blk == 0, R0 + [em_], [bank_])
                        if blk > 0:
                            emp = etm[(Bk - 1) % 6]
                            l1 = ones[:] if isden else vd[:, Bk - 1, :]
                            B.mm(bank_[:, cs], l1, emp[:, 128:256], False, True, R0 + [emp], [bank_])
                    if Bk % 4 == 3:
                        for (bank_, acc) in ((pvb, accn), (dnb, accd)):
                            dst = dil_ap(acc[:], r, 512 * (Bk // 4), 512)
                            if gi == 0:
                                B.copy("act" if acc is accd else "dve", dst, like(bank_[:], dst), [bank_], [acc])
                            else:
                                B.tt("dve", dst, dst, like(bank_[:], dst), ALU.add, [bank_, acc], [acc])

                LOOK = 3
                for Bk in range(16 + LOOK):
                    if Bk < 16:
                        score(Bk)
                    if Bk >= LOOK:
                        pv(Bk - LOOK)
            for n in range(4):
                bk = nb([6, 7])
                for k in range(NK):
                    B.mm(bk[:], wb[:, k, 384:512], hT[:, k, 512 * n:512 * n + 512], k == 0, k == NK - 1,
                         [wb, hT_r[k]], [bk])
                B.act(sza[:, 512 * n:512 * n + 512], bk[:], AF.Silu, [bk], [sza])
            S.op("dve", lambda e, a=accd: e.reciprocal(out=a[:], in_=a[:]), [accd.r], [accd.r])
            B.tt("dve", accn[:], accn[:], accd[:], ALU.mult, [accn, accd], [accn])
            B.tt("dve", yaT[:, j, :], accn[:], sza[:], ALU.mult, [accn, sza], [yaT_r[j]])
        if dbg.get("stop") == "PB":
            dump("dbg_ya", yaT[:], yaT_r)
            break
        pass
        wba = carve(49152, [128, 4, D], BF16, "wba")
        sgg = [carve(57344 + i * 4096, [128, L], BF16, "sgga%d" % i) for i in range(2)]
        tmm = [carve(65536 + i * 2048, [128, 512], F32, "tmm%d" % i) for i in range(2)]
        load_w(wba, io["w_ba"][l])
        for m in range(NK):
            if m % 4 == 0:
                wb = load_wblock(l, 7168 + 512 * (m // 4), 512)
            sg_ = sgg[m % 2]
            for n in range(4):
                bk = nb([6, 7])
                for k in range(NK):
                    B.mm(bk[:], wb[:, k, 128 * (m % 4):128 * (m % 4) + 128], hT[:, k, 512 * n:512 * n + 512],
                         k == 0, k == NK - 1, [wb, hT_r[k]], [bk])
                B.act(sg_[:, 512 * n:512 * n + 512], bk[:], AF.Sigmoid, [bk], [sg_])
            for n in range(4):
                ns = slice(512 * n, 512 * n + 512)
                bk = nb([0, 1, 2, 3])
                for k in range(4):
                    B.mm(bk[:], wba[:, k, 128 * m:128 * m + 128], yaT[:, k, ns], k == 0, k == 3,
                         [wba, yaT_r[k]], [bk])
                t_ = tmm[n % 2]
                B.tt("dve", t_[:], bk[:], sg_[:, ns], ALU.mult, [bk, sg_], [t_])
                B.tt("dve", merged[:, m, ns], merged[:, m, ns], t_[:], ALU.add, [t_, merged_r[m]], [merged_r[m]])
        if dbg.get("stop") == "PC2":
            dump("dbg_m", merged[:], merged_r)
            break
        pass
        wout = carve(32768, [128, NK, D], BF16, "wout")
        ot = carve(49152, [128, NK, 512], F32, "ot")
        sqo = carve(65536, [128, NK, 512], BF16, "sqo")
        rt2 = carve(73728, [128, 512], F32, "rt2")
        rs2 = carve(75776, [128, 512], F32, "rs2")
        tm2 = [carve(77824 + i * 2048, [128, 512], F32, "tm2%d" % i) for i in range(2)]
        load_w(wout, io["w_out"][l])
        for n in range(4):
            ns = slice(512 * n, 512 * n + 512)
            for mo in range(NK):
                bk = nb([0, 1, 2, 3, 4, 5])
                for k in range(NK):
                    B.mm(bk[:], wout[:, k, 128 * mo:128 * mo + 128], merged[:, k, ns], k == 0, k == NK - 1,
                         [wout, merged_r[k]], [bk])
                B.copy("act", ot[:, mo, :], bk[:], [bk], [ot])
                B.act(sqo[:, mo, :], bk[:], AF.Square, [bk], [sqo])
            sb_ = nb([6, 7])
            for mo in range(NK):
                B.mm(sb_[:], ones[:], sqo[:, mo, :], mo == 0, mo == NK - 1, [ones, sqo], [sb_])
            B.act(rt2[:], sb_[:], AF.Sqrt, [sb_, epsT], [rt2], scale=1.0 / D, bias=epsT[:, 0:1])
            S.op("dve", lambda e, a=rs2, b=rt2: e.reciprocal(out=a[:], in_=b[:]), [rt2.r], [rs2.r])
            for mo in range(NK):
                t_ = tm2[mo % 2]
                B.stt(t_[:], ot[:, mo, :], vecs[:, vb + 8 + mo:vb + 9 + mo], rs2[:], ALU.mult, ALU.mult,
                      [ot, vecs, rs2], [t_])
                B.tt("pool", xT[:, mo, ns], xT[:, mo, ns], t_[:], ALU.add, [t_, xT_r[mo][n]], [xT_r[mo][n]])
        pass
    else:
        yout = io["yT"].rearrange("(k p) t -> p k t", p=128)
        for k in range(NK):
            out_toks.append(B.dma("sp", yout[:, k, :], xT[:, k, :], xT_r[k], [], owner=Res("out%d" % k)))
    self.finish(out_toks)


Builder.main = _main


_CACHE = {}


def kernel(**inputs):
    h = prep_host(inputs)
    x = np.asarray(inputs["x"], np.float32)
    if "nc" not in _CACHE:
        b = Builder()
        b.build()
        _CACHE["nc"] = b.nc
    nc = _CACHE["nc"]
    shared = {k: h[k] for k in ("w_in", "w_glu", "w_bs", "w_ba", "w_out", "vecs", "ssm_rb", "ssm_rc", "amask")}
    in_maps = []
    for b_ in range(8):
        m = dict(shared)
        m["xT"] = np.ascontiguousarray(x[b_].T)
        in_maps.append(m)
    res = run_bass_kernel_spmd(nc, in_maps, core_ids=list(range(8)))
    out = np.stack([np.ascontiguousarray(res.results[b_]["yT"].T) for b_ in range(8)], axis=0)
    return out.astype(np.float32)
```

```python
import math
from contextlib import ExitStack

import numpy as np
import ml_dtypes

import concourse.bass as bass
import concourse.mybir as mybir
from concourse.bass_utils import run_bass_kernel_spmd

F32 = mybir.dt.float32
BF16 = mybir.dt.bfloat16
ALU = mybir.AluOpType
AF = mybir.ActivationFunctionType

L = 2048
D = 1024
NK = 8
DEPTH = 4
TS = 16
NCH = L // TS
ENGS = ("pe", "act", "dve", "pool", "sp")
STRICT = False


class Res:
    __slots__ = ("w", "r", "name", "dsem", "dcnt")

    def __init__(self, name=""):
        self.w = []
        self.r = []
        self.name = name
        self.dsem = None
        self.dcnt = 0


class Sched:
    def __init__(self, nc, es):
        self.nc = nc
        self.es = es
        self.prog = {e: [] for e in ENGS}
        self.cnt = {e: 0 for e in ENGS}
        self.sem = {e: es.enter_context(nc.semaphore("sem_" + e)) for e in ENGS if e != "sp"}
        self.known = {e: {} for e in ENGS}
        self.nsem = 0
        self.final = []

    def _waits(self, eng, reads, writes):
        toks = []
        for t in reads:
            toks += t.w
        for t in writes:
            toks += [x for x in t.w if STRICT or x[2] != eng]
            toks += [x for x in t.r if STRICT or x[2] != eng]
        need = {}
        for (sem, val, src) in toks:
            if eng == "pe" and src == "pe":
                continue
            if self.known[eng].get(id(sem), 0) >= val:
                continue
            if need.get(id(sem), (None, 0))[1] < val:
                need[id(sem)] = (sem, val)
        out = []
        for k, (sem, val) in need.items():
            self.known[eng][k] = val
            out.append((sem, val))
        return out

    def _commit(self, tok, reads, writes):
        for t in writes:
            t.w = [tok]
            t.r = []
        for t in reads:
            t.r.append(tok)
            if len(t.r) > 24:
                best = {}
                for (s, v, src) in t.r:
                    if id(s) not in best or best[id(s)][1] < v:
                        best[id(s)] = (s, v, src)
                t.r = list(best.values())

    def op(self, eng, fn, reads=(), writes=()):
        waits = self._waits(eng, reads, writes)
        self.cnt[eng] += 1
        sem = self.sem[eng]
        val = self.cnt[eng]

        def emit(e, waits=waits, fn=fn, sem=sem):
            for (s, v) in waits:
                e.wait_ge(s, v)
            fn(e).then_inc(sem, 1)

        self.prog[eng].append(emit)
        self._commit((sem, val, eng), reads, writes)

    def dma(self, q, out, in_, reads=(), writes=(), owner=None):
        waits = self._waits(q, reads, writes)
        own = owner if owner is not None else (writes[0] if writes else reads[0])
        if own.dsem is None:
            own.dsem = self.es.enter_context(self.nc.semaphore("dsem%d" % self.nsem))
            self.nsem += 1
        own.dcnt += 16
        sem, val = own.dsem, own.dcnt

        def emit(e, waits=waits, sem=sem, out=out, in_=in_):
            for (s, v) in waits:
                e.wait_ge(s, v)
            e.dma_start(out=out, in_=in_).then_inc(sem, 16)

        self.prog[q].append(emit)
        tok = (sem, val, "dma")
        for t in writes:
            t.w = [tok]
            t.r = []
        for t in reads:
            t.r.append(tok)
        return tok

    def wait_all(self, q, toks):
        def emit(e, toks=toks):
            for (s, v, _) in toks:
                e.wait_ge(s, v)
        self.prog[q].append(emit)

    def barrier(self, extra=()):
        for e in ENGS:
            waits = [(s, v) for (s, v, _) in extra]
            for o in ENGS:
                if o == "sp" or o == e or self.cnt[o] == 0:
                    continue
                if self.known[e].get(id(self.sem[o]), 0) < self.cnt[o]:
                    self.known[e][id(self.sem[o])] = self.cnt[o]
                    waits.append((self.sem[o], self.cnt[o]))

            def emit(en, waits=waits):
                for (s, v) in waits:
                    en.wait_ge(s, v)
            self.prog[e].append(emit)

    def run(self):
        nc = self.nc
        with nc.Block() as block:
            @block.tensor
            def _(e):
                for f in self.prog["pe"]:
                    f(e)

            @block.scalar
            def _(e):
                for f in self.prog["act"]:
                    f(e)

            @block.vector
            def _(e):
                for f in self.prog["dve"]:
                    f(e)

            @block.gpsimd
            def _(e):
                for f in self.prog["pool"]:
                    f(e)

            @block.sync
            def _(e):
                for f in self.prog["sp"]:
                    f(e)


def mkap(t, offset, dims):
    return bass.AP(tensor=t, offset=offset, ap=[list(d) for d in dims])


def _in_col_perm():
    cols = list(range(0, 1024))
    for j in range(4):
        for gi in range(3):
            for base in (1024, 2560, 4096):
                c0 = base + (gi * 4 + j) * 128
                cols += list(range(c0, c0 + 128))
        cols += list(range(5632 + j * 128, 5632 + (j + 1) * 128))
    cols += list(range(6144, 8192))
    assert len(cols) == 8192 and len(set(cols)) == 8192
    return np.asarray(cols)


def prep_host(inp):
    f32 = np.float32
    out = {}
    out["w_in"] = np.ascontiguousarray(np.asarray(inp["w_in"], f32)[:, :, _in_col_perm()])
    out["w_glu"] = np.ascontiguousarray(np.asarray(inp["w_glu"], f32))
    out["w_bs"] = np.ascontiguousarray(np.asarray(inp["w_branch_s"], f32))
    out["w_ba"] = np.ascontiguousarray(np.asarray(inp["w_branch_a"], f32))
    out["w_out"] = np.ascontiguousarray(np.asarray(inp["w_out"], f32))
    vecs = np.zeros((128, DEPTH * 24), f32)
    for l in range(DEPTH):
        vecs[:, l * 24 + 0:l * 24 + 8] = np.asarray(inp["pre_norm_g"], f32)[l].reshape(8, 128).T
        vecs[:, l * 24 + 8:l * 24 + 16] = np.asarray(inp["post_norm_g"], f32)[l].reshape(8, 128).T
        vecs[:, l * 24 + 16:l * 24 + 20] = np.asarray(inp["d_skip"], f32)[l].reshape(4, 128).T
        vecs[:, l * 24 + 20:l * 24 + 24] = np.asarray(inp["b_glu"], f32)[l].reshape(4, 128).T
    out["vecs"] = vecs
    lre = np.asarray(inp["lambda_re"], f32)
    lim = np.asarray(inp["lambda_im"], f32)
    ldt = np.asarray(inp["log_dt"], f32)
    bre = np.asarray(inp["b_re"], f32)
    bim = np.asarray(inp["b_im"], f32)
    cre = np.asarray(inp["c_re"], f32)
    cim = np.asarray(inp["c_im"], f32)
    rb = np.zeros((DEPTH, 128, 5, 4, 128), f32)
    rc = np.zeros((DEPTH, 128, 48 + 512), f32)
    for l in range(DEPTH):
        lam_g = lre[l].reshape(4, 4, 2, 64)
        lim_g = lim[l].reshape(4, 4, 2, 64)
        ldt_g = np.broadcast_to(ldt[l].reshape(4, 4, 2, 1), (4, 4, 2, 64))
        for ptl in range(4):
            rows = slice(32 * ptl, 32 * ptl + 32)
            rb[l, rows, 0] = lam_g[:, ptl].reshape(4, 128)[None]
            rb[l, rows, 1] = lim_g[:, ptl].reshape(4, 128)[None]
            rb[l, rows, 2] = ldt_g[:, ptl].reshape(4, 128)[None]
            for h in range(2):
                for j in range(4):
                    g = 8 * j + 2 * ptl + h
                    r0 = 32 * ptl + 16 * h
                    rb[l, r0:r0 + 16, 3, j, 64 * h:64 * h + 64] = bre[l, g].T
                    rb[l, r0:r0 + 16, 4, j, 64 * h:64 * h + 64] = bim[l, g].T
        rc[l, :, 0:16] = lre[l].reshape(16, 128).T
        rc[l, :, 16:32] = lim[l].reshape(16, 128).T
        rc[l, :, 32:48] = np.broadcast_to(ldt[l].reshape(16, 2, 1), (16, 2, 64)).reshape(16, 128).T
        cc = cre[l].reshape(16, 2, 16, 64).transpose(1, 3, 0, 2).reshape(128, 16 * 16)
        ci = cim[l].reshape(16, 2, 16, 64).transpose(1, 3, 0, 2).reshape(128, 16 * 16)
        rc[l, :, 48:48 + 256] = cc
        rc[l, :, 48 + 256:48 + 512] = ci
    out["ssm_rb"] = rb.reshape(DEPTH, 128, 5 * 512)
    out["ssm_rc"] = rc
    am = np.zeros((128, 256), f32)
    kk = np.arange(128)[:, None]
    qq = np.arange(128)[None, :]
    am[:, 0:128] = (qq >= kk)
    am[:, 128:256] = (qq <= kk)
    out["amask"] = am.astype(ml_dtypes.bfloat16)
    return out


class TL:
    def __init__(self, t, name):
        self.t = t
        self.r = Res(name)

    def __getitem__(self, idx):
        return self.t[idx]


class Builder:
    def __init__(self, n_layers=DEPTH, debug=None):
        self.n_layers = n_layers
        self.debug = debug or {}
        self.nc = bass.Bass("TRN2", target_bir_lowering=False)
        self.es = ExitStack()
        self.S = Sched(self.nc, self.es)
        self.uid = 0

    def sb(self, es, shape, dtype, name=None):
        self.uid += 1
        nm = (name or "t") + "_%d" % self.uid
        return TL(es.enter_context(self.nc.sbuf_tensor(nm, list(shape), dtype)), nm)

    def ps(self, es, shape, dtype=F32, name=None):
        self.uid += 1
        nm = (name or "p") + "_%d" % self.uid
        return TL(es.enter_context(self.nc.psum_tensor(nm, list(shape), dtype)), nm)

    @staticmethod
    def _res(lst):
        return [x.r if isinstance(x, TL) else x for x in lst]

    def tt(self, eng, out, a, b, op, R, W):
        self.S.op(eng, lambda e: e.tensor_tensor(out=out, in0=a, in1=b, op=op), self._res(R), self._res(W))

    def ts(self, eng, out, a, s1, op0, R, W, s2=None, op1=None):
        if op1 is None:
            self.S.op(eng, lambda e: e.tensor_scalar(out=out, in0=a, scalar1=s1, scalar2=None, op0=op0),
                      self._res(R), self._res(W))
        else:
            self.S.op(eng, lambda e: e.tensor_scalar(out=out, in0=a, scalar1=s1, scalar2=s2, op0=op0, op1=op1),
                      self._res(R), self._res(W))

    def stt(self, out, a, scalar, b, op0, op1, R, W):
        self.S.op("dve", lambda e: e.scalar_tensor_tensor(out=out, in0=a, scalar=scalar, in1=b, op0=op0, op1=op1),
                  self._res(R), self._res(W))

    def act(self, out, a, func, R, W, scale=1.0, bias=0.0):
        self.S.op("act", lambda e: e.activation(out=out, in_=a, func=func, bias=bias, scale=scale),
                  self._res(R), self._res(W))

    def copy(self, eng, out, a, R, W):
        if eng == "act":
            self.S.op("act", lambda e: e.activation(out=out, in_=a, func=AF.Copy), self._res(R), self._res(W))
        else:
            self.S.op(eng, lambda e: e.tensor_copy(out=out, in_=a), self._res(R), self._res(W))

    def memset(self, eng, ap, val, W):
        self.S.op(eng, lambda e: e.memset(ap, val), [], self._res(W))

    def mm(self, out, lhsT, rhs, start, stop, R, W, tp=None):
        if tp is None:
            self.S.op("pe", lambda e: e.matmul(out, lhsT=lhsT, rhs=rhs, start=start, stop=stop),
                      self._res(R), self._res(W))
        else:
            self.S.op("pe", lambda e: e.matmul(out, lhsT=lhsT, rhs=rhs, start=start, stop=stop, tile_position=tp),
                      self._res(R), self._res(W))

    def dma(self, q, out, in_, R, W, owner=None):
        return self.S.dma(q, out, in_, self._res(R), self._res(W),
                          owner.r if isinstance(owner, TL) else owner)

    def cmul(self, eng, o, a, b, t1, t2, R, W, Tm):
        (orr, oi), (ar, ai), (br, bi) = o, a, b
        self.tt(eng, t1, ar, br, ALU.mult, R, Tm)
        self.tt(eng, t2, ai, bi, ALU.mult, R, Tm)
        self.tt(eng, orr, t1, t2, ALU.subtract, Tm, W)
        self.tt(eng, t1, ar, bi, ALU.mult, R, Tm)
        self.tt(eng, t2, ai, br, ALU.mult, R, Tm)
        self.tt(eng, oi, t1, t2, ALU.add, Tm, W)

    def declare_io(self):
        nc = self.nc
        nl = DEPTH
        d = {}
        d["xT"] = nc.dram_tensor("xT", [D, L], F32, kind="ExternalInput").ap()
        d["w_in"] = nc.dram_tensor("w_in", [nl, D, 8192], F32, kind="ExternalInput").ap()
        d["w_glu"] = nc.dram_tensor("w_glu", [nl, 512, 512], F32, kind="ExternalInput").ap()
        d["w_bs"] = nc.dram_tensor("w_bs", [nl, 512, D], F32, kind="ExternalInput").ap()
        d["w_ba"] = nc.dram_tensor("w_ba", [nl, 512, D], F32, kind="ExternalInput").ap()
        d["w_out"] = nc.dram_tensor("w_out", [nl, D, D], F32, kind="ExternalInput").ap()
        d["vecs"] = nc.dram_tensor("vecs", [128, nl * 24], F32, kind="ExternalInput").ap()
        d["ssm_rb"] = nc.dram_tensor("ssm_rb", [nl, 128, 2560], F32, kind="ExternalInput").ap()
        d["ssm_rc"] = nc.dram_tensor("ssm_rc", [nl, 128, 560], F32, kind="ExternalInput").ap()
        d["amask"] = nc.dram_tensor("amask", [128, 256], BF16, kind="ExternalInput").ap()
        d["yT"] = nc.dram_tensor("yT", [D, L], F32, kind="ExternalOutput").ap()
        tk = "ExternalOutput" if self.debug.get("tables") else "Internal"
        d["tabB"] = nc.dram_tensor("tabB", [nl, 128, 16384], BF16, kind=tk).ap()
        d["tabC"] = nc.dram_tensor("tabC", [nl, 128, 16384], BF16, kind=tk).ap()
        for name, (shape, dt) in self.debug.get("outs", {}).items():
            d[name] = nc.dram_tensor(name, list(shape), dt, kind="ExternalOutput").ap()
        self.io = d

    def lam_alloc(self, es, Fd, tag):
        names = ("dt", "lr", "a", "th", "mag", "em2a", "c", "s", "t1", "t2", "lbr", "lbi")
        return {n: self.sb(es, [128, Fd], F32, tag + n) for n in names}

    def lam_math(self, m, src):
        B = self
        dt, lr, a, th, mag, em2a, c, s, t1, t2, lbr, lbi = [m[n] for n in
                                                          ("dt", "lr", "a", "th", "mag", "em2a", "c", "s", "t1", "t2", "lbr", "lbi")]
        R = src["R"]
        B.act(dt[:], src["ldt"], AF.Exp, R, [dt])
        B.ts("dve", lr[:], src["lre"], -1e-4, ALU.min, R, [lr])
        B.tt("dve", a[:], lr[:], dt[:], ALU.mult, [lr, dt], [a])
        B.tt("dve", th[:], src["lim"], dt[:], ALU.mult, R + [dt], [th])
        B.act(mag[:], a[:], AF.Exp, [a], [mag])
        B.act(em2a[:], a[:], AF.Exp, [a], [em2a], scale=-2.0)
        B.act(s[:], th[:], AF.Sin, [th], [s], scale=1.0 / 16.0)
        B.act(c[:], th[:], AF.Sin, [th, self.halfpi], [c], scale=1.0 / 16.0, bias=self.halfpi[:, 0:1])
        for _ in range(4):
            B.tt("dve", t1[:], c[:], c[:], ALU.mult, [c], [t1])
            B.tt("dve", t2[:], s[:], s[:], ALU.mult, [s], [t2])
            B.stt(s[:], c[:], 2.0, s[:], ALU.mult, ALU.mult, [c, s], [s])
            B.tt("dve", c[:], t1[:], t2[:], ALU.subtract, [t1, t2], [c])
        B.tt("dve", lbr[:], mag[:], c[:], ALU.mult, [mag, c], [lbr])
        B.tt("dve", lbi[:], mag[:], s[:], ALU.mult, [mag, s], [lbi])

    def prologue(self):
        B = self
        io = self.io
        toks = []
        with ExitStack() as es:
            mb = B.lam_alloc(es, 512, "rb")
            inpb = B.sb(es, [128, 5, 512], F32, "rbin")
            mk = lambda n: B.sb(es, [128, 512], F32, "rb" + n)
            den, nr, cr, ci, invr, invi, bbr, bbi = [mk(n) for n in ("den", "nr", "cr", "ci", "invr", "invi", "bbr", "bbi")]
            xs = [(mk("xr0"), mk("xi0")), (mk("xr1"), mk("xi1"))]
            bt = B.sb(es, [128, 4, TS, 2, 128], BF16, "bt")
            mc = B.lam_alloc(es, 16, "rc")
            inpc = B.sb(es, [128, 560], F32, "rcin")
            pw = B.sb(es, [128, 2, 16, TS], F32, "pw")
            ct = B.sb(es, [128, 16, TS, 2, 32], BF16, "ct")
            big = lambda n: B.sb(es, [128, 16, TS, 16], F32, "rc" + n)
            u1, u2, u3 = big("u1"), big("u2"), big("u3")
            B.memset("pool", ct[:], 0.0, [ct])
            for l in range(self.n_layers):
                inp = inpb
                B.dma("sp", inp[:], io["ssm_rb"][l].rearrange("p (a f) -> p a f", a=5), [], [inp])
                B.lam_math(mb, dict(lre=inp[:, 0, :], lim=inp[:, 1, :], ldt=inp[:, 2, :], R=[inp]))
                lr, lbr, lbi, em2a, t1, t2 = [mb[n] for n in ("lr", "lbr", "lbi", "em2a", "t1", "t2")]
                li = inp[:, 1, :]
                B.tt("dve", t1[:], lr[:], lr[:], ALU.mult, [lr], [t1])
                B.tt("dve", t2[:], li, li, ALU.mult, [inp], [t2])
                B.tt("dve", den[:], t1[:], t2[:], ALU.add, [t1, t2], [den])
                B.S.op("dve", lambda e, den=den: e.reciprocal(out=den[:], in_=den[:]), [den.r], [den.r])
                B.ts("dve", nr[:], lbr[:], -1.0, ALU.add, [lbr], [nr])
                B.tt("dve", t1[:], nr[:], lr[:], ALU.mult, [nr, lr], [t1])
                B.tt("dve", t2[:], lbi[:], li, ALU.mult, [lbi, inp], [t2])
                B.tt("dve", t1[:], t1[:], t2[:], ALU.add, [t1, t2], [t1])
                B.tt("dve", cr[:], t1[:], den[:], ALU.mult, [t1, den], [cr])
                B.tt("dve", t1[:], lbi[:], lr[:], ALU.mult, [lbi, lr], [t1])
                B.tt("dve", t2[:], nr[:], li, ALU.mult, [nr, inp], [t2])
                B.tt("dve", t1[:], t1[:], t2[:], ALU.subtract, [t1, t2], [t1])
                B.tt("dve", ci[:], t1[:], den[:], ALU.mult, [t1, den], [ci])
                B.tt("dve", invr[:], lbr[:], em2a[:], ALU.mult, [lbr, em2a], [invr])
                B.stt(invi[:], lbi[:], -1.0, em2a[:], ALU.mult, ALU.mult, [lbi, em2a], [invi])
                B.cmul("dve", (bbr[:], bbi[:]), (cr[:], ci[:]), (inp[:, 3, :], inp[:, 4, :]), t1[:], t2[:],
                       [cr, ci, inp], [bbr, bbi], [t1, t2])
                prev = (bbr, bbi)
                for i in range(TS):
                    cur = xs[i % 2]
                    B.cmul("dve", (cur[0][:], cur[1][:]), (prev[0][:], prev[1][:]), (invr[:], invi[:]), t1[:], t2[:],
                           [prev[0], prev[1], invr, invi], [cur[0], cur[1]], [t1, t2])
                    B.copy("act", bt[:, :, i, 0, :], cur[0][:].rearrange("p (j q) -> p j q", j=4), [cur[0]], [bt])
                    B.copy("act", bt[:, :, i, 1, :], cur[1][:].rearrange("p (j q) -> p j q", j=4), [cur[1]], [bt])
                    prev = cur
                toks.append(B.dma("sp", io["tabB"][l], bt[:].rearrange("p j i r q -> p (j i r q)"),
                                  [bt], [self.tabB_r[l]], owner=bt))
                inp = inpc
                B.dma("sp", inp[:], io["ssm_rc"][l], [], [inp])
                B.lam_math(mc, dict(lre=inp[:, 0:16], lim=inp[:, 16:32], ldt=inp[:, 32:48], R=[inp]))
                lbr, lbi, t1, t2 = [mc[n] for n in ("lbr", "lbi", "t1", "t2")]
                B.copy("dve", pw[:, 0, :, 0], lbr[:], [lbr], [pw])
                B.copy("dve", pw[:, 1, :, 0], lbi[:], [lbi], [pw])
                for i in range(1, TS):
                    B.cmul("dve", (pw[:, 0, :, i], pw[:, 1, :, i]), (pw[:, 0, :, i - 1], pw[:, 1, :, i - 1]),
                           (lbr[:], lbi[:]), t1[:], t2[:], [pw, lbr, lbi], [pw], [t1, t2])
                PL = self.PL
                B.copy("dve", PL[:, l, :, 0], pw[:, 0, :, TS - 1], [pw], [PL])
                B.copy("dve", PL[:, l, :, 1], pw[:, 1, :, TS - 1], [pw], [PL])
                for k in range(1, 7):
                    pr, pi = PL[:, l, :, 3 * k - 3], PL[:, l, :, 3 * k - 2]
                    B.tt("dve", t1[:], pr, pr, ALU.mult, [PL], [t1])
                    B.tt("dve", t2[:], pi, pi, ALU.mult, [PL], [t2])
                    B.tt("dve", PL[:, l, :, 3 * k], t1[:], t2[:], ALU.subtract, [t1, t2], [PL])
                    B.stt(PL[:, l, :, 3 * k + 1], pr, 2.0, pi, ALU.mult, ALU.mult, [PL], [PL])
                for k in range(7):
                    B.ts("dve", PL[:, l, :, 3 * k + 2], PL[:, l, :, 3 * k + 1], -1.0, ALU.mult, [PL], [PL])
                cre = inp[:, 48:304].rearrange("p (t c) -> p t c", c=16)
                cim = inp[:, 304:560].rearrange("p (t c) -> p t c", c=16)

                def bc_c(ap3):
                    return mkap(ap3.tensor, ap3.offset, [ap3.ap[0], ap3.ap[1], [0, TS], ap3.ap[2]])

                def bc_p(k):
                    a = pw[:, k, :, :]
                    return mkap(a.tensor, a.offset, [a.ap[0], a.ap[1], a.ap[2], [0, 16]])
                B.tt("dve", u1[:], bc_c(cre), bc_p(0), ALU.mult, [inp, pw], [u1])
                B.tt("dve", u2[:], bc_c(cim), bc_p(1), ALU.mult, [inp, pw], [u2])
                B.tt("dve", u3[:], u1[:], u2[:], ALU.subtract, [u1, u2], [u3])
                B.copy("act", ct[0:64, :, :, 0, 0:16], u3[0:64], [u3], [ct])
                B.copy("act", ct[64:128, :, :, 0, 16:32], u3[64:128], [u3], [ct])
                B.tt("dve", u1[:], bc_c(cre), bc_p(1), ALU.mult, [inp, pw], [u1])
                B.tt("dve", u2[:], bc_c(cim), bc_p(0), ALU.mult, [inp, pw], [u2])
                B.stt(u3[:], u1[:], -1.0, u2[:], ALU.mult, ALU.subtract, [u1, u2], [u3])
                B.copy("act", ct[0:64, :, :, 1, 0:16], u3[0:64], [u3], [ct])
                B.copy("act", ct[64:128, :, :, 1, 16:32], u3[64:128], [u3], [ct])
                toks.append(B.dma("sp", io["tabC"][l], ct[:].rearrange("p t i r c -> p (t i r c)"),
                                  [ct], [self.tabC_r[l]], owner=ct))
        self.S.barrier(toks)

    def build(self):
        B = self
        nc = self.nc
        es = self.es
        self.declare_io()
        io = self.io
        self.tabB_r = [Res("tabB%d" % l) for l in range(DEPTH)]
        self.tabC_r = [Res("tabC%d" % l) for l in range(DEPTH)]
        self.halfpi = B.sb(es, [128, 1], F32, "halfpi")
        B.memset("dve", self.halfpi[:], math.pi / 2.0, [self.halfpi])
        self.PL = B.sb(es, [128, DEPTH, 16, 24], F32, "PL")
        self.prologue()
        if self.debug.get("tables"):
            self.finish([])
            return
        self.main()

    def finish(self, out_toks):
        self.S.wait_all("sp", out_toks)
        self.S.barrier(out_toks)
        self.S.run()


class V:
    def __init__(self, ap, name=""):
        self.ap = ap
        self.r = Res(name)
        self.lo = self.hi = None

    def __getitem__(self, idx):
        return self.ap[idx]


def _res_of(lst):
    return [x.r if hasattr(x, "r") and not isinstance(x, Res) else x for x in lst]


Builder._res = staticmethod(_res_of)


def dil_ap(base, r, col0, ncols):
    t, off, pdim, st = base.tensor, base.offset, list(base.ap[0]), base.ap[-1][0]
    assert len(base.ap) == 2
    nsub = L // r
    c0, i0 = divmod(col0, nsub)
    if r == 1:
        dims, o = [[st, ncols]], col0
    elif i0 + ncols <= nsub:
        dims, o = [[r * st, ncols]], c0 + r * i0
    else:
        assert i0 == 0 and ncols % nsub == 0
        dims, o = [[st, ncols // nsub], [r * st, nsub]], c0
    return mkap(t, off + o * st, [pdim] + dims)


def deint_dst(base, r, n):
    if r == 1:
        return base[:, 512 * n:512 * n + 512]
    nsub = L // r
    t, off, pdim, st = base.tensor, base.offset, list(base.ap[0]), base.ap[-1][0]
    return mkap(t, off + (512 * n // r) * st, [pdim, [st, 512 // r], [nsub * st, r]])


def like(bank_ap, ap):
    if len(ap.ap) == 3:
        return bank_ap.rearrange("p (a b) -> p a b", a=ap.ap[1][1])
    return bank_ap


AR_BYTES = 88064


def _main(self):
    B = self
    es = self.es
    io = self.io
    S = self.S
    nc = self.nc
    dbg = self.debug
    xT = B.sb(es, [128, NK, L], F32, "xT")
    xT_r = [[Res("xT%d_%d" % (k, n)) for n in range(4)] for k in range(NK)]
    hT = B.sb(es, [128, NK, L], BF16, "hT")
    hT_r = [Res("hT%d" % k) for k in range(NK)]
    vecs = B.sb(es, [128, DEPTH * 24], F32, "vecs")
    amask = B.sb(es, [128, 256], BF16, "amask")
    ones = B.sb(es, [128, 128], BF16, "ones")
    epsT = B.sb(es, [128, 1], F32, "eps")
    wbuf = [B.sb(es, [128, NK, 512], BF16, "wbuf%d" % i) for i in range(2)]
    arena = B.sb(es, [128, AR_BYTES // 2], BF16, "arena")
    banks = [B.ps(es, [128, 512], F32, "bank%d" % i) for i in range(8)]
    self.wi = 0

    def carve(off, shape, dtype, name):
        nel = int(np.prod(shape[1:]))
        esz = 2 if dtype == BF16 else 4
        assert off % 4 == 0 and off + nel * esz <= AR_BYTES, (name, off, nel * esz)
        a = arena.t[:, off // 2: off // 2 + nel * esz // 2]
        if dtype == F32:
            a = a.bitcast(F32)
        if len(shape) == 3:
            a = a.rearrange("p (a b) -> p a b", a=shape[1])
        elif len(shape) == 4:
            a = a.rearrange("p (a b c) -> p a b c", a=shape[1], b=shape[2])
        elif len(shape) == 5:
            a = a.rearrange("p (a b c d) -> p a b c d", a=shape[1], b=shape[2], c=shape[3])
        v = V(a, name)
        region(off, off + nel * esz, [v.r])
        v.lo, v.hi = off, off + nel * esz
        return v

    regs = []

    def region(lo, hi, res_list):
        inherit = []
        keep = []
        for (l0, h0, rl) in regs:
            if l0 < hi and lo < h0:
                for r_ in rl:
                    inherit += r_.w + r_.r
                if lo <= l0 and h0 <= hi:
                    continue
            keep.append((l0, h0, rl))
        regs[:] = keep
        best = {}
        for (sm, vl, src) in inherit:
            if id(sm) not in best or best[id(sm)][1] < vl:
                best[id(sm)] = (sm, vl, "alias")
        for r_ in res_list:
            r_.r = list(best.values())
        regs.append((lo, hi, res_list))

    def reslist(v, names):
        rl = [Res(n) for n in names]
        region(v.lo, v.hi, rl)
        return rl

    B.memset("dve", ones[:], 1.0, [ones])
    B.memset("dve", epsT[:], 1e-6, [epsT])
    B.dma("sp", vecs[:], io["vecs"], [], [vecs])
    B.dma("sp", amask[:], io["amask"], [], [amask])
    xin = io["xT"].rearrange("(k p) t -> p k t", p=128)
    for k in range(NK):
        B.dma("sp", xT[:, k, :], xin[:, k, :], [], xT_r[k], owner=xT_r[k][0])

    out_toks = []

    def dump(name, ap, R):
        out_toks.append(B.dma("sp", io[name], ap, R, [], owner=Res("dump_" + name)))

    def load_wblock(l, col0, ncols):
        wb = wbuf[self.wi % 2]
        self.wi += 1
        src = io["w_in"][l][:, col0:col0 + ncols].rearrange("(k p) c -> p k c", p=128)
        B.dma("pool", wb[:, :, 0:ncols], src, [], [wb])
        return wb

    def load_w(dst, src2d):
        B.dma("pool", dst[:], src2d.rearrange("(k p) c -> p k c", p=128), [], [dst])

    bank_ctr = [0]

    def nb(lst):
        b = banks[lst[bank_ctr[0] % len(lst)]]
        bank_ctr[0] += 1
        return b

    def hk(k):
        return hT.t[:, k, :]

    for l in range(self.n_layers):
        vb = l * 24
        sq = [carve(i * 8192, [128, NK, 512], BF16, "sq%d" % i) for i in range(2)]
        rt = [carve(16384 + i * 2048, [128, 512], F32, "rt%d" % i) for i in range(2)]
        rs = [carve(20480 + i * 2048, [128, 512], F32, "rs%d" % i) for i in range(2)]
        for n in range(4):
            ns = slice(n * 512, (n + 1) * 512)
            s_, rt_, rs_ = sq[n % 2], rt[n % 2], rs[n % 2]
            for k in range(NK):
                B.act(s_[:, k, :], xT[:, k, ns], AF.Square, [xT_r[k][n]], [s_])
            bk = nb([6, 7])
            for k in range(NK):
                B.mm(bk[:], ones[:], s_[:, k, :], k == 0, k == NK - 1, [ones, s_], [bk])
            B.act(rt_[:], bk[:], AF.Sqrt, [bk, epsT], [rt_], scale=1.0 / D, bias=epsT[:, 0:1])
            S.op("dve", lambda e, a=rs_, b=rt_: e.reciprocal(out=a[:], in_=b[:]), [rt_.r], [rs_.r])
            for k in range(NK):
                B.stt(hT[:, k, ns], xT[:, k, ns], vecs[:, vb + k:vb + k + 1], rs_[:], ALU.mult, ALU.mult,
                      [xT_r[k][n], vecs, rs_], [hT_r[k]])
        if dbg.get("stop") == "P0":
            dump("dbg_h", hT[:], hT_r)
            break
        pass
        yT = carve(32768, [128, 4, L], BF16, "yT")
        _fl = reslist(yT, ["yT%d_%d" % (k, n) for k in range(4) for n in range(4)])
        yT_r = [[_fl[4 * k + n] for n in range(4)] for k in range(4)]
        r32 = carve(0, [128, 2, TS, 128], F32, "r32")
        rbf = carve(16384, [128, 2, TS, 128], BF16, "rbf")
        Bt = carve(24576, [128, TS, 2, 128], BF16, "Bt")
        Ct = carve(49152, [128, 5120], BF16, "Ct")
        CtA = Ct[:, 0:3072].rearrange("p (t i r c) -> p t i r c", t=3, i=TS, r=2)
        CtB = Ct[:, 3072:5120].rearrange("p (i r c) -> p i r c", i=TS, r=2)
        uT = carve(59392, [128, L], BF16, "uT")
        wglu = carve(63488, [128, 4, 512], BF16, "wglu")
        Sab = [carve(67584 + i * 1536, [128, 2, 192], F32, "S%d" % i) for i in range(2)]
        zt = carve(70656, [128, 2, 192], F32, "zt")
        tq = carve(72192, [128, 2, 128], F32, "tq")
        Sbf = carve(73216, [128, 2, 128], BF16, "Sbf")
        gv = [carve(73728 + i * 2048, [128, 512], F32, "gv%d" % i) for i in range(2)]
        gw = [carve(77824 + i * 2048, [128, 512], F32, "gw%d" % i) for i in range(2)]
        gs = [carve(81920 + i * 2048, [128, 512], F32, "gs%d" % i) for i in range(2)]
        gz = [carve(86016 + i * 1024, [128, 512], BF16, "gz%d" % i) for i in range(2)]
        for t_ in (Sab[0], Sab[1], zt):
            B.memset("dve", t_[:, :, 0:64], 0.0, [t_])
        Ctz = V(CtB[:, :, :, 0:32], "Ctz")
        region(Ct.lo, Ct.hi, [Ctz.r])
        B.memset("pool", Ctz[:], 0.0, [Ctz])
        B.memset("dve", Sbf[:], 0.0, [Sbf])
        wbU = load_wblock(l, 0, 512)
        wbZ = load_wblock(l, 512, 512)
        load_w(wglu, io["w_glu"][l])
        PLl = self.PL
        for j in dbg.get("js", range(4)):
            for n in range(4):
                bk = nb([6, 7])
                for k in range(NK):
                    B.mm(bk[:], wbU[:, k, 128 * j:128 * j + 128], hT[:, k, 512 * n:512 * n + 512],
                         k == 0, k == NK - 1, [wbU, hT_r[k]], [bk])
                B.copy("act", deint_dst(uT[:], 16, n), bk[:].rearrange("p (a b) -> p a b", b=16), [bk], [uT])
            B.dma("sp", Bt[:].rearrange("p i r q -> p (i r q)"), io["tabB"][l][:, 4096 * j:4096 * (j + 1)],
                  [self.tabB_r[l]], [Bt])
            B.dma("sp", Ct[:, 0:3072], io["tabC"][l][:, 4096 * j:4096 * j + 3072], [self.tabC_r[l]], [Ct])
            B.dma("sp", CtB[:, :, :, 32:64],
                  io["tabC"][l][:, 4096 * j + 3072:4096 * (j + 1)].rearrange("p (i r c) -> p i r c", i=TS, r=2),
                  [self.tabC_r[l]], [Ct])
            def xt_chain(ptl):
                rows = slice(32 * ptl, 32 * ptl + 32)
                for ih in range(8):
                    xb = nb([4, 5])
                    xv = xb[:].rearrange("p (r i c) -> p r i c", r=2, i=2)
                    for ri in range(2):
                        for il in range(2):
                            i = 2 * ih + il
                            B.mm(xv[:, ri, il, :], Bt[rows, i, ri, :], uT[rows, i * 128:(i + 1) * 128],
                                 True, True, [Bt, uT], [xb], tp=(32 * ptl, 0))
                    for il in range(2):
                        i = 2 * ih + il
                        if i == 0:
                            B.copy("dve", r32[:, :, 0, :], xv[:, :, 0, :], [xb], [r32])
                        else:
                            B.tt("dve", r32[:, :, i, :], r32[:, :, i - 1, :], xv[:, :, il, :], ALU.add,
                                 [xb, r32], [r32])

            def l2(ptl):
                pt = 4 * j + ptl
                B.copy("act", rbf[:], r32[:], [r32], [rbf])
                pl = lambda c: PLl[:, l, pt, c:c + 1]
                rr, rim = r32[:, 0, TS - 1, :], r32[:, 1, TS - 1, :]
                B.ts("dve", tq[:, 0, :], rim, pl(1), ALU.mult, [r32, PLl], [tq])
                B.stt(zt[:, 0, 64:192], rr, pl(0), tq[:, 0, :], ALU.mult, ALU.subtract, [r32, PLl, tq], [zt])
                B.ts("dve", tq[:, 1, :], rr, pl(1), ALU.mult, [r32, PLl], [tq])
                B.stt(zt[:, 1, 64:192], rim, pl(0), tq[:, 1, :], ALU.mult, ALU.add, [r32, PLl, tq], [zt])
                old = zt
                for kk in range(7):
                    d = 1 << kk
                    new = Sab[kk % 2]
                    a_, b_, nb_ = pl(3 * kk), pl(3 * kk + 1), pl(3 * kk + 2)
                    B.stt(new[:, :, 64:192], old[:, :, 64 - d:192 - d], a_, old[:, :, 64:192], ALU.mult, ALU.add, [old, PLl], [new])
                    B.stt(new[:, 0, 64:192], old[:, 1, 64 - d:192 - d], nb_, new[:, 0, 64:192], ALU.mult, ALU.add, [old, new, PLl], [new])
                    B.stt(new[:, 1, 64:192], old[:, 0, 64 - d:192 - d], b_, new[:, 1, 64:192], ALU.mult, ALU.add, [old, new, PLl], [new])
                    old = new
                B.copy("dve", Sbf[:, :, 1:128], old[:, :, 64:191], [old], [Sbf])

            def readout(ptl):
                rows = slice(32 * ptl, 32 * ptl + 32)
                for i in range(TS):
                    yb = banks[i // 4]
                    cs_ = slice((i % 4) * 128, (i % 4) * 128 + 128)
                    if ptl == 3:
                        o_ = yb[64:128, cs_]
                        tp = (0, 64)
                        lh = [CtB[:, i, rr_, :] for rr_ in range(2)]
                        RC_ = [Ct, Ctz]
                    else:
                        o_ = yb[rows, cs_]
                        tp = (0, 32 * ptl)
                        lh = [CtA[:, ptl, i, rr_, :] for rr_ in range(2)]
                        RC_ = [Ct]
                    B.mm(o_, lh[0], rbf[:, 0, i, :], True, False, RC_ + [rbf], [yb], tp=tp)
                    B.mm(o_, lh[1], rbf[:, 1, i, :], False, False, RC_ + [rbf], [yb], tp=tp)
                    B.mm(o_, lh[0], Sbf[:, 0, :], False, False, RC_ + [Sbf], [yb], tp=tp)
                    B.mm(o_, lh[1], Sbf[:, 1, :], False, True, RC_ + [Sbf], [yb], tp=tp)

            order = (0, 1, 3, 2)
            xt_chain(order[0])
            for idx, ptl in enumerate(order):
                l2(ptl)
                if idx + 1 < 4:
                    xt_chain(order[idx + 1])
                readout(ptl)
            for ib in range(4):
                cs = slice(512 * ib, 512 * ib + 512)
                v_, w_, g_ = gv[ib % 2], gw[ib % 2], gs[ib % 2]
                B.stt(v_[:], uT[:, cs], vecs[:, vb + 16 + j:vb + 17 + j], banks[ib][:], ALU.mult, ALU.add,
                      [uT, vecs, banks[ib]], [v_])
                B.act(w_[:], v_[:], AF.Square, [v_], [w_])
                B.ts("pool", w_[:], w_[:], 0.044715, ALU.mult, [w_], [w_], s2=1.0, op1=ALU.add)
                B.tt("pool", w_[:], w_[:], v_[:], ALU.mult, [w_, v_], [w_])
                B.act(g_[:], w_[:], AF.Sigmoid, [w_], [g_], scale=1.5957691216057308)
                B.tt("dve", yT[:, j, cs], v_[:], g_[:], ALU.mult, [v_, g_], [yT_r[j][ib]])
        if dbg.get("stop") == "PA1":
            dump("dbg_y", yT[:], [x for row in yT_r for x in row])
            break
        szf = [carve(4096 * mo, [128, L], BF16, "szf%d" % mo) for mo in range(4)]
        for mo in range(4):
            for n in range(4):
                zb = nb([4, 5])
                for k in range(NK):
                    B.mm(zb[:], wbZ[:, k, 128 * mo:128 * mo + 128], hT[:, k, 512 * n:512 * n + 512],
                         k == 0, k == NK - 1, [wbZ, hT_r[k]], [zb])
                B.act(deint_dst(szf[mo][:], 16, n), zb[:].rearrange("p (a b) -> p a b", b=16), AF.Silu, [zb], [szf[mo]])
        for n in range(4):
            cs = slice(512 * n, 512 * n + 512)
            for mo in range(4):
                gb = banks[mo]
                for k in range(4):
                    B.mm(gb[:], wglu[:, k, 128 * mo:128 * mo + 128], yT[:, k, cs], k == 0, k == 3,
                         [wglu, yT_r[k][n]], [gb])
            for mo in range(4):
                g_ = gs[mo % 2]
                B.act(g_[:], banks[mo][:], AF.Sigmoid, [banks[mo], vecs], [g_], bias=vecs[:, vb + 20 + mo:vb + 21 + mo])
                B.tt("dve", g_[:], g_[:], szf[mo][:, cs], ALU.mult, [g_, szf[mo]], [g_])
                B.tt("dve", yT[:, mo, cs], yT[:, mo, cs], g_[:], ALU.mult, [yT_r[mo][n], g_], [yT_r[mo][n]])
        if dbg.get("stop") == "PA":
            dump("dbg_y", yT[:], [x for row in yT_r for x in row])
            break
        pass
        merged = carve(0, [128, NK, L], BF16, "merged")
        merged_r = reslist(merged, ["mg%d" % m for m in range(NK)])
        wbs = carve(49152, [128, 4, D], BF16, "wbs")
        sgg = [carve(57344 + i * 4096, [128, L], BF16, "sgg%d" % i) for i in range(2)]
        load_w(wbs, io["w_bs"][l])
        for m in range(NK):
            if m % 4 == 0:
                wb = load_wblock(l, 6144 + 512 * (m // 4), 512)
            sg_ = sgg[m % 2]
            for n in range(4):
                bk = nb([6, 7])
                for k in range(NK):
                    B.mm(bk[:], wb[:, k, 128 * (m % 4):128 * (m % 4) + 128], hT[:, k, 512 * n:512 * n + 512],
                         k == 0, k == NK - 1, [wb, hT_r[k]], [bk])
                B.act(sg_[:, 512 * n:512 * n + 512], bk[:], AF.Sigmoid, [bk], [sg_])
            for n2 in range(4):
                bk = nb([0, 1, 2, 3])
                for k in range(4):
                    B.mm(bk[:], wbs[:, k, 128 * m:128 * m + 128], yT[:, k, 512 * n2:512 * n2 + 512], k == 0, k == 3,
                         [wbs, yT_r[k][n2]], [bk])
                dst = dil_ap(merged[:, m, :], 16, 512 * n2, 512)
                B.tt("dve", dst, like(bk[:], dst), dil_ap(sg_[:], 16, 512 * n2, 512), ALU.mult,
                     [bk, sg_], [merged_r[m]])
        if dbg.get("stop") == "PA2":
            dump("dbg_m", merged[:], merged_r)
            break
        pass
        yaT = carve(32768, [128, 4, L], BF16, "yaT")
        yaT_r = reslist(yaT, ["ya%d" % j for j in range(4)])
        qT = carve(49152, [128, L], BF16, "qT")
        kT = carve(53248, [128, L], BF16, "kT")
        vd = carve(57344, [128, 16, 128], BF16, "vd")
        etm = [carve(61440 + i * 512, [128, 256], BF16, "etm%d" % i) for i in range(6)]
        ett = [carve(64512 + i * 512, [128, 256], BF16, "ett%d" % i) for i in range(4)]
        accn = carve(66560, [128, L], F32, "accn")
        accd = carve(74752, [128, L], F32, "accd")
        sza = carve(82944, [128, L], BF16, "sza")
        for j in range(4):
            for gi, r in enumerate((1, 4, 16)):
                ncols = 512 if gi == 2 else 384
                wb = load_wblock(l, 1024 + 1280 * j + 384 * gi, ncols)
                nsub = L // r
                nblk = nsub // 128
                for (c_in, dst) in ((0, qT), (128, kT)):
                    for n in range(4):
                        bk = nb([6, 7])
                        for k in range(NK):
                            B.mm(bk[:], wb[:, k, c_in:c_in + 128], hT[:, k, 512 * n:512 * n + 512], k == 0, k == NK - 1,
                                 [wb, hT_r[k]], [bk])
                        src_ = bk[:] if r == 1 else bk[:].rearrange("p (a b) -> p a b", b=r)
                        B.copy("act", deint_dst(dst[:], r, n), src_, [bk], [dst])
                for Bq in range(4):
                    bk = nb([6, 7])
                    for bl in range(4):
                        Bk = 4 * Bq + bl
                        for k in range(NK):
                            B.mm(bk[:, bl * 128:bl * 128 + 128], dil_ap(hk(k), r, 128 * Bk, 128), wb[:, k, 256:384],
                                 k == 0, k == NK - 1, [wb, hT_r[k]], [bk])
                    B.copy("act", vd[:, 4 * Bq:4 * Bq + 4, :], bk[:].rearrange("p (a b) -> p a b", a=4), [bk], [vd])
                def score(Bk):
                    c_, blk = divmod(Bk, nblk)
                    nq = 256 if blk < nblk - 1 else 128
                    sb_ = banks[4 + (Bk // 2) % 2]
                    reg = sb_[:, 256 * (Bk % 2):256 * (Bk % 2) + nq]
                    B.mm(reg, kT[:, 128 * Bk:128 * Bk + 128], qT[:, 128 * Bk:128 * Bk + nq], True, True, [kT, qT], [sb_])
                    et_, em_ = ett[Bk % 4], etm[Bk % 6]
                    B.act(et_[:, :nq], reg, AF.Exp, [sb_], [et_], scale=1.0 / math.sqrt(128.0))
                    B.tt("dve", em_[:, :nq], et_[:, :nq], amask[:, :nq], ALU.mult, [et_, amask], [em_])

                def pv(Bk):
                    c_, blk = divmod(Bk, nblk)
                    em_ = etm[Bk % 6]
                    pvb, dnb = banks[(Bk // 4) % 2], banks[2 + (Bk // 4) % 2]
                    cs = slice((Bk % 4) * 128, (Bk % 4) * 128 + 128)
                    for (bank_, isden) in ((pvb, False), (dnb, True)):
                        l0 = ones[:] if isden else vd[:, Bk, :]
                        R0 = [ones] if isden else [vd]
                        B.mm(bank_[:, cs], l0, em_[:, 0:128], True, blk == 0, R0 + [em_], [bank_])
                        if blk > 0:
                            emp = etm[(Bk - 1) % 6]
                            l1 = ones[:] if isden else vd[:, Bk - 1, :]
                            B.mm(bank_[:, cs], l1, emp[:, 128:256], False, True, R0 + [emp], [bank_])
                    if Bk % 4 == 3:
                        for (bank_, acc) in ((pvb, accn), (dnb, accd)):
                            dst = dil_ap(acc[:], r, 512 * (Bk // 4), 512)
                            if gi == 0:
                                B.copy("act" if acc is accd else "dve", dst, like(bank_[:], dst), [bank_], [acc])
                            else:
                                B.tt("dve", dst, dst, like(bank_[:], dst), ALU.add, [bank_, acc], [acc])

                LOOK = 3
                for Bk in range(16 + LOOK):
                    if Bk < 16:
                        score(Bk)
                    if Bk >= LOOK:
                        pv(Bk - LOOK)
            for n in range(4):
                bk = nb([6, 7])
                for k in range(NK):
                    B.mm(bk[:], wb[:, k, 384:512], hT[:, k, 512 * n:512 * n + 512], k == 0, k == NK - 1,
                         [wb, hT_r[k]], [bk])
                B.act(sza[:, 512 * n:512 * n + 512], bk[:], AF.Silu, [bk], [sza])
            S.op("dve", lambda e, a=accd: e.reciprocal(out=a[:], in_=a[:]), [accd.r], [accd.r])
            B.tt("dve", accn[:], accn[:], accd[:], ALU.mult, [accn, accd], [accn])
            B.tt("dve", yaT[:, j, :], accn[:], sza[:], ALU.mult, [accn, sza], [yaT_r[j]])
        if dbg.get("stop") == "PB":
            dump("dbg_ya", yaT[:], yaT_r)
            break
        pass
        wba = carve(49152, [128, 4, D], BF16, "wba")
        sgg = [carve(57344 + i * 4096, [128, L], BF16, "sgga%d" % i) for i in range(2)]
        tmm = [carve(65536 + i * 2048, [128, 512], F32, "tmm%d" % i) for i in range(2)]
        load_w(wba, io["w_ba"][l])
        for m in range(NK):
            if m % 4 == 0:
                wb = load_wblock(l, 7168 + 512 * (m // 4), 512)
            sg_ = sgg[m % 2]
            for n in range(4):
                bk = nb([6, 7])
                for k in range(NK):
                    B.mm(bk[:], wb[:, k, 128 * (m % 4):128 * (m % 4) + 128], hT[:, k, 512 * n:512 * n + 512],
                         k == 0, k == NK - 1, [wb, hT_r[k]], [bk])
                B.act(sg_[:, 512 * n:512 * n + 512], bk[:], AF.Sigmoid, [bk], [sg_])
            for n in range(4):
                ns = slice(512 * n, 512 * n + 512)
                bk = nb([0, 1, 2, 3])
                for k in range(4):
                    B.mm(bk[:], wba[:, k, 128 * m:128 * m + 128], yaT[:, k, ns], k == 0, k == 3,
                         [wba, yaT_r[k]], [bk])
                t_ = tmm[n % 2]
                B.tt("dve", t_[:], bk[:], sg_[:, ns], ALU.mult, [bk, sg_], [t_])
                B.tt("dve", merged[:, m, ns], merged[:, m, ns], t_[:], ALU.add, [t_, merged_r[m]], [merged_r[m]])
        if dbg.get("stop") == "PC2":
            dump("dbg_m", merged[:], merged_r)
            break
        pass
        wout = carve(32768, [128, NK, D], BF16, "wout")
        ot = carve(49152, [128, NK, 512], F32, "ot")
        sqo = carve(65536, [128, NK, 512], BF16, "sqo")
        rt2 = carve(73728, [128, 512], F32, "rt2")
        rs2 = carve(75776, [128, 512], F32, "rs2")
        tm2 = [carve(77824 + i * 2048, [128, 512], F32, "tm2%d" % i) for i in range(2)]
        load_w(wout, io["w_out"][l])
        for n in range(4):
            ns = slice(512 * n, 512 * n + 512)
            for mo in range(NK):
                bk = nb([0, 1, 2, 3, 4, 5])
                for k in range(NK):
                    B.mm(bk[:], wout[:, k, 128 * mo:128 * mo + 128], merged[:, k, ns], k == 0, k == NK - 1,
                         [wout, merged_r[k]], [bk])
                B.copy("act", ot[:, mo, :], bk[:], [bk], [ot])
                B.act(sqo[:, mo, :], bk[:], AF.Square, [bk], [sqo])
            sb_ = nb([6, 7])
            for mo in range(NK):
                B.mm(sb_[:], ones[:], sqo[:, mo, :], mo == 0, mo == NK - 1, [ones, sqo], [sb_])
            B.act(rt2[:], sb_[:], AF.Sqrt, [sb_, epsT], [rt2], scale=1.0 / D, bias=epsT[:, 0:1])
            S.op("dve", lambda e, a=rs2, b=rt2: e.reciprocal(out=a[:], in_=b[:]), [rt2.r], [rs2.r])
            for mo in range(NK):
                t_ = tm2[mo % 2]
                B.stt(t_[:], ot[:, mo, :], vecs[:, vb + 8 + mo:vb + 9 + mo], rs2[:], ALU.mult, ALU.mult,
                      [ot, vecs, rs2], [t_])
                B.tt("pool", xT[:, mo, ns], xT[:, mo, ns], t_[:], ALU.add, [t_, xT_r[mo][n]], [xT_r[mo][n]])
        pass
    else:
        yout = io["yT"].rearrange("(k p) t -> p k t", p=128)
        for k in range(NK):
            out_toks.append(B.dma("sp", yout[:, k, :], xT[:, k, :], xT_r[k], [], owner=Res("out%d" % k)))
    self.finish(out_toks)


Builder.main = _main


_CACHE = {}


def kernel(**inputs):
    h = prep_host(inputs)
    x = np.asarray(inputs["x"], np.float32)
    if "nc" not in _CACHE:
        b = Builder()
        b.build()
        _CACHE["nc"] = b.nc
    nc = _CACHE["nc"]
    shared = {k: h[k] for k in ("w_in", "w_glu", "w_bs", "w_ba", "w_out", "vecs", "ssm_rb", "ssm_rc", "amask")}
    in_maps = []
    for b_ in range(8):
        m = dict(shared)
        m["xT"] = np.ascontiguousarray(x[b_].T)
        in_maps.append(m)
    res = run_bass_kernel_spmd(nc, in_maps, core_ids=list(range(8)))
    out = np.stack([np.ascontiguousarray(res.results[b_]["yT"].T) for b_ in range(8)], axis=0)
    return out.astype(np.float32)
```

```python
import math
from contextlib import ExitStack

import numpy as np
import ml_dtypes

import concourse.bass as bass
import concourse.mybir as mybir
from concourse.bass_utils import run_bass_kernel_spmd

F32 = mybir.dt.float32
BF16 = mybir.dt.bfloat16
ALU = mybir.AluOpType
AF = mybir.ActivationFunctionType

L = 2048
D = 1024
NK = 8
DEPTH = 4
TS = 16
NCH = L // TS
ENGS = ("pe", "act", "dve", "pool", "sp")
STRICT = False


class Res:
    __slots__ = ("w", "r", "name", "dsem", "dcnt")

    def __init__(self, name=""):
        self.w = []
        self.r = []
        self.name = name
        self.dsem = None
        self.dcnt = 0


class Sched:
    def __init__(self, nc, es):
        self.nc = nc
        self.es = es
        self.prog = {e: [] for e in ENGS}
        self.cnt = {e: 0 for e in ENGS}
        self.sem = {e: es.enter_context(nc.semaphore("sem_" + e)) for e in ENGS if e != "sp"}
        self.known = {e: {} for e in ENGS}
        self.nsem = 0
        self.final = []

    def _waits(self, eng, reads, writes):
        toks = []
        for t in reads:
            toks += t.w
        for t in writes:
            toks += [x for x in t.w if STRICT or x[2] != eng]
            toks += [x for x in t.r if STRICT or x[2] != eng]
        need = {}
        for (sem, val, src) in toks:
            if eng == "pe" and src == "pe":
                continue
            if self.known[eng].get(id(sem), 0) >= val:
                continue
            if need.get(id(sem), (None, 0))[1] < val:
                need[id(sem)] = (sem, val)
        out = []
        for k, (sem, val) in need.items():
            self.known[eng][k] = val
            out.append((sem, val))
        return out

    def _commit(self, tok, reads, writes):
        for t in writes:
            t.w = [tok]
            t.r = []
        for t in reads:
            t.r.append(tok)
            if len(t.r) > 24:
                best = {}
                for (s, v, src) in t.r:
                    if id(s) not in best or best[id(s)][1] < v:
                        best[id(s)] = (s, v, src)
                t.r = list(best.values())

    def op(self, eng, fn, reads=(), writes=()):
        waits = self._waits(eng, reads, writes)
        self.cnt[eng] += 1
        sem = self.sem[eng]
        val = self.cnt[eng]

        def emit(e, waits=waits, fn=fn, sem=sem):
            for (s, v) in waits:
                e.wait_ge(s, v)
            fn(e).then_inc(sem, 1)

        self.prog[eng].append(emit)
        self._commit((sem, val, eng), reads, writes)

    def dma(self, q, out, in_, reads=(), writes=(), owner=None):
        waits = self._waits(q, reads, writes)
        own = owner if owner is not None else (writes[0] if writes else reads[0])
        if own.dsem is None:
            own.dsem = self.es.enter_context(self.nc.semaphore("dsem%d" % self.nsem))
            self.nsem += 1
        own.dcnt += 16
        sem, val = own.dsem, own.dcnt

        def emit(e, waits=waits, sem=sem, out=out, in_=in_):
            for (s, v) in waits:
                e.wait_ge(s, v)
            e.dma_start(out=out, in_=in_).then_inc(sem, 16)

        self.prog[q].append(emit)
        tok = (sem, val, "dma")
        for t in writes:
            t.w = [tok]
            t.r = []
        for t in reads:
            t.r.append(tok)
        return tok

    def wait_all(self, q, toks):
        def emit(e, toks=toks):
            for (s, v, _) in toks:
                e.wait_ge(s, v)
        self.prog[q].append(emit)

    def barrier(self, extra=()):
        for e in ENGS:
            waits = [(s, v) for (s, v, _) in extra]
            for o in ENGS:
                if o == "sp" or o == e or self.cnt[o] == 0:
                    continue
                if self.known[e].get(id(self.sem[o]), 0) < self.cnt[o]:
                    self.known[e][id(self.sem[o])] = self.cnt[o]
                    waits.append((self.sem[o], self.cnt[o]))

            def emit(en, waits=waits):
                for (s, v) in waits:
                    en.wait_ge(s, v)
            self.prog[e].append(emit)

    def run(self):
        nc = self.nc
        with nc.Block() as block:
            @block.tensor
            def _(e):
                for f in self.prog["pe"]:
                    f(e)

            @block.scalar
            def _(e):
                for f in self.prog["act"]:
                    f(e)

            @block.vector
            def _(e):
                for f in self.prog["dve"]:
                    f(e)

            @block.gpsimd
            def _(e):
                for f in self.prog["pool"]:
                    f(e)

            @block.sync
            def _(e):
                for f in self.prog["sp"]:
                    f(e)


def mkap(t, offset, dims):
    return bass.AP(tensor=t, offset=offset, ap=[list(d) for d in dims])


def _in_col_perm():
    cols = list(range(0, 1024))
    for j in range(4):
        for gi in range(3):
            for base in (1024, 2560, 4096):
                c0 = base + (gi * 4 + j) * 128
                cols += list(range(c0, c0 + 128))
        cols += list(range(5632 + j * 128, 5632 + (j + 1) * 128))
    cols += list(range(6144, 8192))
    assert len(cols) == 8192 and len(set(cols)) == 8192
    return np.asarray(cols)


def prep_host(inp):
    f32 = np.float32
    out = {}
    out["w_in"] = np.ascontiguousarray(np.asarray(inp["w_in"], f32)[:, :, _in_col_perm()])
    out["w_glu"] = np.ascontiguousarray(np.asarray(inp["w_glu"], f32))
    out["w_bs"] = np.ascontiguousarray(np.asarray(inp["w_branch_s"], f32))
    out["w_ba"] = np.ascontiguousarray(np.asarray(inp["w_branch_a"], f32))
    out["w_out"] = np.ascontiguousarray(np.asarray(inp["w_out"], f32))
    vecs = np.zeros((128, DEPTH * 24), f32)
    for l in range(DEPTH):
        vecs[:, l * 24 + 0:l * 24 + 8] = np.asarray(inp["pre_norm_g"], f32)[l].reshape(8, 128).T
        vecs[:, l * 24 + 8:l * 24 + 16] = np.asarray(inp["post_norm_g"], f32)[l].reshape(8, 128).T
        vecs[:, l * 24 + 16:l * 24 + 20] = np.asarray(inp["d_skip"], f32)[l].reshape(4, 128).T
        vecs[:, l * 24 + 20:l * 24 + 24] = np.asarray(inp["b_glu"], f32)[l].reshape(4, 128).T
    out["vecs"] = vecs
    lre = np.asarray(inp["lambda_re"], f32)
    lim = np.asarray(inp["lambda_im"], f32)
    ldt = np.asarray(inp["log_dt"], f32)
    bre = np.asarray(inp["b_re"], f32)
    bim = np.asarray(inp["b_im"], f32)
    cre = np.asarray(inp["c_re"], f32)
    cim = np.asarray(inp["c_im"], f32)
    rb = np.zeros((DEPTH, 128, 5, 4, 128), f32)
    rc = np.zeros((DEPTH, 128, 48 + 512), f32)
    for l in range(DEPTH):
        lam_g = lre[l].reshape(4, 4, 2, 64)
        lim_g = lim[l].reshape(4, 4, 2, 64)
        ldt_g = np.broadcast_to(ldt[l].reshape(4, 4, 2, 1), (4, 4, 2, 64))
        for ptl in range(4):
            rows = slice(32 * ptl, 32 * ptl + 32)
            rb[l, rows, 0] = lam_g[:, ptl].reshape(4, 128)[None]
            rb[l, rows, 1] = lim_g[:, ptl].reshape(4, 128)[None]
            rb[l, rows, 2] = ldt_g[:, ptl].reshape(4, 128)[None]
            for h in range(2):
                for j in range(4):
                    g = 8 * j + 2 * ptl + h
                    r0 = 32 * ptl + 16 * h
                    rb[l, r0:r0 + 16, 3, j, 64 * h:64 * h + 64] = bre[l, g].T
                    rb[l, r0:r0 + 16, 4, j, 64 * h:64 * h + 64] = bim[l, g].T
        rc[l, :, 0:16] = lre[l].reshape(16, 128).T
        rc[l, :, 16:32] = lim[l].reshape(16, 128).T
        rc[l, :, 32:48] = np.broadcast_to(ldt[l].reshape(16, 2, 1), (16, 2, 64)).reshape(16, 128).T
        cc = cre[l].reshape(16, 2, 16, 64).transpose(1, 3, 0, 2).reshape(128, 16 * 16)
        ci = cim[l].reshape(16, 2, 16, 64).transpose(1, 3, 0, 2).reshape(128, 16 * 16)
        rc[l, :, 48:48 + 256] = cc
        rc[l, :, 48 + 256:48 + 512] = ci
    out["ssm_rb"] = rb.reshape(DEPTH, 128, 5 * 512)
    out["ssm_rc"] = rc
    am = np.zeros((128, 256), f32)
    kk = np.arange(128)[:, None]
    qq = np.arange(128)[None, :]
    am[:, 0:128] = (qq >= kk)
    am[:, 128:256] = (qq <= kk)
    out["amask"] = am.astype(ml_dtypes.bfloat16)
    return out


class TL:
    def __init__(self, t, name):
        self.t = t
        self.r = Res(name)

    def __getitem__(self, idx):
        return self.t[idx]


class Builder:
    def __init__(self, n_layers=DEPTH, debug=None):
        self.n_layers = n_layers
        self.debug = debug or {}
        self.nc = bass.Bass("TRN2", target_bir_lowering=False)
        self.es = ExitStack()
        self.S = Sched(self.nc, self.es)
        self.uid = 0

    def sb(self, es, shape, dtype, name=None):
        self.uid += 1
        nm = (name or "t") + "_%d" % self.uid
        return TL(es.enter_context(self.nc.sbuf_tensor(nm, list(shape), dtype)), nm)

    def ps(self, es, shape, dtype=F32, name=None):
        self.uid += 1
        nm = (name or "p") + "_%d" % self.uid
        return TL(es.enter_context(self.nc.psum_tensor(nm, list(shape), dtype)), nm)

    @staticmethod
    def _res(lst):
        return [x.r if isinstance(x, TL) else x for x in lst]

    def tt(self, eng, out, a, b, op, R, W):
        self.S.op(eng, lambda e: e.tensor_tensor(out=out, in0=a, in1=b, op=op), self._res(R), self._res(W))

    def ts(self, eng, out, a, s1, op0, R, W, s2=None, op1=None):
        if op1 is None:
            self.S.op(eng, lambda e: e.tensor_scalar(out=out, in0=a, scalar1=s1, scalar2=None, op0=op0),
                      self._res(R), self._res(W))
        else:
            self.S.op(eng, lambda e: e.tensor_scalar(out=out, in0=a, scalar1=s1, scalar2=s2, op0=op0, op1=op1),
                      self._res(R), self._res(W))

    def stt(self, out, a, scalar, b, op0, op1, R, W):
        self.S.op("dve", lambda e: e.scalar_tensor_tensor(out=out, in0=a, scalar=scalar, in1=b, op0=op0, op1=op1),
                  self._res(R), self._res(W))

    def act(self, out, a, func, R, W, scale=1.0, bias=0.0):
        self.S.op("act", lambda e: e.activation(out=out, in_=a, func=func, bias=bias, scale=scale),
                  self._res(R), self._res(W))

    def copy(self, eng, out, a, R, W):
        if eng == "act":
            self.S.op("act", lambda e: e.activation(out=out, in_=a, func=AF.Copy), self._res(R), self._res(W))
        else:
            self.S.op(eng, lambda e: e.tensor_copy(out=out, in_=a), self._res(R), self._res(W))

    def memset(self, eng, ap, val, W):
        self.S.op(eng, lambda e: e.memset(ap, val), [], self._res(W))

    def mm(self, out, lhsT, rhs, start, stop, R, W, tp=None):
        if tp is None:
            self.S.op("pe", lambda e: e.matmul(out, lhsT=lhsT, rhs=rhs, start=start, stop=stop),
                      self._res(R), self._res(W))
        else:
            self.S.op("pe", lambda e: e.matmul(out, lhsT=lhsT, rhs=rhs, start=start, stop=stop, tile_position=tp),
                      self._res(R), self._res(W))

    def dma(self, q, out, in_, R, W, owner=None):
        return self.S.dma(q, out, in_, self._res(R), self._res(W),
                          owner.r if isinstance(owner, TL) else owner)

    def cmul(self, eng, o, a, b, t1, t2, R, W, Tm):
        (orr, oi), (ar, ai), (br, bi) = o, a, b
        self.tt(eng, t1, ar, br, ALU.mult, R, Tm)
        self.tt(eng, t2, ai, bi, ALU.mult, R, Tm)
        self.tt(eng, orr, t1, t2, ALU.subtract, Tm, W)
        self.tt(eng, t1, ar, bi, ALU.mult, R, Tm)
        self.tt(eng, t2, ai, br, ALU.mult, R, Tm)
        self.tt(eng, oi, t1, t2, ALU.add, Tm, W)

    def declare_io(self):
        nc = self.nc
        nl = DEPTH
        d = {}
        d["xT"] = nc.dram_tensor("xT", [D, L], F32, kind="ExternalInput").ap()
        d["w_in"] = nc.dram_tensor("w_in", [nl, D, 8192], F32, kind="ExternalInput").ap()
        d["w_glu"] = nc.dram_tensor("w_glu", [nl, 512, 512], F32, kind="ExternalInput").ap()
        d["w_bs"] = nc.dram_tensor("w_bs", [nl, 512, D], F32, kind="ExternalInput").ap()
        d["w_ba"] = nc.dram_tensor("w_ba", [nl, 512, D], F32, kind="ExternalInput").ap()
        d["w_out"] = nc.dram_tensor("w_out", [nl, D, D], F32, kind="ExternalInput").ap()
        d["vecs"] = nc.dram_tensor("vecs", [128, nl * 24], F32, kind="ExternalInput").ap()
        d["ssm_rb"] = nc.dram_tensor("ssm_rb", [nl, 128, 2560], F32, kind="ExternalInput").ap()
        d["ssm_rc"] = nc.dram_tensor("ssm_rc", [nl, 128, 560], F32, kind="ExternalInput").ap()
        d["amask"] = nc.dram_tensor("amask", [128, 256], BF16, kind="ExternalInput").ap()
        d["yT"] = nc.dram_tensor("yT", [D, L], F32, kind="ExternalOutput").ap()
        tk = "ExternalOutput" if self.debug.get("tables") else "Internal"
        d["tabB"] = nc.dram_tensor("tabB", [nl, 128, 16384], BF16, kind=tk).ap()
        d["tabC"] = nc.dram_tensor("tabC", [nl, 128, 16384], BF16, kind=tk).ap()
        for name, (shape, dt) in self.debug.get("outs", {}).items():
            d[name] = nc.dram_tensor(name, list(shape), dt, kind="ExternalOutput").ap()
        self.io = d

    def lam_alloc(self, es, Fd, tag):
        names = ("dt", "lr", "a", "th", "mag", "em2a", "c", "s", "t1", "t2", "lbr", "lbi")
        return {n: self.sb(es, [128, Fd], F32, tag + n) for n in names}

    def lam_math(self, m, src):
        B = self
        dt, lr, a, th, mag, em2a, c, s, t1, t2, lbr, lbi = [m[n] for n in
                                                          ("dt", "lr", "a", "th", "mag", "em2a", "c", "s", "t1", "t2", "lbr", "lbi")]
        R = src["R"]
        B.act(dt[:], src["ldt"], AF.Exp, R, [dt])
        B.ts("dve", lr[:], src["lre"], -1e-4, ALU.min, R, [lr])
        B.tt("dve", a[:], lr[:], dt[:], ALU.mult, [lr, dt], [a])
        B.tt("dve", th[:], src["lim"], dt[:], ALU.mult, R + [dt], [th])
        B.act(mag[:], a[:], AF.Exp, [a], [mag])
        B.act(em2a[:], a[:], AF.Exp, [a], [em2a], scale=-2.0)
        B.act(s[:], th[:], AF.Sin, [th], [s], scale=1.0 / 16.0)
        B.act(c[:], th[:], AF.Sin, [th, self.halfpi], [c], scale=1.0 / 16.0, bias=self.halfpi[:, 0:1])
        for _ in range(4):
            B.tt("dve", t1[:], c[:], c[:], ALU.mult, [c], [t1])
            B.tt("dve", t2[:], s[:], s[:], ALU.mult, [s], [t2])
            B.stt(s[:], c[:], 2.0, s[:], ALU.mult, ALU.mult, [c, s], [s])
            B.tt("dve", c[:], t1[:], t2[:], ALU.subtract, [t1, t2], [c])
        B.tt("dve", lbr[:], mag[:], c[:], ALU.mult, [mag, c], [lbr])
        B.tt("dve", lbi[:], mag[:], s[:], ALU.mult, [mag, s], [lbi])

    def prologue(self):
        B = self
        io = self.io
        toks = []
        with ExitStack() as es:
            mb = B.lam_alloc(es, 512, "rb")
            inpb = B.sb(es, [128, 5, 512], F32, "rbin")
            mk = lambda n: B.sb(es, [128, 512], F32, "rb" + n)
            den, nr, cr, ci, invr, invi, bbr, bbi = [mk(n) for n in ("den", "nr", "cr", "ci", "invr", "invi", "bbr", "bbi")]
            xs = [(mk("xr0"), mk("xi0")), (mk("xr1"), mk("xi1"))]
            bt = B.sb(es, [128, 4, TS, 2, 128], BF16, "bt")
            mc = B.lam_alloc(es, 16, "rc")
            inpc = B.sb(es, [128, 560], F32, "rcin")
            pw = B.sb(es, [128, 2, 16, TS], F32, "pw")
            ct = B.sb(es, [128, 16, TS, 2, 32], BF16, "ct")
            big = lambda n: B.sb(es, [128, 16, TS, 16], F32, "rc" + n)
            u1, u2, u3 = big("u1"), big("u2"), big("u3")
            B.memset("pool", ct[:], 0.0, [ct])
            for l in range(self.n_layers):
                inp = inpb
                B.dma("sp", inp[:], io["ssm_rb"][l].rearrange("p (a f) -> p a f", a=5), [], [inp])
                B.lam_math(mb, dict(lre=inp[:, 0, :], lim=inp[:, 1, :], ldt=inp[:, 2, :], R=[inp]))
                lr, lbr, lbi, em2a, t1, t2 = [mb[n] for n in ("lr", "lbr", "lbi", "em2a", "t1", "t2")]
                li = inp[:, 1, :]
                B.tt("dve", t1[:], lr[:], lr[:], ALU.mult, [lr], [t1])
                B.tt("dve", t2[:], li, li, ALU.mult, [inp], [t2])
                B.tt("dve", den[:], t1[:], t2[:], ALU.add, [t1, t2], [den])
                B.S.op("dve", lambda e, den=den: e.reciprocal(out=den[:], in_=den[:]), [den.r], [den.r])
                B.ts("dve", nr[:], lbr[:], -1.0, ALU.add, [lbr], [nr])
                B.tt("dve", t1[:], nr[:], lr[:], ALU.mult, [nr, lr], [t1])
                B.tt("dve", t2[:], lbi[:], li, ALU.mult, [lbi, inp], [t2])
                B.tt("dve", t1[:], t1[:], t2[:], ALU.add, [t1, t2], [t1])
                B.tt("dve", cr[:], t1[:], den[:], ALU.mult, [t1, den], [cr])
                B.tt("dve", t1[:], lbi[:], lr[:], ALU.mult, [lbi, lr], [t1])
                B.tt("dve", t2[:], nr[:], li, ALU.mult, [nr, inp], [t2])
                B.tt("dve", t1[:], t1[:], t2[:], ALU.subtract, [t1, t2], [t1])
                B.tt("dve", ci[:], t1[:], den[:], ALU.mult, [t1, den], [ci])
                B.tt("dve", invr[:], lbr[:], em2a[:], ALU.mult, [lbr, em2a], [invr])
                B.stt(invi[:], lbi[:], -1.0, em2a[:], ALU.mult, ALU.mult, [lbi, em2a], [invi])
                B.cmul("dve", (bbr[:], bbi[:]), (cr[:], ci[:]), (inp[:, 3, :], inp[:, 4, :]), t1[:], t2[:],
                       [cr, ci, inp], [bbr, bbi], [t1, t2])
                prev = (bbr, bbi)
                for i in range(TS):
                    cur = xs[i % 2]
                    B.cmul("dve", (cur[0][:], cur[1][:]), (prev[0][:], prev[1][:]), (invr[:], invi[:]), t1[:], t2[:],
                           [prev[0], prev[1], invr, invi], [cur[0], cur[1]], [t1, t2])
                    B.copy("act", bt[:, :, i, 0, :], cur[0][:].rearrange("p (j q) -> p j q", j=4), [cur[0]], [bt])
                    B.copy("act", bt[:, :, i, 1, :], cur[1][:].rearrange("p (j q) -> p j q", j=4), [cur[1]], [bt])
                    prev = cur
                toks.append(B.dma("sp", io["tabB"][l], bt[:].rearrange("p j i r q -> p (j i r q)"),
                                  [bt], [self.tabB_r[l]], owner=bt))
                inp = inpc
                B.dma("sp", inp[:], io["ssm_rc"][l], [], [inp])
                B.lam_math(mc, dict(lre=inp[:, 0:16], lim=inp[:, 16:32], ldt=inp[:, 32:48], R=[inp]))
                lbr, lbi, t1, t2 = [mc[n] for n in ("lbr", "lbi", "t1", "t2")]
                B.copy("dve", pw[:, 0, :, 0], lbr[:], [lbr], [pw])
                B.copy("dve", pw[:, 1, :, 0], lbi[:], [lbi], [pw])
                for i in range(1, TS):
                    B.cmul("dve", (pw[:, 0, :, i], pw[:, 1, :, i]), (pw[:, 0, :, i - 1], pw[:, 1, :, i - 1]),
                           (lbr[:], lbi[:]), t1[:], t2[:], [pw, lbr, lbi], [pw], [t1, t2])
                PL = self.PL
                B.copy("dve", PL[:, l, :, 0], pw[:, 0, :, TS - 1], [pw], [PL])
                B.copy("dve", PL[:, l, :, 1], pw[:, 1, :, TS - 1], [pw], [PL])
                for k in range(1, 7):
                    pr, pi = PL[:, l, :, 3 * k - 3], PL[:, l, :, 3 * k - 2]
                    B.tt("dve", t1[:], pr, pr, ALU.mult, [PL], [t1])
                    B.tt("dve", t2[:], pi, pi, ALU.mult, [PL], [t2])
                    B.tt("dve", PL[:, l, :, 3 * k], t1[:], t2[:], ALU.subtract, [t1, t2], [PL])
                    B.stt(PL[:, l, :, 3 * k + 1], pr, 2.0, pi, ALU.mult, ALU.mult, [PL], [PL])
                for k in range(7):
                    B.ts("dve", PL[:, l, :, 3 * k + 2], PL[:, l, :, 3 * k + 1], -1.0, ALU.mult, [PL], [PL])
                cre = inp[:, 48:304].rearrange("p (t c) -> p t c", c=16)
                cim = inp[:, 304:560].rearrange("p (t c) -> p t c", c=16)

                def bc_c(ap3):
                    return mkap(ap3.tensor, ap3.offset, [ap3.ap[0], ap3.ap[1], [0, TS], ap3.ap[2]])

                def bc_p(k):
                    a = pw[:, k, :, :]
                    return mkap(a.tensor, a.offset, [a.ap[0], a.ap[1], a.ap[2], [0, 16]])
                B.tt("dve", u1[:], bc_c(cre), bc_p(0), ALU.mult, [inp, pw], [u1])
                B.tt("dve", u2[:], bc_c(cim), bc_p(1), ALU.mult, [inp, pw], [u2])
                B.tt("dve", u3[:], u1[:], u2[:], ALU.subtract, [u1, u2], [u3])
                B.copy("act", ct[0:64, :, :, 0, 0:16], u3[0:64], [u3], [ct])
                B.copy("act", ct[64:128, :, :, 0, 16:32], u3[64:128], [u3], [ct])
                B.tt("dve", u1[:], bc_c(cre), bc_p(1), ALU.mult, [inp, pw], [u1])
                B.tt("dve", u2[:], bc_c(cim), bc_p(0), ALU.mult, [inp, pw], [u2])
                B.stt(u3[:], u1[:], -1.0, u2[:], ALU.mult, ALU.subtract, [u1, u2], [u3])
                B.copy("act", ct[0:64, :, :, 1, 0:16], u3[0:64], [u3], [ct])
                B.copy("act", ct[64:128, :, :, 1, 16:32], u3[64:128], [u3], [ct])
                toks.append(B.dma("sp", io["tabC"][l], ct[:].rearrange("p t i r c -> p (t i r c)"),
                                  [ct], [self.tabC_r[l]], owner=ct))
        self.S.barrier(toks)

    def build(self):
        B = self
        nc = self.nc
        es = self.es
        self.declare_io()
        io = self.io
        self.tabB_r = [Res("tabB%d" % l) for l in range(DEPTH)]
        self.tabC_r = [Res("tabC%d" % l) for l in range(DEPTH)]
        self.halfpi = B.sb(es, [128, 1], F32, "halfpi")
        B.memset("dve", self.halfpi[:], math.pi / 2.0, [self.halfpi])
        self.PL = B.sb(es, [128, DEPTH, 16, 24], F32, "PL")
        self.prologue()
        if self.debug.get("tables"):
            self.finish([])
            return
        self.main()

    def finish(self, out_toks):
        self.S.wait_all("sp", out_toks)
        self.S.barrier(out_toks)
        self.S.run()


class V:
    def __init__(self, ap, name=""):
        self.ap = ap
        self.r = Res(name)
        self.lo = self.hi = None

    def __getitem__(self, idx):
        return self.ap[idx]


def _res_of(lst):
    return [x.r if hasattr(x, "r") and not isinstance(x, Res) else x for x in lst]


Builder._res = staticmethod(_res_of)


def dil_ap(base, r, col0, ncols):
    t, off, pdim, st = base.tensor, base.offset, list(base.ap[0]), base.ap[-1][0]
    assert len(base.ap) == 2
    nsub = L // r
    c0, i0 = divmod(col0, nsub)
    if r == 1:
        dims, o = [[st, ncols]], col0
    elif i0 + ncols <= nsub:
        dims, o = [[r * st, ncols]], c0 + r * i0
    else:
        assert i0 == 0 and ncols % nsub == 0
        dims, o = [[st, ncols // nsub], [r * st, nsub]], c0
    return mkap(t, off + o * st, [pdim] + dims)


def deint_dst(base, r, n):
    if r == 1:
        return base[:, 512 * n:512 * n + 512]
    nsub = L // r
    t, off, pdim, st = base.tensor, base.offset, list(base.ap[0]), base.ap[-1][0]
    return mkap(t, off + (512 * n // r) * st, [pdim, [st, 512 // r], [nsub * st, r]])


def like(bank_ap, ap):
    if len(ap.ap) == 3:
        return bank_ap.rearrange("p (a b) -> p a b", a=ap.ap[1][1])
    return bank_ap


AR_BYTES = 88064


def _main(self):
    B = self
    es = self.es
    io = self.io
    S = self.S
    nc = self.nc
    dbg = self.debug
    xT = B.sb(es, [128, NK, L], F32, "xT")
    xT_r = [[Res("xT%d_%d" % (k, n)) for n in range(4)] for k in range(NK)]
    hT = B.sb(es, [128, NK, L], BF16, "hT")
    hT_r = [Res("hT%d" % k) for k in range(NK)]
    vecs = B.sb(es, [128, DEPTH * 24], F32, "vecs")
    amask = B.sb(es, [128, 256], BF16, "amask")
    ones = B.sb(es, [128, 128], BF16, "ones")
    epsT = B.sb(es, [128, 1], F32, "eps")
    wbuf = [B.sb(es, [128, NK, 512], BF16, "wbuf%d" % i) for i in range(2)]
    arena = B.sb(es, [128, AR_BYTES // 2], BF16, "arena")
    banks = [B.ps(es, [128, 512], F32, "bank%d" % i) for i in range(8)]
    self.wi = 0

    def carve(off, shape, dtype, name):
        nel = int(np.prod(shape[1:]))
        esz = 2 if dtype == BF16 else 4
        assert off % 4 == 0 and off + nel * esz <= AR_BYTES, (name, off, nel * esz)
        a = arena.t[:, off // 2: off // 2 + nel * esz // 2]
        if dtype == F32:
            a = a.bitcast(F32)
        if len(shape) == 3:
            a = a.rearrange("p (a b) -> p a b", a=shape[1])
        elif len(shape) == 4:
            a = a.rearrange("p (a b c) -> p a b c", a=shape[1], b=shape[2])
        elif len(shape) == 5:
            a = a.rearrange("p (a b c d) -> p a b c d", a=shape[1], b=shape[2], c=shape[3])
        v = V(a, name)
        region(off, off + nel * esz, [v.r])
        v.lo, v.hi = off, off + nel * esz
        return v

    regs = []

    def region(lo, hi, res_list):
        inherit = []
        keep = []
        for (l0, h0, rl) in regs:
            if l0 < hi and lo < h0:
                for r_ in rl:
                    inherit += r_.w + r_.r
                if lo <= l0 and h0 <= hi:
                    continue
            keep.append((l0, h0, rl))
        regs[:] = keep
        best = {}
        for (sm, vl, src) in inherit:
            if id(sm) not in best or best[id(sm)][1] < vl:
                best[id(sm)] = (sm, vl, "alias")
        for r_ in res_list:
            r_.r = list(best.values())
        regs.append((lo, hi, res_list))

    def reslist(v, names):
        rl = [Res(n) for n in names]
        region(v.lo, v.hi, rl)
        return rl

    B.memset("dve", ones[:], 1.0, [ones])
    B.memset("dve", epsT[:], 1e-6, [epsT])
    B.dma("sp", vecs[:], io["vecs"], [], [vecs])
    B.dma("sp", amask[:], io["amask"], [], [amask])
    xin = io["xT"].rearrange("(k p) t -> p k t", p=128)
    for k in range(NK):
        B.dma("sp", xT[:, k, :], xin[:, k, :], [], xT_r[k], owner=xT_r[k][0])

    out_toks = []

    def dump(name, ap, R):
        out_toks.append(B.dma("sp", io[name], ap, R, [], owner=Res("dump_" + name)))

    def load_wblock(l, col0, ncols):
        wb = wbuf[self.wi % 2]
        self.wi += 1
        src = io["w_in"][l][:, col0:col0 + ncols].rearrange("(k p) c -> p k c", p=128)
        B.dma("pool", wb[:, :, 0:ncols], src, [], [wb])
        return wb

    def load_w(dst, src2d):
        B.dma("pool", dst[:], src2d.rearrange("(k p) c -> p k c", p=128), [], [dst])

    bank_ctr = [0]

    def nb(lst):
        b = banks[lst[bank_ctr[0] % len(lst)]]
        bank_ctr[0] += 1
        return b

    def hk(k):
        return hT.t[:, k, :]

    for l in range(self.n_layers):
        vb = l * 24
        sq = [carve(i * 8192, [128, NK, 512], BF16, "sq%d" % i) for i in range(2)]
        rt = [carve(16384 + i * 2048, [128, 512], F32, "rt%d" % i) for i in range(2)]
        rs = [carve(20480 + i * 2048, [128, 512], F32, "rs%d" % i) for i in range(2)]
        for n in range(4):
            ns = slice(n * 512, (n + 1) * 512)
            s_, rt_, rs_ = sq[n % 2], rt[n % 2], rs[n % 2]
            for k in range(NK):
                B.act(s_[:, k, :], xT[:, k, ns], AF.Square, [xT_r[k][n]], [s_])
            bk = nb([6, 7])
            for k in range(NK):
                B.mm(bk[:], ones[:], s_[:, k, :], k == 0, k == NK - 1, [ones, s_], [bk])
            B.act(rt_[:], bk[:], AF.Sqrt, [bk, epsT], [rt_], scale=1.0 / D, bias=epsT[:, 0:1])
            S.op("dve", lambda e, a=rs_, b=rt_: e.reciprocal(out=a[:], in_=b[:]), [rt_.r], [rs_.r])
            for k in range(NK):
                B.stt(hT[:, k, ns], xT[:, k, ns], vecs[:, vb + k:vb + k + 1], rs_[:], ALU.mult, ALU.mult,
                      [xT_r[k][n], vecs, rs_], [hT_r[k]])
        if dbg.get("stop") == "P0":
            dump("dbg_h", hT[:], hT_r)
            break
        pass
        yT = carve(32768, [128, 4, L], BF16, "yT")
        _fl = reslist(yT, ["yT%d_%d" % (k, n) for k in range(4) for n in range(4)])
        yT_r = [[_fl[4 * k + n] for n in range(4)] for k in range(4)]
        r32 = carve(0, [128, 2, TS, 128], F32, "r32")
        rbf = carve(16384, [128, 2, TS, 128], BF16, "rbf")
        Bt = carve(24576, [128, TS, 2, 128], BF16, "Bt")
        Ct = carve(49152, [128, 5120], BF16, "Ct")
        CtA = Ct[:, 0:3072].rearrange("p (t i r c) -> p t i r c", t=3, i=TS, r=2)
        CtB = Ct[:, 3072:5120].rearrange("p (i r c) -> p i r c", i=TS, r=2)
        uT = carve(59392, [128, L], BF16, "uT")
        wglu = carve(63488, [128, 4, 512], BF16, "wglu")
        Sab = [carve(67584 + i * 1536, [128, 2, 192], F32, "S%d" % i) for i in range(2)]
        zt = carve(70656, [128, 2, 192], F32, "zt")
        tq = carve(72192, [128, 1, 128], F32, "tq")
        tq2 = carve(72704, [128, 1, 128], F32, "tq2")
        Sbf = carve(73216, [128, 2, 128], BF16, "Sbf")
        gv = [carve(73728 + i * 2048, [128, 512], F32, "gv%d" % i) for i in range(2)]
        gw = [carve(77824 + i * 2048, [128, 512], F32, "gw%d" % i) for i in range(2)]
        gs = [carve(81920 + i * 2048, [128, 512], F32, "gs%d" % i) for i in range(2)]
        gz = [carve(86016 + i * 1024, [128, 512], BF16, "gz%d" % i) for i in range(2)]
        for t_ in (Sab[0], Sab[1], zt):
            B.memset("dve", t_[:, :, 0:64], 0.0, [t_])
        Ctz = V(CtB[:, :, :, 0:32], "Ctz")
        region(Ct.lo, Ct.hi, [Ctz.r])
        B.memset("pool", Ctz[:], 0.0, [Ctz])
        B.memset("dve", Sbf[:], 0.0, [Sbf])
        wbU = load_wblock(l, 0, 512)
        wbZ = load_wblock(l, 512, 512)
        load_w(wglu, io["w_glu"][l])
        PLl = self.PL
        for j in dbg.get("js", range(4)):
            for n in range(4):
                bk = nb([6, 7])
                for k in range(NK):
                    B.mm(bk[:], wbU[:, k, 128 * j:128 * j + 128], hT[:, k, 512 * n:512 * n + 512],
                         k == 0, k == NK - 1, [wbU, hT_r[k]], [bk])
                B.copy("act", deint_dst(uT[:], 16, n), bk[:].rearrange("p (a b) -> p a b", b=16), [bk], [uT])
            B.dma("sp", Bt[:].rearrange("p i r q -> p (i r q)"), io["tabB"][l][:, 4096 * j:4096 * (j + 1)],
                  [self.tabB_r[l]], [Bt])
            B.dma("sp", Ct[:, 0:3072], io["tabC"][l][:, 4096 * j:4096 * j + 3072], [self.tabC_r[l]], [Ct])
            B.dma("sp", CtB[:, :, :, 32:64],
                  io["tabC"][l][:, 4096 * j + 3072:4096 * (j + 1)].rearrange("p (i r c) -> p i r c", i=TS, r=2),
                  [self.tabC_r[l]], [Ct])
            def xt_chain(ptl):
                rows = slice(32 * ptl, 32 * ptl + 32)
                for ih in range(8):
                    xb = nb([4, 5])
                    xv = xb[:].rearrange("p (r i c) -> p r i c", r=2, i=2)
                    for ri in range(2):
                        for il in range(2):
                            i = 2 * ih + il
                            B.mm(xv[:, ri, il, :], Bt[rows, i, ri, :], uT[rows, i * 128:(i + 1) * 128],
                                 True, True, [Bt, uT], [xb], tp=(32 * ptl, 0))
                    for il in range(2):
                        i = 2 * ih + il
                        if i == 0:
                            B.copy("dve", r32[:, :, 0, :], xv[:, :, 0, :], [xb], [r32])
                        else:
                            B.tt("dve", r32[:, :, i, :], r32[:, :, i - 1, :], xv[:, :, il, :], ALU.add,
                                 [xb, r32], [r32])
                        yield

            def l2(ptl):
                pt = 4 * j + ptl
                B.copy("act", rbf[:], r32[:], [r32], [rbf])
                pl = lambda c: PLl[:, l, pt, c:c + 1]
                rr, rim = r32[:, 0, TS - 1, :], r32[:, 1, TS - 1, :]
                B.ts("dve", tq[:, 0, :], rim, pl(1), ALU.mult, [r32, PLl], [tq])
                B.stt(zt[:, 0, 64:192], rr, pl(0), tq[:, 0, :], ALU.mult, ALU.subtract, [r32, PLl, tq], [zt])
                B.ts("dve", tq2[:, 0, :], rr, pl(1), ALU.mult, [r32, PLl], [tq2])
                B.stt(zt[:, 1, 64:192], rim, pl(0), tq2[:, 0, :], ALU.mult, ALU.add, [r32, PLl, tq2], [zt])
                yield "z"
                old = zt
                for kk in range(7):
                    d = 1 << kk
                    new = Sab[kk % 2]
                    a_, b_, nb_ = pl(3 * kk), pl(3 * kk + 1), pl(3 * kk + 2)
                    B.stt(new[:, :, 64:192], old[:, :, 64 - d:192 - d], a_, old[:, :, 64:192], ALU.mult, ALU.add, [old, PLl], [new])
                    yield
                    B.stt(new[:, 0, 64:192], old[:, 1, 64 - d:192 - d], nb_, new[:, 0, 64:192], ALU.mult, ALU.add, [old, new, PLl], [new])
                    yield
                    B.stt(new[:, 1, 64:192], old[:, 0, 64 - d:192 - d], b_, new[:, 1, 64:192], ALU.mult, ALU.add, [old, new, PLl], [new])
                    yield
                    old = new
                B.copy("dve", Sbf[:, :, 1:128], old[:, :, 64:191], [old], [Sbf])
                yield

            def interleave(g1, g2):
                for _ in g1:
                    if g2 is not None:
                        if next(g2, "end") == "end":
                            g2 = None
                if g2 is not None:
                    for _ in g2:
                        pass

            def readout(ptl):
                rows = slice(32 * ptl, 32 * ptl + 32)
                for i in range(TS):
                    yb = banks[i // 4]
                    cs_ = slice((i % 4) * 128, (i % 4) * 128 + 128)
                    if ptl == 3:
                        o_ = yb[64:128, cs_]
                        tp = (0, 64)
                        lh = [CtB[:, i, rr_, :] for rr_ in range(2)]
                        RC_ = [Ct, Ctz]
                    else:
                        o_ = yb[rows, cs_]
                        tp = (0, 32 * ptl)
                        lh = [CtA[:, ptl, i, rr_, :] for rr_ in range(2)]
                        RC_ = [Ct]
                    B.mm(o_, lh[0], rbf[:, 0, i, :], True, False, RC_ + [rbf], [yb], tp=tp)
                    B.mm(o_, lh[1], rbf[:, 1, i, :], False, False, RC_ + [rbf], [yb], tp=tp)
                    B.mm(o_, lh[0], Sbf[:, 0, :], False, False, RC_ + [Sbf], [yb], tp=tp)
                    B.mm(o_, lh[1], Sbf[:, 1, :], False, True, RC_ + [Sbf], [yb], tp=tp)

            order = (0, 1, 3, 2)
            for _ in xt_chain(order[0]):
                pass
            for idx, ptl in enumerate(order):
                g1 = l2(ptl)
                for step in g1:
                    if step == "z":
                        break
                g2 = xt_chain(order[idx + 1]) if idx + 1 < 4 else None
                interleave(g1, g2)
                readout(ptl)
            for ib in range(4):
                cs = slice(512 * ib, 512 * ib + 512)
                v_, w_, g_ = gv[ib % 2], gw[ib % 2], gs[ib % 2]
                B.stt(v_[:], uT[:, cs], vecs[:, vb + 16 + j:vb + 17 + j], banks[ib][:], ALU.mult, ALU.add,
                      [uT, vecs, banks[ib]], [v_])
                B.act(w_[:], v_[:], AF.Square, [v_], [w_])
                B.ts("pool", w_[:], w_[:], 0.044715, ALU.mult, [w_], [w_], s2=1.0, op1=ALU.add)
                B.tt("pool", w_[:], w_[:], v_[:], ALU.mult, [w_, v_], [w_])
                B.act(g_[:], w_[:], AF.Sigmoid, [w_], [g_], scale=1.5957691216057308)
                B.tt("dve", yT[:, j, cs], v_[:], g_[:], ALU.mult, [v_, g_], [yT_r[j][ib]])
        if dbg.get("stop") == "PA1":
            dump("dbg_y", yT[:], [x for row in yT_r for x in row])
            break
        szf = [carve(4096 * mo, [128, L], BF16, "szf%d" % mo) for mo in range(4)]
        for mo in range(4):
            for n in range(4):
                zb = nb([4, 5])
                for k in range(NK):
                    B.mm(zb[:], wbZ[:, k, 128 * mo:128 * mo + 128], hT[:, k, 512 * n:512 * n + 512],
                         k == 0, k == NK - 1, [wbZ, hT_r[k]], [zb])
                B.act(deint_dst(szf[mo][:], 16, n), zb[:].rearrange("p (a b) -> p a b", b=16), AF.Silu, [zb], [szf[mo]])
        for n in range(4):
            cs = slice(512 * n, 512 * n + 512)
            for mo in range(4):
                gb = banks[mo]
                for k in range(4):
                    B.mm(gb[:], wglu[:, k, 128 * mo:128 * mo + 128], yT[:, k, cs], k == 0, k == 3,
                         [wglu, yT_r[k][n]], [gb])
            for mo in range(4):
                g_ = gs[mo % 2]
                B.act(g_[:], banks[mo][:], AF.Sigmoid, [banks[mo], vecs], [g_], bias=vecs[:, vb + 20 + mo:vb + 21 + mo])
                B.tt("dve", g_[:], g_[:], szf[mo][:, cs], ALU.mult, [g_, szf[mo]], [g_])
                B.tt("dve", yT[:, mo, cs], yT[:, mo, cs], g_[:], ALU.mult, [yT_r[mo][n], g_], [yT_r[mo][n]])
        if dbg.get("stop") == "PA":
            dump("dbg_y", yT[:], [x for row in yT_r for x in row])
            break
        pass
        merged = carve(0, [128, NK, L], BF16, "merged")
        merged_r = reslist(merged, ["mg%d" % m for m in range(NK)])
        wbs = carve(49152, [128, 4, D], BF16, "wbs")
        sgg = [carve(57344 + i * 4096, [128, L], BF16, "sgg%d" % i) for i in range(2)]
        load_w(wbs, io["w_bs"][l])
        for m in range(NK):
            if m % 4 == 0:
                wb = load_wblock(l, 6144 + 512 * (m // 4), 512)
            sg_ = sgg[m % 2]
            for n in range(4):
                bk = nb([6, 7])
                for k in range(NK):
                    B.mm(bk[:], wb[:, k, 128 * (m % 4):128 * (m % 4) + 128], hT[:, k, 512 * n:512 * n + 512],
                         k == 0, k == NK - 1, [wb, hT_r[k]], [bk])
                B.act(sg_[:, 512 * n:512 * n + 512], bk[:], AF.Sigmoid, [bk], [sg_])
            for n2 in range(4):
                bk = nb([0, 1, 2, 3])
                for k in range(4):
                    B.mm(bk[:], wbs[:, k, 128 * m:128 * m + 128], yT[:, k, 512 * n2:512 * n2 + 512], k == 0, k == 3,
                         [wbs, yT_r[k][n2]], [bk])
                dst = dil_ap(merged[:, m, :], 16, 512 * n2, 512)
                B.tt("dve", dst, like(bk[:], dst), dil_ap(sg_[:], 16, 512 * n2, 512), ALU.mult,
                     [bk, sg_], [merged_r[m]])
        if dbg.get("stop") == "PA2":
            dump("dbg_m", merged[:], merged_r)
            break
        pass
        yaT = carve(32768, [128, 4, L], BF16, "yaT")
        yaT_r = reslist(yaT, ["ya%d" % j for j in range(4)])
        qT = carve(49152, [128, L], BF16, "qT")
        kT = carve(53248, [128, L], BF16, "kT")
        vd = carve(57344, [128, 16, 128], BF16, "vd")
        etm = [carve(61440 + i * 512, [128, 256], BF16, "etm%d" % i) for i in range(6)]
        ett = [carve(64512 + i * 512, [128, 256], BF16, "ett%d" % i) for i in range(4)]
        accn = carve(66560, [128, L], F32, "accn")
        accd = carve(74752, [128, L], F32, "accd")
        sza = carve(82944, [128, L], BF16, "sza")
        for j in range(4):
            for gi, r in enumerate((1, 4, 16)):
                ncols = 512 if gi == 2 else 384
                wb = load_wblock(l, 1024 + 1280 * j + 384 * gi, ncols)
                nsub = L // r
                nblk = nsub // 128
                for (c_in, dst) in ((0, qT), (128, kT)):
                    for n in range(4):
                        bk = nb([6, 7])
                        for k in range(NK):
                            B.mm(bk[:], wb[:, k, c_in:c_in + 128], hT[:, k, 512 * n:512 * n + 512], k == 0, k == NK - 1,
                                 [wb, hT_r[k]], [bk])
                        src_ = bk[:] if r == 1 else bk[:].rearrange("p (a b) -> p a b", b=r)
                        B.copy("act", deint_dst(dst[:], r, n), src_, [bk], [dst])
                for Bq in range(4):
                    bk = nb([6, 7])
                    for bl in range(4):
                        Bk = 4 * Bq + bl
                        for k in range(NK):
                            B.mm(bk[:, bl * 128:bl * 128 + 128], dil_ap(hk(k), r, 128 * Bk, 128), wb[:, k, 256:384],
                                 k == 0, k == NK - 1, [wb, hT_r[k]], [bk])
                    B.copy("act", vd[:, 4 * Bq:4 * Bq + 4, :], bk[:].rearrange("p (a b) -> p a b", a=4), [bk], [vd])
                def score(Bk):
                    c_, blk = divmod(Bk, nblk)
                    nq = 256 if blk < nblk - 1 else 128
                    sb_ = banks[4 + (Bk // 2) % 2]
                    reg = sb_[:, 256 * (Bk % 2):256 * (Bk % 2) + nq]
                    B.mm(reg, kT[:, 128 * Bk:128 * Bk + 128], qT[:, 128 * Bk:128 * Bk + nq], True, True, [kT, qT], [sb_])
                    et_, em_ = ett[Bk % 4], etm[Bk % 6]
                    B.act(et_[:, :nq], reg, AF.Exp, [sb_], [et_], scale=1.0 / math.sqrt(128.0))
                    B.tt("dve", em_[:, :nq], et_[:, :nq], amask[:, :nq], ALU.mult, [et_, amask], [em_])

                def pv(Bk):
                    c_, blk = divmod(Bk, nblk)
                    em_ = etm[Bk % 6]
                    pvb, dnb = banks[(Bk // 4) % 2], banks[2 + (Bk // 4) % 2]
                    cs = slice((Bk % 4) * 128, (Bk % 4) * 128 + 128)
                    for (bank_, isden) in ((pvb, False), (dnb, True)):
                        l0 = ones[:] if isden else vd[:, Bk, :]
                        R0 = [ones] if isden else [vd]
                        B.mm(bank_[:, cs], l0, em_[:, 0:128], True, blk == 0, R0 + [em_], [bank_])
                        if blk > 0:
                            emp = etm[(Bk - 1) % 6]
                            l1 = ones[:] if isden else vd[:, Bk - 1, :]
                            B.mm(bank_[:, cs], l1, emp[:, 128:256], False, True, R0 + [emp], [bank_])
                    if Bk % 4 == 3:
                        for (bank_, acc) in ((pvb, accn), (dnb, accd)):
                            dst = dil_ap(acc[:], r, 512 * (Bk // 4), 512)
                            if gi == 0:
                                B.copy("act" if acc is accd else "dve", dst, like(bank_[:], dst), [bank_], [acc])
                            else:
                                B.tt("dve", dst, dst, like(bank_[:], dst), ALU.add, [bank_, acc], [acc])

                LOOK = 3
                for Bk in range(16 + LOOK):
                    if Bk < 16:
                        score(Bk)
                    if Bk >= LOOK:
                        pv(Bk - LOOK)
            for n in range(4):
                bk = nb([6, 7])
                for k in range(NK):
                    B.mm(bk[:], wb[:, k, 384:512], hT[:, k, 512 * n:512 * n + 512], k == 0, k == NK - 1,
                         [wb, hT_r[k]], [bk])
                B.act(sza[:, 512 * n:512 * n + 512], bk[:], AF.Silu, [bk], [sza])
            S.op("dve", lambda e, a=accd: e.reciprocal(out=a[:], in_=a[:]), [accd.r], [accd.r])
            B.tt("dve", accn[:], accn[:], accd[:], ALU.mult, [accn, accd], [accn])
            B.tt("dve", yaT[:, j, :], accn[:], sza[:], ALU.mult, [accn, sza], [yaT_r[j]])
        if dbg.get("stop") == "PB":
            dump("dbg_ya", yaT[:], yaT_r)
            break
        pass
        wba = carve(49152, [128, 4, D], BF16, "wba")
        sgg = [carve(57344 + i * 4096, [128, L], BF16, "sgga%d" % i) for i in range(2)]
        tmm = [carve(65536 + i * 2048, [128, 512], F32, "tmm%d" % i) for i in range(2)]
        load_w(wba, io["w_ba"][l])
        for m in range(NK):
            if m % 4 == 0:
                wb = load_wblock(l, 7168 + 512 * (m // 4), 512)
            sg_ = sgg[m % 2]
            for n in range(4):
                bk = nb([6, 7])
                for k in range(NK):
                    B.mm(bk[:], wb[:, k, 128 * (m % 4):128 * (m % 4) + 128], hT[:, k, 512 * n:512 * n + 512],
                         k == 0, k == NK - 1, [wb, hT_r[k]], [bk])
                B.act(sg_[:, 512 * n:512 * n + 512], bk[:], AF.Sigmoid, [bk], [sg_])
            for n in range(4):
                ns = slice(512 * n, 512 * n + 512)
                bk = nb([0, 1, 2, 3])
                for k in range(4):
                    B.mm(bk[:], wba[:, k, 128 * m:128 * m + 128], yaT[:, k, ns], k == 0, k == 3,
                         [wba, yaT_r[k]], [bk])
                t_ = tmm[n % 2]
                B.tt("dve", t_[:], bk[:], sg_[:, ns], ALU.mult, [bk, sg_], [t_])
                B.tt("dve", merged[:, m, ns], merged[:, m, ns], t_[:], ALU.add, [t_, merged_r[m]], [merged_r[m]])
        if dbg.get("stop") == "PC2":
            dump("dbg_m", merged[:], merged_r)
            break
        pass
        wout = carve(32768, [128, NK, D], BF16, "wout")
        ot = carve(49152, [128, NK, 512], F32, "ot")
        sqo = carve(65536, [128, NK, 512], BF16, "sqo")
        rt2 = carve(73728, [128, 512], F32, "rt2")
        rs2 = carve(75776, [128, 512], F32, "rs2")
        tm2 = [carve(77824 + i * 2048, [128, 512], F32, "tm2%d" % i) for i in range(2)]
        load_w(wout, io["w_out"][l])
        for n in range(4):
            ns = slice(512 * n, 512 * n + 512)
            for mo in range(NK):
                bk = nb([0, 1, 2, 3, 4, 5])
                for k in range(NK):
                    B.mm(bk[:], wout[:, k, 128 * mo:128 * mo + 128], merged[:, k, ns], k == 0, k == NK - 1,
                         [wout, merged_r[k]], [bk])
                B.copy("act", ot[:, mo, :], bk[:], [bk], [ot])
                B.act(sqo[:, mo, :], bk[:], AF.Square, [bk], [sqo])
            sb_ = nb([6, 7])
            for mo in range(NK):
                B.mm(sb_[:], ones[:], sqo[:, mo, :], mo == 0, mo == NK - 1, [ones, sqo], [sb_])
            B.act(rt2[:], sb_[:], AF.Sqrt, [sb_, epsT], [rt2], scale=1.0 / D, bias=epsT[:, 0:1])
            S.op("dve", lambda e, a=rs2, b=rt2: e.reciprocal(out=a[:], in_=b[:]), [rt2.r], [rs2.r])
            for mo in range(NK):
                t_ = tm2[mo % 2]
                B.stt(t_[:], ot[:, mo, :], vecs[:, vb + 8 + mo:vb + 9 + mo], rs2[:], ALU.mult, ALU.mult,
                      [ot, vecs, rs2], [t_])
                B.tt("pool", xT[:, mo, ns], xT[:, mo, ns], t_[:], ALU.add, [t_, xT_r[mo][n]], [xT_r[mo][n]])
        pass
    else:
        yout = io["yT"].rearrange("(k p) t -> p k t", p=128)
        for k in range(NK):
            out_toks.append(B.dma("sp", yout[:, k, :], xT[:, k, :], xT_r[k], [], owner=Res("out%d" % k)))
    self.finish(out_toks)


Builder.main = _main


_CACHE = {}


def kernel(**inputs):
    h = prep_host(inputs)
    x = np.asarray(inputs["x"], np.float32)
    if "nc" not in _CACHE:
        b = Builder()
        b.build()
        _CACHE["nc"] = b.nc
    nc = _CACHE["nc"]
    shared = {k: h[k] for k in ("w_in", "w_glu", "w_bs", "w_ba", "w_out", "vecs", "ssm_rb", "ssm_rc", "amask")}
    in_maps = []
    for b_ in range(8):
        m = dict(shared)
        m["xT"] = np.ascontiguousarray(x[b_].T)
        in_maps.append(m)
    res = run_bass_kernel_spmd(nc, in_maps, core_ids=list(range(8)))
    out = np.stack([np.ascontiguousarray(res.results[b_]["yT"].T) for b_ in range(8)], axis=0)
    return out.astype(np.float32)
```

```python
import math
from contextlib import ExitStack

import numpy as np
import ml_dtypes

import concourse.bass as bass
import concourse.mybir as mybir
from concourse.bass_utils import run_bass_kernel_spmd

F32 = mybir.dt.float32
BF16 = mybir.dt.bfloat16
ALU = mybir.AluOpType
AF = mybir.ActivationFunctionType

L = 2048
D = 1024
NK = 8
DEPTH = 4
TS = 16
NCH = L // TS
ENGS = ("pe", "act", "dve", "pool", "sp")
STRICT = False


class Res:
    __slots__ = ("w", "r", "name", "dsem", "dcnt")

    def __init__(self, name=""):
        self.w = []
        self.r = []
        self.name = name
        self.dsem = None
        self.dcnt = 0


class Sched:
    def __init__(self, nc, es):
        self.nc = nc
        self.es = es
        self.prog = {e: [] for e in ENGS}
        self.cnt = {e: 0 for e in ENGS}
        self.sem = {e: es.enter_context(nc.semaphore("sem_" + e)) for e in ENGS if e != "sp"}
        self.known = {e: {} for e in ENGS}
        self.nsem = 0
        self.final = []

    def _waits(self, eng, reads, writes):
        toks = []
        for t in reads:
            toks += t.w
        for t in writes:
            toks += [x for x in t.w if STRICT or x[2] != eng]
            toks += [x for x in t.r if STRICT or x[2] != eng]
        need = {}
        for (sem, val, src) in toks:
            if eng == "pe" and src == "pe":
                continue
            if self.known[eng].get(id(sem), 0) >= val:
                continue
            if need.get(id(sem), (None, 0))[1] < val:
                need[id(sem)] = (sem, val)
        out = []
        for k, (sem, val) in need.items():
            self.known[eng][k] = val
            out.append((sem, val))
        return out

    def _commit(self, tok, reads, writes):
        for t in writes:
            t.w = [tok]
            t.r = []
        for t in reads:
            t.r.append(tok)
            if len(t.r) > 24:
                best = {}
                for (s, v, src) in t.r:
                    if id(s) not in best or best[id(s)][1] < v:
                        best[id(s)] = (s, v, src)
                t.r = list(best.values())

    def op(self, eng, fn, reads=(), writes=()):
        waits = self._waits(eng, reads, writes)
        self.cnt[eng] += 1
        sem = self.sem[eng]
        val = self.cnt[eng]

        def emit(e, waits=waits, fn=fn, sem=sem):
            for (s, v) in waits:
                e.wait_ge(s, v)
            fn(e).then_inc(sem, 1)

        self.prog[eng].append(emit)
        self._commit((sem, val, eng), reads, writes)

    def dma(self, q, out, in_, reads=(), writes=(), owner=None):
        waits = self._waits(q, reads, writes)
        own = owner if owner is not None else (writes[0] if writes else reads[0])
        if own.dsem is None:
            own.dsem = self.es.enter_context(self.nc.semaphore("dsem%d" % self.nsem))
            self.nsem += 1
        own.dcnt += 16
        sem, val = own.dsem, own.dcnt

        def emit(e, waits=waits, sem=sem, out=out, in_=in_):
            for (s, v) in waits:
                e.wait_ge(s, v)
            e.dma_start(out=out, in_=in_).then_inc(sem, 16)

        self.prog[q].append(emit)
        tok = (sem, val, "dma")
        for t in writes:
            t.w = [tok]
            t.r = []
        for t in reads:
            t.r.append(tok)
        return tok

    def wait_all(self, q, toks):
        def emit(e, toks=toks):
            for (s, v, _) in toks:
                e.wait_ge(s, v)
        self.prog[q].append(emit)

    def barrier(self, extra=()):
        for e in ENGS:
            waits = [(s, v) for (s, v, _) in extra]
            for o in ENGS:
                if o == "sp" or o == e or self.cnt[o] == 0:
                    continue
                if self.known[e].get(id(self.sem[o]), 0) < self.cnt[o]:
                    self.known[e][id(self.sem[o])] = self.cnt[o]
                    waits.append((self.sem[o], self.cnt[o]))

            def emit(en, waits=waits):
                for (s, v) in waits:
                    en.wait_ge(s, v)
            self.prog[e].append(emit)

    def run(self):
        nc = self.nc
        with nc.Block() as block:
            @block.tensor
            def _(e):
                for f in self.prog["pe"]:
                    f(e)

            @block.scalar
            def _(e):
                for f in self.prog["act"]:
                    f(e)

            @block.vector
            def _(e):
                for f in self.prog["dve"]:
                    f(e)

            @block.gpsimd
            def _(e):
                for f in self.prog["pool"]:
                    f(e)

            @block.sync
            def _(e):
                for f in self.prog["sp"]:
                    f(e)


def mkap(t, offset, dims):
    return bass.AP(tensor=t, offset=offset, ap=[list(d) for d in dims])


def _in_col_perm():
    cols = list(range(0, 1024))
    for j in range(4):
        for gi in range(3):
            for base in (1024, 2560, 4096):
                c0 = base + (gi * 4 + j) * 128
                cols += list(range(c0, c0 + 128))
        cols += list(range(5632 + j * 128, 5632 + (j + 1) * 128))
    cols += list(range(6144, 8192))
    assert len(cols) == 8192 and len(set(cols)) == 8192
    return np.asarray(cols)


def prep_host(inp):
    f32 = np.float32
    out = {}
    out["w_in"] = np.ascontiguousarray(np.asarray(inp["w_in"], f32)[:, :, _in_col_perm()])
    out["w_glu"] = np.ascontiguousarray(np.asarray(inp["w_glu"], f32))
    out["w_bs"] = np.ascontiguousarray(np.asarray(inp["w_branch_s"], f32))
    out["w_ba"] = np.ascontiguousarray(np.asarray(inp["w_branch_a"], f32))
    out["w_out"] = np.ascontiguousarray(np.asarray(inp["w_out"], f32))
    vecs = np.zeros((128, DEPTH * 24), f32)
    for l in range(DEPTH):
        vecs[:, l * 24 + 0:l * 24 + 8] = np.asarray(inp["pre_norm_g"], f32)[l].reshape(8, 128).T
        vecs[:, l * 24 + 8:l * 24 + 16] = np.asarray(inp["post_norm_g"], f32)[l].reshape(8, 128).T
        vecs[:, l * 24 + 16:l * 24 + 20] = np.asarray(inp["d_skip"], f32)[l].reshape(4, 128).T
        vecs[:, l * 24 + 20:l * 24 + 24] = np.asarray(inp["b_glu"], f32)[l].reshape(4, 128).T
    out["vecs"] = vecs
    lre = np.asarray(inp["lambda_re"], f32)
    lim = np.asarray(inp["lambda_im"], f32)
    ldt = np.asarray(inp["log_dt"], f32)
    bre = np.asarray(inp["b_re"], f32)
    bim = np.asarray(inp["b_im"], f32)
    cre = np.asarray(inp["c_re"], f32)
    cim = np.asarray(inp["c_im"], f32)
    rb = np.zeros((DEPTH, 128, 5, 4, 128), f32)
    rc = np.zeros((DEPTH, 128, 48 + 512), f32)
    for l in range(DEPTH):
        lam_g = lre[l].reshape(4, 4, 2, 64)
        lim_g = lim[l].reshape(4, 4, 2, 64)
        ldt_g = np.broadcast_to(ldt[l].reshape(4, 4, 2, 1), (4, 4, 2, 64))
        for ptl in range(4):
            rows = slice(32 * ptl, 32 * ptl + 32)
            rb[l, rows, 0] = lam_g[:, ptl].reshape(4, 128)[None]
            rb[l, rows, 1] = lim_g[:, ptl].reshape(4, 128)[None]
            rb[l, rows, 2] = ldt_g[:, ptl].reshape(4, 128)[None]
            for h in range(2):
                for j in range(4):
                    g = 8 * j + 2 * ptl + h
                    r0 = 32 * ptl + 16 * h
                    rb[l, r0:r0 + 16, 3, j, 64 * h:64 * h + 64] = bre[l, g].T
                    rb[l, r0:r0 + 16, 4, j, 64 * h:64 * h + 64] = bim[l, g].T
        rc[l, :, 0:16] = lre[l].reshape(16, 128).T
        rc[l, :, 16:32] = lim[l].reshape(16, 128).T
        rc[l, :, 32:48] = np.broadcast_to(ldt[l].reshape(16, 2, 1), (16, 2, 64)).reshape(16, 128).T
        cc = cre[l].reshape(16, 2, 16, 64).transpose(1, 3, 0, 2).reshape(128, 16 * 16)
        ci = cim[l].reshape(16, 2, 16, 64).transpose(1, 3, 0, 2).reshape(128, 16 * 16)
        rc[l, :, 48:48 + 256] = cc
        rc[l, :, 48 + 256:48 + 512] = ci
    out["ssm_rb"] = rb.reshape(DEPTH, 128, 5 * 512)
    out["ssm_rc"] = rc
    am = np.zeros((128, 256), f32)
    kk = np.arange(128)[:, None]
    qq = np.arange(128)[None, :]
    am[:, 0:128] = (qq >= kk)
    am[:, 128:256] = (qq <= kk)
    out["amask"] = am.astype(ml_dtypes.bfloat16)
    out["ident"] = np.eye(128, dtype=f32).astype(ml_dtypes.bfloat16)
    return out


class TL:
    def __init__(self, t, name):
        self.t = t
        self.r = Res(name)

    def __getitem__(self, idx):
        return self.t[idx]


class Builder:
    def __init__(self, n_layers=DEPTH, debug=None):
        self.n_layers = n_layers
        self.debug = debug or {}
        self.nc = bass.Bass("TRN2", target_bir_lowering=False)
        self.es = ExitStack()
        self.S = Sched(self.nc, self.es)
        self.uid = 0

    def sb(self, es, shape, dtype, name=None):
        self.uid += 1
        nm = (name or "t") + "_%d" % self.uid
        return TL(es.enter_context(self.nc.sbuf_tensor(nm, list(shape), dtype)), nm)

    def ps(self, es, shape, dtype=F32, name=None):
        self.uid += 1
        nm = (name or "p") + "_%d" % self.uid
        return TL(es.enter_context(self.nc.psum_tensor(nm, list(shape), dtype)), nm)

    @staticmethod
    def _res(lst):
        return [x.r if isinstance(x, TL) else x for x in lst]

    def tt(self, eng, out, a, b, op, R, W):
        self.S.op(eng, lambda e: e.tensor_tensor(out=out, in0=a, in1=b, op=op), self._res(R), self._res(W))

    def ts(self, eng, out, a, s1, op0, R, W, s2=None, op1=None):
        if op1 is None:
            self.S.op(eng, lambda e: e.tensor_scalar(out=out, in0=a, scalar1=s1, scalar2=None, op0=op0),
                      self._res(R), self._res(W))
        else:
            self.S.op(eng, lambda e: e.tensor_scalar(out=out, in0=a, scalar1=s1, scalar2=s2, op0=op0, op1=op1),
                      self._res(R), self._res(W))

    def stt(self, out, a, scalar, b, op0, op1, R, W):
        self.S.op("dve", lambda e: e.scalar_tensor_tensor(out=out, in0=a, scalar=scalar, in1=b, op0=op0, op1=op1),
                  self._res(R), self._res(W))

    def act(self, out, a, func, R, W, scale=1.0, bias=0.0):
        self.S.op("act", lambda e: e.activation(out=out, in_=a, func=func, bias=bias, scale=scale),
                  self._res(R), self._res(W))

    def copy(self, eng, out, a, R, W):
        if eng == "act":
            self.S.op("act", lambda e: e.activation(out=out, in_=a, func=AF.Copy), self._res(R), self._res(W))
        else:
            self.S.op(eng, lambda e: e.tensor_copy(out=out, in_=a), self._res(R), self._res(W))

    def memset(self, eng, ap, val, W):
        self.S.op(eng, lambda e: e.memset(ap, val), [], self._res(W))

    def mm(self, out, lhsT, rhs, start, stop, R, W, tp=None):
        if tp is None:
            self.S.op("pe", lambda e: e.matmul(out, lhsT=lhsT, rhs=rhs, start=start, stop=stop),
                      self._res(R), self._res(W))
        else:
            self.S.op("pe", lambda e: e.matmul(out, lhsT=lhsT, rhs=rhs, start=start, stop=stop, tile_position=tp),
                      self._res(R), self._res(W))

    def dma(self, q, out, in_, R, W, owner=None):
        return self.S.dma(q, out, in_, self._res(R), self._res(W),
                          owner.r if isinstance(owner, TL) else owner)

    def cmul(self, eng, o, a, b, t1, t2, R, W, Tm):
        (orr, oi), (ar, ai), (br, bi) = o, a, b
        self.tt(eng, t1, ar, br, ALU.mult, R, Tm)
        self.tt(eng, t2, ai, bi, ALU.mult, R, Tm)
        self.tt(eng, orr, t1, t2, ALU.subtract, Tm, W)
        self.tt(eng, t1, ar, bi, ALU.mult, R, Tm)
        self.tt(eng, t2, ai, br, ALU.mult, R, Tm)
        self.tt(eng, oi, t1, t2, ALU.add, Tm, W)

    def declare_io(self):
        nc = self.nc
        nl = DEPTH
        d = {}
        d["xT"] = nc.dram_tensor("xT", [D, L], F32, kind="ExternalInput").ap()
        d["w_in"] = nc.dram_tensor("w_in", [nl, D, 8192], F32, kind="ExternalInput").ap()
        d["w_glu"] = nc.dram_tensor("w_glu", [nl, 512, 512], F32, kind="ExternalInput").ap()
        d["w_bs"] = nc.dram_tensor("w_bs", [nl, 512, D], F32, kind="ExternalInput").ap()
        d["w_ba"] = nc.dram_tensor("w_ba", [nl, 512, D], F32, kind="ExternalInput").ap()
        d["w_out"] = nc.dram_tensor("w_out", [nl, D, D], F32, kind="ExternalInput").ap()
        d["vecs"] = nc.dram_tensor("vecs", [128, nl * 24], F32, kind="ExternalInput").ap()
        d["ssm_rb"] = nc.dram_tensor("ssm_rb", [nl, 128, 2560], F32, kind="ExternalInput").ap()
        d["ssm_rc"] = nc.dram_tensor("ssm_rc", [nl, 128, 560], F32, kind="ExternalInput").ap()
        d["amask"] = nc.dram_tensor("amask", [128, 256], BF16, kind="ExternalInput").ap()
        d["ident"] = nc.dram_tensor("ident", [128, 128], BF16, kind="ExternalInput").ap()
        d["yT"] = nc.dram_tensor("yT", [D, L], F32, kind="ExternalOutput").ap()
        tk = "ExternalOutput" if self.debug.get("tables") else "Internal"
        d["tabB"] = nc.dram_tensor("tabB", [nl, 128, 16384], BF16, kind=tk).ap()
        d["tabC"] = nc.dram_tensor("tabC", [nl, 128, 16384], BF16, kind=tk).ap()
        for name, (shape, dt) in self.debug.get("outs", {}).items():
            d[name] = nc.dram_tensor(name, list(shape), dt, kind="ExternalOutput").ap()
        self.io = d

    def lam_alloc(self, es, Fd, tag):
        names = ("dt", "lr", "a", "th", "mag", "em2a", "c", "s", "t1", "t2", "lbr", "lbi")
        return {n: self.sb(es, [128, Fd], F32, tag + n) for n in names}

    def lam_math(self, m, src):
        B = self
        dt, lr, a, th, mag, em2a, c, s, t1, t2, lbr, lbi = [m[n] for n in
                                                          ("dt", "lr", "a", "th", "mag", "em2a", "c", "s", "t1", "t2", "lbr", "lbi")]
        R = src["R"]
        B.act(dt[:], src["ldt"], AF.Exp, R, [dt])
        B.ts("dve", lr[:], src["lre"], -1e-4, ALU.min, R, [lr])
        B.tt("dve", a[:], lr[:], dt[:], ALU.mult, [lr, dt], [a])
        B.tt("dve", th[:], src["lim"], dt[:], ALU.mult, R + [dt], [th])
        B.act(mag[:], a[:], AF.Exp, [a], [mag])
        B.act(em2a[:], a[:], AF.Exp, [a], [em2a], scale=-2.0)
        B.act(s[:], th[:], AF.Sin, [th], [s], scale=1.0 / 16.0)
        B.act(c[:], th[:], AF.Sin, [th, self.halfpi], [c], scale=1.0 / 16.0, bias=self.halfpi[:, 0:1])
        for _ in range(4):
            B.tt("dve", t1[:], c[:], c[:], ALU.mult, [c], [t1])
            B.tt("dve", t2[:], s[:], s[:], ALU.mult, [s], [t2])
            B.stt(s[:], c[:], 2.0, s[:], ALU.mult, ALU.mult, [c, s], [s])
            B.tt("dve", c[:], t1[:], t2[:], ALU.subtract, [t1, t2], [c])
        B.tt("dve", lbr[:], mag[:], c[:], ALU.mult, [mag, c], [lbr])
        B.tt("dve", lbi[:], mag[:], s[:], ALU.mult, [mag, s], [lbi])

    def prologue(self):
        B = self
        io = self.io
        toks = []
        with ExitStack() as es:
            mb = B.lam_alloc(es, 512, "rb")
            inpb = B.sb(es, [128, 5, 512], F32, "rbin")
            mk = lambda n: B.sb(es, [128, 512], F32, "rb" + n)
            den, nr, cr, ci, invr, invi, bbr, bbi = [mk(n) for n in ("den", "nr", "cr", "ci", "invr", "invi", "bbr", "bbi")]
            xs = [(mk("xr0"), mk("xi0")), (mk("xr1"), mk("xi1"))]
            bt = B.sb(es, [128, 4, TS, 2, 128], BF16, "bt")
            mc = B.lam_alloc(es, 16, "rc")
            inpc = B.sb(es, [128, 560], F32, "rcin")
            pw = B.sb(es, [128, 2, 16, TS], F32, "pw")
            ct = B.sb(es, [128, 16, TS, 2, 32], BF16, "ct")
            big = lambda n: B.sb(es, [128, 16, TS, 16], F32, "rc" + n)
            u1, u2, u3 = big("u1"), big("u2"), big("u3")
            B.memset("pool", ct[:], 0.0, [ct])
            for l in range(self.n_layers):
                inp = inpb
                B.dma("sp", inp[:], io["ssm_rb"][l].rearrange("p (a f) -> p a f", a=5), [], [inp])
                B.lam_math(mb, dict(lre=inp[:, 0, :], lim=inp[:, 1, :], ldt=inp[:, 2, :], R=[inp]))
                lr, lbr, lbi, em2a, t1, t2 = [mb[n] for n in ("lr", "lbr", "lbi", "em2a", "t1", "t2")]
                li = inp[:, 1, :]
                B.tt("dve", t1[:], lr[:], lr[:], ALU.mult, [lr], [t1])
                B.tt("dve", t2[:], li, li, ALU.mult, [inp], [t2])
                B.tt("dve", den[:], t1[:], t2[:], ALU.add, [t1, t2], [den])
                B.S.op("dve", lambda e, den=den: e.reciprocal(out=den[:], in_=den[:]), [den.r], [den.r])
                B.ts("dve", nr[:], lbr[:], -1.0, ALU.add, [lbr], [nr])
                B.tt("dve", t1[:], nr[:], lr[:], ALU.mult, [nr, lr], [t1])
                B.tt("dve", t2[:], lbi[:], li, ALU.mult, [lbi, inp], [t2])
                B.tt("dve", t1[:], t1[:], t2[:], ALU.add, [t1, t2], [t1])
                B.tt("dve", cr[:], t1[:], den[:], ALU.mult, [t1, den], [cr])
                B.tt("dve", t1[:], lbi[:], lr[:], ALU.mult, [lbi, lr], [t1])
                B.tt("dve", t2[:], nr[:], li, ALU.mult, [nr, inp], [t2])
                B.tt("dve", t1[:], t1[:], t2[:], ALU.subtract, [t1, t2], [t1])
                B.tt("dve", ci[:], t1[:], den[:], ALU.mult, [t1, den], [ci])
                B.tt("dve", invr[:], lbr[:], em2a[:], ALU.mult, [lbr, em2a], [invr])
                B.stt(invi[:], lbi[:], -1.0, em2a[:], ALU.mult, ALU.mult, [lbi, em2a], [invi])
                B.cmul("dve", (bbr[:], bbi[:]), (cr[:], ci[:]), (inp[:, 3, :], inp[:, 4, :]), t1[:], t2[:],
                       [cr, ci, inp], [bbr, bbi], [t1, t2])
                prev = (bbr, bbi)
                for i in range(TS):
                    cur = xs[i % 2]
                    B.cmul("dve", (cur[0][:], cur[1][:]), (prev[0][:], prev[1][:]), (invr[:], invi[:]), t1[:], t2[:],
                           [prev[0], prev[1], invr, invi], [cur[0], cur[1]], [t1, t2])
                    B.copy("act", bt[:, :, i, 0, :], cur[0][:].rearrange("p (j q) -> p j q", j=4), [cur[0]], [bt])
                    B.copy("act", bt[:, :, i, 1, :], cur[1][:].rearrange("p (j q) -> p j q", j=4), [cur[1]], [bt])
                    prev = cur
                toks.append(B.dma("sp", io["tabB"][l], bt[:].rearrange("p j i r q -> p (j i r q)"),
                                  [bt], [self.tabB_r[l]], owner=bt))
                inp = inpc
                B.dma("sp", inp[:], io["ssm_rc"][l], [], [inp])
                B.lam_math(mc, dict(lre=inp[:, 0:16], lim=inp[:, 16:32], ldt=inp[:, 32:48], R=[inp]))
                lbr, lbi, t1, t2 = [mc[n] for n in ("lbr", "lbi", "t1", "t2")]
                B.copy("dve", pw[:, 0, :, 0], lbr[:], [lbr], [pw])
                B.copy("dve", pw[:, 1, :, 0], lbi[:], [lbi], [pw])
                for i in range(1, TS):
                    B.cmul("dve", (pw[:, 0, :, i], pw[:, 1, :, i]), (pw[:, 0, :, i - 1], pw[:, 1, :, i - 1]),
                           (lbr[:], lbi[:]), t1[:], t2[:], [pw, lbr, lbi], [pw], [t1, t2])
                PL = self.PL
                B.copy("dve", PL[:, l, :, 0], pw[:, 0, :, TS - 1], [pw], [PL])
                B.copy("dve", PL[:, l, :, 1], pw[:, 1, :, TS - 1], [pw], [PL])
                for k in range(1, 7):
                    pr, pi = PL[:, l, :, 3 * k - 3], PL[:, l, :, 3 * k - 2]
                    B.tt("dve", t1[:], pr, pr, ALU.mult, [PL], [t1])
                    B.tt("dve", t2[:], pi, pi, ALU.mult, [PL], [t2])
                    B.tt("dve", PL[:, l, :, 3 * k], t1[:], t2[:], ALU.subtract, [t1, t2], [PL])
                    B.stt(PL[:, l, :, 3 * k + 1], pr, 2.0, pi, ALU.mult, ALU.mult, [PL], [PL])
                for k in range(7):
                    B.ts("dve", PL[:, l, :, 3 * k + 2], PL[:, l, :, 3 * k + 1], -1.0, ALU.mult, [PL], [PL])
                cre = inp[:, 48:304].rearrange("p (t c) -> p t c", c=16)
                cim = inp[:, 304:560].rearrange("p (t c) -> p t c", c=16)

                def bc_c(ap3):
                    return mkap(ap3.tensor, ap3.offset, [ap3.ap[0], ap3.ap[1], [0, TS], ap3.ap[2]])

                def bc_p(k):
                    a = pw[:, k, :, :]
                    return mkap(a.tensor, a.offset, [a.ap[0], a.ap[1], a.ap[2], [0, 16]])
                B.tt("dve", u1[:], bc_c(cre), bc_p(0), ALU.mult, [inp, pw], [u1])
                B.tt("dve", u2[:], bc_c(cim), bc_p(1), ALU.mult, [inp, pw], [u2])
                B.tt("dve", u3[:], u1[:], u2[:], ALU.subtract, [u1, u2], [u3])
                B.copy("act", ct[0:64, :, :, 0, 0:16], u3[0:64], [u3], [ct])
                B.copy("act", ct[64:128, :, :, 0, 16:32], u3[64:128], [u3], [ct])
                B.tt("dve", u1[:], bc_c(cre), bc_p(1), ALU.mult, [inp, pw], [u1])
                B.tt("dve", u2[:], bc_c(cim), bc_p(0), ALU.mult, [inp, pw], [u2])
                B.stt(u3[:], u1[:], -1.0, u2[:], ALU.mult, ALU.subtract, [u1, u2], [u3])
                B.copy("act", ct[0:64, :, :, 1, 0:16], u3[0:64], [u3], [ct])
                B.copy("act", ct[64:128, :, :, 1, 16:32], u3[64:128], [u3], [ct])
                toks.append(B.dma("sp", io["tabC"][l], ct[:].rearrange("p t i r c -> p (t i r c)"),
                                  [ct], [self.tabC_r[l]], owner=ct))
        self.S.barrier(toks)

    def build(self):
        B = self
        nc = self.nc
        es = self.es
        self.declare_io()
        io = self.io
        self.tabB_r = [Res("tabB%d" % l) for l in range(DEPTH)]
        self.tabC_r = [Res("tabC%d" % l) for l in range(DEPTH)]
        self.halfpi = B.sb(es, [128, 1], F32, "halfpi")
        B.memset("dve", self.halfpi[:], math.pi / 2.0, [self.halfpi])
        self.PL = B.sb(es, [128, DEPTH, 16, 24], F32, "PL")
        self.prologue()
        if self.debug.get("tables"):
            self.finish([])
            return
        self.main()

    def finish(self, out_toks):
        self.S.wait_all("sp", out_toks)
        self.S.barrier(out_toks)
        self.S.run()


class V:
    def __init__(self, ap, name=""):
        self.ap = ap
        self.r = Res(name)
        self.lo = self.hi = None

    def __getitem__(self, idx):
        return self.ap[idx]


def _res_of(lst):
    return [x.r if hasattr(x, "r") and not isinstance(x, Res) else x for x in lst]


Builder._res = staticmethod(_res_of)


def dil_ap(base, r, col0, ncols):
    t, off, pdim, st = base.tensor, base.offset, list(base.ap[0]), base.ap[-1][0]
    assert len(base.ap) == 2
    nsub = L // r
    c0, i0 = divmod(col0, nsub)
    if r == 1:
        dims, o = [[st, ncols]], col0
    elif i0 + ncols <= nsub:
        dims, o = [[r * st, ncols]], c0 + r * i0
    else:
        assert i0 == 0 and ncols % nsub == 0
        dims, o = [[st, ncols // nsub], [r * st, nsub]], c0
    return mkap(t, off + o * st, [pdim] + dims)


def deint_dst(base, r, n):
    if r == 1:
        return base[:, 512 * n:512 * n + 512]
    nsub = L // r
    t, off, pdim, st = base.tensor, base.offset, list(base.ap[0]), base.ap[-1][0]
    return mkap(t, off + (512 * n // r) * st, [pdim, [st, 512 // r], [nsub * st, r]])


def like(bank_ap, ap):
    if len(ap.ap) == 3:
        return bank_ap.rearrange("p (a b) -> p a b", a=ap.ap[1][1])
    return bank_ap


AR_BYTES = 88064


def _main(self):
    B = self
    es = self.es
    io = self.io
    S = self.S
    nc = self.nc
    dbg = self.debug
    xT = B.sb(es, [128, NK, L], F32, "xT")
    xT_r = [[Res("xT%d_%d" % (k, n)) for n in range(4)] for k in range(NK)]
    hT = B.sb(es, [128, NK, L], BF16, "hT")
    hT_r = [Res("hT%d" % k) for k in range(NK)]
    vecs = B.sb(es, [128, DEPTH * 24], F32, "vecs")
    amask = B.sb(es, [128, 256], BF16, "amask")
    ident = B.sb(es, [128, 128], BF16, "ident")
    ones = B.sb(es, [128, 128], BF16, "ones")
    epsT = B.sb(es, [128, 1], F32, "eps")
    wbuf = [B.sb(es, [128, NK, 512], BF16, "wbuf%d" % i) for i in range(2)]
    arena = B.sb(es, [128, AR_BYTES // 2], BF16, "arena")
    banks = [B.ps(es, [128, 512], F32, "bank%d" % i) for i in range(8)]
    self.wi = 0

    def carve(off, shape, dtype, name):
        nel = int(np.prod(shape[1:]))
        esz = 2 if dtype == BF16 else 4
        assert off % 4 == 0 and off + nel * esz <= AR_BYTES, (name, off, nel * esz)
        a = arena.t[:, off // 2: off // 2 + nel * esz // 2]
        if dtype == F32:
            a = a.bitcast(F32)
        if len(shape) == 3:
            a = a.rearrange("p (a b) -> p a b", a=shape[1])
        elif len(shape) == 4:
            a = a.rearrange("p (a b c) -> p a b c", a=shape[1], b=shape[2])
        elif len(shape) == 5:
            a = a.rearrange("p (a b c d) -> p a b c d", a=shape[1], b=shape[2], c=shape[3])
        v = V(a, name)
        region(off, off + nel * esz, [v.r])
        v.lo, v.hi = off, off + nel * esz
        return v

    regs = []

    def region(lo, hi, res_list):
        inherit = []
        keep = []
        for (l0, h0, rl) in regs:
            if l0 < hi and lo < h0:
                for r_ in rl:
                    inherit += r_.w + r_.r
                if lo <= l0 and h0 <= hi:
                    continue
            keep.append((l0, h0, rl))
        regs[:] = keep
        best = {}
        for (sm, vl, src) in inherit:
            if id(sm) not in best or best[id(sm)][1] < vl:
                best[id(sm)] = (sm, vl, "alias")
        for r_ in res_list:
            r_.r = list(best.values())
        regs.append((lo, hi, res_list))

    def reslist(v, names):
        rl = [Res(n) for n in names]
        region(v.lo, v.hi, rl)
        return rl

    B.memset("dve", ones[:], 1.0, [ones])
    B.memset("dve", epsT[:], 1e-6, [epsT])
    B.dma("sp", vecs[:], io["vecs"], [], [vecs])
    B.dma("sp", amask[:], io["amask"], [], [amask])
    B.dma("sp", ident[:], io["ident"], [], [ident])
    xin = io["xT"].rearrange("(k p) t -> p k t", p=128)
    for k in range(NK):
        B.dma("sp", xT[:, k, :], xin[:, k, :], [], xT_r[k], owner=xT_r[k][0])

    out_toks = []

    def dump(name, ap, R):
        out_toks.append(B.dma("sp", io[name], ap, R, [], owner=Res("dump_" + name)))

    def load_wblock(l, col0, ncols):
        wb = wbuf[self.wi % 2]
        self.wi += 1
        src = io["w_in"][l][:, col0:col0 + ncols].rearrange("(k p) c -> p k c", p=128)
        B.dma("pool", wb[:, :, 0:ncols], src, [], [wb])
        return wb

    def load_w(dst, src2d):
        B.dma("pool", dst[:], src2d.rearrange("(k p) c -> p k c", p=128), [], [dst])

    bank_ctr = [0]

    def nb(lst):
        b = banks[lst[bank_ctr[0] % len(lst)]]
        bank_ctr[0] += 1
        return b

    def hk(k):
        return hT.t[:, k, :]

    for l in range(self.n_layers):
        vb = l * 24
        sq = [carve(i * 8192, [128, NK, 512], BF16, "sq%d" % i) for i in range(2)]
        rt = [carve(16384 + i * 2048, [128, 512], F32, "rt%d" % i) for i in range(2)]
        rs = [carve(20480 + i * 2048, [128, 512], F32, "rs%d" % i) for i in range(2)]
        for n in range(4):
            ns = slice(n * 512, (n + 1) * 512)
            s_, rt_, rs_ = sq[n % 2], rt[n % 2], rs[n % 2]
            for k in range(NK):
                B.act(s_[:, k, :], xT[:, k, ns], AF.Square, [xT_r[k][n]], [s_])
            bk = nb([6, 7])
            for k in range(NK):
                B.mm(bk[:], ones[:], s_[:, k, :], k == 0, k == NK - 1, [ones, s_], [bk])
            B.act(rt_[:], bk[:], AF.Sqrt, [bk, epsT], [rt_], scale=1.0 / D, bias=epsT[:, 0:1])
            S.op("dve", lambda e, a=rs_, b=rt_: e.reciprocal(out=a[:], in_=b[:]), [rt_.r], [rs_.r])
            for k in range(NK):
                B.stt(hT[:, k, ns], xT[:, k, ns], vecs[:, vb + k:vb + k + 1], rs_[:], ALU.mult, ALU.mult,
                      [xT_r[k][n], vecs, rs_], [hT_r[k]])
        if dbg.get("stop") == "P0":
            dump("dbg_h", hT[:], hT_r)
            break
        pass
        yT = carve(32768, [128, 4, L], BF16, "yT")
        _fl = reslist(yT, ["yT%d_%d" % (k, n) for k in range(4) for n in range(4)])
        yT_r = [[_fl[4 * k + n] for n in range(4)] for k in range(4)]
        r32 = carve(0, [128, 2, TS, 128], F32, "r32")
        rbf = carve(16384, [128, 2, TS, 128], BF16, "rbf")
        Bt = carve(24576, [128, TS, 2, 128], BF16, "Bt")
        Ct = carve(49152, [128, 5120], BF16, "Ct")
        CtA = Ct[:, 0:3072].rearrange("p (t i r c) -> p t i r c", t=3, i=TS, r=2)
        CtB = Ct[:, 3072:5120].rearrange("p (i r c) -> p i r c", i=TS, r=2)
        uT = carve(59392, [128, L], BF16, "uT")
        wglu = carve(63488, [128, 4, 512], BF16, "wglu")
        Sab = [carve(67584 + i * 1536, [128, 2, 192], F32, "S%d" % i) for i in range(2)]
        zt = carve(70656, [128, 2, 192], F32, "zt")
        tq = carve(72192, [128, 1, 128], F32, "tq")
        tq2 = carve(72704, [128, 1, 128], F32, "tq2")
        Sbf = carve(73216, [128, 2, 128], BF16, "Sbf")
        ddt = [carve(73728 + i * 256, [128, 128], BF16, "dd%d" % i) for i in range(2)]
        gw = [carve(77824 + i * 2048, [128, 512], F32, "gw%d" % i) for i in range(2)]
        gs = [carve(81920 + i * 2048, [128, 512], F32, "gs%d" % i) for i in range(2)]
        gz = [carve(86016 + i * 1024, [128, 512], BF16, "gz%d" % i) for i in range(2)]
        for t_ in (Sab[0], Sab[1], zt):
            B.memset("dve", t_[:, :, 0:64], 0.0, [t_])
        Ctz = V(CtB[:, :, :, 0:32], "Ctz")
        region(Ct.lo, Ct.hi, [Ctz.r])
        B.memset("pool", Ctz[:], 0.0, [Ctz])
        B.memset("dve", Sbf[:], 0.0, [Sbf])
        wbU = load_wblock(l, 0, 512)
        wbZ = load_wblock(l, 512, 512)
        load_w(wglu, io["w_glu"][l])
        PLl = self.PL
        def prelude_u(j):
            for n in range(4):
                bk = nb([6, 7])
                for k in range(NK):
                    B.mm(bk[:], wbU[:, k, 128 * j:128 * j + 128], hT[:, k, 512 * n:512 * n + 512],
                         k == 0, k == NK - 1, [wbU, hT_r[k]], [bk])
                B.copy("act", deint_dst(uT[:], 16, n), bk[:].rearrange("p (a b) -> p a b", b=16), [bk], [uT])
            B.dma("sp", Bt[:].rearrange("p i r q -> p (i r q)"), io["tabB"][l][:, 4096 * j:4096 * (j + 1)],
                  [self.tabB_r[l]], [Bt])

        def prelude_c(j):
            B.dma("sp", Ct[:, 0:3072], io["tabC"][l][:, 4096 * j:4096 * j + 3072], [self.tabC_r[l]], [Ct])
            B.dma("sp", CtB[:, :, :, 32:64],
                  io["tabC"][l][:, 4096 * j + 3072:4096 * (j + 1)].rearrange("p (i r c) -> p i r c", i=TS, r=2),
                  [self.tabC_r[l]], [Ct])

        prelude_u(0)
        prelude_c(0)
        for j in dbg.get("js", range(4)):
            dd = ddt[j % 2]
            B.ts("dve", dd[:], ident[:], vecs[:, vb + 16 + j:vb + 17 + j], ALU.mult, [ident, vecs], [dd])
            for ib in range(4):
                B.mm(banks[ib][:], dd[:], uT[:, 512 * ib:512 * ib + 512], True, False, [dd, uT], [banks[ib]])
            def xt_chain(ptl):
                rows = slice(32 * ptl, 32 * ptl + 32)
                for ih in range(8):
                    xb = nb([4, 5])
                    xv = xb[:].rearrange("p (r i c) -> p r i c", r=2, i=2)
                    for ri in range(2):
                        for il in range(2):
                            i = 2 * ih + il
                            B.mm(xv[:, ri, il, :], Bt[rows, i, ri, :], uT[rows, i * 128:(i + 1) * 128],
                                 True, True, [Bt, uT], [xb], tp=(32 * ptl, 0))
                    for il in range(2):
                        i = 2 * ih + il
                        if i == 0:
                            B.copy("dve", r32[:, :, 0, :], xv[:, :, 0, :], [xb], [r32])
                        else:
                            B.tt("dve", r32[:, :, i, :], r32[:, :, i - 1, :], xv[:, :, il, :], ALU.add,
                                 [xb, r32], [r32])
                        yield

            def l2(ptl):
                pt = 4 * j + ptl
                B.copy("act", rbf[:], r32[:], [r32], [rbf])
                pl = lambda c: PLl[:, l, pt, c:c + 1]
                rr, rim = r32[:, 0, TS - 1, :], r32[:, 1, TS - 1, :]
                B.ts("dve", tq[:, 0, :], rim, pl(1), ALU.mult, [r32, PLl], [tq])
                B.stt(zt[:, 0, 64:192], rr, pl(0), tq[:, 0, :], ALU.mult, ALU.subtract, [r32, PLl, tq], [zt])
                B.ts("dve", tq2[:, 0, :], rr, pl(1), ALU.mult, [r32, PLl], [tq2])
                B.stt(zt[:, 1, 64:192], rim, pl(0), tq2[:, 0, :], ALU.mult, ALU.add, [r32, PLl, tq2], [zt])
                yield "z"
                old = zt
                for kk in range(7):
                    d = 1 << kk
                    new = Sab[kk % 2]
                    a_, b_, nb_ = pl(3 * kk), pl(3 * kk + 1), pl(3 * kk + 2)
                    B.stt(new[:, :, 64:192], old[:, :, 64 - d:192 - d], a_, old[:, :, 64:192], ALU.mult, ALU.add, [old, PLl], [new])
                    yield
                    B.stt(new[:, 0, 64:192], old[:, 1, 64 - d:192 - d], nb_, new[:, 0, 64:192], ALU.mult, ALU.add, [old, new, PLl], [new])
                    yield
                    B.stt(new[:, 1, 64:192], old[:, 0, 64 - d:192 - d], b_, new[:, 1, 64:192], ALU.mult, ALU.add, [old, new, PLl], [new])
                    yield
                    old = new
                B.copy("dve", Sbf[:, :, 1:128], old[:, :, 64:191], [old], [Sbf])
                yield

            def interleave(g1, g2):
                for _ in g1:
                    if g2 is not None:
                        if next(g2, "end") == "end":
                            g2 = None
                if g2 is not None:
                    for _ in g2:
                        pass

            def readout(ptl):
                rows = slice(32 * ptl, 32 * ptl + 32)
                for i in range(TS):
                    yb = banks[i // 4]
                    cs_ = slice((i % 4) * 128, (i % 4) * 128 + 128)
                    if ptl == 3:
                        o_ = yb[64:128, cs_]
                        tp = (0, 64)
                        lh = [CtB[:, i, rr_, :] for rr_ in range(2)]
                        RC_ = [Ct, Ctz]
                    else:
                        o_ = yb[rows, cs_]
                        tp = (0, 32 * ptl)
                        lh = [CtA[:, ptl, i, rr_, :] for rr_ in range(2)]
                        RC_ = [Ct]
                    B.mm(o_, lh[0], rbf[:, 0, i, :], False, False, RC_ + [rbf], [yb], tp=tp)
                    B.mm(o_, lh[1], rbf[:, 1, i, :], False, False, RC_ + [rbf], [yb], tp=tp)
                    B.mm(o_, lh[0], Sbf[:, 0, :], False, False, RC_ + [Sbf], [yb], tp=tp)
                    B.mm(o_, lh[1], Sbf[:, 1, :], False, True, RC_ + [Sbf], [yb], tp=tp)

            order = (0, 1, 3, 2)
            for _ in xt_chain(order[0]):
                pass
            for idx, ptl in enumerate(order):
                g1 = l2(ptl)
                for step in g1:
                    if step == "z":
                        break
                g2 = xt_chain(order[idx + 1]) if idx + 1 < 4 else None
                interleave(g1, g2)
                if idx == 2 and j + 1 < 4:
                    prelude_u(j + 1)
                readout(ptl)
            if j + 1 < 4:
                prelude_c(j + 1)
            for ib in range(4):
                cs = slice(512 * ib, 512 * ib + 512)
                w_, g_, yb = gw[ib % 2], gs[ib % 2], banks[ib]
                B.act(w_[:], yb[:], AF.Square, [yb], [w_])
                B.ts("pool", w_[:], w_[:], 0.044715, ALU.mult, [w_], [w_], s2=1.0, op1=ALU.add)
                B.tt("dve", w_[:], w_[:], yb[:], ALU.mult, [w_, yb], [w_])
                B.act(g_[:], w_[:], AF.Sigmoid, [w_], [g_], scale=1.5957691216057308)
                B.tt("dve", yT[:, j, cs], yb[:], g_[:], ALU.mult, [yb, g_], [yT_r[j][ib]])
        if dbg.get("stop") == "PA1":
            dump("dbg_y", yT[:], [x for row in yT_r for x in row])
            break
        szf = [carve(4096 * mo, [128, L], BF16, "szf%d" % mo) for mo in range(4)]
        for mo in range(4):
            for n in range(4):
                zb = nb([4, 5])
                for k in range(NK):
                    B.mm(zb[:], wbZ[:, k, 128 * mo:128 * mo + 128], hT[:, k, 512 * n:512 * n + 512],
                         k == 0, k == NK - 1, [wbZ, hT_r[k]], [zb])
                B.act(deint_dst(szf[mo][:], 16, n), zb[:].rearrange("p (a b) -> p a b", b=16), AF.Silu, [zb], [szf[mo]])
        for n in range(4):
            cs = slice(512 * n, 512 * n + 512)
            for mo in range(4):
                gb = banks[mo]
                for k in range(4):
                    B.mm(gb[:], wglu[:, k, 128 * mo:128 * mo + 128], yT[:, k, cs], k == 0, k == 3,
                         [wglu, yT_r[k][n]], [gb])
            for mo in range(4):
                g_ = gs[mo % 2]
                B.act(g_[:], banks[mo][:], AF.Sigmoid, [banks[mo], vecs], [g_], bias=vecs[:, vb + 20 + mo:vb + 21 + mo])
                B.tt("dve", g_[:], g_[:], szf[mo][:, cs], ALU.mult, [g_, szf[mo]], [g_])
                B.tt("dve", yT[:, mo, cs], yT[:, mo, cs], g_[:], ALU.mult, [yT_r[mo][n], g_], [yT_r[mo][n]])
        if dbg.get("stop") == "PA":
            dump("dbg_y", yT[:], [x for row in yT_r for x in row])
            break
        pass
        merged = carve(0, [128, NK, L], BF16, "merged")
        merged_r = reslist(merged, ["mg%d" % m for m in range(NK)])
        wbs = carve(49152, [128, 4, D], BF16, "wbs")
        sgg = [carve(57344 + i * 4096, [128, L], BF16, "sgg%d" % i) for i in range(2)]
        load_w(wbs, io["w_bs"][l])
        for m in range(NK):
            if m % 4 == 0:
                wb = load_wblock(l, 6144 + 512 * (m // 4), 512)
            sg_ = sgg[m % 2]
            for n in range(4):
                bk = nb([6, 7])
                for k in range(NK):
                    B.mm(bk[:], wb[:, k, 128 * (m % 4):128 * (m % 4) + 128], hT[:, k, 512 * n:512 * n + 512],
                         k == 0, k == NK - 1, [wb, hT_r[k]], [bk])
                B.act(sg_[:, 512 * n:512 * n + 512], bk[:], AF.Sigmoid, [bk], [sg_])
            for n2 in range(4):
                bk = nb([0, 1, 2, 3])
                for k in range(4):
                    B.mm(bk[:], wbs[:, k, 128 * m:128 * m + 128], yT[:, k, 512 * n2:512 * n2 + 512], k == 0, k == 3,
                         [wbs, yT_r[k][n2]], [bk])
                dst = dil_ap(merged[:, m, :], 16, 512 * n2, 512)
                B.tt("dve", dst, like(bk[:], dst), dil_ap(sg_[:], 16, 512 * n2, 512), ALU.mult,
                     [bk, sg_], [merged_r[m]])
        if dbg.get("stop") == "PA2":
            dump("dbg_m", merged[:], merged_r)
            break
        pass
        yaT = carve(32768, [128, 4, L], BF16, "yaT")
        yaT_r = reslist(yaT, ["ya%d" % j for j in range(4)])
        qT = carve(49152, [128, L], BF16, "qT")
        kT = carve(53248, [128, L], BF16, "kT")
        vd = carve(57344, [128, 16, 128], BF16, "vd")
        etm = [carve(61440 + i * 512, [128, 256], BF16, "etm%d" % i) for i in range(6)]
        ett = [carve(64512 + i * 512, [128, 256], BF16, "ett%d" % i) for i in range(4)]
        accn = carve(66560, [128, L], F32, "accn")
        accd = carve(74752, [128, L], F32, "accd")
        sza = carve(82944, [128, L], BF16, "sza")
        for j in range(4):
            for gi, r in enumerate((1, 4, 16)):
                ncols = 512 if gi == 2 else 384
                wb = load_wblock(l, 1024 + 1280 * j + 384 * gi, ncols)
                nsub = L // r
                nblk = nsub // 128
                for (c_in, dst) in ((0, qT), (128, kT)):
                    for n in range(4):
                        bk = nb([6, 7])
                        for k in range(NK):
                            B.mm(bk[:], wb[:, k, c_in:c_in + 128], hT[:, k, 512 * n:512 * n + 512], k == 0, k == NK - 1,
                                 [wb, hT_r[k]], [bk])
                        src_ = bk[:] if r == 1 else bk[:].rearrange("p (a b) -> p a b", b=r)
                        B.copy("act", deint_dst(dst[:], r, n), src_, [bk], [dst])
                for Bq in range(4):
                    bk = nb([6, 7])
                    for bl in range(4):
                        Bk = 4 * Bq + bl
                        for k in range(NK):
                            B.mm(bk[:, bl * 128:bl * 128 + 128], dil_ap(hk(k), r, 128 * Bk, 128), wb[:, k, 256:384],
                                 k == 0, k == NK - 1, [wb, hT_r[k]], [bk])
                    B.copy("act", vd[:, 4 * Bq:4 * Bq + 4, :], bk[:].rearrange("p (a b) -> p a b", a=4), [bk], [vd])
                def score(Bk):
                    c_, blk = divmod(Bk, nblk)
                    nq = 256 if blk < nblk - 1 else 128
                    sb_ = banks[4 + (Bk // 2) % 2]
                    reg = sb_[:, 256 * (Bk % 2):256 * (Bk % 2) + nq]
                    B.mm(reg, kT[:, 128 * Bk:128 * Bk + 128], qT[:, 128 * Bk:128 * Bk + nq], True, True, [kT, qT], [sb_])
                    et_, em_ = ett[Bk % 4], etm[Bk % 6]
                    B.act(et_[:, :nq], reg, AF.Exp, [sb_], [et_], scale=1.0 / math.sqrt(128.0))
                    B.tt("dve", em_[:, :nq], et_[:, :nq], amask[:, :nq], ALU.mult, [et_, amask], [em_])

                def pv(Bk):
                    c_, blk = divmod(Bk, nblk)
                    em_ = etm[Bk % 6]
                    pvb, dnb = banks[(Bk // 4) % 2], banks[2 + (Bk // 4) % 2]
                    cs = slice((Bk % 4) * 128, (Bk % 4) * 128 + 128)
                    for (bank_, isden) in ((pvb, False), (dnb, True)):
                        l0 = ones[:] if isden else vd[:, Bk, :]
                        R0 = [ones] if isden else [vd]
                        B.mm(bank_[:, cs], l0, em_[:, 0:128], True, blk == 0, R0 + [em_], [bank_])
                        if blk > 0:
                            emp = etm[(Bk - 1) % 6]
                            l1 = ones[:] if isden else vd[:, Bk - 1, :]
                            B.mm(bank_[:, cs], l1, emp[:, 128:256], False, True, R0 + [emp], [bank_])
                    if Bk % 4 == 3:
                        for (bank_, acc) in ((pvb, accn), (dnb, accd)):
                            dst = dil_ap(acc[:], r, 512 * (Bk // 4), 512)
                            if gi == 0:
                                B.copy("act" if acc is accd else "dve", dst, like(bank_[:], dst), [bank_], [acc])
                            else:
                                B.tt("dve", dst, dst, like(bank_[:], dst), ALU.add, [bank_, acc], [acc])

                LOOK = 3
                for Bk in range(16 + LOOK):
                    if Bk < 16:
                        score(Bk)
                    if Bk >= LOOK:
                        pv(Bk - LOOK)
            for n in range(4):
                bk = nb([6, 7])
                for k in range(NK):
                    B.mm(bk[:], wb[:, k, 384:512], hT[:, k, 512 * n:512 * n + 512], k == 0, k == NK - 1,
                         [wb, hT_r[k]], [bk])
                B.act(sza[:, 512 * n:512 * n + 512], bk[:], AF.Silu, [bk], [sza])
            S.op("dve", lambda e, a=accd: e.reciprocal(out=a[:], in_=a[:]), [accd.r], [accd.r])
            B.tt("dve", accn[:], accn[:], accd[:], ALU.mult, [accn, accd], [accn])
            B.tt("dve", yaT[:, j, :], accn[:], sza[:], ALU.mult, [accn, sza], [yaT_r[j]])
        if dbg.get("stop") == "PB":
            dump("dbg_ya", yaT[:], yaT_r)
            break
        pass
        wba = carve(49152, [128, 4, D], BF16, "wba")
        sgg = [carve(57344 + i * 4096, [128, L], BF16, "sgga%d" % i) for i in range(2)]
        tmm = [carve(65536 + i * 2048, [128, 512], F32, "tmm%d" % i) for i in range(2)]
        load_w(wba, io["w_ba"][l])
        for m in range(NK):
            if m % 4 == 0:
                wb = load_wblock(l, 7168 + 512 * (m // 4), 512)
            sg_ = sgg[m % 2]
            for n in range(4):
                bk = nb([6, 7])
                for k in range(NK):
                    B.mm(bk[:], wb[:, k, 128 * (m % 4):128 * (m % 4) + 128], hT[:, k, 512 * n:512 * n + 512],
                         k == 0, k == NK - 1, [wb, hT_r[k]], [bk])
                B.act(sg_[:, 512 * n:512 * n + 512], bk[:], AF.Sigmoid, [bk], [sg_])
            for n in range(4):
                ns = slice(512 * n, 512 * n + 512)
                bk = nb([0, 1, 2, 3])
                for k in range(4):
                    B.mm(bk[:], wba[:, k, 128 * m:128 * m + 128], yaT[:, k, ns], k == 0, k == 3,
                         [wba, yaT_r[k]], [bk])
                t_ = tmm[n % 2]
                B.tt("dve", t_[:], bk[:], sg_[:, ns], ALU.mult, [bk, sg_], [t_])
                B.tt("dve", merged[:, m, ns], merged[:, m, ns], t_[:], ALU.add, [t_, merged_r[m]], [merged_r[m]])
        if dbg.get("stop") == "PC2":
            dump("dbg_m", merged[:], merged_r)
            break
        pass
        wout = carve(32768, [128, NK, D], BF16, "wout")
        ot = carve(49152, [128, NK, 512], F32, "ot")
        sqo = carve(65536, [128, NK, 512], BF16, "sqo")
        rt2 = carve(73728, [128, 512], F32, "rt2")
        rs2 = carve(75776, [128, 512], F32, "rs2")
        tm2 = [carve(77824 + i * 2048, [128, 512], F32, "tm2%d" % i) for i in range(2)]
        load_w(wout, io["w_out"][l])
        for n in range(4):
            ns = slice(512 * n, 512 * n + 512)
            for mo in range(NK):
                bk = nb([0, 1, 2, 3, 4, 5])
                for k in range(NK):
                    B.mm(bk[:], wout[:, k, 128 * mo:128 * mo + 128], merged[:, k, ns], k == 0, k == NK - 1,
                         [wout, merged_r[k]], [bk])
                B.copy("act", ot[:, mo, :], bk[:], [bk], [ot])
                B.act(sqo[:, mo, :], bk[:], AF.Square, [bk], [sqo])
            sb_ = nb([6, 7])
            for mo in range(NK):
                B.mm(sb_[:], ones[:], sqo[:, mo, :], mo == 0, mo == NK - 1, [ones, sqo], [sb_])
            B.act(rt2[:], sb_[:], AF.Sqrt, [sb_, epsT], [rt2], scale=1.0 / D, bias=epsT[:, 0:1])
            S.op("dve", lambda e, a=rs2, b=rt2: e.reciprocal(out=a[:], in_=b[:]), [rt2.r], [rs2.r])
            for mo in range(NK):
                t_ = tm2[mo % 2]
                B.stt(t_[:], ot[:, mo, :], vecs[:, vb + 8 + mo:vb + 9 + mo], rs2[:], ALU.mult, ALU.mult,
                      [ot, vecs, rs2], [t_])
                B.tt("pool", xT[:, mo, ns], xT[:, mo, ns], t_[:], ALU.add, [t_, xT_r[mo][n]], [xT_r[mo][n]])
        pass
    else:
        yout = io["yT"].rearrange("(k p) t -> p k t", p=128)
        for k in range(NK):
            out_toks.append(B.dma("sp", yout[:, k, :], xT[:, k, :], xT_r[k], [], owner=Res("out%d" % k)))
    self.finish(out_toks)


Builder.main = _main


_CACHE = {}


def kernel(**inputs):
    h = prep_host(inputs)
    x = np.asarray(inputs["x"], np.float32)
    if "nc" not in _CACHE:
        b = Builder()
        b.build()
        _CACHE["nc"] = b.nc
    nc = _CACHE["nc"]
    shared = {k: h[k] for k in ("w_in", "w_glu", "w_bs", "w_ba", "w_out", "vecs", "ssm_rb", "ssm_rc", "amask", "ident")}
    in_maps = []
    for b_ in range(8):
        m = dict(shared)
        m["xT"] = np.ascontiguousarray(x[b_].T)
        in_maps.append(m)
    res = run_bass_kernel_spmd(nc, in_maps, core_ids=list(range(8)))
    out = np.stack([np.ascontiguousarray(res.results[b_]["yT"].T) for b_ in range(8)], axis=0)
    return out.astype(np.float32)
```

```python
import math
from contextlib import ExitStack

import numpy as np
import ml_dtypes

import concourse.bass as bass
import concourse.mybir as mybir
from concourse.bass_utils import run_bass_kernel_spmd

F32 = mybir.dt.float32
BF16 = mybir.dt.bfloat16
ALU = mybir.AluOpType
AF = mybir.ActivationFunctionType

L = 2048
D = 1024
NK = 8
DEPTH = 4
TS = 16
NCH = L // TS
ENGS = ("pe", "act", "dve", "pool", "sp")
STRICT = False


class Res:
    __slots__ = ("w", "r", "name", "dsem", "dcnt")

    def __init__(self, name=""):
        self.w = []
        self.r = []
        self.name = name
        self.dsem = None
        self.dcnt = 0


class Sched:
    def __init__(self, nc, es):
        self.nc = nc
        self.es = es
        self.prog = {e: [] for e in ENGS}
        self.cnt = {e: 0 for e in ENGS}
        self.sem = {e: es.enter_context(nc.semaphore("sem_" + e)) for e in ENGS if e != "sp"}
        self.known = {e: {} for e in ENGS}
        self.nsem = 0
        self.final = []

    def _waits(self, eng, reads, writes):
        toks = []
        for t in reads:
            toks += t.w
        for t in writes:
            toks += [x for x in t.w if STRICT or x[2] != eng]
            toks += [x for x in t.r if STRICT or x[2] != eng]
        need = {}
        for (sem, val, src) in toks:
            if eng == "pe" and src == "pe":
                continue
            if self.known[eng].get(id(sem), 0) >= val:
                continue
            if need.get(id(sem), (None, 0))[1] < val:
                need[id(sem)] = (sem, val)
        out = []
        for k, (sem, val) in need.items():
            self.known[eng][k] = val
            out.append((sem, val))
        return out

    def _commit(self, tok, reads, writes):
        for t in writes:
            t.w = [tok]
            t.r = []
        for t in reads:
            t.r.append(tok)
            if len(t.r) > 24:
                best = {}
                for (s, v, src) in t.r:
                    if id(s) not in best or best[id(s)][1] < v:
                        best[id(s)] = (s, v, src)
                t.r = list(best.values())

    def op(self, eng, fn, reads=(), writes=()):
        waits = self._waits(eng, reads, writes)
        self.cnt[eng] += 1
        sem = self.sem[eng]
        val = self.cnt[eng]

        def emit(e, waits=waits, fn=fn, sem=sem):
            for (s, v) in waits:
                e.wait_ge(s, v)
            fn(e).then_inc(sem, 1)

        self.prog[eng].append(emit)
        self._commit((sem, val, eng), reads, writes)

    def dma(self, q, out, in_, reads=(), writes=(), owner=None):
        waits = self._waits(q, reads, writes)
        own = owner if owner is not None else (writes[0] if writes else reads[0])
        if own.dsem is None:
            own.dsem = self.es.enter_context(self.nc.semaphore("dsem%d" % self.nsem))
            self.nsem += 1
        own.dcnt += 16
        sem, val = own.dsem, own.dcnt

        def emit(e, waits=waits, sem=sem, out=out, in_=in_):
            for (s, v) in waits:
                e.wait_ge(s, v)
            e.dma_start(out=out, in_=in_).then_inc(sem, 16)

        self.prog[q].append(emit)
        tok = (sem, val, "dma")
        for t in writes:
            t.w = [tok]
            t.r = []
        for t in reads:
            t.r.append(tok)
        return tok

    def wait_all(self, q, toks):
        def emit(e, toks=toks):
            for (s, v, _) in toks:
                e.wait_ge(s, v)
        self.prog[q].append(emit)

    def barrier(self, extra=()):
        for e in ENGS:
            waits = [(s, v) for (s, v, _) in extra]
            for o in ENGS:
                if o == "sp" or o == e or self.cnt[o] == 0:
                    continue
                if self.known[e].get(id(self.sem[o]), 0) < self.cnt[o]:
                    self.known[e][id(self.sem[o])] = self.cnt[o]
                    waits.append((self.sem[o], self.cnt[o]))

            def emit(en, waits=waits):
                for (s, v) in waits:
                    en.wait_ge(s, v)
            self.prog[e].append(emit)

    def run(self):
        nc = self.nc
        with nc.Block() as block:
            @block.tensor
            def _(e):
                for f in self.prog["pe"]:
                    f(e)

            @block.scalar
            def _(e):
                for f in self.prog["act"]:
                    f(e)

            @block.vector
            def _(e):
                for f in self.prog["dve"]:
                    f(e)

            @block.gpsimd
            def _(e):
                for f in self.prog["pool"]:
                    f(e)

            @block.sync
            def _(e):
                for f in self.prog["sp"]:
                    f(e)


def mkap(t, offset, dims):
    return bass.AP(tensor=t, offset=offset, ap=[list(d) for d in dims])


def _in_col_perm():
    cols = list(range(0, 1024))
    for j in range(4):
        for gi in range(3):
            for base in (1024, 2560, 4096):
                c0 = base + (gi * 4 + j) * 128
                cols += list(range(c0, c0 + 128))
        cols += list(range(5632 + j * 128, 5632 + (j + 1) * 128))
    cols += list(range(6144, 8192))
    assert len(cols) == 8192 and len(set(cols)) == 8192
    return np.asarray(cols)


def prep_host(inp):
    f32 = np.float32
    out = {}
    out["w_in"] = np.ascontiguousarray(np.asarray(inp["w_in"], f32)[:, :, _in_col_perm()])
    out["w_glu"] = np.ascontiguousarray(np.asarray(inp["w_glu"], f32))
    out["w_bs"] = np.ascontiguousarray(np.asarray(inp["w_branch_s"], f32))
    out["w_ba"] = np.ascontiguousarray(np.asarray(inp["w_branch_a"], f32))
    out["w_out"] = np.ascontiguousarray(np.asarray(inp["w_out"], f32))
    vecs = np.zeros((128, DEPTH * 24), f32)
    for l in range(DEPTH):
        vecs[:, l * 24 + 0:l * 24 + 8] = np.asarray(inp["pre_norm_g"], f32)[l].reshape(8, 128).T
        vecs[:, l * 24 + 8:l * 24 + 16] = np.asarray(inp["post_norm_g"], f32)[l].reshape(8, 128).T
        vecs[:, l * 24 + 16:l * 24 + 20] = np.asarray(inp["d_skip"], f32)[l].reshape(4, 128).T
        vecs[:, l * 24 + 20:l * 24 + 24] = np.asarray(inp["b_glu"], f32)[l].reshape(4, 128).T
    out["vecs"] = vecs
    lre = np.asarray(inp["lambda_re"], f32)
    lim = np.asarray(inp["lambda_im"], f32)
    ldt = np.asarray(inp["log_dt"], f32)
    bre = np.asarray(inp["b_re"], f32)
    bim = np.asarray(inp["b_im"], f32)
    cre = np.asarray(inp["c_re"], f32)
    cim = np.asarray(inp["c_im"], f32)
    rb = np.zeros((DEPTH, 128, 5, 4, 128), f32)
    rc = np.zeros((DEPTH, 128, 48 + 512), f32)
    for l in range(DEPTH):
        lam_g = lre[l].reshape(4, 4, 2, 64)
        lim_g = lim[l].reshape(4, 4, 2, 64)
        ldt_g = np.broadcast_to(ldt[l].reshape(4, 4, 2, 1), (4, 4, 2, 64))
        for ptl in range(4):
            rows = slice(32 * ptl, 32 * ptl + 32)
            rb[l, rows, 0] = lam_g[:, ptl].reshape(4, 128)[None]
            rb[l, rows, 1] = lim_g[:, ptl].reshape(4, 128)[None]
            rb[l, rows, 2] = ldt_g[:, ptl].reshape(4, 128)[None]
            for h in range(2):
                for j in range(4):
                    g = 8 * j + 2 * ptl + h
                    r0 = 32 * ptl + 16 * h
                    rb[l, r0:r0 + 16, 3, j, 64 * h:64 * h + 64] = bre[l, g].T
                    rb[l, r0:r0 + 16, 4, j, 64 * h:64 * h + 64] = bim[l, g].T
        rc[l, :, 0:16] = lre[l].reshape(16, 128).T
        rc[l, :, 16:32] = lim[l].reshape(16, 128).T
        rc[l, :, 32:48] = np.broadcast_to(ldt[l].reshape(16, 2, 1), (16, 2, 64)).reshape(16, 128).T
        cc = cre[l].reshape(16, 2, 16, 64).transpose(1, 3, 0, 2).reshape(128, 16 * 16)
        ci = cim[l].reshape(16, 2, 16, 64).transpose(1, 3, 0, 2).reshape(128, 16 * 16)
        rc[l, :, 48:48 + 256] = cc
        rc[l, :, 48 + 256:48 + 512] = ci
    out["ssm_rb"] = rb.reshape(DEPTH, 128, 5 * 512)
    out["ssm_rc"] = rc
    am = np.zeros((128, 256), f32)
    kk = np.arange(128)[:, None]
    qq = np.arange(128)[None, :]
    am[:, 0:128] = (qq >= kk)
    am[:, 128:256] = (qq <= kk)
    out["amask"] = am.astype(ml_dtypes.bfloat16)
    out["ident"] = np.eye(128, dtype=f32).astype(ml_dtypes.bfloat16)
    return out


class TL:
    def __init__(self, t, name):
        self.t = t
        self.r = Res(name)

    def __getitem__(self, idx):
        return self.t[idx]


class Builder:
    def __init__(self, n_layers=DEPTH, debug=None):
        self.n_layers = n_layers
        self.debug = debug or {}
        self.nc = bass.Bass("TRN2", target_bir_lowering=False)
        self.es = ExitStack()
        self.S = Sched(self.nc, self.es)
        self.uid = 0

    def sb(self, es, shape, dtype, name=None):
        self.uid += 1
        nm = (name or "t") + "_%d" % self.uid
        return TL(es.enter_context(self.nc.sbuf_tensor(nm, list(shape), dtype)), nm)

    def ps(self, es, shape, dtype=F32, name=None):
        self.uid += 1
        nm = (name or "p") + "_%d" % self.uid
        return TL(es.enter_context(self.nc.psum_tensor(nm, list(shape), dtype)), nm)

    @staticmethod
    def _res(lst):
        return [x.r if isinstance(x, TL) else x for x in lst]

    def tt(self, eng, out, a, b, op, R, W):
        self.S.op(eng, lambda e: e.tensor_tensor(out=out, in0=a, in1=b, op=op), self._res(R), self._res(W))

    def ts(self, eng, out, a, s1, op0, R, W, s2=None, op1=None):
        if op1 is None:
            self.S.op(eng, lambda e: e.tensor_scalar(out=out, in0=a, scalar1=s1, scalar2=None, op0=op0),
                      self._res(R), self._res(W))
        else:
            self.S.op(eng, lambda e: e.tensor_scalar(out=out, in0=a, scalar1=s1, scalar2=s2, op0=op0, op1=op1),
                      self._res(R), self._res(W))

    def stt(self, out, a, scalar, b, op0, op1, R, W):
        self.S.op("dve", lambda e: e.scalar_tensor_tensor(out=out, in0=a, scalar=scalar, in1=b, op0=op0, op1=op1),
                  self._res(R), self._res(W))

    def act(self, out, a, func, R, W, scale=1.0, bias=0.0):
        self.S.op("act", lambda e: e.activation(out=out, in_=a, func=func, bias=bias, scale=scale),
                  self._res(R), self._res(W))

    def copy(self, eng, out, a, R, W):
        if eng == "act":
            self.S.op("act", lambda e: e.activation(out=out, in_=a, func=AF.Copy), self._res(R), self._res(W))
        else:
            self.S.op(eng, lambda e: e.tensor_copy(out=out, in_=a), self._res(R), self._res(W))

    def memset(self, eng, ap, val, W):
        self.S.op(eng, lambda e: e.memset(ap, val), [], self._res(W))

    def mm(self, out, lhsT, rhs, start, stop, R, W, tp=None):
        if tp is None:
            self.S.op("pe", lambda e: e.matmul(out, lhsT=lhsT, rhs=rhs, start=start, stop=stop),
                      self._res(R), self._res(W))
        else:
            self.S.op("pe", lambda e: e.matmul(out, lhsT=lhsT, rhs=rhs, start=start, stop=stop, tile_position=tp),
                      self._res(R), self._res(W))

    def dma(self, q, out, in_, R, W, owner=None):
        return self.S.dma(q, out, in_, self._res(R), self._res(W),
                          owner.r if isinstance(owner, TL) else owner)

    def cmul(self, eng, o, a, b, t1, t2, R, W, Tm):
        (orr, oi), (ar, ai), (br, bi) = o, a, b
        self.tt(eng, t1, ar, br, ALU.mult, R, Tm)
        self.tt(eng, t2, ai, bi, ALU.mult, R, Tm)
        self.tt(eng, orr, t1, t2, ALU.subtract, Tm, W)
        self.tt(eng, t1, ar, bi, ALU.mult, R, Tm)
        self.tt(eng, t2, ai, br, ALU.mult, R, Tm)
        self.tt(eng, oi, t1, t2, ALU.add, Tm, W)

    def declare_io(self):
        nc = self.nc
        nl = DEPTH
        d = {}
        d["xT"] = nc.dram_tensor("xT", [D, L], F32, kind="ExternalInput").ap()
        d["w_in"] = nc.dram_tensor("w_in", [nl, D, 8192], F32, kind="ExternalInput").ap()
        d["w_glu"] = nc.dram_tensor("w_glu", [nl, 512, 512], F32, kind="ExternalInput").ap()
        d["w_bs"] = nc.dram_tensor("w_bs", [nl, 512, D], F32, kind="ExternalInput").ap()
        d["w_ba"] = nc.dram_tensor("w_ba", [nl, 512, D], F32, kind="ExternalInput").ap()
        d["w_out"] = nc.dram_tensor("w_out", [nl, D, D], F32, kind="ExternalInput").ap()
        d["vecs"] = nc.dram_tensor("vecs", [128, nl * 24], F32, kind="ExternalInput").ap()
        d["ssm_rb"] = nc.dram_tensor("ssm_rb", [nl, 128, 2560], F32, kind="ExternalInput").ap()
        d["ssm_rc"] = nc.dram_tensor("ssm_rc", [nl, 128, 560], F32, kind="ExternalInput").ap()
        d["amask"] = nc.dram_tensor("amask", [128, 256], BF16, kind="ExternalInput").ap()
        d["ident"] = nc.dram_tensor("ident", [128, 128], BF16, kind="ExternalInput").ap()
        d["yT"] = nc.dram_tensor("yT", [D, L], F32, kind="ExternalOutput").ap()
        tk = "ExternalOutput" if self.debug.get("tables") else "Internal"
        d["tabB"] = nc.dram_tensor("tabB", [nl, 128, 16384], BF16, kind=tk).ap()
        d["tabC"] = nc.dram_tensor("tabC", [nl, 128, 16384], BF16, kind=tk).ap()
        for name, (shape, dt) in self.debug.get("outs", {}).items():
            d[name] = nc.dram_tensor(name, list(shape), dt, kind="ExternalOutput").ap()
        self.io = d

    def lam_alloc(self, es, Fd, tag):
        names = ("dt", "lr", "a", "th", "mag", "em2a", "c", "s", "t1", "t2", "lbr", "lbi")
        return {n: self.sb(es, [128, Fd], F32, tag + n) for n in names}

    def lam_math(self, m, src):
        B = self
        dt, lr, a, th, mag, em2a, c, s, t1, t2, lbr, lbi = [m[n] for n in
                                                          ("dt", "lr", "a", "th", "mag", "em2a", "c", "s", "t1", "t2", "lbr", "lbi")]
        R = src["R"]
        B.act(dt[:], src["ldt"], AF.Exp, R, [dt])
        B.ts("dve", lr[:], src["lre"], -1e-4, ALU.min, R, [lr])
        B.tt("dve", a[:], lr[:], dt[:], ALU.mult, [lr, dt], [a])
        B.tt("dve", th[:], src["lim"], dt[:], ALU.mult, R + [dt], [th])
        B.act(mag[:], a[:], AF.Exp, [a], [mag])
        B.act(em2a[:], a[:], AF.Exp, [a], [em2a], scale=-2.0)
        B.act(s[:], th[:], AF.Sin, [th], [s], scale=1.0 / 16.0)
        B.act(c[:], th[:], AF.Sin, [th, self.halfpi], [c], scale=1.0 / 16.0, bias=self.halfpi[:, 0:1])
        for _ in range(4):
            B.tt("dve", t1[:], c[:], c[:], ALU.mult, [c], [t1])
            B.tt("dve", t2[:], s[:], s[:], ALU.mult, [s], [t2])
            B.stt(s[:], c[:], 2.0, s[:], ALU.mult, ALU.mult, [c, s], [s])
            B.tt("dve", c[:], t1[:], t2[:], ALU.subtract, [t1, t2], [c])
        B.tt("dve", lbr[:], mag[:], c[:], ALU.mult, [mag, c], [lbr])
        B.tt("dve", lbi[:], mag[:], s[:], ALU.mult, [mag, s], [lbi])

    def prologue(self):
        B = self
        io = self.io
        toks = []
        with ExitStack() as es:
            mb = B.lam_alloc(es, 512, "rb")
            inpb = B.sb(es, [128, 5, 512], F32, "rbin")
            mk = lambda n: B.sb(es, [128, 512], F32, "rb" + n)
            den, nr, cr, ci, invr, invi, bbr, bbi = [mk(n) for n in ("den", "nr", "cr", "ci", "invr", "invi", "bbr", "bbi")]
            xs = [(mk("xr0"), mk("xi0")), (mk("xr1"), mk("xi1"))]
            bt = B.sb(es, [128, 4, TS, 2, 128], BF16, "bt")
            pinr, pini, pt1, ppw = [B.ps(es, [128, 512], F32, n) for n in ("pinr", "pini", "pt1", "ppw")]
            mc = B.lam_alloc(es, 16, "rc")
            inpc = B.sb(es, [128, 560], F32, "rcin")
            pw = B.sb(es, [128, 2, 16, TS], F32, "pw")
            ct = B.sb(es, [128, 16, TS, 2, 32], BF16, "ct")
            big = lambda n: B.sb(es, [128, 16, TS, 16], F32, "rc" + n)
            u1, u2, u3 = big("u1"), big("u2"), big("u3")
            B.memset("pool", ct[:], 0.0, [ct])
            for l in range(self.n_layers):
                inp = inpb
                B.dma("sp", inp[:], io["ssm_rb"][l].rearrange("p (a f) -> p a f", a=5), [], [inp])
                B.lam_math(mb, dict(lre=inp[:, 0, :], lim=inp[:, 1, :], ldt=inp[:, 2, :], R=[inp]))
                lr, lbr, lbi, em2a, t1, t2 = [mb[n] for n in ("lr", "lbr", "lbi", "em2a", "t1", "t2")]
                li = inp[:, 1, :]
                B.tt("dve", t1[:], lr[:], lr[:], ALU.mult, [lr], [t1])
                B.tt("dve", t2[:], li, li, ALU.mult, [inp], [t2])
                B.tt("dve", den[:], t1[:], t2[:], ALU.add, [t1, t2], [den])
                B.S.op("dve", lambda e, den=den: e.reciprocal(out=den[:], in_=den[:]), [den.r], [den.r])
                B.ts("dve", nr[:], lbr[:], -1.0, ALU.add, [lbr], [nr])
                B.tt("dve", t1[:], nr[:], lr[:], ALU.mult, [nr, lr], [t1])
                B.tt("dve", t2[:], lbi[:], li, ALU.mult, [lbi, inp], [t2])
                B.tt("dve", t1[:], t1[:], t2[:], ALU.add, [t1, t2], [t1])
                B.tt("dve", cr[:], t1[:], den[:], ALU.mult, [t1, den], [cr])
                B.tt("dve", t1[:], lbi[:], lr[:], ALU.mult, [lbi, lr], [t1])
                B.tt("dve", t2[:], nr[:], li, ALU.mult, [nr, inp], [t2])
                B.tt("dve", t1[:], t1[:], t2[:], ALU.subtract, [t1, t2], [t1])
                B.tt("dve", ci[:], t1[:], den[:], ALU.mult, [t1, den], [ci])
                B.tt("dve", invr[:], lbr[:], em2a[:], ALU.mult, [lbr, em2a], [invr])
                B.stt(invi[:], lbi[:], -1.0, em2a[:], ALU.mult, ALU.mult, [lbi, em2a], [invi])
                B.cmul("dve", (bbr[:], bbi[:]), (cr[:], ci[:]), (inp[:, 3, :], inp[:, 4, :]), t1[:], t2[:],
                       [cr, ci, inp], [bbr, bbi], [t1, t2])
                prev = (bbr, bbi)
                B.copy("dve", pinr[:], invr[:], [invr], [pinr])
                B.copy("dve", pini[:], invi[:], [invi], [pini])
                for i in range(TS):
                    cur = xs[i % 2]
                    B.cmul("dve", (cur[0][:], cur[1][:]), (prev[0][:], prev[1][:]), (pinr[:], pini[:]), pt1[:], t2[:],
                           [prev[0], prev[1], pinr, pini], [cur[0], cur[1]], [pt1, t2])
                    B.copy("act", bt[:, :, i, 0, :], cur[0][:].rearrange("p (j q) -> p j q", j=4), [cur[0]], [bt])
                    B.copy("act", bt[:, :, i, 1, :], cur[1][:].rearrange("p (j q) -> p j q", j=4), [cur[1]], [bt])
                    prev = cur
                toks.append(B.dma("sp", io["tabB"][l], bt[:].rearrange("p j i r q -> p (j i r q)"),
                                  [bt], [self.tabB_r[l]], owner=bt))
                inp = inpc
                B.dma("sp", inp[:], io["ssm_rc"][l], [], [inp])
                B.lam_math(mc, dict(lre=inp[:, 0:16], lim=inp[:, 16:32], ldt=inp[:, 32:48], R=[inp]))
                lbr, lbi, t1, t2 = [mc[n] for n in ("lbr", "lbi", "t1", "t2")]
                B.copy("dve", pw[:, 0, :, 0], lbr[:], [lbr], [pw])
                B.copy("dve", pw[:, 1, :, 0], lbi[:], [lbi], [pw])
                for i in range(1, TS):
                    B.cmul("dve", (pw[:, 0, :, i], pw[:, 1, :, i]), (pw[:, 0, :, i - 1], pw[:, 1, :, i - 1]),
                           (lbr[:], lbi[:]), t1[:], t2[:], [pw, lbr, lbi], [pw], [t1, t2])
                PL = self.PL
                B.copy("dve", PL[:, l, :, 0], pw[:, 0, :, TS - 1], [pw], [PL])
                B.copy("dve", PL[:, l, :, 1], pw[:, 1, :, TS - 1], [pw], [PL])
                for k in range(1, 7):
                    pr, pi = PL[:, l, :, 3 * k - 3], PL[:, l, :, 3 * k - 2]
                    B.tt("dve", t1[:], pr, pr, ALU.mult, [PL], [t1])
                    B.tt("dve", t2[:], pi, pi, ALU.mult, [PL], [t2])
                    B.tt("dve", PL[:, l, :, 3 * k], t1[:], t2[:], ALU.subtract, [t1, t2], [PL])
                    B.stt(PL[:, l, :, 3 * k + 1], pr, 2.0, pi, ALU.mult, ALU.mult, [PL], [PL])
                for k in range(7):
                    B.ts("dve", PL[:, l, :, 3 * k + 2], PL[:, l, :, 3 * k + 1], -1.0, ALU.mult, [PL], [PL])
                cre = inp[:, 48:304].rearrange("p (t c) -> p t c", c=16)
                cim = inp[:, 304:560].rearrange("p (t c) -> p t c", c=16)

                def bc_c(ap3):
                    return mkap(ap3.tensor, ap3.offset, [ap3.ap[0], ap3.ap[1], [0, TS], ap3.ap[2]])

                B.copy("dve", ppw[:], pw[:].rearrange("p r t i -> p (r t i)"), [pw], [ppw])
                ppw4 = ppw[:].rearrange("p (r t i) -> p r t i", r=2, t=16)

                def bc_p(k):
                    a = ppw4[:, k, :, :]
                    return mkap(a.tensor, a.offset, [a.ap[0], a.ap[1], a.ap[2], [0, 16]])
                B.tt("dve", u1[:], bc_c(cre), bc_p(0), ALU.mult, [inp, ppw], [u1])
                B.tt("dve", u2[:], bc_c(cim), bc_p(1), ALU.mult, [inp, ppw], [u2])
                B.tt("dve", u3[:], u1[:], u2[:], ALU.subtract, [u1, u2], [u3])
                B.copy("act", ct[0:64, :, :, 0, 0:16], u3[0:64], [u3], [ct])
                B.copy("act", ct[64:128, :, :, 0, 16:32], u3[64:128], [u3], [ct])
                B.tt("dve", u1[:], bc_c(cre), bc_p(1), ALU.mult, [inp, ppw], [u1])
                B.tt("dve", u2[:], bc_c(cim), bc_p(0), ALU.mult, [inp, ppw], [u2])
                B.stt(u3[:], u1[:], -1.0, u2[:], ALU.mult, ALU.subtract, [u1, u2], [u3])
                B.copy("act", ct[0:64, :, :, 1, 0:16], u3[0:64], [u3], [ct])
                B.copy("act", ct[64:128, :, :, 1, 16:32], u3[64:128], [u3], [ct])
                toks.append(B.dma("sp", io["tabC"][l], ct[:].rearrange("p t i r c -> p (t i r c)"),
                                  [ct], [self.tabC_r[l]], owner=ct))
        self.S.barrier(toks)

    def build(self):
        B = self
        nc = self.nc
        es = self.es
        self.declare_io()
        io = self.io
        self.tabB_r = [Res("tabB%d" % l) for l in range(DEPTH)]
        self.tabC_r = [Res("tabC%d" % l) for l in range(DEPTH)]
        self.halfpi = B.sb(es, [128, 1], F32, "halfpi")
        B.memset("dve", self.halfpi[:], math.pi / 2.0, [self.halfpi])
        self.PL = B.sb(es, [128, DEPTH, 16, 24], F32, "PL")
        self.prologue()
        if self.debug.get("tables"):
            self.finish([])
            return
        self.main()

    def finish(self, out_toks):
        self.S.wait_all("sp", out_toks)
        self.S.barrier(out_toks)
        self.S.run()


class V:
    def __init__(self, ap, name=""):
        self.ap = ap
        self.r = Res(name)
        self.lo = self.hi = None

    def __getitem__(self, idx):
        return self.ap[idx]


def _res_of(lst):
    return [x.r if hasattr(x, "r") and not isinstance(x, Res) else x for x in lst]


Builder._res = staticmethod(_res_of)


def dil_ap(base, r, col0, ncols):
    t, off, pdim, st = base.tensor, base.offset, list(base.ap[0]), base.ap[-1][0]
    assert len(base.ap) == 2
    nsub = L // r
    c0, i0 = divmod(col0, nsub)
    if r == 1:
        dims, o = [[st, ncols]], col0
    elif i0 + ncols <= nsub:
        dims, o = [[r * st, ncols]], c0 + r * i0
    else:
        assert i0 == 0 and ncols % nsub == 0
        dims, o = [[st, ncols // nsub], [r * st, nsub]], c0
    return mkap(t, off + o * st, [pdim] + dims)


def deint_dst(base, r, n):
    if r == 1:
        return base[:, 512 * n:512 * n + 512]
    nsub = L // r
    t, off, pdim, st = base.tensor, base.offset, list(base.ap[0]), base.ap[-1][0]
    return mkap(t, off + (512 * n // r) * st, [pdim, [st, 512 // r], [nsub * st, r]])


def like(bank_ap, ap):
    if len(ap.ap) == 3:
        return bank_ap.rearrange("p (a b) -> p a b", a=ap.ap[1][1])
    return bank_ap


AR_BYTES = 88064


def _main(self):
    B = self
    es = self.es
    io = self.io
    S = self.S
    nc = self.nc
    dbg = self.debug
    xT = B.sb(es, [128, NK, L], F32, "xT")
    xT_r = [[Res("xT%d_%d" % (k, n)) for n in range(4)] for k in range(NK)]
    hT = B.sb(es, [128, NK, L], BF16, "hT")
    hT_r = [Res("hT%d" % k) for k in range(NK)]
    vecs = B.sb(es, [128, DEPTH * 24], F32, "vecs")
    amask = B.sb(es, [128, 256], BF16, "amask")
    ident = B.sb(es, [128, 128], BF16, "ident")
    ones = B.sb(es, [128, 128], BF16, "ones")
    epsT = B.sb(es, [128, 1], F32, "eps")
    wbuf = [B.sb(es, [128, NK, 512], BF16, "wbuf%d" % i) for i in range(2)]
    arena = B.sb(es, [128, AR_BYTES // 2], BF16, "arena")
    banks = [B.ps(es, [128, 512], F32, "bank%d" % i) for i in range(8)]
    self.wi = 0

    def carve(off, shape, dtype, name):
        nel = int(np.prod(shape[1:]))
        esz = 2 if dtype == BF16 else 4
        assert off % 4 == 0 and off + nel * esz <= AR_BYTES, (name, off, nel * esz)
        a = arena.t[:, off // 2: off // 2 + nel * esz // 2]
        if dtype == F32:
            a = a.bitcast(F32)
        if len(shape) == 3:
            a = a.rearrange("p (a b) -> p a b", a=shape[1])
        elif len(shape) == 4:
            a = a.rearrange("p (a b c) -> p a b c", a=shape[1], b=shape[2])
        elif len(shape) == 5:
            a = a.rearrange("p (a b c d) -> p a b c d", a=shape[1], b=shape[2], c=shape[3])
        v = V(a, name)
        region(off, off + nel * esz, [v.r])
        v.lo, v.hi = off, off + nel * esz
        return v

    regs = []

    def region(lo, hi, res_list):
        inherit = []
        keep = []
        for (l0, h0, rl) in regs:
            if l0 < hi and lo < h0:
                for r_ in rl:
                    inherit += r_.w + r_.r
                if lo <= l0 and h0 <= hi:
                    continue
            keep.append((l0, h0, rl))
        regs[:] = keep
        best = {}
        for (sm, vl, src) in inherit:
            if id(sm) not in best or best[id(sm)][1] < vl:
                best[id(sm)] = (sm, vl, "alias")
        for r_ in res_list:
            r_.r = list(best.values())
        regs.append((lo, hi, res_list))

    def reslist(v, names):
        rl = [Res(n) for n in names]
        region(v.lo, v.hi, rl)
        return rl

    B.memset("dve", ones[:], 1.0, [ones])
    B.memset("dve", epsT[:], 1e-6, [epsT])
    B.dma("sp", vecs[:], io["vecs"], [], [vecs])
    B.dma("sp", amask[:], io["amask"], [], [amask])
    B.dma("sp", ident[:], io["ident"], [], [ident])
    xin = io["xT"].rearrange("(k p) t -> p k t", p=128)
    for k in range(NK):
        B.dma("sp", xT[:, k, :], xin[:, k, :], [], xT_r[k], owner=xT_r[k][0])

    out_toks = []

    def dump(name, ap, R):
        out_toks.append(B.dma("sp", io[name], ap, R, [], owner=Res("dump_" + name)))

    def load_wblock(l, col0, ncols):
        wb = wbuf[self.wi % 2]
        self.wi += 1
        src = io["w_in"][l][:, col0:col0 + ncols].rearrange("(k p) c -> p k c", p=128)
        B.dma("pool", wb[:, :, 0:ncols], src, [], [wb])
        return wb

    def load_w(dst, src2d):
        B.dma("pool", dst[:], src2d.rearrange("(k p) c -> p k c", p=128), [], [dst])

    bank_ctr = [0]

    def nb(lst):
        b = banks[lst[bank_ctr[0] % len(lst)]]
        bank_ctr[0] += 1
        return b

    def hk(k):
        return hT.t[:, k, :]

    for l in range(self.n_layers):
        vb = l * 24
        sq = [carve(i * 8192, [128, NK, 512], BF16, "sq%d" % i) for i in range(2)]
        rt = [carve(16384 + i * 2048, [128, 512], F32, "rt%d" % i) for i in range(2)]
        rs = [carve(20480 + i * 2048, [128, 512], F32, "rs%d" % i) for i in range(2)]
        for n in range(4):
            ns = slice(n * 512, (n + 1) * 512)
            s_, rt_, rs_ = sq[n % 2], rt[n % 2], rs[n % 2]
            for k in range(NK):
                B.act(s_[:, k, :], xT[:, k, ns], AF.Square, [xT_r[k][n]], [s_])
            bk = nb([6, 7])
            for k in range(NK):
                B.mm(bk[:], ones[:], s_[:, k, :], k == 0, k == NK - 1, [ones, s_], [bk])
            B.act(rt_[:], bk[:], AF.Sqrt, [bk, epsT], [rt_], scale=1.0 / D, bias=epsT[:, 0:1])
            S.op("dve", lambda e, a=rs_, b=rt_: e.reciprocal(out=a[:], in_=b[:]), [rt_.r], [rs_.r])
            for k in range(NK):
                B.stt(hT[:, k, ns], xT[:, k, ns], vecs[:, vb + k:vb + k + 1], rs_[:], ALU.mult, ALU.mult,
                      [xT_r[k][n], vecs, rs_], [hT_r[k]])
        if dbg.get("stop") == "P0":
            dump("dbg_h", hT[:], hT_r)
            break
        pass
        yT = carve(32768, [128, 4, L], BF16, "yT")
        _fl = reslist(yT, ["yT%d_%d" % (k, n) for k in range(4) for n in range(4)])
        yT_r = [[_fl[4 * k + n] for n in range(4)] for k in range(4)]
        r32 = carve(0, [128, 2, TS, 128], F32, "r32")
        rbf = carve(16384, [128, 2, TS, 128], BF16, "rbf")
        Bt = carve(24576, [128, TS, 2, 128], BF16, "Bt")
        Ct = carve(49152, [128, 5120], BF16, "Ct")
        CtA = Ct[:, 0:3072].rearrange("p (t i r c) -> p t i r c", t=3, i=TS, r=2)
        CtB = Ct[:, 3072:5120].rearrange("p (i r c) -> p i r c", i=TS, r=2)
        uT = carve(59392, [128, L], BF16, "uT")
        wglu = carve(63488, [128, 4, 512], BF16, "wglu")
        Sab = [carve(67584 + i * 1536, [128, 2, 192], F32, "S%d" % i) for i in range(2)]
        zt = carve(70656, [128, 2, 192], F32, "zt")
        tq = carve(72192, [128, 1, 128], F32, "tq")
        tq2 = carve(72704, [128, 1, 128], F32, "tq2")
        Sbf = carve(73216, [128, 2, 128], BF16, "Sbf")
        ddt = [carve(73728 + i * 256, [128, 128], BF16, "dd%d" % i) for i in range(2)]
        gw = [carve(77824 + i * 2048, [128, 512], F32, "gw%d" % i) for i in range(2)]
        gs = [carve(81920 + i * 2048, [128, 512], F32, "gs%d" % i) for i in range(2)]
        gz = [carve(86016 + i * 1024, [128, 512], BF16, "gz%d" % i) for i in range(2)]
        for t_ in (Sab[0], Sab[1], zt):
            B.memset("dve", t_[:, :, 0:64], 0.0, [t_])
        Ctz = V(CtB[:, :, :, 0:32], "Ctz")
        region(Ct.lo, Ct.hi, [Ctz.r])
        B.memset("pool", Ctz[:], 0.0, [Ctz])
        B.memset("dve", Sbf[:], 0.0, [Sbf])
        wbU = load_wblock(l, 0, 512)
        wbZ = load_wblock(l, 512, 512)
        load_w(wglu, io["w_glu"][l])
        PLl = self.PL
        def prelude_u(j):
            for n in range(4):
                bk = nb([6, 7])
                for k in range(NK):
                    B.mm(bk[:], wbU[:, k, 128 * j:128 * j + 128], hT[:, k, 512 * n:512 * n + 512],
                         k == 0, k == NK - 1, [wbU, hT_r[k]], [bk])
                B.copy("act", deint_dst(uT[:], 16, n), bk[:].rearrange("p (a b) -> p a b", b=16), [bk], [uT])
            B.dma("sp", Bt[:].rearrange("p i r q -> p (i r q)"), io["tabB"][l][:, 4096 * j:4096 * (j + 1)],
                  [self.tabB_r[l]], [Bt])

        def prelude_c(j):
            B.dma("sp", Ct[:, 0:3072], io["tabC"][l][:, 4096 * j:4096 * j + 3072], [self.tabC_r[l]], [Ct])
            B.dma("sp", CtB[:, :, :, 32:64],
                  io["tabC"][l][:, 4096 * j + 3072:4096 * (j + 1)].rearrange("p (i r c) -> p i r c", i=TS, r=2),
                  [self.tabC_r[l]], [Ct])

        prelude_u(0)
        prelude_c(0)
        for j in dbg.get("js", range(4)):
            dd = ddt[j % 2]
            B.ts("dve", dd[:], ident[:], vecs[:, vb + 16 + j:vb + 17 + j], ALU.mult, [ident, vecs], [dd])
            for ib in range(4):
                B.mm(banks[ib][:], dd[:], uT[:, 512 * ib:512 * ib + 512], True, False, [dd, uT], [banks[ib]])
            def xt_chain(ptl):
                rows = slice(32 * ptl, 32 * ptl + 32)
                for ih in range(8):
                    xb = nb([4, 5])
                    xv = xb[:].rearrange("p (r i c) -> p r i c", r=2, i=2)
                    for ri in range(2):
                        for il in range(2):
                            i = 2 * ih + il
                            B.mm(xv[:, ri, il, :], Bt[rows, i, ri, :], uT[rows, i * 128:(i + 1) * 128],
                                 True, True, [Bt, uT], [xb], tp=(32 * ptl, 0))
                    for il in range(2):
                        i = 2 * ih + il
                        if i == 0:
                            B.copy("dve", r32[:, :, 0, :], xv[:, :, 0, :], [xb], [r32])
                        else:
                            B.tt("dve", r32[:, :, i, :], r32[:, :, i - 1, :], xv[:, :, il, :], ALU.add,
                                 [xb, r32], [r32])
                        yield

            def l2(ptl):
                pt = 4 * j + ptl
                B.copy("act", rbf[:], r32[:], [r32], [rbf])
                pl = lambda c: PLl[:, l, pt, c:c + 1]
                rr, rim = r32[:, 0, TS - 1, :], r32[:, 1, TS - 1, :]
                B.ts("dve", tq[:, 0, :], rim, pl(1), ALU.mult, [r32, PLl], [tq])
                B.stt(zt[:, 0, 64:192], rr, pl(0), tq[:, 0, :], ALU.mult, ALU.subtract, [r32, PLl, tq], [zt])
                B.ts("dve", tq2[:, 0, :], rr, pl(1), ALU.mult, [r32, PLl], [tq2])
                B.stt(zt[:, 1, 64:192], rim, pl(0), tq2[:, 0, :], ALU.mult, ALU.add, [r32, PLl, tq2], [zt])
                yield "z"
                old = zt
                for kk in range(7):
                    d = 1 << kk
                    new = Sab[kk % 2]
                    a_, b_, nb_ = pl(3 * kk), pl(3 * kk + 1), pl(3 * kk + 2)
                    B.stt(new[:, :, 64:192], old[:, :, 64 - d:192 - d], a_, old[:, :, 64:192], ALU.mult, ALU.add, [old, PLl], [new])
                    yield
                    B.stt(new[:, 0, 64:192], old[:, 1, 64 - d:192 - d], nb_, new[:, 0, 64:192], ALU.mult, ALU.add, [old, new, PLl], [new])
                    yield
                    B.stt(new[:, 1, 64:192], old[:, 0, 64 - d:192 - d], b_, new[:, 1, 64:192], ALU.mult, ALU.add, [old, new, PLl], [new])
                    yield
                    old = new
                B.copy("dve", Sbf[:, :, 1:128], old[:, :, 64:191], [old], [Sbf])
                yield

            def interleave(g1, g2):
                for _ in g1:
                    if g2 is not None:
                        if next(g2, "end") == "end":
                            g2 = None
                if g2 is not None:
                    for _ in g2:
                        pass

            def readout(ptl):
                rows = slice(32 * ptl, 32 * ptl + 32)
                for i in range(TS):
                    yb = banks[i // 4]
                    cs_ = slice((i % 4) * 128, (i % 4) * 128 + 128)
                    if ptl == 3:
                        o_ = yb[64:128, cs_]
                        tp = (0, 64)
                        lh = [CtB[:, i, rr_, :] for rr_ in range(2)]
                        RC_ = [Ct, Ctz]
                    else:
                        o_ = yb[rows, cs_]
                        tp = (0, 32 * ptl)
                        lh = [CtA[:, ptl, i, rr_, :] for rr_ in range(2)]
                        RC_ = [Ct]
                    B.mm(o_, lh[0], rbf[:, 0, i, :], False, False, RC_ + [rbf], [yb], tp=tp)
                    B.mm(o_, lh[1], rbf[:, 1, i, :], False, False, RC_ + [rbf], [yb], tp=tp)
                    B.mm(o_, lh[0], Sbf[:, 0, :], False, False, RC_ + [Sbf], [yb], tp=tp)
                    B.mm(o_, lh[1], Sbf[:, 1, :], False, True, RC_ + [Sbf], [yb], tp=tp)

            order = (0, 1, 3, 2)
            for _ in xt_chain(order[0]):
                pass
            for idx, ptl in enumerate(order):
                g1 = l2(ptl)
                for step in g1:
                    if step == "z":
                        break
                g2 = xt_chain(order[idx + 1]) if idx + 1 < 4 else None
                interleave(g1, g2)
                if idx == 2 and j + 1 < 4:
                    prelude_u(j + 1)
                readout(ptl)
            if j + 1 < 4:
                prelude_c(j + 1)
            for ib in range(4):
                cs = slice(512 * ib, 512 * ib + 512)
                w_, g_, yb = gw[ib % 2], gs[ib % 2], banks[ib]
                B.act(w_[:], yb[:], AF.Square, [yb], [w_])
                B.ts("pool", w_[:], w_[:], 0.044715, ALU.mult, [w_], [w_], s2=1.0, op1=ALU.add)
                B.tt("dve", w_[:], w_[:], yb[:], ALU.mult, [w_, yb], [w_])
                B.act(g_[:], w_[:], AF.Sigmoid, [w_], [g_], scale=1.5957691216057308)
                B.tt("dve", yT[:, j, cs], yb[:], g_[:], ALU.mult, [yb, g_], [yT_r[j][ib]])
        if dbg.get("stop") == "PA1":
            dump("dbg_y", yT[:], [x for row in yT_r for x in row])
            break
        szf = [carve(4096 * mo, [128, L], BF16, "szf%d" % mo) for mo in range(4)]
        for mo in range(4):
            for n in range(4):
                zb = nb([4, 5])
                for k in range(NK):
                    B.mm(zb[:], wbZ[:, k, 128 * mo:128 * mo + 128], hT[:, k, 512 * n:512 * n + 512],
                         k == 0, k == NK - 1, [wbZ, hT_r[k]], [zb])
                B.act(deint_dst(szf[mo][:], 16, n), zb[:].rearrange("p (a b) -> p a b", b=16), AF.Silu, [zb], [szf[mo]])
        for n in range(4):
            cs = slice(512 * n, 512 * n + 512)
            for mo in range(4):
                gb = banks[mo]
                for k in range(4):
                    B.mm(gb[:], wglu[:, k, 128 * mo:128 * mo + 128], yT[:, k, cs], k == 0, k == 3,
                         [wglu, yT_r[k][n]], [gb])
            for mo in range(4):
                g_ = gs[mo % 2]
                B.act(g_[:], banks[mo][:], AF.Sigmoid, [banks[mo], vecs], [g_], bias=vecs[:, vb + 20 + mo:vb + 21 + mo])
                B.tt("dve", g_[:], g_[:], szf[mo][:, cs], ALU.mult, [g_, szf[mo]], [g_])
                B.tt("dve", yT[:, mo, cs], yT[:, mo, cs], g_[:], ALU.mult, [yT_r[mo][n], g_], [yT_r[mo][n]])
        if dbg.get("stop") == "PA":
            dump("dbg_y", yT[:], [x for row in yT_r for x in row])
            break
        pass
        merged = carve(0, [128, NK, L], BF16, "merged")
        merged_r = reslist(merged, ["mg%d" % m for m in range(NK)])
        wbs = carve(49152, [128, 4, D], BF16, "wbs")
        sgg = [carve(57344 + i * 4096, [128, L], BF16, "sgg%d" % i) for i in range(2)]
        load_w(wbs, io["w_bs"][l])
        for m in range(NK):
            if m % 4 == 0:
                wb = load_wblock(l, 6144 + 512 * (m // 4), 512)
            sg_ = sgg[m % 2]
            for n in range(4):
                bk = nb([6, 7])
                for k in range(NK):
                    B.mm(bk[:], wb[:, k, 128 * (m % 4):128 * (m % 4) + 128], hT[:, k, 512 * n:512 * n + 512],
                         k == 0, k == NK - 1, [wb, hT_r[k]], [bk])
                B.act(sg_[:, 512 * n:512 * n + 512], bk[:], AF.Sigmoid, [bk], [sg_])
            for n2 in range(4):
                bk = nb([0, 1, 2, 3])
                for k in range(4):
                    B.mm(bk[:], wbs[:, k, 128 * m:128 * m + 128], yT[:, k, 512 * n2:512 * n2 + 512], k == 0, k == 3,
                         [wbs, yT_r[k][n2]], [bk])
                dst = dil_ap(merged[:, m, :], 16, 512 * n2, 512)
                B.tt("dve", dst, like(bk[:], dst), dil_ap(sg_[:], 16, 512 * n2, 512), ALU.mult,
                     [bk, sg_], [merged_r[m]])
        if dbg.get("stop") == "PA2":
            dump("dbg_m", merged[:], merged_r)
            break
        pass
        yaT = carve(32768, [128, 4, L], BF16, "yaT")
        yaT_r = reslist(yaT, ["ya%d" % j for j in range(4)])
        qT = carve(49152, [128, L], BF16, "qT")
        kT = carve(53248, [128, L], BF16, "kT")
        vd = carve(57344, [128, 16, 128], BF16, "vd")
        etm = [carve(61440 + i * 512, [128, 256], BF16, "etm%d" % i) for i in range(6)]
        ett = [carve(64512 + i * 512, [128, 256], BF16, "ett%d" % i) for i in range(4)]
        accn = carve(66560, [128, L], F32, "accn")
        accd = carve(74752, [128, L], F32, "accd")
        sza = carve(82944, [128, L], BF16, "sza")
        for j in range(4):
            for gi, r in enumerate((1, 4, 16)):
                ncols = 512 if gi == 2 else 384
                wb = load_wblock(l, 1024 + 1280 * j + 384 * gi, ncols)
                nsub = L // r
                nblk = nsub // 128
                for (c_in, dst) in ((0, qT), (128, kT)):
                    for n in range(4):
                        bk = nb([6, 7])
                        for k in range(NK):
                            B.mm(bk[:], wb[:, k, c_in:c_in + 128], hT[:, k, 512 * n:512 * n + 512], k == 0, k == NK - 1,
                                 [wb, hT_r[k]], [bk])
                        src_ = bk[:] if r == 1 else bk[:].rearrange("p (a b) -> p a b", b=r)
                        B.copy("act", deint_dst(dst[:], r, n), src_, [bk], [dst])
                for Bq in range(4):
                    bk = nb([6, 7])
                    for bl in range(4):
                        Bk = 4 * Bq + bl
                        for k in range(NK):
                            B.mm(bk[:, bl * 128:bl * 128 + 128], dil_ap(hk(k), r, 128 * Bk, 128), wb[:, k, 256:384],
                                 k == 0, k == NK - 1, [wb, hT_r[k]], [bk])
                    B.copy("act", vd[:, 4 * Bq:4 * Bq + 4, :], bk[:].rearrange("p (a b) -> p a b", a=4), [bk], [vd])
                def score(Bk):
                    c_, blk = divmod(Bk, nblk)
                    nq = 256 if blk < nblk - 1 else 128
                    sb_ = banks[4 + (Bk // 2) % 2]
                    reg = sb_[:, 256 * (Bk % 2):256 * (Bk % 2) + nq]
                    B.mm(reg, kT[:, 128 * Bk:128 * Bk + 128], qT[:, 128 * Bk:128 * Bk + nq], True, True, [kT, qT], [sb_])
                    et_, em_ = ett[Bk % 4], etm[Bk % 6]
                    B.act(et_[:, :nq], reg, AF.Exp, [sb_], [et_], scale=1.0 / math.sqrt(128.0))
                    B.tt("dve", em_[:, :nq], et_[:, :nq], amask[:, :nq], ALU.mult, [et_, amask], [em_])

                def pv(Bk):
                    c_, blk = divmod(Bk, nblk)
                    em_ = etm[Bk % 6]
                    pvb, dnb = banks[(Bk // 4) % 2], banks[2 + (Bk // 4) % 2]
                    cs = slice((Bk % 4) * 128, (Bk % 4) * 128 + 128)
                    for (bank_, isden) in ((pvb, False), (dnb, True)):
                        l0 = ones[:] if isden else vd[:, Bk, :]
                        R0 = [ones] if isden else [vd]
                        B.mm(bank_[:, cs], l0, em_[:, 0:128], True, blk == 0, R0 + [em_], [bank_])
                        if blk > 0:
                            emp = etm[(Bk - 1) % 6]
                            l1 = ones[:] if isden else vd[:, Bk - 1, :]
                            B.mm(bank_[:, cs], l1, emp[:, 128:256], False, True, R0 + [emp], [bank_])
                    if Bk % 4 == 3:
                        for (bank_, acc) in ((pvb, accn), (dnb, accd)):
                            dst = dil_ap(acc[:], r, 512 * (Bk // 4), 512)
                            if gi == 0:
                                B.copy("act" if acc is accd else "dve", dst, like(bank_[:], dst), [bank_], [acc])
                            else:
                                B.tt("dve", dst, dst, like(bank_[:], dst), ALU.add, [bank_, acc], [acc])

                LOOK = 3
                for Bk in range(16 + LOOK):
                    if Bk < 16:
                        score(Bk)
                    if Bk >= LOOK:
                        pv(Bk - LOOK)
            for n in range(4):
                bk = nb([6, 7])
                for k in range(NK):
                    B.mm(bk[:], wb[:, k, 384:512], hT[:, k, 512 * n:512 * n + 512], k == 0, k == NK - 1,
                         [wb, hT_r[k]], [bk])
                B.act(sza[:, 512 * n:512 * n + 512], bk[:], AF.Silu, [bk], [sza])
            S.op("dve", lambda e, a=accd: e.reciprocal(out=a[:], in_=a[:]), [accd.r], [accd.r])
            B.tt("dve", accn[:], accn[:], accd[:], ALU.mult, [accn, accd], [accn])
            B.tt("dve", yaT[:, j, :], accn[:], sza[:], ALU.mult, [accn, sza], [yaT_r[j]])
        if dbg.get("stop") == "PB":
            dump("dbg_ya", yaT[:], yaT_r)
            break
        pass
        wba = carve(49152, [128, 4, D], BF16, "wba")
        sgg = [carve(57344 + i * 4096, [128, L], BF16, "sgga%d" % i) for i in range(2)]
        tmm = [carve(65536 + i * 2048, [128, 512], F32, "tmm%d" % i) for i in range(2)]
        load_w(wba, io["w_ba"][l])
        for m in range(NK):
            if m % 4 == 0:
                wb = load_wblock(l, 7168 + 512 * (m // 4), 512)
            sg_ = sgg[m % 2]
            for n in range(4):
                bk = nb([6, 7])
                for k in range(NK):
                    B.mm(bk[:], wb[:, k, 128 * (m % 4):128 * (m % 4) + 128], hT[:, k, 512 * n:512 * n + 512],
                         k == 0, k == NK - 1, [wb, hT_r[k]], [bk])
                B.act(sg_[:, 512 * n:512 * n + 512], bk[:], AF.Sigmoid, [bk], [sg_])
            for n in range(4):
                ns = slice(512 * n, 512 * n + 512)
                bk = nb([0, 1, 2, 3])
                for k in range(4):
                    B.mm(bk[:], wba[:, k, 128 * m:128 * m + 128], yaT[:, k, ns], k == 0, k == 3,
                         [wba, yaT_r[k]], [bk])
                t_ = tmm[n % 2]
                B.tt("dve", t_[:], bk[:], sg_[:, ns], ALU.mult, [bk, sg_], [t_])
                B.tt("dve", merged[:, m, ns], merged[:, m, ns], t_[:], ALU.add, [t_, merged_r[m]], [merged_r[m]])
        if dbg.get("stop") == "PC2":
            dump("dbg_m", merged[:], merged_r)
            break
        pass
        wout = carve(32768, [128, NK, D], BF16, "wout")
        ot = carve(49152, [128, NK, 512], F32, "ot")
        sqo = carve(65536, [128, NK, 512], BF16, "sqo")
        rt2 = carve(73728, [128, 512], F32, "rt2")
        rs2 = carve(75776, [128, 512], F32, "rs2")
        tm2 = [carve(77824 + i * 2048, [128, 512], F32, "tm2%d" % i) for i in range(2)]
        load_w(wout, io["w_out"][l])
        for n in range(4):
            ns = slice(512 * n, 512 * n + 512)
            for mo in range(NK):
                bk = nb([0, 1, 2, 3, 4, 5])
                for k in range(NK):
                    B.mm(bk[:], wout[:, k, 128 * mo:128 * mo + 128], merged[:, k, ns], k == 0, k == NK - 1,
                         [wout, merged_r[k]], [bk])
                B.copy("act", ot[:, mo, :], bk[:], [bk], [ot])
                B.act(sqo[:, mo, :], bk[:], AF.Square, [bk], [sqo])
            sb_ = nb([6, 7])
            for mo in range(NK):
                B.mm(sb_[:], ones[:], sqo[:, mo, :], mo == 0, mo == NK - 1, [ones, sqo], [sb_])
            B.act(rt2[:], sb_[:], AF.Sqrt, [sb_, epsT], [rt2], scale=1.0 / D, bias=epsT[:, 0:1])
            S.op("dve", lambda e, a=rs2, b=rt2: e.reciprocal(out=a[:], in_=b[:]), [rt2.r], [rs2.r])
            for mo in range(NK):
                t_ = tm2[mo % 2]
                B.stt(t_[:], ot[:, mo, :], vecs[:, vb + 8 + mo:vb + 9 + mo], rs2[:], ALU.mult, ALU.mult,
                      [ot, vecs, rs2], [t_])
                B.tt("pool", xT[:, mo, ns], xT[:, mo, ns], t_[:], ALU.add, [t_, xT_r[mo][n]], [xT_r[mo][n]])
        pass
    else:
        yout = io["yT"].rearrange("(k p) t -> p k t", p=128)
        for k in range(NK):
            out_toks.append(B.dma("sp", yout[:, k, :], xT[:, k, :], xT_r[k], [], owner=Res("out%d" % k)))
    self.finish(out_toks)


Builder.main = _main


_CACHE = {}


def kernel(**inputs):
    h = prep_host(inputs)
    x = np.asarray(inputs["x"], np.float32)
    if "nc" not in _CACHE:
        b = Builder()
        b.build()
        _CACHE["nc"] = b.nc
    nc = _CACHE["nc"]
    shared = {k: h[k] for k in ("w_in", "w_glu", "w_bs", "w_ba", "w_out", "vecs", "ssm_rb", "ssm_rc", "amask", "ident")}
    in_maps = []
    for b_ in range(8):
        m = dict(shared)
        m["xT"] = np.ascontiguousarray(x[b_].T)
        in_maps.append(m)
    res = run_bass_kernel_spmd(nc, in_maps, core_ids=list(range(8)))
    out = np.stack([np.ascontiguousarray(res.results[b_]["yT"].T) for b_ in range(8)], axis=0)
    return out.astype(np.float32)
```

```python
import math
from contextlib import ExitStack

import numpy as np
import ml_dtypes

import concourse.bass as bass
import concourse.mybir as mybir
from concourse.bass_utils import run_bass_kernel_spmd

F32 = mybir.dt.float32
BF16 = mybir.dt.bfloat16
ALU = mybir.AluOpType
AF = mybir.ActivationFunctionType

L = 2048
D = 1024
NK = 8
DEPTH = 4
TS = 16
NCH = L // TS
ENGS = ("pe", "act", "dve", "pool", "sp")
STRICT = False


class Res:
    __slots__ = ("w", "r", "name", "dsem", "dcnt")

    def __init__(self, name=""):
        self.w = []
        self.r = []
        self.name = name
        self.dsem = None
        self.dcnt = 0


class Sched:
    def __init__(self, nc, es):
        self.nc = nc
        self.es = es
        self.prog = {e: [] for e in ENGS}
        self.cnt = {e: 0 for e in ENGS}
        self.sem = {e: es.enter_context(nc.semaphore("sem_" + e)) for e in ENGS if e != "sp"}
        self.known = {e: {} for e in ENGS}
        self.nsem = 0
        self.final = []

    def _waits(self, eng, reads, writes):
        toks = []
        for t in reads:
            toks += t.w
        for t in writes:
            toks += [x for x in t.w if STRICT or x[2] != eng]
            toks += [x for x in t.r if STRICT or x[2] != eng]
        need = {}
        for (sem, val, src) in toks:
            if eng == "pe" and src == "pe":
                continue
            if self.known[eng].get(id(sem), 0) >= val:
                continue
            if need.get(id(sem), (None, 0))[1] < val:
                need[id(sem)] = (sem, val)
        out = []
        for k, (sem, val) in need.items():
            self.known[eng][k] = val
            out.append((sem, val))
        return out

    def _commit(self, tok, reads, writes):
        for t in writes:
            t.w = [tok]
            t.r = []
        for t in reads:
            t.r.append(tok)
            if len(t.r) > 24:
                best = {}
                for (s, v, src) in t.r:
                    if id(s) not in best or best[id(s)][1] < v:
                        best[id(s)] = (s, v, src)
                t.r = list(best.values())

    def op(self, eng, fn, reads=(), writes=()):
        waits = self._waits(eng, reads, writes)
        self.cnt[eng] += 1
        sem = self.sem[eng]
        val = self.cnt[eng]

        def emit(e, waits=waits, fn=fn, sem=sem):
            for (s, v) in waits:
                e.wait_ge(s, v)
            fn(e).then_inc(sem, 1)

        self.prog[eng].append(emit)
        self._commit((sem, val, eng), reads, writes)

    def dma(self, q, out, in_, reads=(), writes=(), owner=None):
        waits = self._waits(q, reads, writes)
        own = owner if owner is not None else (writes[0] if writes else reads[0])
        if own.dsem is None:
            own.dsem = self.es.enter_context(self.nc.semaphore("dsem%d" % self.nsem))
            self.nsem += 1
        own.dcnt += 16
        sem, val = own.dsem, own.dcnt

        def emit(e, waits=waits, sem=sem, out=out, in_=in_):
            for (s, v) in waits:
                e.wait_ge(s, v)
            e.dma_start(out=out, in_=in_).then_inc(sem, 16)

        self.prog[q].append(emit)
        tok = (sem, val, "dma")
        for t in writes:
            t.w = [tok]
            t.r = []
        for t in reads:
            t.r.append(tok)
        return tok

    def wait_all(self, q, toks):
        def emit(e, toks=toks):
            for (s, v, _) in toks:
                e.wait_ge(s, v)
        self.prog[q].append(emit)

    def barrier(self, extra=()):
        for e in ENGS:
            waits = [(s, v) for (s, v, _) in extra]
            for o in ENGS:
                if o == "sp" or o == e or self.cnt[o] == 0:
                    continue
                if self.known[e].get(id(self.sem[o]), 0) < self.cnt[o]:
                    self.known[e][id(self.sem[o])] = self.cnt[o]
                    waits.append((self.sem[o], self.cnt[o]))

            def emit(en, waits=waits):
                for (s, v) in waits:
                    en.wait_ge(s, v)
            self.prog[e].append(emit)

    def run(self):
        nc = self.nc
        with nc.Block() as block:
            @block.tensor
            def _(e):
                for f in self.prog["pe"]:
                    f(e)

            @block.scalar
            def _(e):
                for f in self.prog["act"]:
                    f(e)

            @block.vector
            def _(e):
                for f in self.prog["dve"]:
                    f(e)

            @block.gpsimd
            def _(e):
                for f in self.prog["pool"]:
                    f(e)

            @block.sync
            def _(e):
                for f in self.prog["sp"]:
                    f(e)


def mkap(t, offset, dims):
    return bass.AP(tensor=t, offset=offset, ap=[list(d) for d in dims])


def _in_col_perm():
    cols = list(range(0, 1024))
    for j in range(4):
        for gi in range(3):
            for base in (1024, 2560, 4096):
                c0 = base + (gi * 4 + j) * 128
                cols += list(range(c0, c0 + 128))
        cols += list(range(5632 + j * 128, 5632 + (j + 1) * 128))
    cols += list(range(6144, 8192))
    assert len(cols) == 8192 and len(set(cols)) == 8192
    return np.asarray(cols)


def prep_host(inp):
    f32 = np.float32
    out = {}
    out["w_in"] = np.ascontiguousarray(np.asarray(inp["w_in"], f32)[:, :, _in_col_perm()])
    out["w_glu"] = np.ascontiguousarray(np.asarray(inp["w_glu"], f32))
    out["w_bs"] = np.ascontiguousarray(np.asarray(inp["w_branch_s"], f32))
    out["w_ba"] = np.ascontiguousarray(np.asarray(inp["w_branch_a"], f32))
    out["w_out"] = np.ascontiguousarray(np.asarray(inp["w_out"], f32))
    vecs = np.zeros((128, DEPTH * 24), f32)
    for l in range(DEPTH):
        vecs[:, l * 24 + 0:l * 24 + 8] = np.asarray(inp["pre_norm_g"], f32)[l].reshape(8, 128).T
        vecs[:, l * 24 + 8:l * 24 + 16] = np.asarray(inp["post_norm_g"], f32)[l].reshape(8, 128).T
        vecs[:, l * 24 + 16:l * 24 + 20] = np.asarray(inp["d_skip"], f32)[l].reshape(4, 128).T
        vecs[:, l * 24 + 20:l * 24 + 24] = np.asarray(inp["b_glu"], f32)[l].reshape(4, 128).T
    out["vecs"] = vecs
    lre = np.asarray(inp["lambda_re"], f32)
    lim = np.asarray(inp["lambda_im"], f32)
    ldt = np.asarray(inp["log_dt"], f32)
    bre = np.asarray(inp["b_re"], f32)
    bim = np.asarray(inp["b_im"], f32)
    cre = np.asarray(inp["c_re"], f32)
    cim = np.asarray(inp["c_im"], f32)
    rb = np.zeros((DEPTH, 128, 5, 4, 128), f32)
    rc = np.zeros((DEPTH, 128, 48 + 512), f32)
    for l in range(DEPTH):
        lam_g = lre[l].reshape(4, 4, 2, 64)
        lim_g = lim[l].reshape(4, 4, 2, 64)
        ldt_g = np.broadcast_to(ldt[l].reshape(4, 4, 2, 1), (4, 4, 2, 64))
        for ptl in range(4):
            rows = slice(32 * ptl, 32 * ptl + 32)
            rb[l, rows, 0] = lam_g[:, ptl].reshape(4, 128)[None]
            rb[l, rows, 1] = lim_g[:, ptl].reshape(4, 128)[None]
            rb[l, rows, 2] = ldt_g[:, ptl].reshape(4, 128)[None]
            for h in range(2):
                for j in range(4):
                    g = 8 * j + 2 * ptl + h
                    r0 = 32 * ptl + 16 * h
                    rb[l, r0:r0 + 16, 3, j, 64 * h:64 * h + 64] = bre[l, g].T
                    rb[l, r0:r0 + 16, 4, j, 64 * h:64 * h + 64] = bim[l, g].T
        rc[l, :, 0:16] = lre[l].reshape(16, 128).T
        rc[l, :, 16:32] = lim[l].reshape(16, 128).T
        rc[l, :, 32:48] = np.broadcast_to(ldt[l].reshape(16, 2, 1), (16, 2, 64)).reshape(16, 128).T
        cc = cre[l].reshape(16, 2, 16, 64).transpose(1, 3, 0, 2).reshape(128, 16 * 16)
        ci = cim[l].reshape(16, 2, 16, 64).transpose(1, 3, 0, 2).reshape(128, 16 * 16)
        rc[l, :, 48:48 + 256] = cc
        rc[l, :, 48 + 256:48 + 512] = ci
    out["ssm_rb"] = rb.reshape(DEPTH, 128, 5 * 512)
    out["ssm_rc"] = rc
    am = np.zeros((128, 256), f32)
    kk = np.arange(128)[:, None]
    qq = np.arange(128)[None, :]
    am[:, 0:128] = (qq >= kk)
    am[:, 128:256] = (qq <= kk)
    out["amask"] = am.astype(ml_dtypes.bfloat16)
    out["ident"] = np.eye(128, dtype=f32).astype(ml_dtypes.bfloat16)
    return out


class TL:
    def __init__(self, t, name):
        self.t = t
        self.r = Res(name)

    def __getitem__(self, idx):
        return self.t[idx]


class Builder:
    def __init__(self, n_layers=DEPTH, debug=None):
        self.n_layers = n_layers
        self.debug = debug or {}
        self.nc = bass.Bass("TRN2", target_bir_lowering=False)
        self.es = ExitStack()
        self.S = Sched(self.nc, self.es)
        self.uid = 0

    def sb(self, es, shape, dtype, name=None):
        self.uid += 1
        nm = (name or "t") + "_%d" % self.uid
        return TL(es.enter_context(self.nc.sbuf_tensor(nm, list(shape), dtype)), nm)

    def ps(self, es, shape, dtype=F32, name=None):
        self.uid += 1
        nm = (name or "p") + "_%d" % self.uid
        return TL(es.enter_context(self.nc.psum_tensor(nm, list(shape), dtype)), nm)

    @staticmethod
    def _res(lst):
        return [x.r if isinstance(x, TL) else x for x in lst]

    def tt(self, eng, out, a, b, op, R, W):
        self.S.op(eng, lambda e: e.tensor_tensor(out=out, in0=a, in1=b, op=op), self._res(R), self._res(W))

    def ts(self, eng, out, a, s1, op0, R, W, s2=None, op1=None):
        if op1 is None:
            self.S.op(eng, lambda e: e.tensor_scalar(out=out, in0=a, scalar1=s1, scalar2=None, op0=op0),
                      self._res(R), self._res(W))
        else:
            self.S.op(eng, lambda e: e.tensor_scalar(out=out, in0=a, scalar1=s1, scalar2=s2, op0=op0, op1=op1),
                      self._res(R), self._res(W))

    def stt(self, out, a, scalar, b, op0, op1, R, W):
        self.S.op("dve", lambda e: e.scalar_tensor_tensor(out=out, in0=a, scalar=scalar, in1=b, op0=op0, op1=op1),
                  self._res(R), self._res(W))

    def act(self, out, a, func, R, W, scale=1.0, bias=0.0):
        self.S.op("act", lambda e: e.activation(out=out, in_=a, func=func, bias=bias, scale=scale),
                  self._res(R), self._res(W))

    def copy(self, eng, out, a, R, W):
        if eng == "act":
            self.S.op("act", lambda e: e.activation(out=out, in_=a, func=AF.Copy), self._res(R), self._res(W))
        else:
            self.S.op(eng, lambda e: e.tensor_copy(out=out, in_=a), self._res(R), self._res(W))

    def memset(self, eng, ap, val, W):
        self.S.op(eng, lambda e: e.memset(ap, val), [], self._res(W))

    def mm(self, out, lhsT, rhs, start, stop, R, W, tp=None):
        if tp is None:
            self.S.op("pe", lambda e: e.matmul(out, lhsT=lhsT, rhs=rhs, start=start, stop=stop),
                      self._res(R), self._res(W))
        else:
            self.S.op("pe", lambda e: e.matmul(out, lhsT=lhsT, rhs=rhs, start=start, stop=stop, tile_position=tp),
                      self._res(R), self._res(W))

    def dma(self, q, out, in_, R, W, owner=None):
        return self.S.dma(q, out, in_, self._res(R), self._res(W),
                          owner.r if isinstance(owner, TL) else owner)

    def cmul(self, eng, o, a, b, t1, t2, R, W, Tm):
        (orr, oi), (ar, ai), (br, bi) = o, a, b
        self.tt(eng, t1, ar, br, ALU.mult, R, Tm)
        self.tt(eng, t2, ai, bi, ALU.mult, R, Tm)
        self.tt(eng, orr, t1, t2, ALU.subtract, Tm, W)
        self.tt(eng, t1, ar, bi, ALU.mult, R, Tm)
        self.tt(eng, t2, ai, br, ALU.mult, R, Tm)
        self.tt(eng, oi, t1, t2, ALU.add, Tm, W)

    def declare_io(self):
        nc = self.nc
        nl = DEPTH
        d = {}
        d["xT"] = nc.dram_tensor("xT", [D, L], F32, kind="ExternalInput").ap()
        d["w_in"] = nc.dram_tensor("w_in", [nl, D, 8192], F32, kind="ExternalInput").ap()
        d["w_glu"] = nc.dram_tensor("w_glu", [nl, 512, 512], F32, kind="ExternalInput").ap()
        d["w_bs"] = nc.dram_tensor("w_bs", [nl, 512, D], F32, kind="ExternalInput").ap()
        d["w_ba"] = nc.dram_tensor("w_ba", [nl, 512, D], F32, kind="ExternalInput").ap()
        d["w_out"] = nc.dram_tensor("w_out", [nl, D, D], F32, kind="ExternalInput").ap()
        d["vecs"] = nc.dram_tensor("vecs", [128, nl * 24], F32, kind="ExternalInput").ap()
        d["ssm_rb"] = nc.dram_tensor("ssm_rb", [nl, 128, 2560], F32, kind="ExternalInput").ap()
        d["ssm_rc"] = nc.dram_tensor("ssm_rc", [nl, 128, 560], F32, kind="ExternalInput").ap()
        d["amask"] = nc.dram_tensor("amask", [128, 256], BF16, kind="ExternalInput").ap()
        d["ident"] = nc.dram_tensor("ident", [128, 128], BF16, kind="ExternalInput").ap()
        d["yT"] = nc.dram_tensor("yT", [D, L], F32, kind="ExternalOutput").ap()
        tk = "ExternalOutput" if self.debug.get("tables") else "Internal"
        d["tabB"] = nc.dram_tensor("tabB", [nl, 128, 16384], BF16, kind=tk).ap()
        d["tabC"] = nc.dram_tensor("tabC", [nl, 128, 16384], BF16, kind=tk).ap()
        for name, (shape, dt) in self.debug.get("outs", {}).items():
            d[name] = nc.dram_tensor(name, list(shape), dt, kind="ExternalOutput").ap()
        self.io = d

    def lam_alloc(self, es, Fd, tag):
        names = ("dt", "lr", "a", "th", "mag", "em2a", "c", "s", "t1", "t2", "lbr", "lbi")
        return {n: self.sb(es, [128, Fd], F32, tag + n) for n in names}

    def lam_math(self, m, src):
        B = self
        dt, lr, a, th, mag, em2a, c, s, t1, t2, lbr, lbi = [m[n] for n in
                                                          ("dt", "lr", "a", "th", "mag", "em2a", "c", "s", "t1", "t2", "lbr", "lbi")]
        R = src["R"]
        B.act(dt[:], src["ldt"], AF.Exp, R, [dt])
        B.ts("dve", lr[:], src["lre"], -1e-4, ALU.min, R, [lr])
        B.tt("dve", a[:], lr[:], dt[:], ALU.mult, [lr, dt], [a])
        B.tt("dve", th[:], src["lim"], dt[:], ALU.mult, R + [dt], [th])
        B.act(mag[:], a[:], AF.Exp, [a], [mag])
        B.act(em2a[:], a[:], AF.Exp, [a], [em2a], scale=-2.0)
        B.act(s[:], th[:], AF.Sin, [th], [s], scale=1.0 / 16.0)
        B.act(c[:], th[:], AF.Sin, [th, self.halfpi], [c], scale=1.0 / 16.0, bias=self.halfpi[:, 0:1])
        for _ in range(4):
            B.tt("dve", t1[:], c[:], c[:], ALU.mult, [c], [t1])
            B.tt("dve", t2[:], s[:], s[:], ALU.mult, [s], [t2])
            B.stt(s[:], c[:], 2.0, s[:], ALU.mult, ALU.mult, [c, s], [s])
            B.tt("dve", c[:], t1[:], t2[:], ALU.subtract, [t1, t2], [c])
        B.tt("dve", lbr[:], mag[:], c[:], ALU.mult, [mag, c], [lbr])
        B.tt("dve", lbi[:], mag[:], s[:], ALU.mult, [mag, s], [lbi])

    def prologue(self):
        B = self
        io = self.io
        toks = []
        with ExitStack() as es:
            mb = B.lam_alloc(es, 512, "rb")
            inpb = B.sb(es, [128, 5, 512], F32, "rbin")
            mk = lambda n: B.sb(es, [128, 512], F32, "rb" + n)
            den, nr, cr, ci, invr, invi, bbr, bbi = [mk(n) for n in ("den", "nr", "cr", "ci", "invr", "invi", "bbr", "bbi")]
            xs = [(mk("xr0"), mk("xi0")), (mk("xr1"), mk("xi1"))]
            bt = B.sb(es, [128, 4, TS, 2, 128], BF16, "bt")
            mc = B.lam_alloc(es, 16, "rc")
            inpc = B.sb(es, [128, 560], F32, "rcin")
            pw = B.sb(es, [128, 2, 16, TS], F32, "pw")
            ct = B.sb(es, [128, 16, TS, 2, 32], BF16, "ct")
            big = lambda n: B.sb(es, [128, 16, TS, 16], F32, "rc" + n)
            u1, u2, u3 = big("u1"), big("u2"), big("u3")
            B.memset("pool", ct[:], 0.0, [ct])
            for l in range(self.n_layers):
                inp = inpb
                B.dma("sp", inp[:], io["ssm_rb"][l].rearrange("p (a f) -> p a f", a=5), [], [inp])
                B.lam_math(mb, dict(lre=inp[:, 0, :], lim=inp[:, 1, :], ldt=inp[:, 2, :], R=[inp]))
                lr, lbr, lbi, em2a, t1, t2 = [mb[n] for n in ("lr", "lbr", "lbi", "em2a", "t1", "t2")]
                li = inp[:, 1, :]
                B.tt("dve", t1[:], lr[:], lr[:], ALU.mult, [lr], [t1])
                B.tt("dve", t2[:], li, li, ALU.mult, [inp], [t2])
                B.tt("dve", den[:], t1[:], t2[:], ALU.add, [t1, t2], [den])
                B.S.op("dve", lambda e, den=den: e.reciprocal(out=den[:], in_=den[:]), [den.r], [den.r])
                B.ts("dve", nr[:], lbr[:], -1.0, ALU.add, [lbr], [nr])
                B.tt("dve", t1[:], nr[:], lr[:], ALU.mult, [nr, lr], [t1])
                B.tt("dve", t2[:], lbi[:], li, ALU.mult, [lbi, inp], [t2])
                B.tt("dve", t1[:], t1[:], t2[:], ALU.add, [t1, t2], [t1])
                B.tt("dve", cr[:], t1[:], den[:], ALU.mult, [t1, den], [cr])
                B.tt("dve", t1[:], lbi[:], lr[:], ALU.mult, [lbi, lr], [t1])
                B.tt("dve", t2[:], nr[:], li, ALU.mult, [nr, inp], [t2])
                B.tt("dve", t1[:], t1[:], t2[:], ALU.subtract, [t1, t2], [t1])
                B.tt("dve", ci[:], t1[:], den[:], ALU.mult, [t1, den], [ci])
                B.tt("dve", invr[:], lbr[:], em2a[:], ALU.mult, [lbr, em2a], [invr])
                B.stt(invi[:], lbi[:], -1.0, em2a[:], ALU.mult, ALU.mult, [lbi, em2a], [invi])
                B.cmul("dve", (bbr[:], bbi[:]), (cr[:], ci[:]), (inp[:, 3, :], inp[:, 4, :]), t1[:], t2[:],
                       [cr, ci, inp], [bbr, bbi], [t1, t2])
                prev = (bbr, bbi)
                for i in range(TS):
                    cur = xs[i % 2]
                    B.cmul("dve", (cur[0][:], cur[1][:]), (prev[0][:], prev[1][:]), (invr[:], invi[:]), t1[:], t2[:],
                           [prev[0], prev[1], invr, invi], [cur[0], cur[1]], [t1, t2])
                    B.copy("act", bt[:, :, i, 0, :], cur[0][:].rearrange("p (j q) -> p j q", j=4), [cur[0]], [bt])
                    B.copy("act", bt[:, :, i, 1, :], cur[1][:].rearrange("p (j q) -> p j q", j=4), [cur[1]], [bt])
                    prev = cur
                toks.append(B.dma("sp", io["tabB"][l], bt[:].rearrange("p j i r q -> p (j i r q)"),
                                  [bt], [self.tabB_r[l]], owner=bt))
                inp = inpc
                B.dma("sp", inp[:], io["ssm_rc"][l], [], [inp])
                B.lam_math(mc, dict(lre=inp[:, 0:16], lim=inp[:, 16:32], ldt=inp[:, 32:48], R=[inp]))
                lbr, lbi, t1, t2 = [mc[n] for n in ("lbr", "lbi", "t1", "t2")]
                B.copy("dve", pw[:, 0, :, 0], lbr[:], [lbr], [pw])
                B.copy("dve", pw[:, 1, :, 0], lbi[:], [lbi], [pw])
                for i in range(1, TS):
                    B.cmul("dve", (pw[:, 0, :, i], pw[:, 1, :, i]), (pw[:, 0, :, i - 1], pw[:, 1, :, i - 1]),
                           (lbr[:], lbi[:]), t1[:], t2[:], [pw, lbr, lbi], [pw], [t1, t2])
                PL = self.PL
                B.copy("dve", PL[:, l, :, 0], pw[:, 0, :, TS - 1], [pw], [PL])
                B.copy("dve", PL[:, l, :, 1], pw[:, 1, :, TS - 1], [pw], [PL])
                for k in range(1, 7):
                    pr, pi = PL[:, l, :, 3 * k - 3], PL[:, l, :, 3 * k - 2]
                    B.tt("dve", t1[:], pr, pr, ALU.mult, [PL], [t1])
                    B.tt("dve", t2[:], pi, pi, ALU.mult, [PL], [t2])
                    B.tt("dve", PL[:, l, :, 3 * k], t1[:], t2[:], ALU.subtract, [t1, t2], [PL])
                    B.stt(PL[:, l, :, 3 * k + 1], pr, 2.0, pi, ALU.mult, ALU.mult, [PL], [PL])
                for k in range(7):
                    B.ts("dve", PL[:, l, :, 3 * k + 2], PL[:, l, :, 3 * k + 1], -1.0, ALU.mult, [PL], [PL])
                cre = inp[:, 48:304].rearrange("p (t c) -> p t c", c=16)
                cim = inp[:, 304:560].rearrange("p (t c) -> p t c", c=16)

                def bc_c(ap3):
                    return mkap(ap3.tensor, ap3.offset, [ap3.ap[0], ap3.ap[1], [0, TS], ap3.ap[2]])

                def bc_p(k):
                    a = pw[:, k, :, :]
                    return mkap(a.tensor, a.offset, [a.ap[0], a.ap[1], a.ap[2], [0, 16]])
                B.tt("dve", u1[:], bc_c(cre), bc_p(0), ALU.mult, [inp, pw], [u1])
                B.tt("dve", u2[:], bc_c(cim), bc_p(1), ALU.mult, [inp, pw], [u2])
                B.tt("dve", u3[:], u1[:], u2[:], ALU.subtract, [u1, u2], [u3])
                B.copy("act", ct[0:64, :, :, 0, 0:16], u3[0:64], [u3], [ct])
                B.copy("act", ct[64:128, :, :, 0, 16:32], u3[64:128], [u3], [ct])
                B.tt("dve", u1[:], bc_c(cre), bc_p(1), ALU.mult, [inp, pw], [u1])
                B.tt("dve", u2[:], bc_c(cim), bc_p(0), ALU.mult, [inp, pw], [u2])
                B.stt(u3[:], u1[:], -1.0, u2[:], ALU.mult, ALU.subtract, [u1, u2], [u3])
                B.copy("act", ct[0:64, :, :, 1, 0:16], u3[0:64], [u3], [ct])
                B.copy("act", ct[64:128, :, :, 1, 16:32], u3[64:128], [u3], [ct])
                toks.append(B.dma("sp", io["tabC"][l], ct[:].rearrange("p t i r c -> p (t i r c)"),
                                  [ct], [self.tabC_r[l]], owner=ct))
        self.S.barrier(toks)

    def build(self):
        B = self
        nc = self.nc
        es = self.es
        self.declare_io()
        io = self.io
        self.tabB_r = [Res("tabB%d" % l) for l in range(DEPTH)]
        self.tabC_r = [Res("tabC%d" % l) for l in range(DEPTH)]
        self.halfpi = B.sb(es, [128, 1], F32, "halfpi")
        B.memset("dve", self.halfpi[:], math.pi / 2.0, [self.halfpi])
        self.PL = B.sb(es, [128, DEPTH, 16, 24], F32, "PL")
        self.prologue()
        if self.debug.get("tables"):
            self.finish([])
            return
        self.main()

    def finish(self, out_toks):
        self.S.wait_all("sp", out_toks)
        self.S.barrier(out_toks)
        self.S.run()


class V:
    def __init__(self, ap, name=""):
        self.ap = ap
        self.r = Res(name)
        self.lo = self.hi = None

    def __getitem__(self, idx):
        return self.ap[idx]


def _res_of(lst):
    return [x.r if hasattr(x, "r") and not isinstance(x, Res) else x for x in lst]


Builder._res = staticmethod(_res_of)


def dil_ap(base, r, col0, ncols):
    t, off, pdim, st = base.tensor, base.offset, list(base.ap[0]), base.ap[-1][0]
    assert len(base.ap) == 2
    nsub = L // r
    c0, i0 = divmod(col0, nsub)
    if r == 1:
        dims, o = [[st, ncols]], col0
    elif i0 + ncols <= nsub:
        dims, o = [[r * st, ncols]], c0 + r * i0
    else:
        assert i0 == 0 and ncols % nsub == 0
        dims, o = [[st, ncols // nsub], [r * st, nsub]], c0
    return mkap(t, off + o * st, [pdim] + dims)


def deint_dst(base, r, n):
    if r == 1:
        return base[:, 512 * n:512 * n + 512]
    nsub = L // r
    t, off, pdim, st = base.tensor, base.offset, list(base.ap[0]), base.ap[-1][0]
    return mkap(t, off + (512 * n // r) * st, [pdim, [st, 512 // r], [nsub * st, r]])


def like(bank_ap, ap):
    if len(ap.ap) == 3:
        return bank_ap.rearrange("p (a b) -> p a b", a=ap.ap[1][1])
    return bank_ap


AR_BYTES = 88064


def _main(self):
    B = self
    es = self.es
    io = self.io
    S = self.S
    nc = self.nc
    dbg = self.debug
    xT = B.sb(es, [128, NK, L], F32, "xT")
    xT_r = [[Res("xT%d_%d" % (k, n)) for n in range(4)] for k in range(NK)]
    hT = B.sb(es, [128, NK, L], BF16, "hT")
    hT_r = [Res("hT%d" % k) for k in range(NK)]
    vecs = B.sb(es, [128, DEPTH * 24], F32, "vecs")
    amask = B.sb(es, [128, 256], BF16, "amask")
    ident = B.sb(es, [128, 128], BF16, "ident")
    ones = B.sb(es, [128, 128], BF16, "ones")
    epsT = B.sb(es, [128, 1], F32, "eps")
    wbuf = [B.sb(es, [128, NK, 512], BF16, "wbuf%d" % i) for i in range(2)]
    arena = B.sb(es, [128, AR_BYTES // 2], BF16, "arena")
    banks = [B.ps(es, [128, 512], F32, "bank%d" % i) for i in range(8)]
    self.wi = 0

    def carve(off, shape, dtype, name):
        nel = int(np.prod(shape[1:]))
        esz = 2 if dtype == BF16 else 4
        assert off % 4 == 0 and off + nel * esz <= AR_BYTES, (name, off, nel * esz)
        a = arena.t[:, off // 2: off // 2 + nel * esz // 2]
        if dtype == F32:
            a = a.bitcast(F32)
        if len(shape) == 3:
            a = a.rearrange("p (a b) -> p a b", a=shape[1])
        elif len(shape) == 4:
            a = a.rearrange("p (a b c) -> p a b c", a=shape[1], b=shape[2])
        elif len(shape) == 5:
            a = a.rearrange("p (a b c d) -> p a b c d", a=shape[1], b=shape[2], c=shape[3])
        v = V(a, name)
        region(off, off + nel * esz, [v.r])
        v.lo, v.hi = off, off + nel * esz
        return v

    regs = []

    def region(lo, hi, res_list):
        inherit = []
        keep = []
        for (l0, h0, rl) in regs:
            if l0 < hi and lo < h0:
                for r_ in rl:
                    inherit += r_.w + r_.r
                if lo <= l0 and h0 <= hi:
                    continue
            keep.append((l0, h0, rl))
        regs[:] = keep
        best = {}
        for (sm, vl, src) in inherit:
            if id(sm) not in best or best[id(sm)][1] < vl:
                best[id(sm)] = (sm, vl, "alias")
        for r_ in res_list:
            r_.r = list(best.values())
        regs.append((lo, hi, res_list))

    def reslist(v, names):
        rl = [Res(n) for n in names]
        region(v.lo, v.hi, rl)
        return rl

    B.memset("dve", ones[:], 1.0, [ones])
    B.memset("dve", epsT[:], 1e-6, [epsT])
    B.dma("sp", vecs[:], io["vecs"], [], [vecs])
    B.dma("sp", amask[:], io["amask"], [], [amask])
    B.dma("sp", ident[:], io["ident"], [], [ident])
    xin = io["xT"].rearrange("(k p) t -> p k t", p=128)
    for k in range(NK):
        B.dma("sp", xT[:, k, :], xin[:, k, :], [], xT_r[k], owner=xT_r[k][0])

    out_toks = []

    def dump(name, ap, R):
        out_toks.append(B.dma("sp", io[name], ap, R, [], owner=Res("dump_" + name)))

    def load_wblock(l, col0, ncols):
        wb = wbuf[self.wi % 2]
        self.wi += 1
        src = io["w_in"][l][:, col0:col0 + ncols].rearrange("(k p) c -> p k c", p=128)
        B.dma("pool", wb[:, :, 0:ncols], src, [], [wb])
        return wb

    def load_w(dst, src2d):
        B.dma("pool", dst[:], src2d.rearrange("(k p) c -> p k c", p=128), [], [dst])

    bank_ctr = [0]

    def nb(lst):
        b = banks[lst[bank_ctr[0] % len(lst)]]
        bank_ctr[0] += 1
        return b

    def hk(k):
        return hT.t[:, k, :]

    for l in range(self.n_layers):
        vb = l * 24
        sq = [carve(i * 8192, [128, NK, 512], BF16, "sq%d" % i) for i in range(2)]
        rt = [carve(16384 + i * 2048, [128, 512], F32, "rt%d" % i) for i in range(2)]
        rs = [carve(20480 + i * 2048, [128, 512], F32, "rs%d" % i) for i in range(2)]
        for n in range(4):
            ns = slice(n * 512, (n + 1) * 512)
            s_, rt_, rs_ = sq[n % 2], rt[n % 2], rs[n % 2]
            for k in range(NK):
                B.act(s_[:, k, :], xT[:, k, ns], AF.Square, [xT_r[k][n]], [s_])
            bk = nb([6, 7])
            for k in range(NK):
                B.mm(bk[:], ones[:], s_[:, k, :], k == 0, k == NK - 1, [ones, s_], [bk])
            B.act(rt_[:], bk[:], AF.Sqrt, [bk, epsT], [rt_], scale=1.0 / D, bias=epsT[:, 0:1])
            S.op("dve", lambda e, a=rs_, b=rt_: e.reciprocal(out=a[:], in_=b[:]), [rt_.r], [rs_.r])
            for k in range(NK):
                B.stt(hT[:, k, ns], xT[:, k, ns], vecs[:, vb + k:vb + k + 1], rs_[:], ALU.mult, ALU.mult,
                      [xT_r[k][n], vecs, rs_], [hT_r[k]])
        if dbg.get("stop") == "P0":
            dump("dbg_h", hT[:], hT_r)
            break
        pass
        yT = carve(32768, [128, 4, L], BF16, "yT")
        _fl = reslist(yT, ["yT%d_%d" % (k, n) for k in range(4) for n in range(4)])
        yT_r = [[_fl[4 * k + n] for n in range(4)] for k in range(4)]
        r32 = carve(0, [128, 2, TS, 128], F32, "r32")
        rbf = carve(16384, [128, 2, TS, 128], BF16, "rbf")
        Bt = carve(24576, [128, TS, 2, 128], BF16, "Bt")
        Ct = carve(49152, [128, 5120], BF16, "Ct")
        CtA = Ct[:, 0:3072].rearrange("p (t i r c) -> p t i r c", t=3, i=TS, r=2)
        CtB = Ct[:, 3072:5120].rearrange("p (i r c) -> p i r c", i=TS, r=2)
        uT = carve(59392, [128, L], BF16, "uT")
        wglu = carve(63488, [128, 4, 512], BF16, "wglu")
        Sab = [carve(67584 + i * 1536, [128, 2, 192], F32, "S%d" % i) for i in range(2)]
        zt = carve(70656, [128, 2, 192], F32, "zt")
        tq = carve(72192, [128, 1, 128], F32, "tq")
        tq2 = carve(72704, [128, 1, 128], F32, "tq2")
        Sbf = carve(73216, [128, 2, 128], BF16, "Sbf")
        ddt = [carve(73728 + i * 256, [128, 128], BF16, "dd%d" % i) for i in range(2)]
        gw = [carve(77824 + i * 2048, [128, 512], F32, "gw%d" % i) for i in range(2)]
        gs = [carve(81920 + i * 2048, [128, 512], F32, "gs%d" % i) for i in range(2)]
        gz = [carve(86016 + i * 1024, [128, 512], BF16, "gz%d" % i) for i in range(2)]
        for t_ in (Sab[0], Sab[1], zt):
            B.memset("dve", t_[:, :, 0:64], 0.0, [t_])
        Ctz = V(CtB[:, :, :, 0:32], "Ctz")
        region(Ct.lo, Ct.hi, [Ctz.r])
        B.memset("pool", Ctz[:], 0.0, [Ctz])
        B.memset("dve", Sbf[:], 0.0, [Sbf])
        wbU = load_wblock(l, 0, 512)
        wbZ = load_wblock(l, 512, 512)
        load_w(wglu, io["w_glu"][l])
        PLl = self.PL
        def prelude_u(j):
            for n in range(4):
                bk = nb([6, 7])
                for k in range(NK):
                    B.mm(bk[:], wbU[:, k, 128 * j:128 * j + 128], hT[:, k, 512 * n:512 * n + 512],
                         k == 0, k == NK - 1, [wbU, hT_r[k]], [bk])
                B.copy("act", deint_dst(uT[:], 16, n), bk[:].rearrange("p (a b) -> p a b", b=16), [bk], [uT])
            B.dma("sp", Bt[:].rearrange("p i r q -> p (i r q)"), io["tabB"][l][:, 4096 * j:4096 * (j + 1)],
                  [self.tabB_r[l]], [Bt])

        def prelude_c(j):
            B.dma("sp", Ct[:, 0:3072], io["tabC"][l][:, 4096 * j:4096 * j + 3072], [self.tabC_r[l]], [Ct])
            B.dma("sp", CtB[:, :, :, 32:64],
                  io["tabC"][l][:, 4096 * j + 3072:4096 * (j + 1)].rearrange("p (i r c) -> p i r c", i=TS, r=2),
                  [self.tabC_r[l]], [Ct])

        prelude_u(0)
        prelude_c(0)
        for j in dbg.get("js", range(4)):
            dd = ddt[j % 2]
            B.ts("dve", dd[:], ident[:], vecs[:, vb + 16 + j:vb + 17 + j], ALU.mult, [ident, vecs], [dd])
            for ib in range(4):
                B.mm(banks[ib][:], dd[:], uT[:, 512 * ib:512 * ib + 512], True, False, [dd, uT], [banks[ib]])
            def xt_chain(ptl):
                rows = slice(32 * ptl, 32 * ptl + 32)
                for ih in range(8):
                    xb = nb([4, 5])
                    xv = xb[:].rearrange("p (r i c) -> p r i c", r=2, i=2)
                    for ri in range(2):
                        for il in range(2):
                            i = 2 * ih + il
                            B.mm(xv[:, ri, il, :], Bt[rows, i, ri, :], uT[rows, i * 128:(i + 1) * 128],
                                 True, True, [Bt, uT], [xb], tp=(32 * ptl, 0))
                    for il in range(2):
                        i = 2 * ih + il
                        if i == 0:
                            B.copy("dve", r32[:, :, 0, :], xv[:, :, 0, :], [xb], [r32])
                        else:
                            B.tt("dve", r32[:, :, i, :], r32[:, :, i - 1, :], xv[:, :, il, :], ALU.add,
                                 [xb, r32], [r32])
                        yield

            def l2(ptl):
                pt = 4 * j + ptl
                B.copy("act", rbf[:], r32[:], [r32], [rbf])
                pl = lambda c: PLl[:, l, pt, c:c + 1]
                rr, rim = r32[:, 0, TS - 1, :], r32[:, 1, TS - 1, :]
                B.ts("dve", tq[:, 0, :], rim, pl(1), ALU.mult, [r32, PLl], [tq])
                B.stt(zt[:, 0, 64:192], rr, pl(0), tq[:, 0, :], ALU.mult, ALU.subtract, [r32, PLl, tq], [zt])
                B.ts("dve", tq2[:, 0, :], rr, pl(1), ALU.mult, [r32, PLl], [tq2])
                B.stt(zt[:, 1, 64:192], rim, pl(0), tq2[:, 0, :], ALU.mult, ALU.add, [r32, PLl, tq2], [zt])
                yield "z"
                old = zt
                for kk in range(7):
                    d = 1 << kk
                    new = Sab[kk % 2]
                    a_, b_, nb_ = pl(3 * kk), pl(3 * kk + 1), pl(3 * kk + 2)
                    B.stt(new[:, :, 64:192], old[:, :, 64 - d:192 - d], a_, old[:, :, 64:192], ALU.mult, ALU.add, [old, PLl], [new])
                    yield
                    B.stt(new[:, 0, 64:192], old[:, 1, 64 - d:192 - d], nb_, new[:, 0, 64:192], ALU.mult, ALU.add, [old, new, PLl], [new])
                    yield
                    B.stt(new[:, 1, 64:192], old[:, 0, 64 - d:192 - d], b_, new[:, 1, 64:192], ALU.mult, ALU.add, [old, new, PLl], [new])
                    yield
                    old = new
                B.copy("dve", Sbf[:, :, 1:128], old[:, :, 64:191], [old], [Sbf])
                yield

            def interleave(g1, g2):
                for _ in g1:
                    if g2 is not None:
                        if next(g2, "end") == "end":
                            g2 = None
                if g2 is not None:
                    for _ in g2:
                        pass

            def readout(ptl):
                rows = slice(32 * ptl, 32 * ptl + 32)
                for i in range(TS):
                    yb = banks[i // 4]
                    cs_ = slice((i % 4) * 128, (i % 4) * 128 + 128)
                    if ptl == 3:
                        o_ = yb[64:128, cs_]
                        tp = (0, 64)
                        lh = [CtB[:, i, rr_, :] for rr_ in range(2)]
                        RC_ = [Ct, Ctz]
                    else:
                        o_ = yb[rows, cs_]
                        tp = (0, 32 * ptl)
                        lh = [CtA[:, ptl, i, rr_, :] for rr_ in range(2)]
                        RC_ = [Ct]
                    B.mm(o_, lh[0], rbf[:, 0, i, :], False, False, RC_ + [rbf], [yb], tp=tp)
                    B.mm(o_, lh[1], rbf[:, 1, i, :], False, False, RC_ + [rbf], [yb], tp=tp)
                    B.mm(o_, lh[0], Sbf[:, 0, :], False, False, RC_ + [Sbf], [yb], tp=tp)
                    B.mm(o_, lh[1], Sbf[:, 1, :], False, True, RC_ + [Sbf], [yb], tp=tp)

            order = (0, 1, 3, 2)
            for _ in xt_chain(order[0]):
                pass
            for idx, ptl in enumerate(order):
                g1 = l2(ptl)
                for step in g1:
                    if step == "z":
                        break
                g2 = xt_chain(order[idx + 1]) if idx + 1 < 4 else None
                interleave(g1, g2)
                if idx == 2 and j + 1 < 4:
                    prelude_u(j + 1)
                readout(ptl)
            if j + 1 < 4:
                prelude_c(j + 1)
            for ib in range(4):
                cs = slice(512 * ib, 512 * ib + 512)
                w_, g_, yb = gw[ib % 2], gs[ib % 2], banks[ib]
                B.act(w_[:], yb[:], AF.Square, [yb], [w_])
                B.ts("pool", w_[:], w_[:], 0.044715, ALU.mult, [w_], [w_], s2=1.0, op1=ALU.add)
                B.tt("dve", w_[:], w_[:], yb[:], ALU.mult, [w_, yb], [w_])
                B.act(g_[:], w_[:], AF.Sigmoid, [w_], [g_], scale=1.5957691216057308)
                B.tt("dve", yT[:, j, cs], yb[:], g_[:], ALU.mult, [yb, g_], [yT_r[j][ib]])
        if dbg.get("stop") == "PA1":
            dump("dbg_y", yT[:], [x for row in yT_r for x in row])
            break
        szf = [carve(4096 * mo, [128, L], BF16, "szf%d" % mo) for mo in range(4)]
        for mo in range(4):
            for n in range(4):
                zb = nb([4, 5])
                for k in range(NK):
                    B.mm(zb[:], wbZ[:, k, 128 * mo:128 * mo + 128], hT[:, k, 512 * n:512 * n + 512],
                         k == 0, k == NK - 1, [wbZ, hT_r[k]], [zb])
                B.act(deint_dst(szf[mo][:], 16, n), zb[:].rearrange("p (a b) -> p a b", b=16), AF.Silu, [zb], [szf[mo]])
        for n in range(4):
            cs = slice(512 * n, 512 * n + 512)
            for mo in range(4):
                gb = banks[mo]
                for k in range(4):
                    B.mm(gb[:], wglu[:, k, 128 * mo:128 * mo + 128], yT[:, k, cs], k == 0, k == 3,
                         [wglu, yT_r[k][n]], [gb])
            for mo in range(4):
                g_ = gs[mo % 2]
                B.act(g_[:], banks[mo][:], AF.Sigmoid, [banks[mo], vecs], [g_], bias=vecs[:, vb + 20 + mo:vb + 21 + mo])
                B.tt("dve", g_[:], g_[:], szf[mo][:, cs], ALU.mult, [g_, szf[mo]], [g_])
                B.tt("dve", yT[:, mo, cs], yT[:, mo, cs], g_[:], ALU.mult, [yT_r[mo][n], g_], [yT_r[mo][n]])
        if dbg.get("stop") == "PA":
            dump("dbg_y", yT[:], [x for row in yT_r for x in row])
            break
        pass
        merged = carve(0, [128, NK, L], BF16, "merged")
        merged_r = reslist(merged, ["mg%d" % m for m in range(NK)])
        wbs = carve(49152, [128, 4, D], BF16, "wbs")
        sgg = [carve(57344 + i * 4096, [128, L], BF16, "sgg%d" % i) for i in range(2)]
        load_w(wbs, io["w_bs"][l])
        for m in range(NK):
            if m % 4 == 0:
                wb = load_wblock(l, 6144 + 512 * (m // 4), 512)
            sg_ = sgg[m % 2]
            for n in range(4):
                bk = nb([6, 7])
                for k in range(NK):
                    B.mm(bk[:], wb[:, k, 128 * (m % 4):128 * (m % 4) + 128], hT[:, k, 512 * n:512 * n + 512],
                         k == 0, k == NK - 1, [wb, hT_r[k]], [bk])
                B.act(sg_[:, 512 * n:512 * n + 512], bk[:], AF.Sigmoid, [bk], [sg_])
            for n2 in range(4):
                bk = nb([0, 1, 2, 3])
                for k in range(4):
                    B.mm(bk[:], wbs[:, k, 128 * m:128 * m + 128], yT[:, k, 512 * n2:512 * n2 + 512], k == 0, k == 3,
                         [wbs, yT_r[k][n2]], [bk])
                dst = dil_ap(merged[:, m, :], 16, 512 * n2, 512)
                B.tt("dve", dst, like(bk[:], dst), dil_ap(sg_[:], 16, 512 * n2, 512), ALU.mult,
                     [bk, sg_], [merged_r[m]])
        if dbg.get("stop") == "PA2":
            dump("dbg_m", merged[:], merged_r)
            break
        pass
        yaT = carve(32768, [128, 4, L], BF16, "yaT")
        yaT_r = reslist(yaT, ["ya%d" % j for j in range(4)])
        qT = carve(49152, [128, L], BF16, "qT")
        kT = carve(53248, [128, L], BF16, "kT")
        vd = carve(57344, [128, 16, 128], BF16, "vd")
        etm = [carve(61440 + i * 512, [128, 256], BF16, "etm%d" % i) for i in range(6)]
        ett = [carve(64512 + i * 512, [128, 256], BF16, "ett%d" % i) for i in range(4)]
        accn = carve(66560, [128, L], F32, "accn")
        accd = carve(74752, [128, L], F32, "accd")
        sza = carve(82944, [128, L], BF16, "sza")
        for j in range(4):
            for gi, r in enumerate((1, 4, 16)):
                ncols = 512 if gi == 2 else 384
                wb = load_wblock(l, 1024 + 1280 * j + 384 * gi, ncols)
                nsub = L // r
                nblk = nsub // 128
                for (c_in, dst) in ((0, qT), (128, kT)):
                    for n in range(4):
                        bk = nb([6, 7])
                        for k in range(NK):
                            B.mm(bk[:], wb[:, k, c_in:c_in + 128], hT[:, k, 512 * n:512 * n + 512], k == 0, k == NK - 1,
                                 [wb, hT_r[k]], [bk])
                        src_ = bk[:] if r == 1 else bk[:].rearrange("p (a b) -> p a b", b=r)
                        B.copy("act", deint_dst(dst[:], r, n), src_, [bk], [dst])
                for Bq in range(4):
                    bk = nb([6, 7])
                    for bl in range(4):
                        Bk = 4 * Bq + bl
                        for k in range(NK):
                            B.mm(bk[:, bl * 128:bl * 128 + 128], dil_ap(hk(k), r, 128 * Bk, 128), wb[:, k, 256:384],
                                 k == 0, k == NK - 1, [wb, hT_r[k]], [bk])
                    B.copy("act", vd[:, 4 * Bq:4 * Bq + 4, :], bk[:].rearrange("p (a b) -> p a b", a=4), [bk], [vd])
                def score(Bk):
                    c_, blk = divmod(Bk, nblk)
                    nq = 256 if blk < nblk - 1 else 128
                    sb_ = banks[4 + (Bk // 2) % 2]
                    reg = sb_[:, 256 * (Bk % 2):256 * (Bk % 2) + nq]
                    B.mm(reg, kT[:, 128 * Bk:128 * Bk + 128], qT[:, 128 * Bk:128 * Bk + nq], True, True, [kT, qT], [sb_])
                    et_, em_ = ett[Bk % 4], etm[Bk % 6]
                    B.act(et_[:, :nq], reg, AF.Exp, [sb_], [et_], scale=1.0 / math.sqrt(128.0))
                    B.tt("dve", em_[:, :nq], et_[:, :nq], amask[:, :nq], ALU.mult, [et_, amask], [em_])

                def pv(Bk):
                    c_, blk = divmod(Bk, nblk)
                    em_ = etm[Bk % 6]
                    pvb, dnb = banks[(Bk // 4) % 2], banks[2 + (Bk // 4) % 2]
                    cs = slice((Bk % 4) * 128, (Bk % 4) * 128 + 128)
                    for (bank_, isden) in ((pvb, False), (dnb, True)):
                        l0 = ones[:] if isden else vd[:, Bk, :]
                        R0 = [ones] if isden else [vd]
                        B.mm(bank_[:, cs], l0, em_[:, 0:128], True, blk == 0, R0 + [em_], [bank_])
                        if blk > 0:
                            emp = etm[(Bk - 1) % 6]
                            l1 = ones[:] if isden else vd[:, Bk - 1, :]
                            B.mm(bank_[:, cs], l1, emp[:, 128:256], False, True, R0 + [emp], [bank_])
                    if Bk % 4 == 3:
                        for (bank_, acc) in ((pvb, accn), (dnb, accd)):
                            dst = dil_ap(acc[:], r, 512 * (Bk // 4), 512)
                            if gi == 0:
                                B.copy("act" if acc is accd else "dve", dst, like(bank_[:], dst), [bank_], [acc])
                            else:
                                B.tt("dve", dst, dst, like(bank_[:], dst), ALU.add, [bank_, acc], [acc])

                LOOK = 3
                for Bk in range(16 + LOOK):
                    if Bk < 16:
                        score(Bk)
                    if Bk >= LOOK:
                        pv(Bk - LOOK)
            for n in range(4):
                bk = nb([6, 7])
                for k in range(NK):
                    B.mm(bk[:], wb[:, k, 384:512], hT[:, k, 512 * n:512 * n + 512], k == 0, k == NK - 1,
                         [wb, hT_r[k]], [bk])
                B.act(sza[:, 512 * n:512 * n + 512], bk[:], AF.Silu, [bk], [sza])
            S.op("dve", lambda e, a=accd: e.reciprocal(out=a[:], in_=a[:]), [accd.r], [accd.r])
            B.tt("dve", accn[:], accn[:], accd[:], ALU.mult, [accn, accd], [accn])
            B.tt("dve", yaT[:, j, :], accn[:], sza[:], ALU.mult, [accn, sza], [yaT_r[j]])
        if dbg.get("stop") == "PB":
            dump("dbg_ya", yaT[:], yaT_r)
            break
        pass
        wba = carve(49152, [128, 4, D], BF16, "wba")
        sgg = [carve(57344 + i * 4096, [128, L], BF16, "sgga%d" % i) for i in range(2)]
        tmm = [carve(65536 + i * 2048, [128, 512], F32, "tmm%d" % i) for i in range(2)]
        load_w(wba, io["w_ba"][l])
        for m in range(NK):
            if m % 4 == 0:
                wb = load_wblock(l, 7168 + 512 * (m // 4), 512)
            sg_ = sgg[m % 2]
            for n in range(4):
                bk = nb([6, 7])
                for k in range(NK):
                    B.mm(bk[:], wb[:, k, 128 * (m % 4):128 * (m % 4) + 128], hT[:, k, 512 * n:512 * n + 512],
                         k == 0, k == NK - 1, [wb, hT_r[k]], [bk])
                B.act(sg_[:, 512 * n:512 * n + 512], bk[:], AF.Sigmoid, [bk], [sg_])
            for n in range(4):
                ns = slice(512 * n, 512 * n + 512)
                bk = nb([0, 1, 2, 3])
                for k in range(4):
                    B.mm(bk[:], wba[:, k, 128 * m:128 * m + 128], yaT[:, k, ns], k == 0, k == 3,
                         [wba, yaT_r[k]], [bk])
                t_ = tmm[n % 2]
                B.tt("dve", t_[:], bk[:], sg_[:, ns], ALU.mult, [bk, sg_], [t_])
                B.tt("dve", merged[:, m, ns], merged[:, m, ns], t_[:], ALU.add, [t_, merged_r[m]], [merged_r[m]])
        if dbg.get("stop") == "PC2":
            dump("dbg_m", merged[:], merged_r)
            break
        pass
        wout = carve(32768, [128, NK, D], BF16, "wout")
        ot = carve(49152, [128, NK, 512], F32, "ot")
        sqo = carve(65536, [128, NK, 512], BF16, "sqo")
        ot_r = reslist(ot, ["ot%d" % m for m in range(NK)])
        sqo_r = reslist(sqo, ["sqo%d" % m for m in range(NK)])
        rt2 = carve(73728, [128, 512], F32, "rt2")
        rs2 = carve(75776, [128, 512], F32, "rs2")
        tm2 = [carve(77824 + i * 2048, [128, 512], F32, "tm2%d" % i) for i in range(2)]
        load_w(wout, io["w_out"][l])
        for n in range(4):
            ns = slice(512 * n, 512 * n + 512)
            for mo in range(NK):
                bk = nb([0, 1, 2, 3, 4, 5])
                for k in range(NK):
                    B.mm(bk[:], wout[:, k, 128 * mo:128 * mo + 128], merged[:, k, ns], k == 0, k == NK - 1,
                         [wout, merged_r[k]], [bk])
                B.copy("act", ot[:, mo, :], bk[:], [bk], [ot_r[mo]])
                B.act(sqo[:, mo, :], bk[:], AF.Square, [bk], [sqo_r[mo]])
            sb_ = nb([6, 7])
            for mo in range(NK):
                B.mm(sb_[:], ones[:], sqo[:, mo, :], mo == 0, mo == NK - 1, [ones, sqo_r[mo]], [sb_])
            B.act(rt2[:], sb_[:], AF.Sqrt, [sb_, epsT], [rt2], scale=1.0 / D, bias=epsT[:, 0:1])
            S.op("dve", lambda e, a=rs2, b=rt2: e.reciprocal(out=a[:], in_=b[:]), [rt2.r], [rs2.r])
            for mo in range(NK):
                t_ = tm2[mo % 2]
                B.stt(t_[:], ot[:, mo, :], vecs[:, vb + 8 + mo:vb + 9 + mo], rs2[:], ALU.mult, ALU.mult,
                      [ot_r[mo], vecs, rs2], [t_])
                B.tt("pool", xT[:, mo, ns], xT[:, mo, ns], t_[:], ALU.add, [t_, xT_r[mo][n]], [xT_r[mo][n]])
        pass
    else:
        yout = io["yT"].rearrange("(k p) t -> p k t", p=128)
        for k in range(NK):
            out_toks.append(B.dma("sp", yout[:, k, :], xT[:, k, :], xT_r[k], [], owner=Res("out%d" % k)))
    self.finish(out_toks)


Builder.main = _main


_CACHE = {}


def kernel(**inputs):
    h = prep_host(inputs)
    x = np.asarray(inputs["x"], np.float32)
    if "nc" not in _CACHE:
        b = Builder()
        b.build()
        _CACHE["nc"] = b.nc
    nc = _CACHE["nc"]
    shared = {k: h[k] for k in ("w_in", "w_glu", "w_bs", "w_ba", "w_out", "vecs", "ssm_rb", "ssm_rc", "amask", "ident")}
    in_maps = []
    for b_ in range(8):
        m = dict(shared)
        m["xT"] = np.ascontiguousarray(x[b_].T)
        in_maps.append(m)
    res = run_bass_kernel_spmd(nc, in_maps, core_ids=list(range(8)))
    out = np.stack([np.ascontiguousarray(res.results[b_]["yT"].T) for b_ in range(8)], axis=0)
    return out.astype(np.float32)
```

```python
import math
from contextlib import ExitStack

import numpy as np
import ml_dtypes

import concourse.bass as bass
import concourse.mybir as mybir
from concourse.bass_utils import run_bass_kernel_spmd

F32 = mybir.dt.float32
BF16 = mybir.dt.bfloat16
ALU = mybir.AluOpType
AF = mybir.ActivationFunctionType

L = 2048
D = 1024
NK = 8
DEPTH = 4
TS = 16
NCH = L // TS
ENGS = ("pe", "act", "dve", "pool", "sp")
STRICT = False


class Res:
    __slots__ = ("w", "r", "name", "dsem", "dcnt")

    def __init__(self, name=""):
        self.w = []
        self.r = []
        self.name = name
        self.dsem = None
        self.dcnt = 0


class Sched:
    def __init__(self, nc, es):
        self.nc = nc
        self.es = es
        self.prog = {e: [] for e in ENGS}
        self.cnt = {e: 0 for e in ENGS}
        self.sem = {e: es.enter_context(nc.semaphore("sem_" + e)) for e in ENGS if e != "sp"}
        self.known = {e: {} for e in ENGS}
        self.nsem = 0
        self.final = []

    def _waits(self, eng, reads, writes):
        toks = []
        for t in reads:
            toks += t.w
        for t in writes:
            toks += [x for x in t.w if STRICT or x[2] != eng]
            toks += [x for x in t.r if STRICT or x[2] != eng]
        need = {}
        for (sem, val, src) in toks:
            if eng == "pe" and src == "pe":
                continue
            if self.known[eng].get(id(sem), 0) >= val:
                continue
            if need.get(id(sem), (None, 0))[1] < val:
                need[id(sem)] = (sem, val)
        out = []
        for k, (sem, val) in need.items():
            self.known[eng][k] = val
            out.append((sem, val))
        return out

    def _commit(self, tok, reads, writes):
        for t in writes:
            t.w = [tok]
            t.r = []
        for t in reads:
            t.r.append(tok)
            if len(t.r) > 24:
                best = {}
                for (s, v, src) in t.r:
                    if id(s) not in best or best[id(s)][1] < v:
                        best[id(s)] = (s, v, src)
                t.r = list(best.values())

    def op(self, eng, fn, reads=(), writes=()):
        waits = self._waits(eng, reads, writes)
        self.cnt[eng] += 1
        sem = self.sem[eng]
        val = self.cnt[eng]

        def emit(e, waits=waits, fn=fn, sem=sem):
            for (s, v) in waits:
                e.wait_ge(s, v)
            fn(e).then_inc(sem, 1)

        self.prog[eng].append(emit)
        self._commit((sem, val, eng), reads, writes)

    def dma(self, q, out, in_, reads=(), writes=(), owner=None):
        waits = self._waits(q, reads, writes)
        own = owner if owner is not None else (writes[0] if writes else reads[0])
        if own.dsem is None:
            own.dsem = self.es.enter_context(self.nc.semaphore("dsem%d" % self.nsem))
            self.nsem += 1
        own.dcnt += 16
        sem, val = own.dsem, own.dcnt

        def emit(e, waits=waits, sem=sem, out=out, in_=in_):
            for (s, v) in waits:
                e.wait_ge(s, v)
            e.dma_start(out=out, in_=in_).then_inc(sem, 16)

        self.prog[q].append(emit)
        tok = (sem, val, "dma")
        for t in writes:
            t.w = [tok]
            t.r = []
        for t in reads:
            t.r.append(tok)
        return tok

    def wait_all(self, q, toks):
        def emit(e, toks=toks):
            for (s, v, _) in toks:
                e.wait_ge(s, v)
        self.prog[q].append(emit)

    def barrier(self, extra=()):
        for e in ENGS:
            waits = [(s, v) for (s, v, _) in extra]
            for o in ENGS:
                if o == "sp" or o == e or self.cnt[o] == 0:
                    continue
                if self.known[e].get(id(self.sem[o]), 0) < self.cnt[o]:
                    self.known[e][id(self.sem[o])] = self.cnt[o]
                    waits.append((self.sem[o], self.cnt[o]))

            def emit(en, waits=waits):
                for (s, v) in waits:
                    en.wait_ge(s, v)
            self.prog[e].append(emit)

    def run(self):
        nc = self.nc
        with nc.Block() as block:
            @block.tensor
            def _(e):
                for f in self.prog["pe"]:
                    f(e)

            @block.scalar
            def _(e):
                for f in self.prog["act"]:
                    f(e)

            @block.vector
            def _(e):
                for f in self.prog["dve"]:
                    f(e)

            @block.gpsimd
            def _(e):
                for f in self.prog["pool"]:
                    f(e)

            @block.sync
            def _(e):
                for f in self.prog["sp"]:
                    f(e)


def mkap(t, offset, dims):
    return bass.AP(tensor=t, offset=offset, ap=[list(d) for d in dims])


def _in_col_perm():
    cols = list(range(0, 1024))
    for j in range(4):
        for gi in range(3):
            for base in (1024, 2560, 4096):
                c0 = base + (gi * 4 + j) * 128
                cols += list(range(c0, c0 + 128))
        cols += list(range(5632 + j * 128, 5632 + (j + 1) * 128))
    cols += list(range(6144, 8192))
    assert len(cols) == 8192 and len(set(cols)) == 8192
    return np.asarray(cols)


def prep_host(inp):
    f32 = np.float32
    out = {}
    out["w_in"] = np.ascontiguousarray(np.asarray(inp["w_in"], f32)[:, :, _in_col_perm()])
    out["w_glu"] = np.ascontiguousarray(np.asarray(inp["w_glu"], f32))
    out["w_bs"] = np.ascontiguousarray(np.asarray(inp["w_branch_s"], f32))
    out["w_ba"] = np.ascontiguousarray(np.asarray(inp["w_branch_a"], f32))
    out["w_out"] = np.ascontiguousarray(np.asarray(inp["w_out"], f32))
    vecs = np.zeros((128, DEPTH * 24), f32)
    for l in range(DEPTH):
        vecs[:, l * 24 + 0:l * 24 + 8] = np.asarray(inp["pre_norm_g"], f32)[l].reshape(8, 128).T
        vecs[:, l * 24 + 8:l * 24 + 16] = np.asarray(inp["post_norm_g"], f32)[l].reshape(8, 128).T
        vecs[:, l * 24 + 16:l * 24 + 20] = np.asarray(inp["d_skip"], f32)[l].reshape(4, 128).T
        vecs[:, l * 24 + 20:l * 24 + 24] = np.asarray(inp["b_glu"], f32)[l].reshape(4, 128).T
    out["vecs"] = vecs
    lre = np.asarray(inp["lambda_re"], f32)
    lim = np.asarray(inp["lambda_im"], f32)
    ldt = np.asarray(inp["log_dt"], f32)
    bre = np.asarray(inp["b_re"], f32)
    bim = np.asarray(inp["b_im"], f32)
    cre = np.asarray(inp["c_re"], f32)
    cim = np.asarray(inp["c_im"], f32)
    rb = np.zeros((DEPTH, 128, 5, 4, 128), f32)
    rc = np.zeros((DEPTH, 128, 48 + 512), f32)
    for l in range(DEPTH):
        lam_g = lre[l].reshape(4, 4, 2, 64)
        lim_g = lim[l].reshape(4, 4, 2, 64)
        ldt_g = np.broadcast_to(ldt[l].reshape(4, 4, 2, 1), (4, 4, 2, 64))
        for ptl in range(4):
            rows = slice(32 * ptl, 32 * ptl + 32)
            rb[l, rows, 0] = lam_g[:, ptl].reshape(4, 128)[None]
            rb[l, rows, 1] = lim_g[:, ptl].reshape(4, 128)[None]
            rb[l, rows, 2] = ldt_g[:, ptl].reshape(4, 128)[None]
            for h in range(2):
                for j in range(4):
                    g = 8 * j + 2 * ptl + h
                    r0 = 32 * ptl + 16 * h
                    rb[l, r0:r0 + 16, 3, j, 64 * h:64 * h + 64] = bre[l, g].T
                    rb[l, r0:r0 + 16, 4, j, 64 * h:64 * h + 64] = bim[l, g].T
        rc[l, :, 0:16] = lre[l].reshape(16, 128).T
        rc[l, :, 16:32] = lim[l].reshape(16, 128).T
        rc[l, :, 32:48] = np.broadcast_to(ldt[l].reshape(16, 2, 1), (16, 2, 64)).reshape(16, 128).T
        cc = cre[l].reshape(16, 2, 16, 64).transpose(1, 3, 0, 2).reshape(128, 16 * 16)
        ci = cim[l].reshape(16, 2, 16, 64).transpose(1, 3, 0, 2).reshape(128, 16 * 16)
        rc[l, :, 48:48 + 256] = cc
        rc[l, :, 48 + 256:48 + 512] = ci
    out["ssm_rb"] = rb.reshape(DEPTH, 128, 5 * 512)
    out["ssm_rc"] = rc
    am = np.zeros((128, 256), f32)
    kk = np.arange(128)[:, None]
    qq = np.arange(128)[None, :]
    am[:, 0:128] = (qq >= kk)
    am[:, 128:256] = (qq <= kk)
    out["amask"] = am.astype(ml_dtypes.bfloat16)
    out["ident"] = np.eye(128, dtype=f32).astype(ml_dtypes.bfloat16)
    return out


class TL:
    def __init__(self, t, name):
        self.t = t
        self.r = Res(name)

    def __getitem__(self, idx):
        return self.t[idx]


class Builder:
    def __init__(self, n_layers=DEPTH, debug=None):
        self.n_layers = n_layers
        self.debug = debug or {}
        self.nc = bass.Bass("TRN2", target_bir_lowering=False)
        self.es = ExitStack()
        self.S = Sched(self.nc, self.es)
        self.uid = 0

    def sb(self, es, shape, dtype, name=None):
        self.uid += 1
        nm = (name or "t") + "_%d" % self.uid
        return TL(es.enter_context(self.nc.sbuf_tensor(nm, list(shape), dtype)), nm)

    def ps(self, es, shape, dtype=F32, name=None):
        self.uid += 1
        nm = (name or "p") + "_%d" % self.uid
        return TL(es.enter_context(self.nc.psum_tensor(nm, list(shape), dtype)), nm)

    @staticmethod
    def _res(lst):
        return [x.r if isinstance(x, TL) else x for x in lst]

    def tt(self, eng, out, a, b, op, R, W):
        self.S.op(eng, lambda e: e.tensor_tensor(out=out, in0=a, in1=b, op=op), self._res(R), self._res(W))

    def ts(self, eng, out, a, s1, op0, R, W, s2=None, op1=None):
        if op1 is None:
            self.S.op(eng, lambda e: e.tensor_scalar(out=out, in0=a, scalar1=s1, scalar2=None, op0=op0),
                      self._res(R), self._res(W))
        else:
            self.S.op(eng, lambda e: e.tensor_scalar(out=out, in0=a, scalar1=s1, scalar2=s2, op0=op0, op1=op1),
                      self._res(R), self._res(W))

    def stt(self, out, a, scalar, b, op0, op1, R, W):
        self.S.op("dve", lambda e: e.scalar_tensor_tensor(out=out, in0=a, scalar=scalar, in1=b, op0=op0, op1=op1),
                  self._res(R), self._res(W))

    def act(self, out, a, func, R, W, scale=1.0, bias=0.0):
        self.S.op("act", lambda e: e.activation(out=out, in_=a, func=func, bias=bias, scale=scale),
                  self._res(R), self._res(W))

    def copy(self, eng, out, a, R, W):
        if eng == "act":
            self.S.op("act", lambda e: e.activation(out=out, in_=a, func=AF.Copy), self._res(R), self._res(W))
        else:
            self.S.op(eng, lambda e: e.tensor_copy(out=out, in_=a), self._res(R), self._res(W))

    def memset(self, eng, ap, val, W):
        self.S.op(eng, lambda e: e.memset(ap, val), [], self._res(W))

    def mm(self, out, lhsT, rhs, start, stop, R, W, tp=None):
        if tp is None:
            self.S.op("pe", lambda e: e.matmul(out, lhsT=lhsT, rhs=rhs, start=start, stop=stop),
                      self._res(R), self._res(W))
        else:
            self.S.op("pe", lambda e: e.matmul(out, lhsT=lhsT, rhs=rhs, start=start, stop=stop, tile_position=tp),
                      self._res(R), self._res(W))

    def dma(self, q, out, in_, R, W, owner=None):
        return self.S.dma(q, out, in_, self._res(R), self._res(W),
                          owner.r if isinstance(owner, TL) else owner)

    def cmul(self, eng, o, a, b, t1, t2, R, W, Tm):
        (orr, oi), (ar, ai), (br, bi) = o, a, b
        self.tt(eng, t1, ar, br, ALU.mult, R, Tm)
        self.tt(eng, t2, ai, bi, ALU.mult, R, Tm)
        self.tt(eng, orr, t1, t2, ALU.subtract, Tm, W)
        self.tt(eng, t1, ar, bi, ALU.mult, R, Tm)
        self.tt(eng, t2, ai, br, ALU.mult, R, Tm)
        self.tt(eng, oi, t1, t2, ALU.add, Tm, W)

    def declare_io(self):
        nc = self.nc
        nl = DEPTH
        d = {}
        d["xT"] = nc.dram_tensor("xT", [D, L], F32, kind="ExternalInput").ap()
        d["w_in"] = nc.dram_tensor("w_in", [nl, D, 8192], F32, kind="ExternalInput").ap()
        d["w_glu"] = nc.dram_tensor("w_glu", [nl, 512, 512], F32, kind="ExternalInput").ap()
        d["w_bs"] = nc.dram_tensor("w_bs", [nl, 512, D], F32, kind="ExternalInput").ap()
        d["w_ba"] = nc.dram_tensor("w_ba", [nl, 512, D], F32, kind="ExternalInput").ap()
        d["w_out"] = nc.dram_tensor("w_out", [nl, D, D], F32, kind="ExternalInput").ap()
        d["vecs"] = nc.dram_tensor("vecs", [128, nl * 24], F32, kind="ExternalInput").ap()
        d["ssm_rb"] = nc.dram_tensor("ssm_rb", [nl, 128, 2560], F32, kind="ExternalInput").ap()
        d["ssm_rc"] = nc.dram_tensor("ssm_rc", [nl, 128, 560], F32, kind="ExternalInput").ap()
        d["amask"] = nc.dram_tensor("amask", [128, 256], BF16, kind="ExternalInput").ap()
        d["ident"] = nc.dram_tensor("ident", [128, 128], BF16, kind="ExternalInput").ap()
        d["yT"] = nc.dram_tensor("yT", [D, L], F32, kind="ExternalOutput").ap()
        tk = "ExternalOutput" if self.debug.get("tables") else "Internal"
        d["tabB"] = nc.dram_tensor("tabB", [nl, 128, 16384], BF16, kind=tk).ap()
        d["tabC"] = nc.dram_tensor("tabC", [nl, 128, 16384], BF16, kind=tk).ap()
        for name, (shape, dt) in self.debug.get("outs", {}).items():
            d[name] = nc.dram_tensor(name, list(shape), dt, kind="ExternalOutput").ap()
        self.io = d

    def lam_alloc(self, es, Fd, tag):
        names = ("dt", "lr", "a", "th", "mag", "em2a", "c", "s", "t1", "t2", "lbr", "lbi")
        return {n: self.sb(es, [128, Fd], F32, tag + n) for n in names}

    def lam_math(self, m, src):
        B = self
        dt, lr, a, th, mag, em2a, c, s, t1, t2, lbr, lbi = [m[n] for n in
                                                          ("dt", "lr", "a", "th", "mag", "em2a", "c", "s", "t1", "t2", "lbr", "lbi")]
        R = src["R"]
        B.act(dt[:], src["ldt"], AF.Exp, R, [dt])
        B.ts("dve", lr[:], src["lre"], -1e-4, ALU.min, R, [lr])
        B.tt("dve", a[:], lr[:], dt[:], ALU.mult, [lr, dt], [a])
        B.tt("dve", th[:], src["lim"], dt[:], ALU.mult, R + [dt], [th])
        B.act(mag[:], a[:], AF.Exp, [a], [mag])
        B.act(em2a[:], a[:], AF.Exp, [a], [em2a], scale=-2.0)
        B.act(s[:], th[:], AF.Sin, [th], [s], scale=1.0 / 16.0)
        B.act(c[:], th[:], AF.Sin, [th, self.halfpi], [c], scale=1.0 / 16.0, bias=self.halfpi[:, 0:1])
        for _ in range(4):
            B.tt("dve", t1[:], c[:], c[:], ALU.mult, [c], [t1])
            B.tt("dve", t2[:], s[:], s[:], ALU.mult, [s], [t2])
            B.stt(s[:], c[:], 2.0, s[:], ALU.mult, ALU.mult, [c, s], [s])
            B.tt("dve", c[:], t1[:], t2[:], ALU.subtract, [t1, t2], [c])
        B.tt("dve", lbr[:], mag[:], c[:], ALU.mult, [mag, c], [lbr])
        B.tt("dve", lbi[:], mag[:], s[:], ALU.mult, [mag, s], [lbi])

    def prologue(self):
        B = self
        io = self.io
        toks = []
        with ExitStack() as es:
            mb = B.lam_alloc(es, 512, "rb")
            inpb = B.sb(es, [128, 5, 512], F32, "rbin")
            mk = lambda n: B.sb(es, [128, 512], F32, "rb" + n)
            den, nr, cr, ci, invr, invi, bbr, bbi = [mk(n) for n in ("den", "nr", "cr", "ci", "invr", "invi", "bbr", "bbi")]
            xs = [(mk("xr0"), mk("xi0")), (mk("xr1"), mk("xi1"))]
            bt = B.sb(es, [128, 4, TS, 2, 128], BF16, "bt")
            mc = B.lam_alloc(es, 16, "rc")
            inpc = B.sb(es, [128, 560], F32, "rcin")
            pw = B.sb(es, [128, 2, 16, TS], F32, "pw")
            ct = B.sb(es, [128, 16, TS, 2, 32], BF16, "ct")
            big = lambda n: B.sb(es, [128, 16, TS, 16], F32, "rc" + n)
            u1, u2, u3 = big("u1"), big("u2"), big("u3")
            B.memset("pool", ct[:], 0.0, [ct])
            for l in range(self.n_layers):
                inp = inpb
                B.dma("sp", inp[:], io["ssm_rb"][l].rearrange("p (a f) -> p a f", a=5), [], [inp])
                B.lam_math(mb, dict(lre=inp[:, 0, :], lim=inp[:, 1, :], ldt=inp[:, 2, :], R=[inp]))
                lr, lbr, lbi, em2a, t1, t2 = [mb[n] for n in ("lr", "lbr", "lbi", "em2a", "t1", "t2")]
                li = inp[:, 1, :]
                B.tt("dve", t1[:], lr[:], lr[:], ALU.mult, [lr], [t1])
                B.tt("dve", t2[:], li, li, ALU.mult, [inp], [t2])
                B.tt("dve", den[:], t1[:], t2[:], ALU.add, [t1, t2], [den])
                B.S.op("dve", lambda e, den=den: e.reciprocal(out=den[:], in_=den[:]), [den.r], [den.r])
                B.ts("dve", nr[:], lbr[:], -1.0, ALU.add, [lbr], [nr])
                B.tt("dve", t1[:], nr[:], lr[:], ALU.mult, [nr, lr], [t1])
                B.tt("dve", t2[:], lbi[:], li, ALU.mult, [lbi, inp], [t2])
                B.tt("dve", t1[:], t1[:], t2[:], ALU.add, [t1, t2], [t1])
                B.tt("dve", cr[:], t1[:], den[:], ALU.mult, [t1, den], [cr])
                B.tt("dve", t1[:], lbi[:], lr[:], ALU.mult, [lbi, lr], [t1])
                B.tt("dve", t2[:], nr[:], li, ALU.mult, [nr, inp], [t2])
                B.tt("dve", t1[:], t1[:], t2[:], ALU.subtract, [t1, t2], [t1])
                B.tt("dve", ci[:], t1[:], den[:], ALU.mult, [t1, den], [ci])
                B.tt("dve", invr[:], lbr[:], em2a[:], ALU.mult, [lbr, em2a], [invr])
                B.stt(invi[:], lbi[:], -1.0, em2a[:], ALU.mult, ALU.mult, [lbi, em2a], [invi])
                B.cmul("dve", (bbr[:], bbi[:]), (cr[:], ci[:]), (inp[:, 3, :], inp[:, 4, :]), t1[:], t2[:],
                       [cr, ci, inp], [bbr, bbi], [t1, t2])
                prev = (bbr, bbi)
                for i in range(TS):
                    cur = xs[i % 2]
                    B.cmul("dve", (cur[0][:], cur[1][:]), (prev[0][:], prev[1][:]), (invr[:], invi[:]), t1[:], t2[:],
                           [prev[0], prev[1], invr, invi], [cur[0], cur[1]], [t1, t2])
                    B.copy("act", bt[:, :, i, 0, :], cur[0][:].rearrange("p (j q) -> p j q", j=4), [cur[0]], [bt])
                    B.copy("act", bt[:, :, i, 1, :], cur[1][:].rearrange("p (j q) -> p j q", j=4), [cur[1]], [bt])
                    prev = cur
                toks.append(B.dma("sp", io["tabB"][l], bt[:].rearrange("p j i r q -> p (j i r q)"),
                                  [bt], [self.tabB_r[l]], owner=bt))
                inp = inpc
                B.dma("sp", inp[:], io["ssm_rc"][l], [], [inp])
                B.lam_math(mc, dict(lre=inp[:, 0:16], lim=inp[:, 16:32], ldt=inp[:, 32:48], R=[inp]))
                lbr, lbi, t1, t2 = [mc[n] for n in ("lbr", "lbi", "t1", "t2")]
                B.copy("dve", pw[:, 0, :, 0], lbr[:], [lbr], [pw])
                B.copy("dve", pw[:, 1, :, 0], lbi[:], [lbi], [pw])
                for i in range(1, TS):
                    B.cmul("dve", (pw[:, 0, :, i], pw[:, 1, :, i]), (pw[:, 0, :, i - 1], pw[:, 1, :, i - 1]),
                           (lbr[:], lbi[:]), t1[:], t2[:], [pw, lbr, lbi], [pw], [t1, t2])
                PL = self.PL
                B.copy("dve", PL[:, l, :, 0], pw[:, 0, :, TS - 1], [pw], [PL])
                B.copy("dve", PL[:, l, :, 1], pw[:, 1, :, TS - 1], [pw], [PL])
                for k in range(1, 7):
                    pr, pi = PL[:, l, :, 3 * k - 3], PL[:, l, :, 3 * k - 2]
                    B.tt("dve", t1[:], pr, pr, ALU.mult, [PL], [t1])
                    B.tt("dve", t2[:], pi, pi, ALU.mult, [PL], [t2])
                    B.tt("dve", PL[:, l, :, 3 * k], t1[:], t2[:], ALU.subtract, [t1, t2], [PL])
                    B.stt(PL[:, l, :, 3 * k + 1], pr, 2.0, pi, ALU.mult, ALU.mult, [PL], [PL])
                for k in range(7):
                    B.ts("dve", PL[:, l, :, 3 * k + 2], PL[:, l, :, 3 * k + 1], -1.0, ALU.mult, [PL], [PL])
                cre = inp[:, 48:304].rearrange("p (t c) -> p t c", c=16)
                cim = inp[:, 304:560].rearrange("p (t c) -> p t c", c=16)

                def bc_c(ap3):
                    return mkap(ap3.tensor, ap3.offset, [ap3.ap[0], ap3.ap[1], [0, TS], ap3.ap[2]])

                def bc_p(k):
                    a = pw[:, k, :, :]
                    return mkap(a.tensor, a.offset, [a.ap[0], a.ap[1], a.ap[2], [0, 16]])
                B.tt("dve", u1[:], bc_c(cre), bc_p(0), ALU.mult, [inp, pw], [u1])
                B.tt("dve", u2[:], bc_c(cim), bc_p(1), ALU.mult, [inp, pw], [u2])
                B.tt("dve", u3[:], u1[:], u2[:], ALU.subtract, [u1, u2], [u3])
                B.copy("act", ct[0:64, :, :, 0, 0:16], u3[0:64], [u3], [ct])
                B.copy("act", ct[64:128, :, :, 0, 16:32], u3[64:128], [u3], [ct])
                B.tt("dve", u1[:], bc_c(cre), bc_p(1), ALU.mult, [inp, pw], [u1])
                B.tt("dve", u2[:], bc_c(cim), bc_p(0), ALU.mult, [inp, pw], [u2])
                B.stt(u3[:], u1[:], -1.0, u2[:], ALU.mult, ALU.subtract, [u1, u2], [u3])
                B.copy("act", ct[0:64, :, :, 1, 0:16], u3[0:64], [u3], [ct])
                B.copy("act", ct[64:128, :, :, 1, 16:32], u3[64:128], [u3], [ct])
                toks.append(B.dma("sp", io["tabC"][l], ct[:].rearrange("p t i r c -> p (t i r c)"),
                                  [ct], [self.tabC_r[l]], owner=ct))
        self.S.barrier(toks)

    def build(self):
        B = self
        nc = self.nc
        es = self.es
        self.declare_io()
        io = self.io
        self.tabB_r = [Res("tabB%d" % l) for l in range(DEPTH)]
        self.tabC_r = [Res("tabC%d" % l) for l in range(DEPTH)]
        self.halfpi = B.sb(es, [128, 1], F32, "halfpi")
        B.memset("dve", self.halfpi[:], math.pi / 2.0, [self.halfpi])
        self.PL = B.sb(es, [128, DEPTH, 16, 24], F32, "PL")
        self.prologue()
        if self.debug.get("tables"):
            self.finish([])
            return
        self.main()

    def finish(self, out_toks):
        self.S.wait_all("sp", out_toks)
        self.S.barrier(out_toks)
        self.S.run()


class V:
    def __init__(self, ap, name=""):
        self.ap = ap
        self.r = Res(name)
        self.lo = self.hi = None

    def __getitem__(self, idx):
        return self.ap[idx]


def _res_of(lst):
    return [x.r if hasattr(x, "r") and not isinstance(x, Res) else x for x in lst]


Builder._res = staticmethod(_res_of)


def dil_ap(base, r, col0, ncols):
    t, off, pdim, st = base.tensor, base.offset, list(base.ap[0]), base.ap[-1][0]
    assert len(base.ap) == 2
    nsub = L // r
    c0, i0 = divmod(col0, nsub)
    if r == 1:
        dims, o = [[st, ncols]], col0
    elif i0 + ncols <= nsub:
        dims, o = [[r * st, ncols]], c0 + r * i0
    else:
        assert i0 == 0 and ncols % nsub == 0
        dims, o = [[st, ncols // nsub], [r * st, nsub]], c0
    return mkap(t, off + o * st, [pdim] + dims)


def deint_dst(base, r, n):
    if r == 1:
        return base[:, 512 * n:512 * n + 512]
    nsub = L // r
    t, off, pdim, st = base.tensor, base.offset, list(base.ap[0]), base.ap[-1][0]
    return mkap(t, off + (512 * n // r) * st, [pdim, [st, 512 // r], [nsub * st, r]])


def like(bank_ap, ap):
    if len(ap.ap) == 3:
        return bank_ap.rearrange("p (a b) -> p a b", a=ap.ap[1][1])
    return bank_ap


AR_BYTES = 88064


def _main(self):
    B = self
    es = self.es
    io = self.io
    S = self.S
    nc = self.nc
    dbg = self.debug
    xT = B.sb(es, [128, NK, L], F32, "xT")
    xT_r = [[Res("xT%d_%d" % (k, n)) for n in range(4)] for k in range(NK)]
    hT = B.sb(es, [128, NK, L], BF16, "hT")
    hT_r = [Res("hT%d" % k) for k in range(NK)]
    vecs = B.sb(es, [128, DEPTH * 24], F32, "vecs")
    amask = B.sb(es, [128, 256], BF16, "amask")
    ident = B.sb(es, [128, 128], BF16, "ident")
    ones = B.sb(es, [128, 128], BF16, "ones")
    epsT = B.sb(es, [128, 1], F32, "eps")
    wbuf = [B.sb(es, [128, NK, 512], BF16, "wbuf%d" % i) for i in range(2)]
    arena = B.sb(es, [128, AR_BYTES // 2], BF16, "arena")
    banks = [B.ps(es, [128, 512], F32, "bank%d" % i) for i in range(8)]
    self.wi = 0

    def carve(off, shape, dtype, name):
        nel = int(np.prod(shape[1:]))
        esz = 2 if dtype == BF16 else 4
        assert off % 4 == 0 and off + nel * esz <= AR_BYTES, (name, off, nel * esz)
        a = arena.t[:, off // 2: off // 2 + nel * esz // 2]
        if dtype == F32:
            a = a.bitcast(F32)
        if len(shape) == 3:
            a = a.rearrange("p (a b) -> p a b", a=shape[1])
        elif len(shape) == 4:
            a = a.rearrange("p (a b c) -> p a b c", a=shape[1], b=shape[2])
        elif len(shape) == 5:
            a = a.rearrange("p (a b c d) -> p a b c d", a=shape[1], b=shape[2], c=shape[3])
        v = V(a, name)
        region(off, off + nel * esz, [v.r])
        v.lo, v.hi = off, off + nel * esz
        return v

    regs = []

    def region(lo, hi, res_list):
        inherit = []
        keep = []
        for (l0, h0, rl) in regs:
            if l0 < hi and lo < h0:
                for r_ in rl:
                    inherit += r_.w + r_.r
                if lo <= l0 and h0 <= hi:
                    continue
            keep.append((l0, h0, rl))
        regs[:] = keep
        best = {}
        for (sm, vl, src) in inherit:
            if id(sm) not in best or best[id(sm)][1] < vl:
                best[id(sm)] = (sm, vl, "alias")
        for r_ in res_list:
            r_.r = list(best.values())
        regs.append((lo, hi, res_list))

    def reslist(v, names):
        rl = [Res(n) for n in names]
        region(v.lo, v.hi, rl)
        return rl

    B.memset("dve", ones[:], 1.0, [ones])
    B.memset("dve", epsT[:], 1e-6, [epsT])
    B.dma("sp", vecs[:], io["vecs"], [], [vecs])
    B.dma("sp", amask[:], io["amask"], [], [amask])
    B.dma("sp", ident[:], io["ident"], [], [ident])
    xin = io["xT"].rearrange("(k p) t -> p k t", p=128)
    for k in range(NK):
        B.dma("sp", xT[:, k, :], xin[:, k, :], [], xT_r[k], owner=xT_r[k][0])

    out_toks = []

    def dump(name, ap, R):
        out_toks.append(B.dma("sp", io[name], ap, R, [], owner=Res("dump_" + name)))

    def load_wblock(l, col0, ncols):
        wb = wbuf[self.wi % 2]
        self.wi += 1
        src = io["w_in"][l][:, col0:col0 + ncols].rearrange("(k p) c -> p k c", p=128)
        B.dma("pool", wb[:, :, 0:ncols], src, [], [wb])
        return wb

    def load_w(dst, src2d):
        B.dma("pool", dst[:], src2d.rearrange("(k p) c -> p k c", p=128), [], [dst])

    bank_ctr = [0]

    def nb(lst):
        b = banks[lst[bank_ctr[0] % len(lst)]]
        bank_ctr[0] += 1
        return b

    def hk(k):
        return hT.t[:, k, :]

    for l in range(self.n_layers):
        vb = l * 24
        sq = [carve(i * 8192, [128, NK, 512], BF16, "sq%d" % i) for i in range(2)]
        rt = [carve(16384 + i * 2048, [128, 512], F32, "rt%d" % i) for i in range(2)]
        rs = [carve(20480 + i * 2048, [128, 512], F32, "rs%d" % i) for i in range(2)]
        for n in range(4):
            ns = slice(n * 512, (n + 1) * 512)
            s_, rt_, rs_ = sq[n % 2], rt[n % 2], rs[n % 2]
            for k in range(NK):
                B.act(s_[:, k, :], xT[:, k, ns], AF.Square, [xT_r[k][n]], [s_])
            bk = nb([6, 7])
            for k in range(NK):
                B.mm(bk[:], ones[:], s_[:, k, :], k == 0, k == NK - 1, [ones, s_], [bk])
            B.act(rt_[:], bk[:], AF.Sqrt, [bk, epsT], [rt_], scale=1.0 / D, bias=epsT[:, 0:1])
            S.op("dve", lambda e, a=rs_, b=rt_: e.reciprocal(out=a[:], in_=b[:]), [rt_.r], [rs_.r])
            for k in range(NK):
                B.stt(hT[:, k, ns], xT[:, k, ns], vecs[:, vb + k:vb + k + 1], rs_[:], ALU.mult, ALU.mult,
                      [xT_r[k][n], vecs, rs_], [hT_r[k]])
        if dbg.get("stop") == "P0":
            dump("dbg_h", hT[:], hT_r)
            break
        pass
        yT = carve(32768, [128, 4, L], BF16, "yT")
        _fl = reslist(yT, ["yT%d_%d" % (k, n) for k in range(4) for n in range(4)])
        yT_r = [[_fl[4 * k + n] for n in range(4)] for k in range(4)]
        r32 = carve(0, [128, 2, TS, 128], F32, "r32")
        rbf = carve(16384, [128, 2, TS, 128], BF16, "rbf")
        Bt = carve(24576, [128, TS, 2, 128], BF16, "Bt")
        Ct = carve(49152, [128, 5120], BF16, "Ct")
        CtA = Ct[:, 0:3072].rearrange("p (t i r c) -> p t i r c", t=3, i=TS, r=2)
        CtB = Ct[:, 3072:5120].rearrange("p (i r c) -> p i r c", i=TS, r=2)
        uT = carve(59392, [128, L], BF16, "uT")
        wglu = carve(63488, [128, 4, 512], BF16, "wglu")
        Sab = [carve(67584 + i * 1536, [128, 2, 192], F32, "S%d" % i) for i in range(2)]
        zt = carve(70656, [128, 2, 192], F32, "zt")
        tq = carve(72192, [128, 1, 128], F32, "tq")
        tq2 = carve(72704, [128, 1, 128], F32, "tq2")
        Sbf = carve(73216, [128, 2, 128], BF16, "Sbf")
        ddt = [carve(73728 + i * 256, [128, 128], BF16, "dd%d" % i) for i in range(2)]
        gw = [carve(77824 + i * 2048, [128, 512], F32, "gw%d" % i) for i in range(2)]
        gs = [carve(81920 + i * 2048, [128, 512], F32, "gs%d" % i) for i in range(2)]
        gz = [carve(86016 + i * 1024, [128, 512], BF16, "gz%d" % i) for i in range(2)]
        for t_ in (Sab[0], Sab[1], zt):
            B.memset("dve", t_[:, :, 0:64], 0.0, [t_])
        Ctz = V(CtB[:, :, :, 0:32], "Ctz")
        region(Ct.lo, Ct.hi, [Ctz.r])
        B.memset("pool", Ctz[:], 0.0, [Ctz])
        B.memset("dve", Sbf[:], 0.0, [Sbf])
        wbU = load_wblock(l, 0, 512)
        wbZ = load_wblock(l, 512, 512)
        load_w(wglu, io["w_glu"][l])
        PLl = self.PL
        def prelude_u(j):
            for n in range(4):
                bk = nb([6, 7])
                for k in range(NK):
                    B.mm(bk[:], wbU[:, k, 128 * j:128 * j + 128], hT[:, k, 512 * n:512 * n + 512],
                         k == 0, k == NK - 1, [wbU, hT_r[k]], [bk])
                B.copy("act", deint_dst(uT[:], 16, n), bk[:].rearrange("p (a b) -> p a b", b=16), [bk], [uT])
            B.dma("sp", Bt[:].rearrange("p i r q -> p (i r q)"), io["tabB"][l][:, 4096 * j:4096 * (j + 1)],
                  [self.tabB_r[l]], [Bt])

        def prelude_c(j):
            B.dma("sp", Ct[:, 0:3072], io["tabC"][l][:, 4096 * j:4096 * j + 3072], [self.tabC_r[l]], [Ct])
            B.dma("sp", CtB[:, :, :, 32:64],
                  io["tabC"][l][:, 4096 * j + 3072:4096 * (j + 1)].rearrange("p (i r c) -> p i r c", i=TS, r=2),
                  [self.tabC_r[l]], [Ct])

        def xt_chain(ptl):
            rows = slice(32 * ptl, 32 * ptl + 32)
            for ih in range(8):
                xb = nb([4, 5])
                xv = xb[:].rearrange("p (r i c) -> p r i c", r=2, i=2)
                for ri in range(2):
                    for il in range(2):
                        i = 2 * ih + il
                        B.mm(xv[:, ri, il, :], Bt[rows, i, ri, :], uT[rows, i * 128:(i + 1) * 128],
                             True, True, [Bt, uT], [xb], tp=(32 * ptl, 0))
                for il in range(2):
                    i = 2 * ih + il
                    if i == 0:
                        B.copy("dve", r32[:, :, 0, :], xv[:, :, 0, :], [xb], [r32])
                    else:
                        B.tt("dve", r32[:, :, i, :], r32[:, :, i - 1, :], xv[:, :, il, :], ALU.add,
                             [xb, r32], [r32])
                    yield


        prelude_u(0)
        prelude_c(0)
        for _ in xt_chain(0):
            pass
        for j in dbg.get("js", range(4)):
            dd = ddt[j % 2]
            B.ts("dve", dd[:], ident[:], vecs[:, vb + 16 + j:vb + 17 + j], ALU.mult, [ident, vecs], [dd])
            for ib in range(4):
                B.mm(banks[ib][:], dd[:], uT[:, 512 * ib:512 * ib + 512], True, False, [dd, uT], [banks[ib]])
            def l2(ptl):
                pt = 4 * j + ptl
                B.copy("act", rbf[:], r32[:], [r32], [rbf])
                pl = lambda c: PLl[:, l, pt, c:c + 1]
                rr, rim = r32[:, 0, TS - 1, :], r32[:, 1, TS - 1, :]
                B.ts("dve", tq[:, 0, :], rim, pl(1), ALU.mult, [r32, PLl], [tq])
                B.stt(zt[:, 0, 64:192], rr, pl(0), tq[:, 0, :], ALU.mult, ALU.subtract, [r32, PLl, tq], [zt])
                B.ts("dve", tq2[:, 0, :], rr, pl(1), ALU.mult, [r32, PLl], [tq2])
                B.stt(zt[:, 1, 64:192], rim, pl(0), tq2[:, 0, :], ALU.mult, ALU.add, [r32, PLl, tq2], [zt])
                yield "z"
                old = zt
                for kk in range(7):
                    d = 1 << kk
                    new = Sab[kk % 2]
                    a_, b_, nb_ = pl(3 * kk), pl(3 * kk + 1), pl(3 * kk + 2)
                    B.stt(new[:, :, 64:192], old[:, :, 64 - d:192 - d], a_, old[:, :, 64:192], ALU.mult, ALU.add, [old, PLl], [new])
                    yield
                    B.stt(new[:, 0, 64:192], old[:, 1, 64 - d:192 - d], nb_, new[:, 0, 64:192], ALU.mult, ALU.add, [old, new, PLl], [new])
                    yield
                    B.stt(new[:, 1, 64:192], old[:, 0, 64 - d:192 - d], b_, new[:, 1, 64:192], ALU.mult, ALU.add, [old, new, PLl], [new])
                    yield
                    old = new
                B.copy("dve", Sbf[:, :, 1:128], old[:, :, 64:191], [old], [Sbf])
                yield

            def interleave(g1, g2):
                for _ in g1:
                    if g2 is not None:
                        if next(g2, "end") == "end":
                            g2 = None
                if g2 is not None:
                    for _ in g2:
                        pass

            def readout(ptl):
                rows = slice(32 * ptl, 32 * ptl + 32)
                for i in range(TS):
                    yb = banks[i // 4]
                    cs_ = slice((i % 4) * 128, (i % 4) * 128 + 128)
                    if ptl == 3:
                        o_ = yb[64:128, cs_]
                        tp = (0, 64)
                        lh = [CtB[:, i, rr_, :] for rr_ in range(2)]
                        RC_ = [Ct, Ctz]
                    else:
                        o_ = yb[rows, cs_]
                        tp = (0, 32 * ptl)
                        lh = [CtA[:, ptl, i, rr_, :] for rr_ in range(2)]
                        RC_ = [Ct]
                    B.mm(o_, lh[0], rbf[:, 0, i, :], False, False, RC_ + [rbf], [yb], tp=tp)
                    B.mm(o_, lh[1], rbf[:, 1, i, :], False, False, RC_ + [rbf], [yb], tp=tp)
                    B.mm(o_, lh[0], Sbf[:, 0, :], False, False, RC_ + [Sbf], [yb], tp=tp)
                    B.mm(o_, lh[1], Sbf[:, 1, :], False, True, RC_ + [Sbf], [yb], tp=tp)

            order = (0, 1, 3, 2)
            for idx, ptl in enumerate(order):
                g1 = l2(ptl)
                for step in g1:
                    if step == "z":
                        break
                g2 = xt_chain(order[idx + 1]) if idx + 1 < 4 else None
                interleave(g1, g2)
                if idx == 2 and j + 1 < 4:
                    prelude_u(j + 1)
                readout(ptl)
            if j + 1 < 4:
                prelude_c(j + 1)
                for _ in xt_chain(order[0]):
                    pass
            for ib in range(4):
                cs = slice(512 * ib, 512 * ib + 512)
                w_, g_, yb = gw[ib % 2], gs[ib % 2], banks[ib]
                B.act(w_[:], yb[:], AF.Square, [yb], [w_])
                B.ts("pool", w_[:], w_[:], 0.044715, ALU.mult, [w_], [w_], s2=1.0, op1=ALU.add)
                B.tt("dve", w_[:], w_[:], yb[:], ALU.mult, [w_, yb], [w_])
                B.act(g_[:], w_[:], AF.Sigmoid, [w_], [g_], scale=1.5957691216057308)
                B.tt("dve", yT[:, j, cs], yb[:], g_[:], ALU.mult, [yb, g_], [yT_r[j][ib]])
        if dbg.get("stop") == "PA1":
            dump("dbg_y", yT[:], [x for row in yT_r for x in row])
            break
        szf = [carve(4096 * mo, [128, L], BF16, "szf%d" % mo) for mo in range(4)]
        for mo in range(4):
            for n in range(4):
                zb = nb([4, 5])
                for k in range(NK):
                    B.mm(zb[:], wbZ[:, k, 128 * mo:128 * mo + 128], hT[:, k, 512 * n:512 * n + 512],
                         k == 0, k == NK - 1, [wbZ, hT_r[k]], [zb])
                B.act(deint_dst(szf[mo][:], 16, n), zb[:].rearrange("p (a b) -> p a b", b=16), AF.Silu, [zb], [szf[mo]])
        for n in range(4):
            cs = slice(512 * n, 512 * n + 512)
            for mo in range(4):
                gb = banks[mo]
                for k in range(4):
                    B.mm(gb[:], wglu[:, k, 128 * mo:128 * mo + 128], yT[:, k, cs], k == 0, k == 3,
                         [wglu, yT_r[k][n]], [gb])
            for mo in range(4):
                g_ = gs[mo % 2]
                B.act(g_[:], banks[mo][:], AF.Sigmoid, [banks[mo], vecs], [g_], bias=vecs[:, vb + 20 + mo:vb + 21 + mo])
                B.tt("dve", g_[:], g_[:], szf[mo][:, cs], ALU.mult, [g_, szf[mo]], [g_])
                B.tt("dve", yT[:, mo, cs], yT[:, mo, cs], g_[:], ALU.mult, [yT_r[mo][n], g_], [yT_r[mo][n]])
        if dbg.get("stop") == "PA":
            dump("dbg_y", yT[:], [x for row in yT_r for x in row])
            break
        pass
        merged = carve(0, [128, NK, L], BF16, "merged")
        merged_r = reslist(merged, ["mg%d" % m for m in range(NK)])
        wbs = carve(49152, [128, 4, D], BF16, "wbs")
        sgg = [carve(57344 + i * 4096, [128, L], BF16, "sgg%d" % i) for i in range(2)]
        load_w(wbs, io["w_bs"][l])
        for m in range(NK):
            if m % 4 == 0:
                wb = load_wblock(l, 6144 + 512 * (m // 4), 512)
            sg_ = sgg[m % 2]
            for n in range(4):
                bk = nb([6, 7])
                for k in range(NK):
                    B.mm(bk[:], wb[:, k, 128 * (m % 4):128 * (m % 4) + 128], hT[:, k, 512 * n:512 * n + 512],
                         k == 0, k == NK - 1, [wb, hT_r[k]], [bk])
                B.act(sg_[:, 512 * n:512 * n + 512], bk[:], AF.Sigmoid, [bk], [sg_])
            for n2 in range(4):
                bk = nb([0, 1, 2, 3])
                for k in range(4):
                    B.mm(bk[:], wbs[:, k, 128 * m:128 * m + 128], yT[:, k, 512 * n2:512 * n2 + 512], k == 0, k == 3,
                         [wbs, yT_r[k][n2]], [bk])
                dst = dil_ap(merged[:, m, :], 16, 512 * n2, 512)
                B.tt("dve", dst, like(bk[:], dst), dil_ap(sg_[:], 16, 512 * n2, 512), ALU.mult,
                     [bk, sg_], [merged_r[m]])
        if dbg.get("stop") == "PA2":
            dump("dbg_m", merged[:], merged_r)
            break
        pass
        yaT = carve(32768, [128, 4, L], BF16, "yaT")
        yaT_r = reslist(yaT, ["ya%d" % j for j in range(4)])
        qT = carve(49152, [128, L], BF16, "qT")
        kT = carve(53248, [128, L], BF16, "kT")
        vd = carve(57344, [128, 16, 128], BF16, "vd")
        etm = [carve(61440 + i * 512, [128, 256], BF16, "etm%d" % i) for i in range(6)]
        ett = [carve(64512 + i * 512, [128, 256], BF16, "ett%d" % i) for i in range(4)]
        accn = carve(66560, [128, L], F32, "accn")
        accd = carve(74752, [128, L], F32, "accd")
        sza = carve(82944, [128, L], BF16, "sza")
        for j in range(4):
            for gi, r in enumerate((1, 4, 16)):
                ncols = 512 if gi == 2 else 384
                wb = load_wblock(l, 1024 + 1280 * j + 384 * gi, ncols)
                nsub = L // r
                nblk = nsub // 128
                for (c_in, dst) in ((0, qT), (128, kT)):
                    for n in range(4):
                        bk = nb([6, 7])
                        for k in range(NK):
                            B.mm(bk[:], wb[:, k, c_in:c_in + 128], hT[:, k, 512 * n:512 * n + 512], k == 0, k == NK - 1,
                                 [wb, hT_r[k]], [bk])
                        src_ = bk[:] if r == 1 else bk[:].rearrange("p (a b) -> p a b", b=r)
                        B.copy("act", deint_dst(dst[:], r, n), src_, [bk], [dst])
                for Bq in range(4):
                    bk = nb([6, 7])
                    for bl in range(4):
                        Bk = 4 * Bq + bl
                        for k in range(NK):
                            B.mm(bk[:, bl * 128:bl * 128 + 128], dil_ap(hk(k), r, 128 * Bk, 128), wb[:, k, 256:384],
                                 k == 0, k == NK - 1, [wb, hT_r[k]], [bk])
                    B.copy("act", vd[:, 4 * Bq:4 * Bq + 4, :], bk[:].rearrange("p (a b) -> p a b", a=4), [bk], [vd])
                def score(Bk):
                    c_, blk = divmod(Bk, nblk)
                    nq = 256 if blk < nblk - 1 else 128
                    sb_ = banks[4 + (Bk // 2) % 2]
                    reg = sb_[:, 256 * (Bk % 2):256 * (Bk % 2) + nq]
                    B.mm(reg, kT[:, 128 * Bk:128 * Bk + 128], qT[:, 128 * Bk:128 * Bk + nq], True, True, [kT, qT], [sb_])
                    et_, em_ = ett[Bk % 4], etm[Bk % 6]
                    B.act(et_[:, :nq], reg, AF.Exp, [sb_], [et_], scale=1.0 / math.sqrt(128.0))
                    B.tt("dve", em_[:, :nq], et_[:, :nq], amask[:, :nq], ALU.mult, [et_, amask], [em_])

                def pv(Bk):
                    c_, blk = divmod(Bk, nblk)
                    em_ = etm[Bk % 6]
                    pvb, dnb = banks[(Bk // 4) % 2], banks[2 + (Bk // 4) % 2]
                    cs = slice((Bk % 4) * 128, (Bk % 4) * 128 + 128)
                    for (bank_, isden) in ((pvb, False), (dnb, True)):
                        l0 = ones[:] if isden else vd[:, Bk, :]
                        R0 = [ones] if isden else [vd]
                        B.mm(bank_[:, cs], l0, em_[:, 0:128], True, blk == 0, R0 + [em_], [bank_])
                        if blk > 0:
                            emp = etm[(Bk - 1) % 6]
                            l1 = ones[:] if isden else vd[:, Bk - 1, :]
                            B.mm(bank_[:, cs], l1, emp[:, 128:256], False, True, R0 + [emp], [bank_])
                    if Bk % 4 == 3:
                        for (bank_, acc) in ((pvb, accn), (dnb, accd)):
                            dst = dil_ap(acc[:], r, 512 * (Bk // 4), 512)
                            if gi == 0:
                                B.copy("act" if acc is accd else "dve", dst, like(bank_[:], dst), [bank_], [acc])
                            else:
                                B.tt("dve", dst, dst, like(bank_[:], dst), ALU.add, [bank_, acc], [acc])

                LOOK = 3
                for Bk in range(16 + LOOK):
                    if Bk < 16:
                        score(Bk)
                    if Bk >= LOOK:
                        pv(Bk - LOOK)
            for n in range(4):
                bk = nb([6, 7])
                for k in range(NK):
                    B.mm(bk[:], wb[:, k, 384:512], hT[:, k, 512 * n:512 * n + 512], k == 0, k == NK - 1,
                         [wb, hT_r[k]], [bk])
                B.act(sza[:, 512 * n:512 * n + 512], bk[:], AF.Silu, [bk], [sza])
            S.op("dve", lambda e, a=accd: e.reciprocal(out=a[:], in_=a[:]), [accd.r], [accd.r])
            B.tt("dve", accn[:], accn[:], accd[:], ALU.mult, [accn, accd], [accn])
            B.tt("dve", yaT[:, j, :], accn[:], sza[:], ALU.mult, [accn, sza], [yaT_r[j]])
        if dbg.get("stop") == "PB":
            dump("dbg_ya", yaT[:], yaT_r)
            break
        pass
        wba = carve(49152, [128, 4, D], BF16, "wba")
        sgg = [carve(57344 + i * 4096, [128, L], BF16, "sgga%d" % i) for i in range(2)]
        tmm = [carve(65536 + i * 2048, [128, 512], F32, "tmm%d" % i) for i in range(2)]
        load_w(wba, io["w_ba"][l])
        for m in range(NK):
            if m % 4 == 0:
                wb = load_wblock(l, 7168 + 512 * (m // 4), 512)
            sg_ = sgg[m % 2]
            for n in range(4):
                bk = nb([6, 7])
                for k in range(NK):
                    B.mm(bk[:], wb[:, k, 128 * (m % 4):128 * (m % 4) + 128], hT[:, k, 512 * n:512 * n + 512],
                         k == 0, k == NK - 1, [wb, hT_r[k]], [bk])
                B.act(sg_[:, 512 * n:512 * n + 512], bk[:], AF.Sigmoid, [bk], [sg_])
            for n in range(4):
                ns = slice(512 * n, 512 * n + 512)
                bk = nb([0, 1, 2, 3])
                for k in range(4):
                    B.mm(bk[:], wba[:, k, 128 * m:128 * m + 128], yaT[:, k, ns], k == 0, k == 3,
                         [wba, yaT_r[k]], [bk])
                t_ = tmm[n % 2]
                B.tt("dve", t_[:], bk[:], sg_[:, ns], ALU.mult, [bk, sg_], [t_])
                B.tt("dve", merged[:, m, ns], merged[:, m, ns], t_[:], ALU.add, [t_, merged_r[m]], [merged_r[m]])
        if dbg.get("stop") == "PC2":
            dump("dbg_m", merged[:], merged_r)
            break
        pass
        wout = carve(32768, [128, NK, D], BF16, "wout")
        ot = carve(49152, [128, NK, 512], F32, "ot")
        sqo = carve(65536, [128, NK, 512], BF16, "sqo")
        ot_r = reslist(ot, ["ot%d" % m for m in range(NK)])
        sqo_r = reslist(sqo, ["sqo%d" % m for m in range(NK)])
        rt2 = carve(73728, [128, 512], F32, "rt2")
        rs2 = carve(75776, [128, 512], F32, "rs2")
        tm2 = [carve(77824 + i * 2048, [128, 512], F32, "tm2%d" % i) for i in range(2)]
        load_w(wout, io["w_out"][l])
        for n in range(4):
            ns = slice(512 * n, 512 * n + 512)
            for mo in range(NK):
                bk = nb([0, 1, 2, 3, 4, 5])
                for k in range(NK):
                    B.mm(bk[:], wout[:, k, 128 * mo:128 * mo + 128], merged[:, k, ns], k == 0, k == NK - 1,
                         [wout, merged_r[k]], [bk])
                B.copy("act", ot[:, mo, :], bk[:], [bk], [ot_r[mo]])
                B.act(sqo[:, mo, :], bk[:], AF.Square, [bk], [sqo_r[mo]])
            sb_ = nb([6, 7])
            for mo in range(NK):
                B.mm(sb_[:], ones[:], sqo[:, mo, :], mo == 0, mo == NK - 1, [ones, sqo_r[mo]], [sb_])
            B.act(rt2[:], sb_[:], AF.Sqrt, [sb_, epsT], [rt2], scale=1.0 / D, bias=epsT[:, 0:1])
            S.op("dve", lambda e, a=rs2, b=rt2: e.reciprocal(out=a[:], in_=b[:]), [rt2.r], [rs2.r])
            for mo in range(NK):
                t_ = tm2[mo % 2]
                B.stt(t_[:], ot[:, mo, :], vecs[:, vb + 8 + mo:vb + 9 + mo], rs2[:], ALU.mult, ALU.mult,
                      [ot_r[mo], vecs, rs2], [t_])
                B.tt("pool", xT[:, mo, ns], xT[:, mo, ns], t_[:], ALU.add, [t_, xT_r[mo][n]], [xT_r[mo][n]])
        pass
    else:
        yout = io["yT"].rearrange("(k p) t -> p k t", p=128)
        for k in range(NK):
            out_toks.append(B.dma("sp", yout[:, k, :], xT[:, k, :], xT_r[k], [], owner=Res("out%d" % k)))
    self.finish(out_toks)


Builder.main = _main


_CACHE = {}


def kernel(**inputs):
    h = prep_host(inputs)
    x = np.asarray(inputs["x"], np.float32)
    if "nc" not in _CACHE:
        b = Builder()
        b.build()
        _CACHE["nc"] = b.nc
    nc = _CACHE["nc"]
    shared = {k: h[k] for k in ("w_in", "w_glu", "w_bs", "w_ba", "w_out", "vecs", "ssm_rb", "ssm_rc", "amask", "ident")}
    in_maps = []
    for b_ in range(8):
        m = dict(shared)
        m["xT"] = np.ascontiguousarray(x[b_].T)
        in_maps.append(m)
    res = run_bass_kernel_spmd(nc, in_maps, core_ids=list(range(8)))
    out = np.stack([np.ascontiguousarray(res.results[b_]["yT"].T) for b_ in range(8)], axis=0)
    return out.astype(np.float32)
```
